# Optimizing a Trainium2 kernel written in Bass

```python
import jax, jax.numpy as jnp
from jax import lax
import numpy as np

D_MODEL = 1024
BATCH = 32
SEQ = 2048
DEPTH = 4

N_A_LAYERS = DEPTH // 2
N_B_LAYERS = DEPTH - N_A_LAYERS
LRU_WIDTH = D_MODEL
LRU_HEADS = 8
LRU_BLOCK = LRU_WIDTH // LRU_HEADS
CONV_WIDTH = 4
LRU_C = 8.0
HEAD_DIM = 64
N_HEADS = D_MODEL // HEAD_DIM
N_KV_GROUPS = 4
HEADS_PER_GROUP = N_HEADS // N_KV_GROUPS
CMP_BLOCK = 32
CMP_STRIDE = 16
CMP_HIDDEN = 256
SLC_BLOCK = 64
SLC_TOPK = 16
N_LOCAL_BLOCKS = 2
WINDOW = 512
Q_BLOCK = 128
ROPE_THETA = 10000.0
FFN_HIDDEN = ((8 * D_MODEL // 3 + 255) // 256) * 256
EPS = 1e-6
NEG = -1e30
FORCE = 1e30

kernel_name = "yoco_rglru_nsa_hybrid"


def rmsnorm(x, g):
    xf = x.astype(jnp.float32)
    y = xf * lax.rsqrt(jnp.mean(xf * xf, axis=-1, keepdims=True) + EPS)
    return (y * g.astype(jnp.float32)).astype(x.dtype)


def rope(x, pos):
    half = HEAD_DIM // 2
    freqs = jnp.power(ROPE_THETA, -jnp.arange(half, dtype=jnp.float32) / half)
    ang = pos.astype(jnp.float32)[:, None] * freqs[None, :]
    cos = jnp.cos(ang)[:, None, :]
    sin = jnp.sin(ang)[:, None, :]
    xf = x.astype(jnp.float32)
    x1, x2 = xf[..., :half], xf[..., half:]
    return jnp.concatenate([x1 * cos - x2 * sin, x1 * sin + x2 * cos], axis=-1).astype(x.dtype)


def masked_softmax(s, mask):
    s = jnp.where(mask, s, NEG)
    return jax.nn.softmax(s, axis=-1) * mask


def causal_conv(x, w, b):
    S = x.shape[1]
    xp = jnp.pad(x, ((0, 0), (CONV_WIDTH - 1, 0), (0, 0)))
    out = b
    for k in range(CONV_WIDTH):
        out = out + xp[:, k:k + S] * w[k]
    return out


def rg_lru(x, gate_w, gate_b, lam):
    B, S, R = x.shape
    xf = x.astype(jnp.float32)
    xh = xf.reshape(B, S, LRU_HEADS, LRU_BLOCK)
    gates = jnp.einsum('bshi,khij->kbshj', xh, gate_w.astype(jnp.float32)).reshape(2, B, S, R)
    gates = gates + gate_b.astype(jnp.float32)[:, None, None, :]
    r = jax.nn.sigmoid(gates[0])
    i = jax.nn.sigmoid(gates[1])
    log_a = -LRU_C * jax.nn.softplus(-lam.astype(jnp.float32)) * r
    a = jnp.exp(log_a)
    bterm = jnp.sqrt(-jnp.expm1(2.0 * log_a)) * (i * xf)

    def combine(e1, e2):
        a1, b1 = e1
        a2, b2 = e2
        return a1 * a2, a2 * b1 + b2

    _, h = lax.associative_scan(combine, (a, bterm), axis=1)
    return h.astype(x.dtype)


def recurrent_block(h, norm_g, w_in, conv_w, conv_b, gate_w, gate_b, lam, w_out):
    u = rmsnorm(h, norm_g)
    z = u @ w_in
    y = jax.nn.gelu(z[..., :LRU_WIDTH])
    xr = causal_conv(z[..., LRU_WIDTH:], conv_w, conv_b)
    hr = rg_lru(xr, gate_w, gate_b, lam)
    return (y * hr) @ w_out


def swiglu(h, norm_g, w_in, w_out):
    u = rmsnorm(h, norm_g)
    gu = u @ w_in
    return (jax.nn.silu(gu[..., :FFN_HIDDEN]) * gu[..., FFN_HIDDEN:]) @ w_out


def compress(t, pos_emb, w1, b1, w2, b2):
    B, S = t.shape[0], t.shape[1]
    n_cmp = (S - CMP_BLOCK) // CMP_STRIDE + 1
    idx = jnp.arange(n_cmp)[:, None] * CMP_STRIDE + jnp.arange(CMP_BLOCK)[None, :]
    blocks = t[:, idx] + pos_emb[:, None, :]
    blocks = blocks.transpose(0, 1, 3, 2, 4).reshape(B, n_cmp, N_KV_GROUPS, CMP_BLOCK * HEAD_DIM)
    return jax.nn.gelu(blocks @ w1 + b1) @ w2 + b2


def shared_kv(h, kv_norm, kv_w, k_norm, cmp_pos, cmp_w1, cmp_b1, cmp_w2, cmp_b2):
    B, S, _ = h.shape
    u = rmsnorm(h, kv_norm)
    kv = (u @ kv_w).reshape(B, S, 6, N_KV_GROUPS, HEAD_DIM)
    k_c, v_c, k_s, v_s, k_w, v_w = [kv[:, :, j] for j in range(6)]
    pos = jnp.arange(S)
    k_s = rope(rmsnorm(k_s, k_norm[1]), pos)
    k_w = rope(rmsnorm(k_w, k_norm[2]), pos)
    n_cmp = (S - CMP_BLOCK) // CMP_STRIDE + 1
    cmp_last = jnp.arange(n_cmp) * CMP_STRIDE + CMP_BLOCK - 1
    k_cmp = compress(k_c, cmp_pos[0], cmp_w1[0], cmp_b1[0], cmp_w2[0], cmp_b2[0])
    k_cmp = rope(rmsnorm(k_cmp, k_norm[0]), cmp_last)
    v_cmp = compress(v_c, cmp_pos[1], cmp_w1[1], cmp_b1[1], cmp_w2[1], cmp_b2[1])
    return (k_cmp, v_cmp, k_s, v_s, k_w, v_w)


def nsa_single(args):
    q, gate, k_cmp, v_cmp, k_slc, v_slc, k_win, v_win = args
    S = q.shape[0]
    G, Hg, dk = N_KV_GROUPS, HEADS_PER_GROUP, HEAD_DIM
    n_cmp = k_cmp.shape[0]
    n_slc = S // SLC_BLOCK
    top_n = min(SLC_TOPK, n_slc)
    scale = HEAD_DIM ** -0.5
    cmp_start = jnp.arange(n_cmp) * CMP_STRIDE
    cmp_last = cmp_start + CMP_BLOCK - 1
    slc_start = jnp.arange(n_slc) * SLC_BLOCK
    overlap = jnp.clip(jnp.minimum(cmp_start[:, None] + CMP_BLOCK, slc_start[None, :] + SLC_BLOCK)
                       - jnp.maximum(cmp_start[:, None], slc_start[None, :]), 0).astype(jnp.float32) / CMP_BLOCK
    kb = k_slc.reshape(n_slc, SLC_BLOCK, G, dk).transpose(2, 0, 1, 3)
    vb = v_slc.reshape(n_slc, SLC_BLOCK, G, dk).transpose(2, 0, 1, 3)
    kw_p = jnp.pad(k_win, ((WINDOW, 0), (0, 0), (0, 0)))
    vw_p = jnp.pad(v_win, ((WINDOW, 0), (0, 0), (0, 0)))
    qg = q.reshape(S, G, Hg, dk)
    blk = jnp.arange(n_slc)
    grp = jnp.arange(G)[None, :, None]

    def block(qb):
        s0 = qb * Q_BLOCK
        qt = lax.dynamic_slice_in_dim(qg, s0, Q_BLOCK, 0)
        gt = lax.dynamic_slice_in_dim(gate, s0, Q_BLOCK, 0).reshape(Q_BLOCK, 3, G, Hg)
        t = s0 + jnp.arange(Q_BLOCK)
        s = jnp.einsum('tghd,cgd->tghc', qt, k_cmp).astype(jnp.float32) * scale
        p_cmp = masked_softmax(s, (cmp_last[None, :] <= t[:, None])[:, None, None, :])
        o_cmp = jnp.einsum('tghc,cgd->tghd', p_cmp.astype(v_cmp.dtype), v_cmp)
        imp = jnp.einsum('tgc,cj->tgj', p_cmp.sum(axis=2), overlap)
        cur = (t // SLC_BLOCK)[:, None]
        causal_blk = blk[None, :] <= cur
        forced = (blk[None, :] == 0) | (causal_blk & (cur - blk[None, :] < N_LOCAL_BLOCKS))
        score = jnp.where(forced[:, None, :], FORCE, jnp.where(causal_blk[:, None, :], imp, NEG))
        _, idx = lax.top_k(score, top_n)
        ksel = kb[grp, idx].reshape(Q_BLOCK, G, top_n * SLC_BLOCK, dk)
        vsel = vb[grp, idx].reshape(Q_BLOCK, G, top_n * SLC_BLOCK, dk)
        kpos = (idx[..., None] * SLC_BLOCK + jnp.arange(SLC_BLOCK)).reshape(Q_BLOCK, G, top_n * SLC_BLOCK)
        s = jnp.einsum('tghd,tgkd->tghk', qt, ksel).astype(jnp.float32) * scale
        p = masked_softmax(s, (kpos <= t[:, None, None])[:, :, None, :])
        o_slc = jnp.einsum('tghk,tgkd->tghd', p.astype(vsel.dtype), vsel)
        kw = lax.dynamic_slice_in_dim(kw_p, s0, WINDOW + Q_BLOCK, 0)
        vw = lax.dynamic_slice_in_dim(vw_p, s0, WINDOW + Q_BLOCK, 0)
        kp = s0 - WINDOW + jnp.arange(WINDOW + Q_BLOCK)
        mw = (kp[None, :] <= t[:, None]) & (kp[None, :] > t[:, None] - WINDOW) & (kp[None, :] >= 0)
        s = jnp.einsum('tghd,kgd->tghk', qt, kw).astype(jnp.float32) * scale
        p = masked_softmax(s, mw[:, None, None, :])
        o_win = jnp.einsum('tghk,kgd->tghd', p.astype(vw.dtype), vw)
        o = gt[:, 0][..., None] * o_cmp + gt[:, 1][..., None] * o_slc + gt[:, 2][..., None] * o_win
        return o.reshape(Q_BLOCK, N_HEADS, dk)

    out = lax.map(block, jnp.arange(S // Q_BLOCK))
    return out.reshape(S, N_HEADS, dk)


def nsa_layer(h, norm_g, w_in, gate_b, q_norm_g, w_out, k_cmp, v_cmp, k_s, v_s, k_w, v_w):
    B, S, _ = h.shape
    u = rmsnorm(h, norm_g)
    z = u @ w_in
    q = z[..., :N_HEADS * HEAD_DIM].reshape(B, S, N_HEADS, HEAD_DIM)
    q = rope(rmsnorm(q, q_norm_g), jnp.arange(S))
    gates = jax.nn.sigmoid(z[..., N_HEADS * HEAD_DIM:] + gate_b).reshape(B, S, 3, N_HEADS)
    o = lax.map(nsa_single, (q, gates, k_cmp, v_cmp, k_s, v_s, k_w, v_w))
    return o.reshape(B, S, N_HEADS * HEAD_DIM) @ w_out


def setup_inputs(seed: int = 0) -> dict:
    key = jax.random.key(seed)
    ks = jax.random.split(key, 26)

    def nrm(k, shape, scale):
        return jax.random.normal(k, shape, jnp.float32) * scale

    def gain(k, shape):
        return 1.0 + 0.02 * jax.random.normal(k, shape, jnp.float32)

    R, NA, NB = LRU_WIDTH, N_A_LAYERS, N_B_LAYERS
    u = jax.random.uniform(ks[7], (NA, R), jnp.float32, minval=0.9, maxval=0.999)
    s = u ** (1.0 / LRU_C)
    a_lambda = jnp.log(s) - jnp.log1p(-s)
    nq = N_HEADS * HEAD_DIM + 3 * N_HEADS
    return {
        "x": nrm(ks[0], (BATCH, SEQ, D_MODEL), 1.0),
        "a_norm": gain(ks[1], (NA, D_MODEL)),
        "a_w_in": nrm(ks[2], (NA, D_MODEL, 2 * R), D_MODEL ** -0.5),
        "a_conv_w": nrm(ks[3], (NA, CONV_WIDTH, R), CONV_WIDTH ** -0.5),
        "a_conv_b": nrm(ks[4], (NA, R), 0.01),
        "a_gate_w": nrm(ks[5], (NA, 2, LRU_HEADS, LRU_BLOCK, LRU_BLOCK), LRU_BLOCK ** -0.5),
        "a_gate_b": nrm(ks[6], (NA, 2, R), 0.01),
        "a_lambda": a_lambda,
        "a_w_out": nrm(ks[8], (NA, R, D_MODEL), R ** -0.5),
        "kv_norm": gain(ks[9], (D_MODEL,)),
        "kv_w": nrm(ks[10], (D_MODEL, 6 * N_KV_GROUPS * HEAD_DIM), D_MODEL ** -0.5),
        "k_norm": gain(ks[11], (3, HEAD_DIM)),
        "cmp_pos": nrm(ks[12], (2, CMP_BLOCK, HEAD_DIM), 0.02),
        "cmp_w1": nrm(ks[13], (2, CMP_BLOCK * HEAD_DIM, CMP_HIDDEN), (CMP_BLOCK * HEAD_DIM) ** -0.5),
        "cmp_b1": nrm(ks[14], (2, CMP_HIDDEN), 0.01),
        "cmp_w2": nrm(ks[15], (2, CMP_HIDDEN, HEAD_DIM), CMP_HIDDEN ** -0.5),
        "cmp_b2": nrm(ks[16], (2, HEAD_DIM), 0.01),
        "b_norm": gain(ks[17], (NB, D_MODEL)),
        "b_w_in": nrm(ks[18], (NB, D_MODEL, nq), D_MODEL ** -0.5),
        "b_gate_b": nrm(ks[19], (NB, 3 * N_HEADS), 0.01),
        "q_norm": gain(ks[20], (NB, HEAD_DIM)),
        "b_w_out": nrm(ks[21], (NB, N_HEADS * HEAD_DIM, D_MODEL), (N_HEADS * HEAD_DIM) ** -0.5),
        "f_norm": gain(ks[22], (DEPTH, D_MODEL)),
        "f_w_in": nrm(ks[23], (DEPTH, D_MODEL, 2 * FFN_HIDDEN), D_MODEL ** -0.5),
        "f_w_out": nrm(ks[24], (DEPTH, FFN_HIDDEN, D_MODEL), FFN_HIDDEN ** -0.5),
    }


def reference(x, a_norm, a_w_in, a_conv_w, a_conv_b, a_gate_w, a_gate_b, a_lambda, a_w_out,
              kv_norm, kv_w, k_norm, cmp_pos, cmp_w1, cmp_b1, cmp_w2, cmp_b2,
              b_norm, b_w_in, b_gate_b, q_norm, b_w_out,
              f_norm, f_w_in, f_w_out):
    h = x
    shared = None
    for layer in range(DEPTH):
        if layer < N_A_LAYERS:
            i = layer
            h = h + recurrent_block(h, a_norm[i], a_w_in[i], a_conv_w[i], a_conv_b[i],
                                    a_gate_w[i], a_gate_b[i], a_lambda[i], a_w_out[i])
        else:
            if layer == N_A_LAYERS:
                shared = shared_kv(h, kv_norm, kv_w, k_norm, cmp_pos, cmp_w1, cmp_b1, cmp_w2, cmp_b2)
            j = layer - N_A_LAYERS
            h = h + nsa_layer(h, b_norm[j], b_w_in[j], b_gate_b[j], q_norm[j], b_w_out[j], *shared)
        h = h + swiglu(h, f_norm[layer], f_w_in[layer], f_w_out[layer])
    return h
```

```python
import numpy as np
import concourse.bass as bass
import concourse.mybir as mybir
from concourse.bass_utils import run_bass_kernel_spmd
from concourse.alu_op_type import AluOpType as ALU

F32 = mybir.dt.float32
BF16 = mybir.dt.bfloat16
AF = mybir.ActivationFunctionType

S = 2048
D = 1024
NCH = 8
TT = 512
NT = S // TT
FH = 2816
FC = 22
NCORES = 8
EPS = 1e-6
CONV_CH = 128 * 2048
SCALE = 0.125
NEGM = -30000.0


def _kt(W, c0, width=128):
    K = W.shape[0]
    return np.ascontiguousarray(W[:, c0:c0 + width].reshape(K // 128, 128, width).transpose(1, 0, 2)).reshape(128, -1)


class WLayout:
    def __init__(self):
        self.off = {}
        self.free = {}
        self.n = 0

    def add(self, name, free):
        self.off[name] = self.n
        self.free[name] = free
        self.n += 128 * free

    def total(self):
        return ((self.n + CONV_CH - 1) // CONV_CH) * CONV_CH


def weight_layout():
    L = WLayout()
    for l in range(2):
        for c in range(8):
            L.add(("a_in", l, c), 2304)
        for m in range(8):
            L.add(("a_out", l, m), 1024)
    for l in range(4):
        for c in range(FC):
            L.add(("f_in", l, c), 2048)
        for m in range(8):
            L.add(("f_out", l, m), FH)
    for i in range(4):
        L.add(("kvk", i), 1024)
    for i in range(4):
        L.add(("kvc", i), 1024)
    for i in range(2):
        L.add(("kvv", i), 2048)
    for i in range(2):
        L.add(("cw1", i), 8192)
    L.add(("cw2",), 256)
    for j in range(2):
        L.add(("bg", j), 384)
        for m in range(8):
            L.add(("bq", j, m), 1024)
        for m in range(8):
            L.add(("bo", j, m), 1024)
    return L


def pack_weights(inp, L):
    out = np.zeros(L.total(), np.float32)

    def put(name, arr):
        arr = np.asarray(arr, np.float32).reshape(128, -1)
        assert arr.shape[1] == L.free[name], (name, arr.shape)
        out[L.off[name]:L.off[name] + arr.size] = arr.reshape(-1)

    for l in range(2):
        Win = inp["a_w_in"][l]
        for c in range(8):
            put(("a_in", l, c), np.concatenate(
                [_kt(Win, c * 128), _kt(Win, 1024 + c * 128), inp["a_gate_w"][l][0, c], inp["a_gate_w"][l][1, c]], axis=1))
        for m in range(8):
            put(("a_out", l, m), _kt(inp["a_w_out"][l], m * 128))
    for l in range(4):
        W = inp["f_w_in"][l]
        for c in range(FC):
            put(("f_in", l, c), np.concatenate([_kt(W, c * 128), _kt(W, FH + c * 128)], axis=1))
        for m in range(8):
            put(("f_out", l, m), _kt(inp["f_w_out"][l], m * 128))
    kvw = inp["kv_w"]
    i = 0
    for jj in (2, 4):
        for cc in range(2):
            put(("kvk", i), _kt(kvw, jj * 256 + cc * 128))
            i += 1
    i = 0
    for jj in (0, 1):
        for cc in range(2):
            put(("kvc", i), _kt(kvw, jj * 256 + cc * 128))
            i += 1
    for i, jj in enumerate((3, 5)):
        put(("kvv", i), _kt(kvw, jj * 256, 256))
    for kv in range(2):
        t = np.zeros((128, 32, 256), np.float32)
        t[:64] = inp["cmp_w1"][kv].reshape(32, 64, 256).transpose(1, 0, 2)
        put(("cw1", kv), t)
    t = np.zeros((128, 2, 2, 64), np.float32)
    for kv in range(2):
        t[:, kv] = inp["cmp_w2"][kv].reshape(2, 128, 64).transpose(1, 0, 2)
    put(("cw2",), t)
    for j in range(2):
        put(("bg", j), _kt(inp["b_w_in"][j], 1024, 48))
        for m in range(8):
            put(("bq", j, m), _kt(inp["b_w_in"][j], m * 128))
        for m in range(8):
            put(("bo", j, m), _kt(inp["b_w_out"][j], m * 128))
    return out


VEC = {}
_nv = 0


def _vadd(name, n):
    global _nv
    VEC[name] = _nv
    _nv += n


for _l in range(2):
    _vadd(("a_norm", _l), 8)
    for _k in range(4):
        _vadd(("a_cw", _l, _k), 8)
    _vadd(("a_cb", _l), 8)
    _vadd(("a_gb", _l, 0), 8)
    _vadd(("a_gb", _l, 1), 8)
    _vadd(("a_lam", _l), 8)
for _l in range(4):
    _vadd(("f_norm", _l), 8)
_vadd(("kv_norm",), 8)
for _j in range(2):
    _vadd(("b_norm", _j), 8)
    _vadd(("q_norm", _j), 1)
for _i in range(3):
    _vadd(("k_norm", _i), 1)
for _i in range(2):
    _vadd(("c_b1", _i), 2)
    _vadd(("c_b2", _i), 1)
    _vadd(("c_pos", _i), 32)
NV = _nv
BV_GB = 0
BV_CB2V = 96
NBV = 160


def pack_vecs(inp):
    v = np.zeros((128, NV), np.float32)

    def fm(x):
        return np.asarray(x, np.float32).reshape(8, 128).T

    for l in range(2):
        v[:, VEC[("a_norm", l)]:][:, :8] = fm(inp["a_norm"][l])
        for k in range(4):
            v[:, VEC[("a_cw", l, k)]:][:, :8] = fm(inp["a_conv_w"][l][k])
        v[:, VEC[("a_cb", l)]:][:, :8] = fm(inp["a_conv_b"][l])
        v[:, VEC[("a_gb", l, 0)]:][:, :8] = fm(inp["a_gate_b"][l][0])
        v[:, VEC[("a_gb", l, 1)]:][:, :8] = fm(inp["a_gate_b"][l][1])
        v[:, VEC[("a_lam", l)]:][:, :8] = fm(inp["a_lambda"][l])
    for l in range(4):
        v[:, VEC[("f_norm", l)]:][:, :8] = fm(inp["f_norm"][l])
    v[:, VEC[("kv_norm",)]:][:, :8] = fm(inp["kv_norm"])
    for j in range(2):
        v[:, VEC[("b_norm", j)]:][:, :8] = fm(inp["b_norm"][j])
        v[:, VEC[("q_norm", j)]] = np.tile(np.asarray(inp["q_norm"][j], np.float32), 2)
    for i in range(3):
        v[:, VEC[("k_norm", i)]] = np.tile(np.asarray(inp["k_norm"][i], np.float32), 2)
    for i in range(2):
        v[:, VEC[("c_b1", i)]:][:, :2] = np.asarray(inp["cmp_b1"][i], np.float32).reshape(2, 128).T
        v[:, VEC[("c_b2", i)]] = np.tile(np.asarray(inp["cmp_b2"][i], np.float32), 2)
        v[:64, VEC[("c_pos", i)]:][:, :32] = np.asarray(inp["cmp_pos"][i], np.float32).T
    bv = np.zeros((128, NBV), np.float32)
    for j in range(2):
        bv[:, BV_GB + 48 * j: BV_GB + 48 * j + 48] = np.asarray(inp["b_gate_b"][j], np.float32)[None, :]
    bv[:, BV_CB2V:BV_CB2V + 64] = np.asarray(inp["cmp_b2"][1], np.float32)[None, :]
    return v, bv


CST = {}
_nc_ = 0


def _cadd(name, n):
    global _nc_
    CST[name] = _nc_
    _nc_ += n


_cadd("ident", 128)
_cadd("prot", 128)
_cadd("bones", 128)
_cadd("triL", 128)
_cadd("triU", 128)
_cadd("ovl", 32)
_cadd("cos", 2048)
_cadd("sin", 2048)
_cadd("cmask", 2048)
_cadd("eind", 2048)
_cadd("selb", 512)
NCST = _nc_


def make_consts():
    c = np.zeros((128, NCST), np.float32)
    p = np.arange(128)
    c[:, CST["ident"]:][:, :128] = np.eye(128)
    perm = (p // 64) * 64 + ((p % 64) + 32) % 64
    pr = np.zeros((128, 128), np.float32)
    pr[perm, p] = 1.0
    c[:, CST["prot"]:][:, :128] = pr
    c[:, CST["bones"]:][:, :128] = (p[:, None] // 64 == p[None, :] // 64)
    c[:, CST["triL"]:][:, :128] = (p[:, None] <= p[None, :])
    c[:, CST["triU"]:][:, :128] = (p[:, None] > p[None, :])
    cs = np.arange(127) * 16
    sl = np.arange(32) * 64
    ov = np.clip(np.minimum(cs[:, None] + 32, sl[None, :] + 64) - np.maximum(cs[:, None], sl[None, :]), 0, None) / 32.0
    c[:127, CST["ovl"]:][:, :32] = ov
    t = np.arange(2048, dtype=np.float64)
    fi = (p % 64) % 32
    freqs = (10000.0 ** (-(np.arange(32, dtype=np.float32) / np.float32(32)))).astype(np.float32)
    ang = (t[None, :].astype(np.float32) * freqs[fi][:, None]).astype(np.float32)
    c[:, CST["cos"]:][:, :2048] = np.cos(ang)
    sg = np.where((p % 64) < 32, -1.0, 1.0)[:, None]
    c[:, CST["sin"]:][:, :2048] = np.sin(ang) * sg
    cl = np.arange(127) * 16 + 31
    c[:127, CST["cmask"]:][:, :2048] = (cl[:, None] <= t[None, :])
    b = np.arange(32)
    c[64:96, CST["eind"]:][:, :2048] = (t[None, :].astype(np.int64) // 64 == b[:, None])
    sb = np.zeros((128, 16, 32), np.float32)
    for qb in range(16):
        tq = qb * 128 + p
        cur = (tq // 64)[:, None]
        causal = b[None, :] <= cur
        forced = (b[None, :] == 0) | (causal & (cur - b[None, :] < 2))
        sb[:, qb, :] = np.where(forced, 1e30, np.where(causal, 0.0, -1e30))
    c[:, CST["selb"]:][:, :512] = sb.reshape(128, 512)
    return c


class Trk:
    __slots__ = ("w", "r")

    def __init__(self):
        self.w = None
        self.r = {}


NDS = 24


class Prog:
    def __init__(self, nc):
        self.nc = nc
        self.eng = {"pe": nc.tensor, "act": nc.scalar, "dve": nc.vector, "pool": nc.gpsimd, "sp": nc.sync}
        self.sem = {e: nc.alloc_semaphore(name="s_" + e) for e in ("pe", "act", "dve", "pool")}
        self.cnt = {e: 0 for e in ("pe", "act", "dve", "pool")}
        self.seen = {e: {} for e in self.eng}
        self.dsem = [nc.alloc_semaphore(name="s_dma%d" % i) for i in range(NDS)]
        self.dn = 0
        self.ninst = 0

    def _semof(self, key):
        return self.sem[key] if isinstance(key, str) else self.dsem[key[1]]

    def _wait(self, e, key, val):
        if key == "pe" and e == "pe":
            return
        if self.seen[e].get(key, 0) >= val:
            return
        self.eng[e].wait_ge(self._semof(key), val)
        self.seen[e][key] = val

    def _deps(self, e, r, w):
        for t in r:
            if t.w is not None:
                self._wait(e, t.w[0], t.w[1])
        for t in w:
            if t.w is not None:
                self._wait(e, t.w[0], t.w[1])
            for k, v in t.r.items():
                self._wait(e, k, v)

    def op(self, e, fn, r=(), w=()):
        self._deps(e, r, w)
        inst = fn(self.eng[e])
        self.cnt[e] += 1
        self.ninst += 1
        inst.then_inc(self.sem[e], 1)
        v = self.cnt[e]
        for t in r:
            t.r[e] = v
        for t in w:
            t.w = (e, v)
            t.r = {}
        return inst

    def dma(self, q, out, in_, r=(), w=()):
        self._deps(q, r, w)
        i = self.dn % NDS
        gen = self.dn // NDS
        key = ("d", i)
        if gen > 0:
            self._wait(q, key, 16 * gen)
        inst = self.eng[q].dma_start(out=out, in_=in_)
        inst.then_inc(self.dsem[i], 16)
        self.dn += 1
        self.ninst += 1
        v = 16 * (gen + 1)
        for t in r:
            t.r[key] = v
        for t in w:
            t.w = (key, v)
            t.r = {}
        return (key, v)

    def wait_all(self, e, trks):
        self._deps(e, trks, trks)


class Buf:
    def __init__(self, t):
        self.t = t
        self.k = Trk()

    def __getitem__(self, idx):
        return self.t[idx]


class Builder:
    def __init__(self, nb, n_layers=4, debug_out=None):
        self.nb = nb
        self.n_layers = n_layers
        self.L = weight_layout()
        nc = bass.Bass("TRN2", target_bir_lowering=False)
        self.nc = nc
        self.P = Prog(nc)
        NW = self.L.total()
        self.x = nc.dram_tensor("x", [nb, S, D], F32, kind="ExternalInput").ap()
        self.wall = nc.dram_tensor("wall", [NW], F32, kind="ExternalInput").ap()
        self.vecs_d = nc.dram_tensor("vecs", [128, NV], F32, kind="ExternalInput").ap()
        self.bvecs_d = nc.dram_tensor("bvecs", [128, NBV], F32, kind="ExternalInput").ap()
        self.cst_d = nc.dram_tensor("cst", [128, NCST], F32, kind="ExternalInput").ap()
        self.y = nc.dram_tensor("y", [nb, S, D], F32, kind="ExternalOutput").ap()
        self.wbf = nc.dram_tensor("wbf", [NW], BF16, kind="Internal").ap()
        self.wchunk = [Trk() for _ in range(NW // CONV_CH)]
        self.out_trks = []

        def sb(name, shape, dt):
            return Buf(nc.alloc_sbuf_tensor(name, shape, dt))

        self.sb = sb
        self.hT = nc.alloc_sbuf_tensor("hT", [128, NCH, S], F32)
        self.hk = [[Trk() for _ in range(NT)] for _ in range(NCH)]
        self.vecs = sb("vecs_sb", [128, NV], F32)
        self.bvecs = sb("bvecs_sb", [128, NBV], F32)
        self.ident_f = sb("ident_f", [128, 128], F32)
        self.ident_b = sb("ident_b", [128, 128], BF16)
        self.ones_b = sb("ones_b", [128, 128], BF16)
        self.coef = sb("coef", [128, 2, 2, 8], F32)
        self.cos_b = sb("cos_b", [128, S], BF16)
        self.sin_b = sb("sin_b", [128, S], BF16)
        self.cmask_b = sb("cmask_b", [128, S], BF16)
        self.triL_b = sb("triL_b", [128, 128], BF16)
        self.triU_b = sb("triU_b", [128, 128], BF16)
        self.prot_b = sb("prot_b", [128, 128], BF16)
        self.bones_b = sb("bones_b", [128, 128], BF16)
        self.selb = sb("selb", [128, 16, 32], F32)
        self.KC = sb("KC", [128, 4, 128], BF16)
        self.VC = sb("VC", [128, 4, 98], BF16)
        self.posb = sb("posb", [128, 2, 32], BF16)
        self.cbias = sb("cbias", [128, 2, 2], F32)
        self.KSd = nc.dram_tensor("KSd", [2, 64, 4, S], BF16, kind="Internal").ap()
        self.Vd = nc.dram_tensor("Vd", [2, 128, 16, 4, 66], BF16, kind="Internal").ap()
        self.ksd_k = [[[Trk() for _ in range(NT)] for _ in range(4)] for _ in range(2)]
        self.vd_k = [[Trk() for _ in range(NT)] for _ in range(2)]
        self.ps = [Buf(nc.alloc_psum_tensor("ps%d" % i, [128, 512], F32)) for i in range(8)]
        self.ps_rr = 0
        self.pool_misc = [0, 1, 2, 3, 4, 5, 6, 7]
        self.NSLOT = 3
        self.ring = [sb("wring%d" % i, [128, FH], BF16) for i in range(self.NSLOT)]
        self.ring_n = 0
        self.wq = []
        self.wplan = []
        self.wplan_i = 0
        self.PHB = 100 * 1024
        self.ph = nc.alloc_sbuf_tensor("phase", [128, self.PHB // 4], F32)
        self.phk = {}

    def bank(self, pool=None):
        if pool is None:
            b = self.ps[self.ps_rr % 8]
        else:
            b = self.ps[pool[self.ps_rr % len(pool)]]
        self.ps_rr += 1
        return b

    def vec(self, name, c=0):
        i = VEC[name] + c
        return self.vecs[:, i:i + 1]

    def phase_view(self, byte_off, shape, dt):
        esz = 4 if dt == F32 else 2
        n = int(np.prod(shape[1:]))
        assert byte_off % 4 == 0 and byte_off + n * esz <= self.PHB, (byte_off, shape)
        if dt == F32:
            ap = self.ph[:, byte_off // 4: byte_off // 4 + n]
        else:
            ap = self.ph[:].bitcast(BF16)[:, byte_off // 2: byte_off // 2 + n]
        if len(shape) == 2:
            return ap
        names = " ".join("a%d" % i for i in range(len(shape) - 1))
        kw = {"a%d" % i: shape[i + 1] for i in range(len(shape) - 1)}
        return ap.rearrange("p (%s) -> p %s" % (names, names), **kw)

    def new_phase(self, trks):
        merged = {}
        for t in self.phase_cur:
            if t.w is not None:
                merged[t.w[0]] = max(merged.get(t.w[0], 0), t.w[1])
            for k, v in t.r.items():
                merged[k] = max(merged.get(k, 0), v)
        for t in trks:
            t.w = None
            t.r = dict(merged)
        self.phase_cur = list(trks)

    def plan_weights(self, names):
        self.wplan = list(names)
        self.wplan_i = 0
        self.wq = []

    def _issue_w(self):
        name = self.wplan[self.wplan_i]
        self.wplan_i += 1
        slot = self.ring[self.ring_n % self.NSLOT]
        self.ring_n += 1
        off = self.L.off[name]
        free = self.L.free[name]
        src = self.wbf[off:off + 128 * free].rearrange("(p f) -> p f", p=128)
        c0 = off // CONV_CH
        c1 = (off + 128 * free - 1) // CONV_CH
        self.P.dma("sp", slot[:, 0:free], src, r=[self.wchunk[c] for c in range(c0, c1 + 1)], w=[slot.k])
        self.wq.append((name, slot))

    def next_w(self, name):
        while len(self.wq) < self.NSLOT - 1 and self.wplan_i < len(self.wplan):
            self._issue_w()
        n, slot = self.wq.pop(0)
        assert n == name, (n, name)
        return slot

    def prologue(self):
        P = self.P
        nc = self.nc
        P.dma("sp", self.vecs[:], self.vecs_d, w=[self.vecs.k])
        P.dma("sp", self.bvecs[:], self.bvecs_d, w=[self.bvecs.k])
        P.dma("sp", self.ident_f[:], self.cst_d[:, CST["ident"]:CST["ident"] + 128], w=[self.ident_f.k])
        P.op("dve", lambda e: e.tensor_copy(out=self.ident_b[:], in_=self.ident_f[:]), r=[self.ident_f.k], w=[self.ident_b.k])
        P.op("pool", lambda e: e.memset(self.ones_b[:], 1.0), w=[self.ones_b.k])
        tmp = self.sb("coef_tmp", [128, 16], F32)
        for l in range(2):
            lam = self.vecs[:, VEC[("a_lam", l)]:VEC[("a_lam", l)] + 8]
            P.op("act", lambda e: e.activation(out=tmp[:, l * 8:l * 8 + 8], in_=lam, func=AF.Exp, scale=-1.0), r=[self.vecs.k], w=[tmp.k])
            P.op("act", lambda e: e.activation(out=tmp[:, l * 8:l * 8 + 8], in_=tmp[:, l * 8:l * 8 + 8], func=AF.Ln, bias=1.0), r=[tmp.k], w=[tmp.k])
            P.op("dve", lambda e: e.tensor_scalar(out=self.coef[:, l, 0, :], in0=tmp[:, l * 8:l * 8 + 8], scalar1=-8.0, scalar2=None, op0=ALU.mult), r=[tmp.k], w=[self.coef.k])
            P.op("dve", lambda e: e.tensor_scalar(out=self.coef[:, l, 1, :], in0=tmp[:, l * 8:l * 8 + 8], scalar1=-16.0, scalar2=None, op0=ALU.mult), r=[tmp.k], w=[self.coef.k])
        stg = self.phase_view(0, [128, 2048], F32)
        stk = Trk()
        self.phase_cur = [stk]

        def ctab(name, n, dst, dstk, eng):
            P.dma("sp", stg[:, 0:n], self.cst_d[:, CST[name]:CST[name] + n], w=[stk])
            if eng == "act":
                P.op("act", lambda e: e.activation(out=dst, in_=stg[:, 0:n], func=AF.Copy), r=[stk], w=[dstk])
            else:
                P.op(eng, lambda e: e.tensor_copy(out=dst, in_=stg[:, 0:n]), r=[stk], w=[dstk])

        ctab("cos", 2048, self.cos_b[:], self.cos_b.k, "dve")
        ctab("sin", 2048, self.sin_b[:], self.sin_b.k, "act")
        ctab("cmask", 2048, self.cmask_b[:], self.cmask_b.k, "dve")
        ctab("triL", 128, self.triL_b[:], self.triL_b.k, "dve")
        ctab("triU", 128, self.triU_b[:], self.triU_b.k, "dve")
        ctab("prot", 128, self.prot_b[:], self.prot_b.k, "dve")
        ctab("bones", 128, self.bones_b[:], self.bones_b.k, "dve")
        P.dma("sp", self.selb[:].rearrange("p a b -> p (a b)"), self.cst_d[:, CST["selb"]:CST["selb"] + 512], w=[self.selb.k])
        P.op("pool", lambda e: e.memset(self.VC[:], 0.0), w=[self.VC.k])
        P.op("pool", lambda e: e.memset(self.KC[:], 0.0), w=[self.KC.k])
        P.op("pool", lambda e: e.memset(self.VC[:, :, 64:65], 1.0), w=[self.VC.k])
        P.dma("sp", stg[:, 0:32], self.cst_d[:, CST["ovl"]:CST["ovl"] + 32], w=[stk])
        for g in range(4):
            P.op("dve", lambda e: e.tensor_copy(out=self.VC[:, g, 65:97], in_=stg[:, 0:32]), r=[stk], w=[self.VC.k])
        for kv in range(2):
            o0 = VEC[("c_pos", kv)]
            P.op("dve", lambda e: e.tensor_copy(out=self.posb[:, kv, :], in_=self.vecs[:, o0:o0 + 32]), r=[self.vecs.k], w=[self.posb.k])
        nst = 3
        stf = [(self.phase_view(i * 12288, [128, 2048], F32), Trk()) for i in range(nst)]
        stb = [(self.phase_view(i * 12288 + 8192, [128, 2048], BF16), Trk()) for i in range(nst)]
        self.new_phase([k for _, k in stf] + [k for _, k in stb])
        nchunk = len(self.wchunk)
        engs = ["dve", "act", "pool"]
        for i in range(nchunk):
            sf, kf = stf[i % nst]
            sbb, kb = stb[i % nst]
            src = self.wall[i * CONV_CH:(i + 1) * CONV_CH].rearrange("(p f) -> p f", p=128)
            dst = self.wbf[i * CONV_CH:(i + 1) * CONV_CH].rearrange("(p f) -> p f", p=128)
            P.dma("sp", sf, src, w=[kf])
            e = engs[i % 3]
            if e == "act":
                P.op("act", lambda en: en.activation(out=sbb, in_=sf, func=AF.Copy), r=[kf], w=[kb])
            else:
                P.op(e, lambda en: en.tensor_copy(out=sbb, in_=sf), r=[kf], w=[kb])
            P.dma("sp", dst, sbb, r=[kb], w=[self.wchunk[i]])
        self.phase_cur = [k for _, k in stf] + [k for _, k in stb] + [stk]

    def load_x(self, b):
        P = self.P
        xin = [(self.phase_view(j * 4096, [128, 1024], F32), Trk()) for j in range(4)]
        self.new_phase([k for _, k in xin])
        for n in range(NT):
            for j in range(4):
                t0 = n * TT + j * 128
                P.dma("sp", xin[j][0], self.x[b, t0:t0 + 128, :], w=[xin[j][1]])
            for c in range(NCH):
                pb = self.bank()
                for j in range(4):
                    P.op("pe", lambda e: e.transpose(out=pb[:, j * 128:(j + 1) * 128], in_=xin[j][0][:, c * 128:(c + 1) * 128], identity=self.ident_f[:]),
                         r=[xin[j][1], self.ident_f.k], w=[pb.k])
                eng = "act" if c % 2 == 0 else "dve"
                dst = self.hT[:, c, n * TT:(n + 1) * TT]
                if eng == "act":
                    P.op("act", lambda e: e.activation(out=dst, in_=pb[:], func=AF.Copy), r=[pb.k], w=[self.hk[c][n]])
                else:
                    P.op("dve", lambda e: e.tensor_copy(out=dst, in_=pb[:]), r=[pb.k], w=[self.hk[c][n]])

    def store_y(self, b):
        P = self.P
        yo = [(self.phase_view(j * 4096, [128, 1024], F32), Trk()) for j in range(4)]
        self.new_phase([k for _, k in yo])
        for n in range(NT):
            for j in range(4):
                t0 = n * TT + j * 128
                for half in range(2):
                    pb = self.bank()
                    for cc in range(4):
                        c = half * 4 + cc
                        P.op("pe", lambda e: e.transpose(out=pb[:, cc * 128:(cc + 1) * 128], in_=self.hT[:, c, t0:t0 + 128], identity=self.ident_f[:]),
                             r=[self.hk[c][n], self.ident_f.k], w=[pb.k])
                    dst = yo[j][0][:, half * 512:(half + 1) * 512]
                    wl = [yo[j][1]]
                    if half == 0:
                        P.op("act", lambda e: e.activation(out=dst, in_=pb[:], func=AF.Copy), r=[pb.k], w=wl)
                    else:
                        P.op("dve", lambda e: e.tensor_copy(out=dst, in_=pb[:]), r=[pb.k], w=wl)
                ot = Trk()
                P.dma("sp", self.y[b, t0:t0 + 128, :], yo[j][0], r=[yo[j][1]], w=[ot])
                self.out_trks.append(ot)

    def norm_tile(self, n, gname, u_ap, u_k, sq_bufs, rstd_buf):
        P = self.P
        pb = self.bank()
        for c in range(NCH):
            sq, sk = sq_bufs[c % len(sq_bufs)]
            hsl = self.hT[:, c, n * TT:(n + 1) * TT]
            P.op("act", lambda e: e.activation(out=sq, in_=hsl, func=AF.Square), r=[self.hk[c][n]], w=[sk])
            P.op("pe", lambda e: e.matmul(pb[:], lhsT=self.ones_b[:], rhs=sq, start=(c == 0), stop=(c == NCH - 1)),
                 r=[sk, self.ones_b.k], w=[pb.k])
        rs, rk = rstd_buf
        P.op("act", lambda e: e.activation(out=rs, in_=pb[:], func=AF.Sqrt, scale=1.0 / D, bias=EPS), r=[pb.k], w=[rk])
        P.op("dve", lambda e: e.reciprocal(out=rs, in_=rs), r=[rk], w=[rk])
        for c in range(NCH):
            hsl = self.hT[:, c, n * TT:(n + 1) * TT]
            g = self.vec(gname, c)
            P.op("dve", lambda e: e.scalar_tensor_tensor(out=u_ap[:, c, :], in0=hsl, scalar=g, in1=rs, op0=ALU.mult, op1=ALU.mult),
                 r=[self.hk[c][n], rk, self.vecs.k], w=[u_k])

    def a_phase_setup(self):
        v = self.phase_view
        A = {}
        A["u"] = (v(0, [128, 8, 512], BF16), Trk())
        A["m"] = (v(8192, [128, 8, 512], BF16), Trk())
        A["sq"] = [(v(16384 + i * 1024, [128, 512], BF16), Trk()) for i in range(2)]
        A["rstd"] = (v(18432, [128, 512], F32), Trk())
        A["xp"] = [(v(20480 + i * 2080, [128, 516], F32), Trk()) for i in range(2)]
        base = 24640
        sets = []
        for i in range(2):
            o = base + i * 14336
            sets.append({
                "y": (v(o, [128, 512], BF16), Trk()),
                "xrb": (v(o + 1024, [128, 512], BF16), Trk()),
                "xr": (v(o + 2048, [128, 512], F32), Trk()),
                "r": (v(o + 4096, [128, 512], F32), Trk()),
                "i": (v(o + 6144, [128, 512], F32), Trk()),
                "a": (v(o + 8192, [128, 512], F32), Trk()),
                "a2": (v(o + 10240, [128, 512], F32), Trk()),
                "hr": (v(o + 12288, [128, 512], F32), Trk()),
            })
        A["sets"] = sets
        trks = [A["u"][1], A["m"][1], A["rstd"][1]] + [k for _, k in A["sq"]] + [k for _, k in A["xp"]]
        for st in sets:
            trks += [k for _, k in st.values()]
        self.new_phase(trks)
        self.A = A

    def a_mixer_tile(self, l, n):
        P = self.P
        A = self.A
        u, uk = A["u"]
        m, mk = A["m"]
        self.norm_tile(n, ("a_norm", l), u, uk, A["sq"], A["rstd"])
        for c in range(NCH):
            w = self.next_w(("a_in", l, c))
            pg = self.bank()
            pr = self.bank()
            for k in range(NCH):
                P.op("pe", lambda e: e.matmul(pg[:], lhsT=w[:, k * 128:(k + 1) * 128], rhs=u[:, k, :], start=(k == 0), stop=(k == NCH - 1)),
                     r=[w.k, uk], w=[pg.k])
            for k in range(NCH):
                P.op("pe", lambda e: e.matmul(pr[:], lhsT=w[:, 1024 + k * 128:1024 + (k + 1) * 128], rhs=u[:, k, :], start=(k == 0), stop=(k == NCH - 1)),
                     r=[w.k, uk], w=[pr.k])
            st = A["sets"][c % 2]
            y, yk = st["y"]
            xr, xrk = st["xr"]
            xrb, xrbk = st["xrb"]
            rr, rrk = st["r"]
            ii, iik = st["i"]
            aa, aak = st["a"]
            a2, a2k = st["a2"]
            hr, hrk = st["hr"]
            xp, xpk = A["xp"][c % 2]
            P.op("act", lambda e: e.activation(out=y, in_=pg[:], func=AF.Gelu_apprx_tanh), r=[pg.k], w=[yk])
            P.op("pool", lambda e: e.tensor_copy(out=xp[:, 0:3], in_=self.convc[:, c, :]), r=[self.convc.k], w=[xpk])
            P.op("act", lambda e: e.activation(out=xp[:, 3:515], in_=pr[:], func=AF.Copy), r=[pr.k], w=[xpk])
            P.op("dve", lambda e: e.tensor_scalar(out=xr, in0=xp[:, 0:512], scalar1=self.vec(("a_cw", l, 0), c), scalar2=self.vec(("a_cb", l), c),
                                                  op0=ALU.mult, op1=ALU.add), r=[xpk, self.vecs.k], w=[xrk])
            for kk in range(1, 4):
                P.op("dve", lambda e: e.scalar_tensor_tensor(out=xr, in0=xp[:, kk:kk + 512], scalar=self.vec(("a_cw", l, kk), c), in1=xr,
                                                             op0=ALU.mult, op1=ALU.add), r=[xpk, xrk, self.vecs.k], w=[xrk])
            P.op("pool", lambda e: e.tensor_copy(out=self.convc[:, c, :], in_=xp[:, 512:515]), r=[xpk], w=[self.convc.k])
            P.op("pool", lambda e: e.tensor_copy(out=xrb, in_=xr), r=[xrk], w=[xrbk])
            p1 = self.bank()
            p2 = self.bank()
            P.op("pe", lambda e: e.matmul(p1[:], lhsT=w[:, 2048:2176], rhs=xrb, start=True, stop=True), r=[w.k, xrbk], w=[p1.k])
            P.op("pe", lambda e: e.matmul(p2[:], lhsT=w[:, 2176:2304], rhs=xrb, start=True, stop=True), r=[w.k, xrbk], w=[p2.k])
            P.op("act", lambda e: e.activation(out=rr, in_=p1[:], func=AF.Sigmoid, bias=self.vec(("a_gb", l, 0), c)), r=[p1.k, self.vecs.k], w=[rrk])
            P.op("act", lambda e: e.activation(out=ii, in_=p2[:], func=AF.Sigmoid, bias=self.vec(("a_gb", l, 1), c)), r=[p2.k, self.vecs.k], w=[iik])
            P.op("act", lambda e: e.activation(out=aa, in_=rr, func=AF.Exp, scale=self.coef[:, l, 0, c:c + 1]), r=[rrk, self.coef.k], w=[aak])
            P.op("act", lambda e: e.activation(out=a2, in_=rr, func=AF.Exp, scale=self.coef[:, l, 1, c:c + 1]), r=[rrk, self.coef.k], w=[a2k])
            P.op("act", lambda e: e.activation(out=a2, in_=a2, func=AF.Sqrt, scale=-1.0, bias=1.0), r=[a2k], w=[a2k])
            P.op("dve", lambda e: e.tensor_tensor(out=ii, in0=ii, in1=xr, op=ALU.mult), r=[iik, xrk], w=[iik])
            P.op("dve", lambda e: e.tensor_tensor(out=ii, in0=ii, in1=a2, op=ALU.mult), r=[iik, a2k], w=[iik])
            P.op("dve", lambda e: e.tensor_tensor_scan(out=hr, data0=aa, data1=ii, initial=self.hst[:, c:c + 1], op0=ALU.mult, op1=ALU.add),
                 r=[aak, iik, self.hst.k], w=[hrk])
            P.op("pool", lambda e: e.tensor_copy(out=self.hst[:, c:c + 1], in_=hr[:, 511:512]), r=[hrk], w=[self.hst.k])
            P.op("pool", lambda e: e.tensor_tensor(out=m[:, c, :], in0=hr, in1=y, op=ALU.mult), r=[hrk, yk], w=[mk])
        for mm in range(NCH):
            w = self.next_w(("a_out", l, mm))
            po = self.bank()
            for k in range(NCH):
                P.op("pe", lambda e: e.matmul(po[:], lhsT=w[:, k * 128:(k + 1) * 128], rhs=m[:, k, :], start=(k == 0), stop=(k == NCH - 1)),
                     r=[w.k, mk], w=[po.k])
            hsl = self.hT[:, mm, n * TT:(n + 1) * TT]
            P.op("dve", lambda e: e.tensor_tensor(out=hsl, in0=po[:], in1=hsl, op=ALU.add), r=[po.k, self.hk[mm][n]], w=[self.hk[mm][n]])

    def a_layer(self, l):
        P = self.P
        self.a_phase_setup()
        P.op("pool", lambda e: e.memset(self.convc[:], 0.0), w=[self.convc.k])
        P.op("pool", lambda e: e.memset(self.hst[:], 0.0), w=[self.hst.k])
        for n in range(NT):
            self.a_mixer_tile(l, n)

    def ffn_phase_setup(self):
        v = self.phase_view
        Fz = {}
        Fz["u"] = [(v(s * 8192, [128, 8, 512], BF16), Trk()) for s in range(2)]
        Fz["act"] = [[(v(16384 + (c * 2 + s) * 1024, [128, 512], BF16), Trk()) for s in range(2)] for c in range(FC)]
        Fz["sq"] = [(v(61440 + i * 1024, [128, 512], BF16), Trk()) for i in range(2)]
        Fz["rstd"] = (v(63488, [128, 512], F32), Trk())
        Fz["sg"] = [(v(65536 + i * 1024, [128, 512], BF16), Trk()) for i in range(4)]
        trks = [k for _, k in Fz["u"]] + [k for row in Fz["act"] for _, k in row] + [k for _, k in Fz["sq"]] + [Fz["rstd"][1]] + [k for _, k in Fz["sg"]]
        self.new_phase(trks)
        self.F = Fz

    def ffn_tile(self, L, t2):
        P = self.P
        Fz = self.F
        for s in range(2):
            self.norm_tile(2 * t2 + s, ("f_norm", L), Fz["u"][s][0], Fz["u"][s][1], Fz["sq"], Fz["rstd"])
        nsg = 0
        for c in range(FC):
            w = self.next_w(("f_in", L, c))
            for s in range(2):
                u, uk = Fz["u"][s]
                pg = self.bank()
                pu = self.bank()
                for k in range(NCH):
                    P.op("pe", lambda e: e.matmul(pg[:], lhsT=w[:, k * 128:(k + 1) * 128], rhs=u[:, k, :], start=(k == 0), stop=(k == NCH - 1)),
                         r=[w.k, uk], w=[pg.k])
                for k in range(NCH):
                    P.op("pe", lambda e: e.matmul(pu[:], lhsT=w[:, 1024 + k * 128:1024 + (k + 1) * 128], rhs=u[:, k, :], start=(k == 0), stop=(k == NCH - 1)),
                         r=[w.k, uk], w=[pu.k])
                sg, sgk = Fz["sg"][nsg % 4]
                nsg += 1
                a, ak = Fz["act"][c][s]
                P.op("act", lambda e: e.activation(out=sg, in_=pg[:], func=AF.Silu), r=[pg.k], w=[sgk])
                P.op("dve", lambda e: e.tensor_tensor(out=a, in0=sg, in1=pu[:], op=ALU.mult), r=[sgk, pu.k], w=[ak])
        for mm in range(NCH):
            w = self.next_w(("f_out", L, mm))
            for s in range(2):
                n = 2 * t2 + s
                po = self.bank()
                for c in range(FC):
                    a, ak = Fz["act"][c][s]
                    P.op("pe", lambda e: e.matmul(po[:], lhsT=w[:, c * 128:(c + 1) * 128], rhs=a, start=(c == 0), stop=(c == FC - 1)),
                         r=[w.k, ak], w=[po.k])
                hsl = self.hT[:, mm, n * TT:(n + 1) * TT]
                P.op("dve", lambda e: e.tensor_tensor(out=hsl, in0=po[:], in1=hsl, op=ALU.add), r=[po.k, self.hk[mm][n]], w=[self.hk[mm][n]])

    def ffn_layer(self, L):
        self.ffn_phase_setup()
        for t2 in range(2):
            self.ffn_tile(L, t2)

    def headnorm_rope(self, pk, R, C, gvec, cos_ap, sin_ap, T, bias=None):
        P = self.P
        sq, sqk = T["sq"]
        rs, rsk = T["rs"]
        qn, qnk = T["qn"]
        t1, t1k = T["t1"]
        t2, t2k = T["t2"]
        if bias is not None:
            xf, xfk = T["xf"]
            P.op("act", lambda e: e.activation(out=xf[0:R, 0:C], in_=pk[0:R, 0:C], func=AF.Identity, bias=bias), r=[pk.k, self.vecs.k], w=[xfk])
            src, srck = xf[0:R, 0:C], xfk
        else:
            src, srck = pk[0:R, 0:C], pk.k
        P.op("act", lambda e: e.activation(out=sq[0:R, 0:C], in_=src, func=AF.Square), r=[srck], w=[sqk])
        pss = self.bank(self.pool_misc)
        P.op("pe", lambda e: e.matmul(pss[0:R, 0:C], lhsT=self.bones_b[0:R, 0:R], rhs=sq[0:R, 0:C], start=True, stop=True), r=[sqk, self.bones_b.k], w=[pss.k])
        P.op("act", lambda e: e.activation(out=rs[0:R, 0:C], in_=pss[0:R, 0:C], func=AF.Sqrt, scale=1.0 / 64.0, bias=EPS), r=[pss.k], w=[rsk])
        P.op("dve", lambda e: e.reciprocal(out=rs[0:R, 0:C], in_=rs[0:R, 0:C]), r=[rsk], w=[rsk])
        P.op("dve", lambda e: e.scalar_tensor_tensor(out=qn[0:R, 0:C], in0=src, scalar=gvec, in1=rs[0:R, 0:C], op0=ALU.mult, op1=ALU.mult),
             r=[srck, rsk, self.vecs.k], w=[qnk])
        prt = self.bank(self.pool_misc)
        P.op("pe", lambda e: e.matmul(prt[0:R, 0:C], lhsT=self.prot_b[0:R, 0:R], rhs=qn[0:R, 0:C], start=True, stop=True), r=[qnk, self.prot_b.k], w=[prt.k])
        P.op("pool", lambda e: e.tensor_tensor(out=t1[0:R, 0:C], in0=qn[0:R, 0:C], in1=cos_ap, op=ALU.mult), r=[qnk, self.cos_b.k], w=[t1k])
        P.op("dve", lambda e: e.tensor_tensor(out=t2[0:R, 0:C], in0=prt[0:R, 0:C], in1=sin_ap, op=ALU.mult), r=[prt.k, self.sin_b.k], w=[t2k])

    def kv_phase(self):
        P = self.P
        v = self.phase_view
        u, uk = v(0, [128, 8, 512], BF16), Trk()
        sqn = [(v(8192 + i * 1024, [128, 512], BF16), Trk()) for i in range(2)]
        rstd = (v(10240, [128, 512], F32), Trk())
        kcT, kcTk = v(12288, [128, 8, S], BF16), [Trk() for _ in range(8)]
        cw1, cw1k = v(45056, [128, 2, 32, 256], BF16), Trk()
        T = {"sq": (v(77824, [128, 512], BF16), Trk()), "rs": (v(78848, [128, 512], F32), Trk()), "qn": (v(80896, [128, 512], BF16), Trk()),
             "t1": (v(81920, [128, 512], F32), Trk()), "t2": (v(83968, [128, 512], F32), Trk()), "xf": (v(86016, [128, 512], F32), Trk())}
        kout = [(v(88064 + i * 1024, [128, 512], BF16), Trk()) for i in range(2)]
        vst = [(v(90112 + i * 528, [128, 4, 66], BF16), Trk()) for i in range(2)]
        hid = [(v(91264 + i * 512, [128, 2, 128], BF16), Trk()) for i in range(2)]
        trks = [uk, rstd[1], cw1k] + [k for _, k in sqn] + kcTk + [k for _, k in T.values()] + [k for _, k in kout] + [k for _, k in vst] + [k for _, k in hid]
        self.new_phase(trks)
        self.pool_misc = [0, 1, 2, 3, 4, 5, 6, 7]
        for kv in range(2):
            off = self.L.off[("cw1", kv)]
            src = self.wbf[off:off + 128 * 8192].rearrange("(p f) -> p f", p=128)
            c0, c1 = off // CONV_CH, (off + 128 * 8192 - 1) // CONV_CH
            P.dma("sp", cw1[0:64, kv].rearrange("p a b -> p (a b)"), src[0:64, :], r=[self.wchunk[c] for c in range(c0, c1 + 1)], w=[cw1k])
        for i in range(2):
            P.op("pool", lambda e: e.memset(vst[i][0][:, :, 64:66], 1.0), w=[vst[i][1]])
        nko = 0
        nvs = 0
        for n in range(NT):
            tsl = slice(n * TT, (n + 1) * TT)
            self.norm_tile(n, ("kv_norm",), u, uk, sqn, rstd)
            for i in range(4):
                which, cc = i // 2, i % 2
                w = self.next_w(("kvk", i))
                pk = self.bank()
                for k in range(NCH):
                    P.op("pe", lambda e: e.matmul(pk[:], lhsT=w[:, k * 128:(k + 1) * 128], rhs=u[:, k, :], start=(k == 0), stop=(k == NCH - 1)), r=[w.k, uk], w=[pk.k])
                self.headnorm_rope(pk, 128, 512, self.vec(("k_norm", 1 + which)), self.cos_b[:, tsl], self.sin_b[:, tsl], T)
                ko, kok = kout[nko % 2]
                nko += 1
                P.op("dve", lambda e: e.tensor_tensor(out=ko, in0=T["t1"][0], in1=T["t2"][0], op=ALU.add), r=[T["t1"][1], T["t2"][1]], w=[kok])
                for hh in range(2):
                    P.dma("sp", self.KSd[which, :, 2 * cc + hh, tsl], ko[hh * 64:(hh + 1) * 64, :], r=[kok], w=[self.ksd_k[which][2 * cc + hh][n]])
            for i in range(4):
                sel, cc = i // 2, i % 2
                w = self.next_w(("kvc", i))
                for gg in range(2):
                    pc = self.bank()
                    for k in range(NCH):
                        P.op("pe", lambda e: e.matmul(pc[0:64, :], lhsT=w[:, k * 128 + gg * 64:k * 128 + gg * 64 + 64], rhs=u[:, k, :], start=(k == 0), stop=(k == NCH - 1)),
                             r=[w.k, uk], w=[pc.k])
                    idx = sel * 4 + 2 * cc + gg
                    P.op("act", lambda e: e.activation(out=kcT[0:64, idx, tsl], in_=pc[0:64, :], func=AF.Copy), r=[pc.k], w=[kcTk[idx]])
            for i in range(2):
                w = self.next_w(("kvv", i))
                for jb in range(4):
                    pv = self.bank()
                    for k in range(NCH):
                        P.op("pe", lambda e: e.matmul(pv[:, 0:256], lhsT=u[:, k, jb * 128:(jb + 1) * 128], rhs=w[:, k * 256:(k + 1) * 256], start=(k == 0), stop=(k == NCH - 1)),
                             r=[w.k, uk], w=[pv.k])
                    vs, vsk = vst[nvs % 2]
                    nvs += 1
                    P.op("act", lambda e: e.activation(out=vs[:, :, 0:64], in_=pv[:, 0:256].rearrange("p (g d) -> p g d", g=4), func=AF.Copy), r=[pv.k], w=[vsk])
                    P.dma("sp", self.Vd[i, :, 4 * n + jb, :, :], vs, r=[vsk], w=[self.vd_k[i][n]])
        w2 = self.next_w(("cw2",))
        for sel in range(2):
            pcv = self.bank()
            for cc in range(2):
                for l in range(32):
                    P.op("pe", lambda e: e.matmul(pcv[:, cc:cc + 1], lhsT=cw1[0:64, sel, l, cc * 128:(cc + 1) * 128], rhs=self.posb[0:64, sel, l:l + 1],
                                                  start=(cc == 0 and l == 0), stop=(cc == 1 and l == 31)), r=[cw1k, self.posb.k], w=[pcv.k])
            b1o = VEC[("c_b1", sel)]
            P.op("dve", lambda e: e.tensor_tensor(out=self.cbias[:, sel, :], in0=pcv[:, 0:2], in1=self.vecs[:, b1o:b1o + 2], op=ALU.add), r=[pcv.k, self.vecs.k], w=[self.cbias.k])
        nh = 0
        for sel in range(2):
            for g in range(4):
                hd, hdk = hid[nh % 2]
                nh += 1
                for cc in range(2):
                    ph = self.bank()
                    for l in range(32):
                        P.op("pe", lambda e: e.matmul(ph[:, 0:127], lhsT=cw1[0:64, sel, l, cc * 128:(cc + 1) * 128], rhs=kcT[0:64, sel * 4 + g, l:l + 16 * 126 + 1:16],
                                                      start=(l == 0), stop=(l == 31)), r=[cw1k, kcTk[sel * 4 + g]], w=[ph.k])
                    P.op("act", lambda e: e.activation(out=hd[:, cc, 0:127], in_=ph[:, 0:127], func=AF.Gelu_apprx_tanh, bias=self.cbias[:, sel, cc:cc + 1]),
                         r=[ph.k, self.cbias.k], w=[hdk])
                if sel == 0:
                    pk = self.bank()
                    for cc in range(2):
                        P.op("pe", lambda e: e.matmul(pk[0:64, 0:127], lhsT=w2[:, (0 * 2 + cc) * 64:(0 * 2 + cc) * 64 + 64], rhs=hd[:, cc, 0:127], start=(cc == 0), stop=(cc == 1)),
                             r=[w2.k, hdk], w=[pk.k])
                    self.headnorm_rope(pk, 64, 127, self.vecs[0:64, VEC[("k_norm", 0)]:VEC[("k_norm", 0)] + 1],
                                       self.cos_b[0:64, 31:31 + 16 * 126 + 1:16], self.sin_b[0:64, 31:31 + 16 * 126 + 1:16], T,
                                       bias=self.vecs[0:64, VEC[("c_b2", 0)]:VEC[("c_b2", 0)] + 1])
                    P.op("dve", lambda e: e.tensor_tensor(out=self.KC[0:64, g, 0:127], in0=T["t1"][0][0:64, 0:127], in1=T["t2"][0][0:64, 0:127], op=ALU.add),
                         r=[T["t1"][1], T["t2"][1]], w=[self.KC.k])
                else:
                    pv = self.bank()
                    for cc in range(2):
                        P.op("pe", lambda e: e.matmul(pv[0:127, 0:64], lhsT=hd[:, cc, 0:127], rhs=w2[:, (1 * 2 + cc) * 64:(1 * 2 + cc) * 64 + 64], start=(cc == 0), stop=(cc == 1)),
                             r=[w2.k, hdk], w=[pv.k])
                    P.op("dve", lambda e: e.tensor_tensor(out=self.VC[0:127, g, 0:64], in0=pv[0:127, 0:64], in1=self.bvecs[0:127, BV_CB2V:BV_CB2V + 64], op=ALU.add),
                         r=[pv.k, self.bvecs.k], w=[self.VC.k])

    def b_layer(self, j):
        P = self.P
        v = self.phase_view
        KS, KSk = v(0, [128, 4, S], BF16), [Trk() for _ in range(4)]
        KW, KWk = v(16384, [128, 4, S], BF16), [Trk() for _ in range(4)]
        VS, VSk = v(32768, [128, 16, 4, 66], BF16), Trk()
        VW, VWk = v(41216, [128, 16, 4, 66], BF16), Trk()
        u, uk = v(49664, [128, 8, 512], BF16), Trk()
        Q, Qk = v(57856, [128, 4, 16, 128], BF16), [Trk() for _ in range(4)]
        oT, oTk = v(74240, [128, 8, 512], BF16), Trk()
        ob = [(v(82432 + i * 2048, [128, 1024], BF16), Trk()) for i in range(2)]
        PT = [(v(86528 + i * 1024, [128, 512], BF16), Trk()) for i in range(4)]
        sqn = [(v(90624 + i * 1024, [128, 512], BF16), Trk()) for i in range(2)]
        rstd = (v(92672, [128, 512], F32), Trk())
        gat, gatk = v(94720, [128, 4, 48], F32), Trk()
        T = {"sq": sqn[0], "rs": rstd, "qn": (v(95488, [128, 512], BF16), Trk()),
             "t1": (v(96512, [128, 512], F32), Trk()), "t2": (v(98560, [128, 512], F32), Trk())}
        rd, rdk = v(100608, [128, 16], F32), Trk()
        s3, s3k = v(100672, [128, 12], F32), Trk()
        sc, sck = v(100736, [128, 32], F32), Trk()
        sc2, sc2k = v(100864, [128, 32], F32), Trk()
        nsl, nslk = v(100992, [128, 32], F32), Trk()
        m8, m8k = v(101120, [128, 16], F32), Trk()
        acc, acck = v(101184, [128, 256], F32), Trk()
        trks = KSk + KWk + [VSk, VWk, uk, oTk, gatk, rdk, s3k, sck, sc2k, nslk, m8k, acck] + Qk + [k for _, k in ob] + [k for _, k in PT] \
            + [k for _, k in sqn] + [rstd[1]] + [T["qn"][1], T["t1"][1], T["t2"][1]]
        self.new_phase(trks)
        self.pool_misc = [6, 7]
        pool_st = [0, 1, 2]
        for g in range(4):
            P.dma("sp", KS[0:64, g, :], self.KSd[0, :, g, :], r=self.ksd_k[0][g], w=[KSk[g]])
            P.dma("sp", KW[0:64, g, :], self.KSd[1, :, g, :], r=self.ksd_k[1][g], w=[KWk[g]])
        P.dma("sp", VS.rearrange("p a b c -> p (a b c)"), self.Vd[0].rearrange("p a b c -> p (a b c)"), r=self.vd_k[0], w=[VSk])
        P.dma("sp", VW.rearrange("p a b c -> p (a b c)"), self.Vd[1].rearrange("p a b c -> p (a b c)"), r=self.vd_k[1], w=[VWk])
        est = self.ph[64:96, 57856 // 4:57856 // 4 + 2048]
        P.dma("sp", est, self.cst_d[64:96, CST["eind"]:CST["eind"] + 2048], w=Qk)
        for g in range(4):
            P.op("dve", lambda e: e.tensor_copy(out=KS[64:96, g, :], in_=est), r=Qk, w=[KSk[g]])
        gbo = BV_GB + 48 * j
        npt = 0
        nob = 0
        for n in range(NT):
            tsl = slice(n * TT, (n + 1) * TT)
            self.norm_tile(n, ("b_norm", j), u, uk, sqn, rstd)
            wg = self.next_w(("bg", j))
            for qb in range(4):
                pg = self.bank(self.pool_misc)
                for k in range(NCH):
                    P.op("pe", lambda e: e.matmul(pg[:, 0:48], lhsT=u[:, k, qb * 128:(qb + 1) * 128], rhs=wg[:, k * 48:(k + 1) * 48], start=(k == 0), stop=(k == NCH - 1)),
                         r=[wg.k, uk], w=[pg.k])
                P.op("dve", lambda e: e.tensor_tensor(out=gat[:, qb, :], in0=pg[:, 0:48], in1=self.bvecs[:, gbo:gbo + 48], op=ALU.add), r=[pg.k, self.bvecs.k], w=[gatk])
            P.op("act", lambda e: e.activation(out=gat, in_=gat, func=AF.Sigmoid), r=[gatk], w=[gatk])
            for m in range(NCH):
                w = self.next_w(("bq", j, m))
                pq = self.bank(pool_st)
                for k in range(NCH):
                    P.op("pe", lambda e: e.matmul(pq[:], lhsT=w[:, k * 128:(k + 1) * 128], rhs=u[:, k, :], start=(k == 0), stop=(k == NCH - 1)), r=[w.k, uk], w=[pq.k])
                self.headnorm_rope(pq, 128, 512, self.vec(("q_norm", j)), self.cos_b[:, tsl], self.sin_b[:, tsl], T)
                for hh in range(2):
                    h = 2 * m + hh
                    rsl = slice(hh * 64, (hh + 1) * 64)
                    eng = "dve" if hh == 0 else "pool"
                    P.op(eng, lambda e: e.tensor_tensor(out=Q[0:64, :, h, :], in0=T["t1"][0][rsl, :].rearrange("p (a b) -> p a b", a=4),
                                                        in1=T["t2"][0][rsl, :].rearrange("p (a b) -> p a b", a=4), op=ALU.add),
                         r=[T["t1"][1], T["t2"][1]], w=Qk)
            for qb in range(4):
                qbg = 4 * n + qb
                o, ok_ = ob[nob % 2]
                nob += 1
                for g in range(4):
                    Qsl = Q[0:64, qb, 4 * g:4 * g + 4, :]
                    Qaug = Q[0:96, qb, 4 * g:4 * g + 4, :]
                    pst = self.bank(pool_st)
                    P.op("pe", lambda e: e.matmul(pst[0:127, :], lhsT=self.KC[0:64, g, 0:127], rhs=Qsl, start=True, stop=True), r=[self.KC.k, Qk[qb]], w=[pst.k])
                    pt, ptk = PT[npt % 4]
                    npt += 1
                    P.op("act", lambda e: e.activation(out=pt[0:127, :], in_=pst[0:127, :], func=AF.Exp, scale=SCALE), r=[pst.k], w=[ptk])
                    cm = self.cmask_b[0:127, qbg * 128:(qbg + 1) * 128].unsqueeze(1).broadcast_to([127, 4, 128])
                    P.op("pool", lambda e: e.tensor_tensor(out=pt[0:127, :].rearrange("p (a b) -> p a b", a=4), in0=pt[0:127, :].rearrange("p (a b) -> p a b", a=4), in1=cm, op=ALU.mult),
                         r=[ptk, self.cmask_b.k], w=[ptk])
                    poc = self.ps[3]
                    for h in range(4):
                        P.op("pe", lambda e: e.matmul(poc[:, h * 128:h * 128 + 97], lhsT=pt[0:127, h * 128:(h + 1) * 128], rhs=self.VC[0:127, g, 0:97],
                                                      start=(h == 0), stop=(h == 3), skip_group_check=True), r=[ptk, self.VC.k], w=[poc.k])
                    pocv = poc[:].rearrange("p (a b) -> p a b", a=4)
                    P.op("dve", lambda e: e.tensor_scalar(out=rd[:, 0:4], in0=pocv[:, :, 64], scalar1=1e-30, scalar2=None, op0=ALU.max), r=[poc.k], w=[rdk])
                    P.op("dve", lambda e: e.reciprocal(out=rd[:, 0:4], in_=rd[:, 0:4]), r=[rdk], w=[rdk])
                    for h in range(4):
                        src1 = self.selb[:, qbg, :] if h == 0 else sc
                        P.op("dve", lambda e: e.scalar_tensor_tensor(out=sc, in0=poc[:, h * 128 + 65:h * 128 + 97], scalar=rd[:, h:h + 1], in1=src1, op0=ALU.mult, op1=ALU.add),
                             r=[poc.k, rdk, sck, self.selb.k], w=[sck])
                    P.op("dve", lambda e: e.max(out=m8[:, 0:8], in_=sc), r=[sck], w=[m8k])
                    P.op("dve", lambda e: e.match_replace(out=sc2, in_to_replace=m8[:, 0:8], in_values=sc, imm_value=-3.0e38), r=[sck, m8k], w=[sc2k])
                    P.op("dve", lambda e: e.max(out=m8[:, 8:16], in_=sc2), r=[sc2k], w=[m8k])
                    P.op("dve", lambda e: e.tensor_scalar(out=nsl, in0=sc, scalar1=m8[:, 15:16], scalar2=NEGM, op0=ALU.is_lt, op1=ALU.mult), r=[sck, m8k], w=[nslk])
                    pm = self.bank(self.pool_misc)
                    P.op("pe", lambda e: e.transpose(out=pm[0:32, 0:128], in_=nsl, identity=self.ident_f[:]), r=[nslk, self.ident_f.k], w=[pm.k])
                    P.op("act", lambda e: e.activation(out=Q[64:96, qb, 4 * g:4 * g + 4, :], in_=pm[0:32, 0:128].unsqueeze(1).broadcast_to([32, 4, 128]), func=AF.Copy),
                         r=[pm.k], w=[Qk[qb]])
                    pos = self.ps[4]
                    first = True
                    for kt in range(qbg + 1):
                        pst = self.bank(pool_st)
                        P.op("pe", lambda e: e.matmul(pst[:], lhsT=KS[0:96, g, kt * 128:(kt + 1) * 128], rhs=Qaug, start=True, stop=True), r=[KSk[g], Qk[qb]], w=[pst.k])
                        pt, ptk = PT[npt % 4]
                        npt += 1
                        P.op("act", lambda e: e.activation(out=pt, in_=pst[:], func=AF.Exp, scale=SCALE), r=[pst.k], w=[ptk])
                        if kt == qbg:
                            tm = self.triL_b[:].unsqueeze(1).broadcast_to([128, 4, 128])
                            P.op("pool", lambda e: e.tensor_tensor(out=pt.rearrange("p (a b) -> p a b", a=4), in0=pt.rearrange("p (a b) -> p a b", a=4), in1=tm, op=ALU.mult),
                                 r=[ptk, self.triL_b.k], w=[ptk])
                        for h in range(4):
                            P.op("pe", lambda e: e.matmul(pos[:, h * 128:h * 128 + 65], lhsT=pt[:, h * 128:(h + 1) * 128], rhs=VS[:, kt, g, 0:65],
                                                          start=first, stop=(kt == qbg and h == 3), skip_group_check=True), r=[ptk, VSk], w=[pos.k])
                            first = False
                    pow_ = self.ps[5]
                    first = True
                    k0 = max(0, qbg - 4)
                    for kt in range(k0, qbg + 1):
                        pst = self.bank(pool_st)
                        P.op("pe", lambda e: e.matmul(pst[:], lhsT=KW[0:64, g, kt * 128:(kt + 1) * 128], rhs=Qsl, start=True, stop=True), r=[KWk[g], Qk[qb]], w=[pst.k])
                        pt, ptk = PT[npt % 4]
                        npt += 1
                        P.op("act", lambda e: e.activation(out=pt, in_=pst[:], func=AF.Exp, scale=SCALE), r=[pst.k], w=[ptk])
                        msk = None
                        if kt == qbg:
                            msk = self.triL_b
                        elif kt == qbg - 4:
                            msk = self.triU_b
                        if msk is not None:
                            tm = msk[:].unsqueeze(1).broadcast_to([128, 4, 128])
                            P.op("pool", lambda e: e.tensor_tensor(out=pt.rearrange("p (a b) -> p a b", a=4), in0=pt.rearrange("p (a b) -> p a b", a=4), in1=tm, op=ALU.mult),
                                 r=[ptk, msk.k], w=[ptk])
                        for h in range(4):
                            P.op("pe", lambda e: e.matmul(pow_[:, h * 128:h * 128 + 65], lhsT=pt[:, h * 128:(h + 1) * 128], rhs=VW[:, kt, g, 0:65],
                                                          start=first, stop=(kt == qbg and h == 3), skip_group_check=True), r=[ptk, VWk], w=[pow_.k])
                            first = False
                    for bi, pob in enumerate((poc, pos, pow_)):
                        pv_ = pob[:].rearrange("p (a b) -> p a b", a=4)
                        P.op("dve", lambda e: e.tensor_scalar(out=rd[:, 4 + bi * 4:8 + bi * 4], in0=pv_[:, :, 64], scalar1=1e-30, scalar2=None, op0=ALU.max), r=[pob.k], w=[rdk])
                    P.op("dve", lambda e: e.reciprocal(out=rd[:, 4:16], in_=rd[:, 4:16]), r=[rdk], w=[rdk])
                    P.op("dve", lambda e: e.tensor_tensor(out=s3.rearrange("p (a b) -> p a b", a=3), in0=rd[:, 4:16].rearrange("p (a b) -> p a b", a=3),
                                                          in1=gat[:, qb, :].rearrange("p (a b) -> p a b", a=3)[:, :, 4 * g:4 * g + 4], op=ALU.mult), r=[rdk, gatk], w=[s3k])
                    for h in range(4):
                        ah = acc[:, h * 64:(h + 1) * 64]
                        P.op("dve", lambda e: e.tensor_scalar(out=ah, in0=poc[:, h * 128:h * 128 + 64], scalar1=s3[:, h:h + 1], scalar2=None, op0=ALU.mult), r=[poc.k, s3k], w=[acck])
                        P.op("dve", lambda e: e.scalar_tensor_tensor(out=ah, in0=pos[:, h * 128:h * 128 + 64], scalar=s3[:, 4 + h:5 + h], in1=ah, op0=ALU.mult, op1=ALU.add),
                             r=[pos.k, s3k, acck], w=[acck])
                        P.op("dve", lambda e: e.scalar_tensor_tensor(out=o[:, g * 256 + h * 64:g * 256 + (h + 1) * 64], in0=pow_[:, h * 128:h * 128 + 64], scalar=s3[:, 8 + h:9 + h], in1=ah,
                                                                     op0=ALU.mult, op1=ALU.add), r=[pow_.k, s3k, acck], w=[ok_])
                pT = self.bank(self.pool_misc)
                pTb = pT[:].bitcast(BF16)
                for c in range(NCH):
                    P.op("pe", lambda e: e.transpose(out=pTb[:, c * 128:(c + 1) * 128], in_=o[:, c * 128:(c + 1) * 128], identity=self.ident_b[:]), r=[ok_, self.ident_b.k], w=[pT.k])
                P.op("act", lambda e: e.activation(out=oT[:, :, qb * 128:(qb + 1) * 128], in_=pTb.rearrange("p (a b) -> p a b", a=8), func=AF.Copy), r=[pT.k], w=[oTk])
            for mm in range(NCH):
                w = self.next_w(("bo", j, mm))
                po = self.bank(pool_st)
                for k in range(NCH):
                    P.op("pe", lambda e: e.matmul(po[:], lhsT=w[:, k * 128:(k + 1) * 128], rhs=oT[:, k, :], start=(k == 0), stop=(k == NCH - 1)), r=[w.k, oTk], w=[po.k])
                hsl = self.hT[:, mm, tsl]
                P.op("dve", lambda e: e.tensor_tensor(out=hsl, in0=po[:], in1=hsl, op=ALU.add), r=[po.k, self.hk[mm][n]], w=[self.hk[mm][n]])
        self.pool_misc = [0, 1, 2, 3, 4, 5, 6, 7]

    def make_plan(self):
        plan = []
        for b in range(self.nb):
            for layer in range(self.n_layers):
                if layer < 2:
                    for n in range(NT):
                        plan += [("a_in", layer, c) for c in range(8)]
                        plan += [("a_out", layer, m) for m in range(8)]
                else:
                    if layer == 2:
                        for n in range(NT):
                            plan += [("kvk", i) for i in range(4)] + [("kvc", i) for i in range(4)] + [("kvv", i) for i in range(2)]
                        plan += [("cw2",)]
                    for n in range(NT):
                        plan += [("bg", layer - 2)] + [("bq", layer - 2, m) for m in range(8)] + [("bo", layer - 2, m) for m in range(8)]
                for t2 in range(2):
                    plan += [("f_in", layer, c) for c in range(FC)]
                    plan += [("f_out", layer, m) for m in range(8)]
        return plan

    def build(self):
        P = self.P
        self.convc = self.sb("convc", [128, 8, 3], F32)
        self.hst = self.sb("hst", [128, 8], F32)
        self.prologue()
        self.plan_weights(self.make_plan())
        for b in range(self.nb):
            self.load_x(b)
            for layer in range(self.n_layers):
                if layer < 2:
                    self.a_layer(layer)
                else:
                    if layer == 2:
                        self.kv_phase()
                    self.b_layer(layer - 2)
                self.ffn_layer(layer)
            self.store_y(b)
        P.wait_all("sp", self.out_trks)
        return self.nc


_CACHE = {}


def _prep_inputs(inputs):
    L = weight_layout()
    wall = pack_weights(inputs, L)
    vecs, bvecs = pack_vecs(inputs)
    cst = make_consts()
    return wall, vecs, bvecs, cst


def kernel(**inputs):
    inputs = {k: np.asarray(v) for k, v in inputs.items()}
    x = np.ascontiguousarray(inputs["x"], dtype=np.float32)
    B = x.shape[0]
    nb = B // NCORES
    wall, vecs, bvecs, cst = _prep_inputs(inputs)
    nc = Builder(nb).build()
    in_maps = []
    for c in range(NCORES):
        in_maps.append({"x": np.ascontiguousarray(x[c * nb:(c + 1) * nb]), "wall": wall, "vecs": vecs, "bvecs": bvecs, "cst": cst})
    res = run_bass_kernel_spmd(nc, in_maps, core_ids=list(range(NCORES)))
    out = np.concatenate([np.asarray(r["y"]).reshape(nb, S, D) for r in res.results], axis=0)
    return out.astype(np.float32)
```

```python
import numpy as np
import concourse.bass as bass
import concourse.mybir as mybir
from concourse.bass_utils import run_bass_kernel_spmd
from concourse.alu_op_type import AluOpType as ALU

F32 = mybir.dt.float32
BF16 = mybir.dt.bfloat16
AF = mybir.ActivationFunctionType

S = 2048
D = 1024
NCH = 8
TT = 512
NT = S // TT
FH = 2816
FC = 22
NCORES = 8
EPS = 1e-6
CONV_CH = 128 * 2048
SCALE = 0.125
NEGM = -30000.0


def _kt(W, c0, width=128):
    K = W.shape[0]
    return np.ascontiguousarray(W[:, c0:c0 + width].reshape(K // 128, 128, width).transpose(1, 0, 2)).reshape(128, -1)


class WLayout:
    def __init__(self):
        self.off = {}
        self.free = {}
        self.n = 0

    def add(self, name, free):
        self.off[name] = self.n
        self.free[name] = free
        self.n += 128 * free

    def total(self):
        return ((self.n + CONV_CH - 1) // CONV_CH) * CONV_CH


def weight_layout():
    L = WLayout()
    for l in range(2):
        for c in range(8):
            L.add(("a_in", l, c), 2304)
        for m in range(8):
            L.add(("a_out", l, m), 1024)
    for l in range(4):
        for c in range(FC):
            L.add(("f_in", l, c), 2048)
        for m in range(8):
            L.add(("f_out", l, m), FH)
    for i in range(4):
        L.add(("kvk", i), 1024)
    for i in range(4):
        L.add(("kvc", i), 1024)
    for i in range(2):
        L.add(("kvv", i), 2048)
    for i in range(2):
        L.add(("cw1", i), 8192)
    L.add(("cw2",), 256)
    for j in range(2):
        L.add(("bg", j), 384)
        for m in range(8):
            L.add(("bq", j, m), 1024)
        for m in range(8):
            L.add(("bo", j, m), 1024)
    return L


def pack_weights(inp, L):
    out = np.zeros(L.total(), np.float32)

    def put(name, arr):
        arr = np.asarray(arr, np.float32).reshape(128, -1)
        assert arr.shape[1] == L.free[name], (name, arr.shape)
        out[L.off[name]:L.off[name] + arr.size] = arr.reshape(-1)

    for l in range(2):
        Win = inp["a_w_in"][l]
        for c in range(8):
            put(("a_in", l, c), np.concatenate(
                [_kt(Win, c * 128), _kt(Win, 1024 + c * 128), inp["a_gate_w"][l][0, c], inp["a_gate_w"][l][1, c]], axis=1))
        for m in range(8):
            put(("a_out", l, m), _kt(inp["a_w_out"][l], m * 128))
    for l in range(4):
        W = inp["f_w_in"][l]
        for c in range(FC):
            put(("f_in", l, c), np.concatenate([_kt(W, c * 128), _kt(W, FH + c * 128)], axis=1))
        for m in range(8):
            put(("f_out", l, m), _kt(inp["f_w_out"][l], m * 128))
    kvw = inp["kv_w"]
    i = 0
    for jj in (2, 4):
        for cc in range(2):
            put(("kvk", i), _kt(kvw, jj * 256 + cc * 128))
            i += 1
    i = 0
    for jj in (0, 1):
        for cc in range(2):
            put(("kvc", i), _kt(kvw, jj * 256 + cc * 128))
            i += 1
    for i, jj in enumerate((3, 5)):
        put(("kvv", i), _kt(kvw, jj * 256, 256))
    for kv in range(2):
        t = np.zeros((128, 32, 256), np.float32)
        t[:64] = inp["cmp_w1"][kv].reshape(32, 64, 256).transpose(1, 0, 2)
        put(("cw1", kv), t)
    t = np.zeros((128, 2, 2, 64), np.float32)
    for kv in range(2):
        t[:, kv] = inp["cmp_w2"][kv].reshape(2, 128, 64).transpose(1, 0, 2)
    put(("cw2",), t)
    for j in range(2):
        put(("bg", j), _kt(inp["b_w_in"][j], 1024, 48))
        for m in range(8):
            put(("bq", j, m), _kt(inp["b_w_in"][j], m * 128))
        for m in range(8):
            put(("bo", j, m), _kt(inp["b_w_out"][j], m * 128))
    return out


VEC = {}
_nv = 0


def _vadd(name, n):
    global _nv
    VEC[name] = _nv
    _nv += n


for _l in range(2):
    _vadd(("a_norm", _l), 8)
    for _k in range(4):
        _vadd(("a_cw", _l, _k), 8)
    _vadd(("a_cb", _l), 8)
    _vadd(("a_gb", _l, 0), 8)
    _vadd(("a_gb", _l, 1), 8)
    _vadd(("a_lam", _l), 8)
for _l in range(4):
    _vadd(("f_norm", _l), 8)
_vadd(("kv_norm",), 8)
for _j in range(2):
    _vadd(("b_norm", _j), 8)
    _vadd(("q_norm", _j), 1)
for _i in range(3):
    _vadd(("k_norm", _i), 1)
for _i in range(2):
    _vadd(("c_b1", _i), 2)
    _vadd(("c_b2", _i), 1)
    _vadd(("c_pos", _i), 32)
NV = _nv
BV_GB = 0
BV_CB2V = 96
NBV = 160


def pack_vecs(inp):
    v = np.zeros((128, NV), np.float32)

    def fm(x):
        return np.asarray(x, np.float32).reshape(8, 128).T

    for l in range(2):
        v[:, VEC[("a_norm", l)]:][:, :8] = fm(inp["a_norm"][l])
        for k in range(4):
            v[:, VEC[("a_cw", l, k)]:][:, :8] = fm(inp["a_conv_w"][l][k])
        v[:, VEC[("a_cb", l)]:][:, :8] = fm(inp["a_conv_b"][l])
        v[:, VEC[("a_gb", l, 0)]:][:, :8] = fm(inp["a_gate_b"][l][0])
        v[:, VEC[("a_gb", l, 1)]:][:, :8] = fm(inp["a_gate_b"][l][1])
        v[:, VEC[("a_lam", l)]:][:, :8] = fm(inp["a_lambda"][l])
    for l in range(4):
        v[:, VEC[("f_norm", l)]:][:, :8] = fm(inp["f_norm"][l])
    v[:, VEC[("kv_norm",)]:][:, :8] = fm(inp["kv_norm"])
    for j in range(2):
        v[:, VEC[("b_norm", j)]:][:, :8] = fm(inp["b_norm"][j])
        v[:, VEC[("q_norm", j)]] = np.tile(np.asarray(inp["q_norm"][j], np.float32), 2)
    for i in range(3):
        v[:, VEC[("k_norm", i)]] = np.tile(np.asarray(inp["k_norm"][i], np.float32), 2)
    for i in range(2):
        v[:, VEC[("c_b1", i)]:][:, :2] = np.asarray(inp["cmp_b1"][i], np.float32).reshape(2, 128).T
        v[:, VEC[("c_b2", i)]] = np.tile(np.asarray(inp["cmp_b2"][i], np.float32), 2)
        v[:64, VEC[("c_pos", i)]:][:, :32] = np.asarray(inp["cmp_pos"][i], np.float32).T
    bv = np.zeros((128, NBV), np.float32)
    for j in range(2):
        bv[:, BV_GB + 48 * j: BV_GB + 48 * j + 48] = np.asarray(inp["b_gate_b"][j], np.float32)[None, :]
    bv[:, BV_CB2V:BV_CB2V + 64] = np.asarray(inp["cmp_b2"][1], np.float32)[None, :]
    return v, bv


CST = {}
_nc_ = 0


def _cadd(name, n):
    global _nc_
    CST[name] = _nc_
    _nc_ += n


_cadd("ident", 128)
_cadd("prot", 128)
_cadd("bones", 128)
_cadd("triL", 128)
_cadd("triU", 128)
_cadd("ovl", 32)
_cadd("cos", 2048)
_cadd("sin", 2048)
_cadd("cmask", 2048)
_cadd("eind", 2048)
_cadd("selb", 512)
NCST = _nc_


def make_consts():
    c = np.zeros((128, NCST), np.float32)
    p = np.arange(128)
    c[:, CST["ident"]:][:, :128] = np.eye(128)
    perm = (p // 64) * 64 + ((p % 64) + 32) % 64
    pr = np.zeros((128, 128), np.float32)
    pr[perm, p] = 1.0
    c[:, CST["prot"]:][:, :128] = pr
    c[:, CST["bones"]:][:, :128] = (p[:, None] // 64 == p[None, :] // 64)
    c[:, CST["triL"]:][:, :128] = (p[:, None] <= p[None, :])
    c[:, CST["triU"]:][:, :128] = (p[:, None] > p[None, :])
    cs = np.arange(127) * 16
    sl = np.arange(32) * 64
    ov = np.clip(np.minimum(cs[:, None] + 32, sl[None, :] + 64) - np.maximum(cs[:, None], sl[None, :]), 0, None) / 32.0
    c[:127, CST["ovl"]:][:, :32] = ov
    t = np.arange(2048, dtype=np.float64)
    fi = (p % 64) % 32
    freqs = (10000.0 ** (-(np.arange(32, dtype=np.float32) / np.float32(32)))).astype(np.float32)
    ang = (t[None, :].astype(np.float32) * freqs[fi][:, None]).astype(np.float32)
    c[:, CST["cos"]:][:, :2048] = np.cos(ang)
    sg = np.where((p % 64) < 32, -1.0, 1.0)[:, None]
    c[:, CST["sin"]:][:, :2048] = np.sin(ang) * sg
    cl = np.arange(127) * 16 + 31
    c[:127, CST["cmask"]:][:, :2048] = (cl[:, None] <= t[None, :])
    b = np.arange(32)
    c[64:96, CST["eind"]:][:, :2048] = (t[None, :].astype(np.int64) // 64 == b[:, None])
    sb = np.zeros((128, 16, 32), np.float32)
    for qb in range(16):
        tq = qb * 128 + p
        cur = (tq // 64)[:, None]
        causal = b[None, :] <= cur
        forced = (b[None, :] == 0) | (causal & (cur - b[None, :] < 2))
        sb[:, qb, :] = np.where(forced, 1e30, np.where(causal, 0.0, -1e30))
    c[:, CST["selb"]:][:, :512] = sb.reshape(128, 512)
    return c


class Trk:
    __slots__ = ("w", "r")

    def __init__(self):
        self.w = None
        self.r = {}


NDS = 24


class Prog:
    def __init__(self, nc):
        self.nc = nc
        self.eng = {"pe": nc.tensor, "act": nc.scalar, "dve": nc.vector, "pool": nc.gpsimd, "sp": nc.sync}
        self.sem = {e: nc.alloc_semaphore(name="s_" + e) for e in ("pe", "act", "dve", "pool")}
        self.cnt = {e: 0 for e in ("pe", "act", "dve", "pool")}
        self.seen = {e: {} for e in self.eng}
        self.dsem = [nc.alloc_semaphore(name="s_dma%d" % i) for i in range(NDS)]
        self.dn = 0
        self.ninst = 0

    def _semof(self, key):
        return self.sem[key] if isinstance(key, str) else self.dsem[key[1]]

    def _wait(self, e, key, val):
        if key == "pe" and e == "pe":
            return
        if self.seen[e].get(key, 0) >= val:
            return
        self.eng[e].wait_ge(self._semof(key), val)
        self.seen[e][key] = val

    def _deps(self, e, r, w):
        for t in r:
            if t.w is not None:
                self._wait(e, t.w[0], t.w[1])
        for t in w:
            if t.w is not None:
                self._wait(e, t.w[0], t.w[1])
            for k, v in t.r.items():
                self._wait(e, k, v)

    def op(self, e, fn, r=(), w=()):
        self._deps(e, r, w)
        inst = fn(self.eng[e])
        self.cnt[e] += 1
        self.ninst += 1
        inst.then_inc(self.sem[e], 1)
        v = self.cnt[e]
        for t in r:
            t.r[e] = v
        for t in w:
            t.w = (e, v)
            t.r = {}
        return inst

    def dma(self, q, out, in_, r=(), w=()):
        self._deps(q, r, w)
        i = self.dn % NDS
        gen = self.dn // NDS
        key = ("d", i)
        if gen > 0:
            self._wait(q, key, 16 * gen)
        inst = self.eng[q].dma_start(out=out, in_=in_)
        inst.then_inc(self.dsem[i], 16)
        self.dn += 1
        self.ninst += 1
        v = 16 * (gen + 1)
        for t in r:
            t.r[key] = v
        for t in w:
            t.w = (key, v)
            t.r = {}
        return (key, v)

    def wait_all(self, e, trks):
        self._deps(e, trks, trks)


class Buf:
    def __init__(self, t):
        self.t = t
        self.k = Trk()

    def __getitem__(self, idx):
        return self.t[idx]


class Builder:
    def __init__(self, nb, n_layers=4, debug_out=None):
        self.nb = nb
        self.n_layers = n_layers
        self.L = weight_layout()
        nc = bass.Bass("TRN2", target_bir_lowering=False)
        self.nc = nc
        self.P = Prog(nc)
        NW = self.L.total()
        self.x = nc.dram_tensor("x", [nb, S, D], F32, kind="ExternalInput").ap()
        self.wall = nc.dram_tensor("wall", [NW], F32, kind="ExternalInput").ap()
        self.vecs_d = nc.dram_tensor("vecs", [128, NV], F32, kind="ExternalInput").ap()
        self.bvecs_d = nc.dram_tensor("bvecs", [128, NBV], F32, kind="ExternalInput").ap()
        self.cst_d = nc.dram_tensor("cst", [128, NCST], F32, kind="ExternalInput").ap()
        self.y = nc.dram_tensor("y", [nb, S, D], F32, kind="ExternalOutput").ap()
        self.wbf = nc.dram_tensor("wbf", [NW], BF16, kind="Internal").ap()
        self.wchunk = [Trk() for _ in range(NW // CONV_CH)]
        self.out_trks = []
        self.marks = []

        def sb(name, shape, dt):
            return Buf(nc.alloc_sbuf_tensor(name, shape, dt))

        self.sb = sb
        self.hT = nc.alloc_sbuf_tensor("hT", [128, NCH, S], F32)
        self.hk = [[Trk() for _ in range(NT)] for _ in range(NCH)]
        self.vecs = sb("vecs_sb", [128, NV], F32)
        self.bvecs = sb("bvecs_sb", [128, NBV], F32)
        self.ident_f = sb("ident_f", [128, 128], F32)
        self.ident_b = sb("ident_b", [128, 128], BF16)
        self.ones_b = sb("ones_b", [128, 128], BF16)
        self.coef = sb("coef", [128, 2, 2, 8], F32)
        self.cos_b = sb("cos_b", [128, S], BF16)
        self.sin_b = sb("sin_b", [128, S], BF16)
        self.cmask_b = sb("cmask_b", [128, S], BF16)
        self.triL_b = sb("triL_b", [128, 128], BF16)
        self.triU_b = sb("triU_b", [128, 128], BF16)
        self.prot_b = sb("prot_b", [128, 128], BF16)
        self.bones_b = sb("bones_b", [128, 128], BF16)
        self.selb = sb("selb", [128, 16, 32], F32)
        self.KC = sb("KC", [128, 4, 128], BF16)
        self.VC = sb("VC", [128, 4, 98], BF16)
        self.posb = sb("posb", [128, 2, 32], BF16)
        self.cbias = sb("cbias", [128, 2, 2], F32)
        self.pocs = [sb("pocs%d" % i, [128, 4, 97], F32) for i in range(2)]
        self.pows = sb("pows", [128, 4, 65], F32)
        self.KSd = nc.dram_tensor("KSd", [2, 64, 4, S], BF16, kind="Internal").ap()
        self.Vd = nc.dram_tensor("Vd", [2, 128, 16, 4, 66], BF16, kind="Internal").ap()
        self.ksd_k = [[[Trk() for _ in range(NT)] for _ in range(4)] for _ in range(2)]
        self.vd_k = [[Trk() for _ in range(NT)] for _ in range(2)]
        self.ps = [Buf(nc.alloc_psum_tensor("ps%d" % i, [128, 512], F32)) for i in range(8)]
        self.ps_rr = 0
        self.pool_misc = [0, 1, 2, 3, 4, 5, 6, 7]
        self.NSLOT = 3
        self.ring = [sb("wring%d" % i, [128, FH], BF16) for i in range(self.NSLOT)]
        self.ring_n = 0
        self.wq = []
        self.wplan = []
        self.wplan_i = 0
        self.PHB = 100 * 1024
        self.ph = nc.alloc_sbuf_tensor("phase", [128, self.PHB // 4], F32)
        self.phk = {}

    def bank(self, pool=None):
        if pool is None:
            b = self.ps[self.ps_rr % 8]
        else:
            b = self.ps[pool[self.ps_rr % len(pool)]]
        self.ps_rr += 1
        return b

    def vec(self, name, c=0):
        i = VEC[name] + c
        return self.vecs[:, i:i + 1]

    def phase_view(self, byte_off, shape, dt):
        esz = 4 if dt == F32 else 2
        n = int(np.prod(shape[1:]))
        assert byte_off % 4 == 0 and byte_off + n * esz <= self.PHB, (byte_off, shape)
        if dt == F32:
            ap = self.ph[:, byte_off // 4: byte_off // 4 + n]
        else:
            ap = self.ph[:].bitcast(BF16)[:, byte_off // 2: byte_off // 2 + n]
        if len(shape) == 2:
            return ap
        names = " ".join("a%d" % i for i in range(len(shape) - 1))
        kw = {"a%d" % i: shape[i + 1] for i in range(len(shape) - 1)}
        return ap.rearrange("p (%s) -> p %s" % (names, names), **kw)

    def mark(self, name):
        self.marks.append((name, dict(self.P.cnt)))

    def new_phase(self, trks):
        merged = {}
        for t in self.phase_cur:
            if t.w is not None:
                merged[t.w[0]] = max(merged.get(t.w[0], 0), t.w[1])
            for k, v in t.r.items():
                merged[k] = max(merged.get(k, 0), v)
        for t in trks:
            t.w = None
            t.r = dict(merged)
        self.phase_cur = list(trks)

    def plan_weights(self, names):
        self.wplan = list(names)
        self.wplan_i = 0
        self.wq = []

    def _issue_w(self):
        name = self.wplan[self.wplan_i]
        self.wplan_i += 1
        slot = self.ring[self.ring_n % self.NSLOT]
        self.ring_n += 1
        off = self.L.off[name]
        free = self.L.free[name]
        src = self.wbf[off:off + 128 * free].rearrange("(p f) -> p f", p=128)
        c0 = off // CONV_CH
        c1 = (off + 128 * free - 1) // CONV_CH
        self.P.dma("sp", slot[:, 0:free], src, r=[self.wchunk[c] for c in range(c0, c1 + 1)], w=[slot.k])
        self.wq.append((name, slot))

    def next_w(self, name):
        while len(self.wq) < self.NSLOT - 1 and self.wplan_i < len(self.wplan):
            self._issue_w()
        n, slot = self.wq.pop(0)
        assert n == name, (n, name)
        return slot

    def prologue(self):
        P = self.P
        nc = self.nc
        P.dma("sp", self.vecs[:], self.vecs_d, w=[self.vecs.k])
        P.dma("sp", self.bvecs[:], self.bvecs_d, w=[self.bvecs.k])
        P.dma("sp", self.ident_f[:], self.cst_d[:, CST["ident"]:CST["ident"] + 128], w=[self.ident_f.k])
        P.op("dve", lambda e: e.tensor_copy(out=self.ident_b[:], in_=self.ident_f[:]), r=[self.ident_f.k], w=[self.ident_b.k])
        P.op("pool", lambda e: e.memset(self.ones_b[:], 1.0), w=[self.ones_b.k])
        tmp = self.sb("coef_tmp", [128, 16], F32)
        for l in range(2):
            lam = self.vecs[:, VEC[("a_lam", l)]:VEC[("a_lam", l)] + 8]
            P.op("act", lambda e: e.activation(out=tmp[:, l * 8:l * 8 + 8], in_=lam, func=AF.Exp, scale=-1.0), r=[self.vecs.k], w=[tmp.k])
            P.op("act", lambda e: e.activation(out=tmp[:, l * 8:l * 8 + 8], in_=tmp[:, l * 8:l * 8 + 8], func=AF.Ln, bias=1.0), r=[tmp.k], w=[tmp.k])
            P.op("dve", lambda e: e.tensor_scalar(out=self.coef[:, l, 0, :], in0=tmp[:, l * 8:l * 8 + 8], scalar1=-8.0, scalar2=None, op0=ALU.mult), r=[tmp.k], w=[self.coef.k])
            P.op("dve", lambda e: e.tensor_scalar(out=self.coef[:, l, 1, :], in0=tmp[:, l * 8:l * 8 + 8], scalar1=-16.0, scalar2=None, op0=ALU.mult), r=[tmp.k], w=[self.coef.k])
        stg = self.phase_view(0, [128, 2048], F32)
        stk = Trk()
        self.phase_cur = [stk]

        def ctab(name, n, dst, dstk, eng):
            P.dma("sp", stg[:, 0:n], self.cst_d[:, CST[name]:CST[name] + n], w=[stk])
            if eng == "act":
                P.op("act", lambda e: e.activation(out=dst, in_=stg[:, 0:n], func=AF.Copy), r=[stk], w=[dstk])
            else:
                P.op(eng, lambda e: e.tensor_copy(out=dst, in_=stg[:, 0:n]), r=[stk], w=[dstk])

        ctab("cos", 2048, self.cos_b[:], self.cos_b.k, "dve")
        ctab("sin", 2048, self.sin_b[:], self.sin_b.k, "act")
        ctab("cmask", 2048, self.cmask_b[:], self.cmask_b.k, "dve")
        ctab("triL", 128, self.triL_b[:], self.triL_b.k, "dve")
        ctab("triU", 128, self.triU_b[:], self.triU_b.k, "dve")
        ctab("prot", 128, self.prot_b[:], self.prot_b.k, "dve")
        ctab("bones", 128, self.bones_b[:], self.bones_b.k, "dve")
        P.dma("sp", self.selb[:].rearrange("p a b -> p (a b)"), self.cst_d[:, CST["selb"]:CST["selb"] + 512], w=[self.selb.k])
        P.op("pool", lambda e: e.memset(self.VC[:], 0.0), w=[self.VC.k])
        P.op("pool", lambda e: e.memset(self.KC[:], 0.0), w=[self.KC.k])
        P.op("pool", lambda e: e.memset(self.VC[:, :, 64:65], 1.0), w=[self.VC.k])
        P.dma("sp", stg[:, 0:32], self.cst_d[:, CST["ovl"]:CST["ovl"] + 32], w=[stk])
        for g in range(4):
            P.op("dve", lambda e: e.tensor_copy(out=self.VC[:, g, 65:97], in_=stg[:, 0:32]), r=[stk], w=[self.VC.k])
        for kv in range(2):
            o0 = VEC[("c_pos", kv)]
            P.op("dve", lambda e: e.tensor_copy(out=self.posb[:, kv, :], in_=self.vecs[:, o0:o0 + 32]), r=[self.vecs.k], w=[self.posb.k])
        nst = 3
        stf = [(self.phase_view(i * 12288, [128, 2048], F32), Trk()) for i in range(nst)]
        stb = [(self.phase_view(i * 12288 + 8192, [128, 2048], BF16), Trk()) for i in range(nst)]
        self.new_phase([k for _, k in stf] + [k for _, k in stb])
        nchunk = len(self.wchunk)
        engs = ["dve", "act", "pool"]
        for i in range(nchunk):
            sf, kf = stf[i % nst]
            sbb, kb = stb[i % nst]
            src = self.wall[i * CONV_CH:(i + 1) * CONV_CH].rearrange("(p f) -> p f", p=128)
            dst = self.wbf[i * CONV_CH:(i + 1) * CONV_CH].rearrange("(p f) -> p f", p=128)
            P.dma("sp", sf, src, w=[kf])
            e = engs[i % 3]
            if e == "act":
                P.op("act", lambda en: en.activation(out=sbb, in_=sf, func=AF.Copy), r=[kf], w=[kb])
            else:
                P.op(e, lambda en: en.tensor_copy(out=sbb, in_=sf), r=[kf], w=[kb])
            P.dma("sp", dst, sbb, r=[kb], w=[self.wchunk[i]])
        self.phase_cur = [k for _, k in stf] + [k for _, k in stb] + [stk]

    def load_x(self, b):
        P = self.P
        xin = [(self.phase_view(j * 4096, [128, 1024], F32), Trk()) for j in range(4)]
        self.new_phase([k for _, k in xin])
        for n in range(NT):
            for j in range(4):
                t0 = n * TT + j * 128
                P.dma("sp", xin[j][0], self.x[b, t0:t0 + 128, :], w=[xin[j][1]])
            for c in range(NCH):
                pb = self.bank()
                for j in range(4):
                    P.op("pe", lambda e: e.transpose(out=pb[:, j * 128:(j + 1) * 128], in_=xin[j][0][:, c * 128:(c + 1) * 128], identity=self.ident_f[:]),
                         r=[xin[j][1], self.ident_f.k], w=[pb.k])
                eng = "act" if c % 2 == 0 else "dve"
                dst = self.hT[:, c, n * TT:(n + 1) * TT]
                if eng == "act":
                    P.op("act", lambda e: e.activation(out=dst, in_=pb[:], func=AF.Copy), r=[pb.k], w=[self.hk[c][n]])
                else:
                    P.op("dve", lambda e: e.tensor_copy(out=dst, in_=pb[:]), r=[pb.k], w=[self.hk[c][n]])

    def store_y(self, b):
        P = self.P
        yo = [(self.phase_view(j * 4096, [128, 1024], F32), Trk()) for j in range(4)]
        self.new_phase([k for _, k in yo])
        for n in range(NT):
            for j in range(4):
                t0 = n * TT + j * 128
                for half in range(2):
                    pb = self.bank()
                    for cc in range(4):
                        c = half * 4 + cc
                        P.op("pe", lambda e: e.transpose(out=pb[:, cc * 128:(cc + 1) * 128], in_=self.hT[:, c, t0:t0 + 128], identity=self.ident_f[:]),
                             r=[self.hk[c][n], self.ident_f.k], w=[pb.k])
                    dst = yo[j][0][:, half * 512:(half + 1) * 512]
                    wl = [yo[j][1]]
                    if half == 0:
                        P.op("act", lambda e: e.activation(out=dst, in_=pb[:], func=AF.Copy), r=[pb.k], w=wl)
                    else:
                        P.op("dve", lambda e: e.tensor_copy(out=dst, in_=pb[:]), r=[pb.k], w=wl)
                ot = Trk()
                P.dma("sp", self.y[b, t0:t0 + 128, :], yo[j][0], r=[yo[j][1]], w=[ot])
                self.out_trks.append(ot)

    def norm_tile(self, n, gname, u_ap, u_k, sq_bufs, rstd_buf):
        P = self.P
        pb = self.bank()
        for c in range(NCH):
            sq, sk = sq_bufs[c % len(sq_bufs)]
            hsl = self.hT[:, c, n * TT:(n + 1) * TT]
            P.op("act", lambda e: e.activation(out=sq, in_=hsl, func=AF.Square), r=[self.hk[c][n]], w=[sk])
            P.op("pe", lambda e: e.matmul(pb[:], lhsT=self.ones_b[:], rhs=sq, start=(c == 0), stop=(c == NCH - 1)),
                 r=[sk, self.ones_b.k], w=[pb.k])
        rs, rk = rstd_buf
        P.op("act", lambda e: e.activation(out=rs, in_=pb[:], func=AF.Sqrt, scale=1.0 / D, bias=EPS), r=[pb.k], w=[rk])
        P.op("dve", lambda e: e.reciprocal(out=rs, in_=rs), r=[rk], w=[rk])
        for c in range(NCH):
            hsl = self.hT[:, c, n * TT:(n + 1) * TT]
            g = self.vec(gname, c)
            P.op("dve", lambda e: e.scalar_tensor_tensor(out=u_ap[:, c, :], in0=hsl, scalar=g, in1=rs, op0=ALU.mult, op1=ALU.mult),
                 r=[self.hk[c][n], rk, self.vecs.k], w=[u_k])

    def a_phase_setup(self):
        v = self.phase_view
        A = {}
        A["u"] = (v(0, [128, 8, 512], BF16), Trk())
        A["m"] = (v(8192, [128, 8, 512], BF16), Trk())
        A["sq"] = [(v(16384 + i * 1024, [128, 512], BF16), Trk()) for i in range(2)]
        A["rstd"] = (v(18432, [128, 512], F32), Trk())
        A["xp"] = [(v(20480 + i * 2080, [128, 516], F32), Trk()) for i in range(2)]
        base = 24640
        sets = []
        for i in range(2):
            o = base + i * 14336
            sets.append({
                "y": (v(o, [128, 512], BF16), Trk()),
                "xrb": (v(o + 1024, [128, 512], BF16), Trk()),
                "xr": (v(o + 2048, [128, 512], F32), Trk()),
                "r": (v(o + 4096, [128, 512], F32), Trk()),
                "i": (v(o + 6144, [128, 512], F32), Trk()),
                "a": (v(o + 8192, [128, 512], F32), Trk()),
                "a2": (v(o + 10240, [128, 512], F32), Trk()),
                "hr": (v(o + 12288, [128, 512], F32), Trk()),
            })
        A["sets"] = sets
        trks = [A["u"][1], A["m"][1], A["rstd"][1]] + [k for _, k in A["sq"]] + [k for _, k in A["xp"]]
        for st in sets:
            trks += [k for _, k in st.values()]
        self.new_phase(trks)
        self.A = A

    def a_mixer_tile(self, l, n):
        P = self.P
        A = self.A
        u, uk = A["u"]
        m, mk = A["m"]
        self.norm_tile(n, ("a_norm", l), u, uk, A["sq"], A["rstd"])
        for c in range(NCH):
            w = self.next_w(("a_in", l, c))
            pg = self.bank()
            pr = self.bank()
            for k in range(NCH):
                P.op("pe", lambda e: e.matmul(pg[:], lhsT=w[:, k * 128:(k + 1) * 128], rhs=u[:, k, :], start=(k == 0), stop=(k == NCH - 1)),
                     r=[w.k, uk], w=[pg.k])
            for k in range(NCH):
                P.op("pe", lambda e: e.matmul(pr[:], lhsT=w[:, 1024 + k * 128:1024 + (k + 1) * 128], rhs=u[:, k, :], start=(k == 0), stop=(k == NCH - 1)),
                     r=[w.k, uk], w=[pr.k])
            st = A["sets"][c % 2]
            y, yk = st["y"]
            xr, xrk = st["xr"]
            xrb, xrbk = st["xrb"]
            rr, rrk = st["r"]
            ii, iik = st["i"]
            aa, aak = st["a"]
            a2, a2k = st["a2"]
            hr, hrk = st["hr"]
            xp, xpk = A["xp"][c % 2]
            P.op("act", lambda e: e.activation(out=y, in_=pg[:], func=AF.Gelu_apprx_tanh), r=[pg.k], w=[yk])
            P.op("pool", lambda e: e.tensor_copy(out=xp[:, 0:3], in_=self.convc[:, c, :]), r=[self.convc.k], w=[xpk])
            P.op("act", lambda e: e.activation(out=xp[:, 3:515], in_=pr[:], func=AF.Copy), r=[pr.k], w=[xpk])
            P.op("dve", lambda e: e.tensor_scalar(out=xr, in0=xp[:, 0:512], scalar1=self.vec(("a_cw", l, 0), c), scalar2=self.vec(("a_cb", l), c),
                                                  op0=ALU.mult, op1=ALU.add), r=[xpk, self.vecs.k], w=[xrk])
            for kk in range(1, 4):
                P.op("dve", lambda e: e.scalar_tensor_tensor(out=xr, in0=xp[:, kk:kk + 512], scalar=self.vec(("a_cw", l, kk), c), in1=xr,
                                                             op0=ALU.mult, op1=ALU.add), r=[xpk, xrk, self.vecs.k], w=[xrk])
            P.op("pool", lambda e: e.tensor_copy(out=self.convc[:, c, :], in_=xp[:, 512:515]), r=[xpk], w=[self.convc.k])
            P.op("pool", lambda e: e.tensor_copy(out=xrb, in_=xr), r=[xrk], w=[xrbk])
            p1 = self.bank()
            p2 = self.bank()
            P.op("pe", lambda e: e.matmul(p1[:], lhsT=w[:, 2048:2176], rhs=xrb, start=True, stop=True), r=[w.k, xrbk], w=[p1.k])
            P.op("pe", lambda e: e.matmul(p2[:], lhsT=w[:, 2176:2304], rhs=xrb, start=True, stop=True), r=[w.k, xrbk], w=[p2.k])
            P.op("act", lambda e: e.activation(out=rr, in_=p1[:], func=AF.Sigmoid, bias=self.vec(("a_gb", l, 0), c)), r=[p1.k, self.vecs.k], w=[rrk])
            P.op("act", lambda e: e.activation(out=ii, in_=p2[:], func=AF.Sigmoid, bias=self.vec(("a_gb", l, 1), c)), r=[p2.k, self.vecs.k], w=[iik])
            P.op("act", lambda e: e.activation(out=aa, in_=rr, func=AF.Exp, scale=self.coef[:, l, 0, c:c + 1]), r=[rrk, self.coef.k], w=[aak])
            P.op("act", lambda e: e.activation(out=a2, in_=rr, func=AF.Exp, scale=self.coef[:, l, 1, c:c + 1]), r=[rrk, self.coef.k], w=[a2k])
            P.op("act", lambda e: e.activation(out=a2, in_=a2, func=AF.Sqrt, scale=-1.0, bias=1.0), r=[a2k], w=[a2k])
            P.op("dve", lambda e: e.tensor_tensor(out=ii, in0=ii, in1=xr, op=ALU.mult), r=[iik, xrk], w=[iik])
            P.op("dve", lambda e: e.tensor_tensor(out=ii, in0=ii, in1=a2, op=ALU.mult), r=[iik, a2k], w=[iik])
            P.op("dve", lambda e: e.tensor_tensor_scan(out=hr, data0=aa, data1=ii, initial=self.hst[:, c:c + 1], op0=ALU.mult, op1=ALU.add),
                 r=[aak, iik, self.hst.k], w=[hrk])
            P.op("pool", lambda e: e.tensor_copy(out=self.hst[:, c:c + 1], in_=hr[:, 511:512]), r=[hrk], w=[self.hst.k])
            P.op("pool", lambda e: e.tensor_tensor(out=m[:, c, :], in0=hr, in1=y, op=ALU.mult), r=[hrk, yk], w=[mk])
        for mm in range(NCH):
            w = self.next_w(("a_out", l, mm))
            po = self.bank()
            for k in range(NCH):
                P.op("pe", lambda e: e.matmul(po[:], lhsT=w[:, k * 128:(k + 1) * 128], rhs=m[:, k, :], start=(k == 0), stop=(k == NCH - 1)),
                     r=[w.k, mk], w=[po.k])
            hsl = self.hT[:, mm, n * TT:(n + 1) * TT]
            P.op("dve", lambda e: e.tensor_tensor(out=hsl, in0=po[:], in1=hsl, op=ALU.add), r=[po.k, self.hk[mm][n]], w=[self.hk[mm][n]])

    def a_layer(self, l):
        P = self.P
        self.a_phase_setup()
        P.op("pool", lambda e: e.memset(self.convc[:], 0.0), w=[self.convc.k])
        P.op("pool", lambda e: e.memset(self.hst[:], 0.0), w=[self.hst.k])
        for n in range(NT):
            self.a_mixer_tile(l, n)

    def ffn_phase_setup(self):
        v = self.phase_view
        Fz = {}
        Fz["u"] = [(v(s * 8192, [128, 8, 512], BF16), Trk()) for s in range(2)]
        Fz["act"] = [[(v(16384 + (c * 2 + s) * 1024, [128, 512], BF16), Trk()) for s in range(2)] for c in range(FC)]
        Fz["sq"] = [(v(61440 + i * 1024, [128, 512], BF16), Trk()) for i in range(2)]
        Fz["rstd"] = (v(63488, [128, 512], F32), Trk())
        Fz["sg"] = [(v(65536 + i * 1024, [128, 512], BF16), Trk()) for i in range(4)]
        trks = [k for _, k in Fz["u"]] + [k for row in Fz["act"] for _, k in row] + [k for _, k in Fz["sq"]] + [Fz["rstd"][1]] + [k for _, k in Fz["sg"]]
        self.new_phase(trks)
        self.F = Fz

    def ffn_tile(self, L, t2):
        P = self.P
        Fz = self.F
        for s in range(2):
            self.norm_tile(2 * t2 + s, ("f_norm", L), Fz["u"][s][0], Fz["u"][s][1], Fz["sq"], Fz["rstd"])
        nsg = 0
        for c in range(FC):
            w = self.next_w(("f_in", L, c))
            for s in range(2):
                u, uk = Fz["u"][s]
                pg = self.bank()
                pu = self.bank()
                for k in range(NCH):
                    P.op("pe", lambda e: e.matmul(pg[:], lhsT=w[:, k * 128:(k + 1) * 128], rhs=u[:, k, :], start=(k == 0), stop=(k == NCH - 1)),
                         r=[w.k, uk], w=[pg.k])
                for k in range(NCH):
                    P.op("pe", lambda e: e.matmul(pu[:], lhsT=w[:, 1024 + k * 128:1024 + (k + 1) * 128], rhs=u[:, k, :], start=(k == 0), stop=(k == NCH - 1)),
                         r=[w.k, uk], w=[pu.k])
                sg, sgk = Fz["sg"][nsg % 4]
                nsg += 1
                a, ak = Fz["act"][c][s]
                P.op("act", lambda e: e.activation(out=sg, in_=pg[:], func=AF.Silu), r=[pg.k], w=[sgk])
                P.op("dve", lambda e: e.tensor_tensor(out=a, in0=sg, in1=pu[:], op=ALU.mult), r=[sgk, pu.k], w=[ak])
        for mm in range(NCH):
            w = self.next_w(("f_out", L, mm))
            for s in range(2):
                n = 2 * t2 + s
                po = self.bank()
                for c in range(FC):
                    a, ak = Fz["act"][c][s]
                    P.op("pe", lambda e: e.matmul(po[:], lhsT=w[:, c * 128:(c + 1) * 128], rhs=a, start=(c == 0), stop=(c == FC - 1)),
                         r=[w.k, ak], w=[po.k])
                hsl = self.hT[:, mm, n * TT:(n + 1) * TT]
                P.op("dve", lambda e: e.tensor_tensor(out=hsl, in0=po[:], in1=hsl, op=ALU.add), r=[po.k, self.hk[mm][n]], w=[self.hk[mm][n]])

    def ffn_layer(self, L):
        self.ffn_phase_setup()
        for t2 in range(2):
            self.ffn_tile(L, t2)

    def headnorm_rope(self, pk, R, C, gvec, cos_ap, sin_ap, T, bias=None):
        P = self.P
        sq, sqk = T["sq"]
        rs, rsk = T["rs"]
        qn, qnk = T["qn"]
        t1, t1k = T["t1"]
        t2, t2k = T["t2"]
        if bias is not None:
            xf, xfk = T["xf"]
            P.op("act", lambda e: e.activation(out=xf[0:R, 0:C], in_=pk[0:R, 0:C], func=AF.Identity, bias=bias), r=[pk.k, self.vecs.k], w=[xfk])
            src, srck = xf[0:R, 0:C], xfk
        else:
            src, srck = pk[0:R, 0:C], pk.k
        P.op("act", lambda e: e.activation(out=sq[0:R, 0:C], in_=src, func=AF.Square), r=[srck], w=[sqk])
        pss = self.bank(self.pool_misc)
        P.op("pe", lambda e: e.matmul(pss[0:R, 0:C], lhsT=self.bones_b[0:R, 0:R], rhs=sq[0:R, 0:C], start=True, stop=True), r=[sqk, self.bones_b.k], w=[pss.k])
        P.op("act", lambda e: e.activation(out=rs[0:R, 0:C], in_=pss[0:R, 0:C], func=AF.Sqrt, scale=1.0 / 64.0, bias=EPS), r=[pss.k], w=[rsk])
        P.op("dve", lambda e: e.reciprocal(out=rs[0:R, 0:C], in_=rs[0:R, 0:C]), r=[rsk], w=[rsk])
        P.op("dve", lambda e: e.scalar_tensor_tensor(out=qn[0:R, 0:C], in0=src, scalar=gvec, in1=rs[0:R, 0:C], op0=ALU.mult, op1=ALU.mult),
             r=[srck, rsk, self.vecs.k], w=[qnk])
        prt = self.bank(self.pool_misc)
        P.op("pe", lambda e: e.matmul(prt[0:R, 0:C], lhsT=self.prot_b[0:R, 0:R], rhs=qn[0:R, 0:C], start=True, stop=True), r=[qnk, self.prot_b.k], w=[prt.k])
        P.op("pool", lambda e: e.tensor_tensor(out=t1[0:R, 0:C], in0=qn[0:R, 0:C], in1=cos_ap, op=ALU.mult), r=[qnk, self.cos_b.k], w=[t1k])
        P.op("dve", lambda e: e.tensor_tensor(out=t2[0:R, 0:C], in0=prt[0:R, 0:C], in1=sin_ap, op=ALU.mult), r=[prt.k, self.sin_b.k], w=[t2k])

    def kv_phase(self):
        P = self.P
        v = self.phase_view
        u, uk = v(0, [128, 8, 512], BF16), Trk()
        sqn = [(v(8192 + i * 1024, [128, 512], BF16), Trk()) for i in range(2)]
        rstd = (v(10240, [128, 512], F32), Trk())
        kcT, kcTk = v(12288, [128, 8, S], BF16), [Trk() for _ in range(8)]
        cw1, cw1k = v(45056, [128, 2, 32, 256], BF16), Trk()
        T = {"sq": (v(77824, [128, 512], BF16), Trk()), "rs": (v(78848, [128, 512], F32), Trk()), "qn": (v(80896, [128, 512], BF16), Trk()),
             "t1": (v(81920, [128, 512], F32), Trk()), "t2": (v(83968, [128, 512], F32), Trk()), "xf": (v(86016, [128, 512], F32), Trk())}
        kout = [(v(88064 + i * 1024, [128, 512], BF16), Trk()) for i in range(2)]
        vst = [(v(90112 + i * 528, [128, 4, 66], BF16), Trk()) for i in range(2)]
        hid = [(v(91264 + i * 512, [128, 2, 128], BF16), Trk()) for i in range(2)]
        trks = [uk, rstd[1], cw1k] + [k for _, k in sqn] + kcTk + [k for _, k in T.values()] + [k for _, k in kout] + [k for _, k in vst] + [k for _, k in hid]
        self.new_phase(trks)
        self.pool_misc = [0, 1, 2, 3, 4, 5, 6, 7]
        for kv in range(2):
            off = self.L.off[("cw1", kv)]
            src = self.wbf[off:off + 128 * 8192].rearrange("(p f) -> p f", p=128)
            c0, c1 = off // CONV_CH, (off + 128 * 8192 - 1) // CONV_CH
            P.dma("sp", cw1[0:64, kv].rearrange("p a b -> p (a b)"), src[0:64, :], r=[self.wchunk[c] for c in range(c0, c1 + 1)], w=[cw1k])
        for i in range(2):
            P.op("pool", lambda e: e.memset(vst[i][0][:, :, 64:66], 1.0), w=[vst[i][1]])
        nko = 0
        nvs = 0
        for n in range(NT):
            tsl = slice(n * TT, (n + 1) * TT)
            self.norm_tile(n, ("kv_norm",), u, uk, sqn, rstd)
            for i in range(4):
                which, cc = i // 2, i % 2
                w = self.next_w(("kvk", i))
                pk = self.bank()
                for k in range(NCH):
                    P.op("pe", lambda e: e.matmul(pk[:], lhsT=w[:, k * 128:(k + 1) * 128], rhs=u[:, k, :], start=(k == 0), stop=(k == NCH - 1)), r=[w.k, uk], w=[pk.k])
                self.headnorm_rope(pk, 128, 512, self.vec(("k_norm", 1 + which)), self.cos_b[:, tsl], self.sin_b[:, tsl], T)
                ko, kok = kout[nko % 2]
                nko += 1
                P.op("dve", lambda e: e.tensor_tensor(out=ko, in0=T["t1"][0], in1=T["t2"][0], op=ALU.add), r=[T["t1"][1], T["t2"][1]], w=[kok])
                for hh in range(2):
                    P.dma("sp", self.KSd[which, :, 2 * cc + hh, tsl], ko[hh * 64:(hh + 1) * 64, :], r=[kok], w=[self.ksd_k[which][2 * cc + hh][n]])
            for i in range(4):
                sel, cc = i // 2, i % 2
                w = self.next_w(("kvc", i))
                for gg in range(2):
                    pc = self.bank()
                    for k in range(NCH):
                        P.op("pe", lambda e: e.matmul(pc[0:64, :], lhsT=w[:, k * 128 + gg * 64:k * 128 + gg * 64 + 64], rhs=u[:, k, :], start=(k == 0), stop=(k == NCH - 1)),
                             r=[w.k, uk], w=[pc.k])
                    idx = sel * 4 + 2 * cc + gg
                    P.op("act", lambda e: e.activation(out=kcT[0:64, idx, tsl], in_=pc[0:64, :], func=AF.Copy), r=[pc.k], w=[kcTk[idx]])
            for i in range(2):
                w = self.next_w(("kvv", i))
                for jb in range(4):
                    pv = self.bank()
                    for k in range(NCH):
                        P.op("pe", lambda e: e.matmul(pv[:, 0:256], lhsT=u[:, k, jb * 128:(jb + 1) * 128], rhs=w[:, k * 256:(k + 1) * 256], start=(k == 0), stop=(k == NCH - 1)),
                             r=[w.k, uk], w=[pv.k])
                    vs, vsk = vst[nvs % 2]
                    nvs += 1
                    P.op("act", lambda e: e.activation(out=vs[:, :, 0:64], in_=pv[:, 0:256].rearrange("p (g d) -> p g d", g=4), func=AF.Copy), r=[pv.k], w=[vsk])
                    P.dma("sp", self.Vd[i, :, 4 * n + jb, :, :], vs, r=[vsk], w=[self.vd_k[i][n]])
        w2 = self.next_w(("cw2",))
        for sel in range(2):
            pcv = self.bank()
            for cc in range(2):
                for l in range(32):
                    P.op("pe", lambda e: e.matmul(pcv[:, cc:cc + 1], lhsT=cw1[0:64, sel, l, cc * 128:(cc + 1) * 128], rhs=self.posb[0:64, sel, l:l + 1],
                                                  start=(cc == 0 and l == 0), stop=(cc == 1 and l == 31)), r=[cw1k, self.posb.k], w=[pcv.k])
            b1o = VEC[("c_b1", sel)]
            P.op("dve", lambda e: e.tensor_tensor(out=self.cbias[:, sel, :], in0=pcv[:, 0:2], in1=self.vecs[:, b1o:b1o + 2], op=ALU.add), r=[pcv.k, self.vecs.k], w=[self.cbias.k])
        nh = 0
        for sel in range(2):
            for g in range(4):
                hd, hdk = hid[nh % 2]
                nh += 1
                for cc in range(2):
                    ph = self.bank()
                    for l in range(32):
                        P.op("pe", lambda e: e.matmul(ph[:, 0:127], lhsT=cw1[0:64, sel, l, cc * 128:(cc + 1) * 128], rhs=kcT[0:64, sel * 4 + g, l:l + 16 * 126 + 1:16],
                                                      start=(l == 0), stop=(l == 31)), r=[cw1k, kcTk[sel * 4 + g]], w=[ph.k])
                    P.op("act", lambda e: e.activation(out=hd[:, cc, 0:127], in_=ph[:, 0:127], func=AF.Gelu_apprx_tanh, bias=self.cbias[:, sel, cc:cc + 1]),
                         r=[ph.k, self.cbias.k], w=[hdk])
                if sel == 0:
                    pk = self.bank()
                    for cc in range(2):
                        P.op("pe", lambda e: e.matmul(pk[0:64, 0:127], lhsT=w2[:, (0 * 2 + cc) * 64:(0 * 2 + cc) * 64 + 64], rhs=hd[:, cc, 0:127], start=(cc == 0), stop=(cc == 1)),
                             r=[w2.k, hdk], w=[pk.k])
                    self.headnorm_rope(pk, 64, 127, self.vecs[0:64, VEC[("k_norm", 0)]:VEC[("k_norm", 0)] + 1],
                                       self.cos_b[0:64, 31:31 + 16 * 126 + 1:16], self.sin_b[0:64, 31:31 + 16 * 126 + 1:16], T,
                                       bias=self.vecs[0:64, VEC[("c_b2", 0)]:VEC[("c_b2", 0)] + 1])
                    P.op("dve", lambda e: e.tensor_tensor(out=self.KC[0:64, g, 0:127], in0=T["t1"][0][0:64, 0:127], in1=T["t2"][0][0:64, 0:127], op=ALU.add),
                         r=[T["t1"][1], T["t2"][1]], w=[self.KC.k])
                else:
                    pv = self.bank()
                    for cc in range(2):
                        P.op("pe", lambda e: e.matmul(pv[0:127, 0:64], lhsT=hd[:, cc, 0:127], rhs=w2[:, (1 * 2 + cc) * 64:(1 * 2 + cc) * 64 + 64], start=(cc == 0), stop=(cc == 1)),
                             r=[w2.k, hdk], w=[pv.k])
                    P.op("dve", lambda e: e.tensor_tensor(out=self.VC[0:127, g, 0:64], in0=pv[0:127, 0:64], in1=self.bvecs[0:127, BV_CB2V:BV_CB2V + 64], op=ALU.add),
                         r=[pv.k, self.bvecs.k], w=[self.VC.k])

    def b_layer(self, j):
        P = self.P
        v = self.phase_view
        KS, KSk = v(0, [128, 4, S], BF16), [Trk() for _ in range(4)]
        KW, KWk = v(16384, [128, 4, S], BF16), [Trk() for _ in range(4)]
        VS, VSk = v(32768, [128, 16, 4, 66], BF16), Trk()
        VW, VWk = v(41216, [128, 16, 4, 66], BF16), Trk()
        u, uk = v(49664, [128, 8, 512], BF16), Trk()
        Q, Qk = v(57856, [128, 4, 16, 128], BF16), [Trk() for _ in range(4)]
        Nk = [[Trk() for _ in range(4)] for _ in range(4)]
        oT, oTk = v(74240, [128, 8, 512], BF16), Trk()
        ob = [(v(82432 + i * 2048, [128, 1024], BF16), Trk()) for i in range(2)]
        PT = [(v(86528 + i * 1024, [128, 512], BF16), Trk()) for i in range(4)]
        sqn = [(v(90624 + i * 1024, [128, 512], BF16), Trk()) for i in range(2)]
        rstd = (v(92672, [128, 512], F32), Trk())
        gat, gatk = v(94720, [128, 4, 48], F32), Trk()
        T = {"sq": sqn[0], "rs": rstd, "qn": (v(95488, [128, 512], BF16), Trk()),
             "t1": (v(96512, [128, 512], F32), Trk()), "t2": (v(98560, [128, 512], F32), Trk())}
        rd, rdk = v(100608, [128, 16], F32), Trk()
        s3, s3k = v(100672, [128, 12], F32), Trk()
        sc, sck = v(100736, [128, 32], F32), Trk()
        sc2, sc2k = v(100864, [128, 32], F32), Trk()
        nsl, nslk = v(100992, [128, 32], F32), Trk()
        m8, m8k = v(101120, [128, 16], F32), Trk()
        acc, acck = v(101184, [128, 256], F32), Trk()
        trks = KSk + KWk + [VSk, VWk, uk, oTk, gatk, rdk, s3k, sck, sc2k, nslk, m8k, acck] + Qk + [k for _, k in ob] + [k for _, k in PT] \
            + [k for _, k in sqn] + [rstd[1]] + [T["qn"][1], T["t1"][1], T["t2"][1]] + [k for row in Nk for k in row]
        self.new_phase(trks)
        self.pool_misc = [6, 7]
        pool_st = [0, 1, 2]
        for g in range(4):
            P.dma("sp", KS[0:64, g, :], self.KSd[0, :, g, :], r=self.ksd_k[0][g], w=[KSk[g]])
            P.dma("sp", KW[0:64, g, :], self.KSd[1, :, g, :], r=self.ksd_k[1][g], w=[KWk[g]])
        P.dma("sp", VS.rearrange("p a b c -> p (a b c)"), self.Vd[0].rearrange("p a b c -> p (a b c)"), r=self.vd_k[0], w=[VSk])
        P.dma("sp", VW.rearrange("p a b c -> p (a b c)"), self.Vd[1].rearrange("p a b c -> p (a b c)"), r=self.vd_k[1], w=[VWk])
        est = self.ph[64:96, 57856 // 4:57856 // 4 + 2048]
        allq = Qk + [k for row in Nk for k in row]
        P.dma("sp", est, self.cst_d[64:96, CST["eind"]:CST["eind"] + 2048], w=allq)
        for g in range(4):
            P.op("dve", lambda e: e.tensor_copy(out=KS[64:96, g, :], in_=est), r=allq, w=[KSk[g]])
        gbo = BV_GB + 48 * j
        npt = 0
        nob = 0
        for n in range(NT):
            tsl = slice(n * TT, (n + 1) * TT)
            self.norm_tile(n, ("b_norm", j), u, uk, sqn, rstd)
            wg = self.next_w(("bg", j))
            for qb in range(4):
                pg = self.bank(self.pool_misc)
                for k in range(NCH):
                    P.op("pe", lambda e: e.matmul(pg[:, 0:48], lhsT=u[:, k, qb * 128:(qb + 1) * 128], rhs=wg[:, k * 48:(k + 1) * 48], start=(k == 0), stop=(k == NCH - 1)),
                         r=[wg.k, uk], w=[pg.k])
                P.op("dve", lambda e: e.tensor_tensor(out=gat[:, qb, :], in0=pg[:, 0:48], in1=self.bvecs[:, gbo:gbo + 48], op=ALU.add), r=[pg.k, self.bvecs.k], w=[gatk])
            P.op("act", lambda e: e.activation(out=gat, in_=gat, func=AF.Sigmoid), r=[gatk], w=[gatk])
            for m in range(NCH):
                w = self.next_w(("bq", j, m))
                pq = self.bank(pool_st)
                for k in range(NCH):
                    P.op("pe", lambda e: e.matmul(pq[:], lhsT=w[:, k * 128:(k + 1) * 128], rhs=u[:, k, :], start=(k == 0), stop=(k == NCH - 1)), r=[w.k, uk], w=[pq.k])
                self.headnorm_rope(pq, 128, 512, self.vec(("q_norm", j)), self.cos_b[:, tsl], self.sin_b[:, tsl], T)
                for hh in range(2):
                    h = 2 * m + hh
                    rsl = slice(hh * 64, (hh + 1) * 64)
                    eng = "dve" if hh == 0 else "pool"
                    P.op(eng, lambda e: e.tensor_tensor(out=Q[0:64, :, h, :], in0=T["t1"][0][rsl, :].rearrange("p (a b) -> p a b", a=4),
                                                        in1=T["t2"][0][rsl, :].rearrange("p (a b) -> p a b", a=4), op=ALU.add),
                         r=[T["t1"][1], T["t2"][1]], w=Qk)
            self.attn_tile(n, dict(KS=KS, KSk=KSk, KW=KW, KWk=KWk, VS=VS, VSk=VSk, VW=VW, VWk=VWk, Q=Q, Qk=Qk, Nk=Nk, oT=oT, oTk=oTk, ob=ob, PT=PT,
                                   gat=gat, gatk=gatk, rd=rd, rdk=rdk, s3=s3, s3k=s3k, sc=sc, sck=sck, sc2=sc2, sc2k=sc2k, nsl=nsl, nslk=nslk,
                                   m8=m8, m8k=m8k, acc=acc, acck=acck, pool_st=pool_st))
            for mm in range(NCH):
                w = self.next_w(("bo", j, mm))
                po = self.bank(pool_st)
                for k in range(NCH):
                    P.op("pe", lambda e: e.matmul(po[:], lhsT=w[:, k * 128:(k + 1) * 128], rhs=oT[:, k, :], start=(k == 0), stop=(k == NCH - 1)), r=[w.k, oTk], w=[po.k])
                hsl = self.hT[:, mm, tsl]
                P.op("dve", lambda e: e.tensor_tensor(out=hsl, in0=po[:], in1=hsl, op=ALU.add), r=[po.k, self.hk[mm][n]], w=[self.hk[mm][n]])
        self.pool_misc = [0, 1, 2, 3, 4, 5, 6, 7]

    def attn_tile(self, n, C):
        P = self.P
        KS, KSk, KW, KWk, VS, VSk, VW, VWk = C["KS"], C["KSk"], C["KW"], C["KWk"], C["VS"], C["VSk"], C["VW"], C["VWk"]
        Q, Qk, Nk, oT, oTk, ob, PT = C["Q"], C["Qk"], C["Nk"], C["oT"], C["oTk"], C["ob"], C["PT"]
        gat, gatk, rd, rdk, s3, s3k = C["gat"], C["gatk"], C["rd"], C["rdk"], C["s3"], C["s3k"]
        sc, sck, sc2, sc2k, nsl, nslk, m8, m8k, acc, acck = C["sc"], C["sck"], C["sc2"], C["sc2k"], C["nsl"], C["nslk"], C["m8"], C["m8k"], C["acc"], C["acck"]
        pool_st = C["pool_st"]
        poc, pos, pow_ = self.ps[3], self.ps[4], self.ps[5]
        st = {"npt": 0}
        pairs = [(qb, g) for qb in range(4) for g in range(4)]
        jobs = []

        def r4(ap):
            return ap.rearrange("p (a b) -> p a b", a=4)

        def mk_job(kind, qb, g, kt, first, last, pidx):
            qbg = 4 * n + qb
            job = {"pre": [], "post": []}
            box = {}

            def st1():
                pst = self.bank(pool_st)
                pt, ptk = PT[st["npt"] % 4]
                st["npt"] += 1
                box["pt"], box["ptk"] = pt, ptk
                Qsl = Q[0:64, qb, 4 * g:4 * g + 4, :]
                if kind == "c":
                    P.op("pe", lambda e: e.matmul(pst[0:127, :], lhsT=self.KC[0:64, g, 0:127], rhs=Qsl, start=True, stop=True), r=[self.KC.k, Qk[qb]], w=[pst.k])
                    P.op("act", lambda e: e.activation(out=pt[0:127, :], in_=pst[0:127, :], func=AF.Exp, scale=SCALE), r=[pst.k], w=[ptk])
                    cm = self.cmask_b[0:127, qbg * 128:(qbg + 1) * 128].unsqueeze(1).broadcast_to([127, 4, 128])
                    P.op("pool", lambda e: e.tensor_tensor(out=r4(pt[0:127, :]), in0=r4(pt[0:127, :]), in1=cm, op=ALU.mult), r=[ptk, self.cmask_b.k], w=[ptk])
                    return
                if kind == "s":
                    P.op("pe", lambda e: e.matmul(pst[:], lhsT=KS[0:96, g, kt * 128:(kt + 1) * 128], rhs=Q[0:96, qb, 4 * g:4 * g + 4, :], start=True, stop=True),
                         r=[KSk[g], Qk[qb], Nk[qb][g]], w=[pst.k])
                else:
                    P.op("pe", lambda e: e.matmul(pst[:], lhsT=KW[0:64, g, kt * 128:(kt + 1) * 128], rhs=Qsl, start=True, stop=True), r=[KWk[g], Qk[qb]], w=[pst.k])
                P.op("act", lambda e: e.activation(out=pt, in_=pst[:], func=AF.Exp, scale=SCALE), r=[pst.k], w=[ptk])
                msk = None
                if kt == qbg:
                    msk = self.triL_b
                elif kind == "w" and kt == qbg - 4:
                    msk = self.triU_b
                if msk is not None:
                    tm = msk[:].unsqueeze(1).broadcast_to([128, 4, 128])
                    P.op("pool", lambda e: e.tensor_tensor(out=r4(pt), in0=r4(pt), in1=tm, op=ALU.mult), r=[ptk, msk.k], w=[ptk])

            def st2():
                pt, ptk = box["pt"], box["ptk"]
                for h in range(4):
                    if kind == "c":
                        P.op("pe", lambda e: e.matmul(poc[:, h * 128:h * 128 + 97], lhsT=pt[0:127, h * 128:(h + 1) * 128], rhs=self.VC[0:127, g, 0:97],
                                                      start=(h == 0), stop=(h == 3), skip_group_check=True), r=[ptk, self.VC.k], w=[poc.k])
                    elif kind == "s":
                        P.op("pe", lambda e: e.matmul(pos[:, h * 128:h * 128 + 65], lhsT=pt[:, h * 128:(h + 1) * 128], rhs=VS[:, kt, g, 0:65],
                                                      start=(first and h == 0), stop=(last and h == 3), skip_group_check=True), r=[ptk, VSk], w=[pos.k])
                    else:
                        P.op("pe", lambda e: e.matmul(pow_[:, h * 128:h * 128 + 65], lhsT=pt[:, h * 128:(h + 1) * 128], rhs=VW[:, kt, g, 0:65],
                                                      start=(first and h == 0), stop=(last and h == 3), skip_group_check=True), r=[ptk, VWk], w=[pow_.k])

            job["st1"], job["st2"] = st1, st2
            return job

        def select_chain(qb, g, pidx):
            qbg = 4 * n + qb
            pcs = self.pocs[pidx % 2]
            P.op("act", lambda e: e.activation(out=pcs[:], in_=r4(poc[:])[:, :, 0:97], func=AF.Copy), r=[poc.k], w=[pcs.k])
            P.op("dve", lambda e: e.tensor_scalar(out=rd[:, 0:4], in0=pcs[:, :, 64], scalar1=1e-30, scalar2=None, op0=ALU.max), r=[pcs.k], w=[rdk])
            P.op("dve", lambda e: e.reciprocal(out=rd[:, 0:4], in_=rd[:, 0:4]), r=[rdk], w=[rdk])
            for h in range(4):
                src1 = self.selb[:, qbg, :] if h == 0 else sc
                P.op("dve", lambda e: e.scalar_tensor_tensor(out=sc, in0=pcs[:, h, 65:97], scalar=rd[:, h:h + 1], in1=src1, op0=ALU.mult, op1=ALU.add),
                     r=[pcs.k, rdk, sck, self.selb.k], w=[sck])
            P.op("dve", lambda e: e.max(out=m8[:, 0:8], in_=sc), r=[sck], w=[m8k])
            P.op("dve", lambda e: e.match_replace(out=sc2, in_to_replace=m8[:, 0:8], in_values=sc, imm_value=-3.0e38), r=[sck, m8k], w=[sc2k])
            P.op("dve", lambda e: e.max(out=m8[:, 8:16], in_=sc2), r=[sc2k], w=[m8k])
            P.op("dve", lambda e: e.tensor_scalar(out=nsl, in0=sc, scalar1=m8[:, 15:16], scalar2=NEGM, op0=ALU.is_lt, op1=ALU.mult), r=[sck, m8k], w=[nslk])

        def nsel_install(qb, g):
            pm = self.bank(self.pool_misc)
            P.op("pe", lambda e: e.transpose(out=pm[0:32, 0:128], in_=nsl, identity=self.ident_f[:]), r=[nslk, self.ident_f.k], w=[pm.k])
            P.op("act", lambda e: e.activation(out=Q[64:96, qb, 4 * g:4 * g + 4, :], in_=pm[0:32, 0:128].unsqueeze(1).broadcast_to([32, 4, 128]), func=AF.Copy),
                 r=[pm.k], w=[Nk[qb][g]])

        def evac_win():
            P.op("act", lambda e: e.activation(out=self.pows[:], in_=r4(pow_[:])[:, :, 0:65], func=AF.Copy), r=[pow_.k], w=[self.pows.k])

        def combine(qb, g, pidx):
            pcs = self.pocs[pidx % 2]
            o, ok_ = ob[qb % 2]
            P.op("dve", lambda e: e.tensor_scalar(out=rd[:, 4:8], in0=pcs[:, :, 64], scalar1=1e-30, scalar2=None, op0=ALU.max), r=[pcs.k], w=[rdk])
            P.op("dve", lambda e: e.tensor_scalar(out=rd[:, 8:12], in0=r4(pos[:])[:, :, 64], scalar1=1e-30, scalar2=None, op0=ALU.max), r=[pos.k], w=[rdk])
            P.op("dve", lambda e: e.tensor_scalar(out=rd[:, 12:16], in0=self.pows[:, :, 64], scalar1=1e-30, scalar2=None, op0=ALU.max), r=[self.pows.k], w=[rdk])
            P.op("dve", lambda e: e.reciprocal(out=rd[:, 4:16], in_=rd[:, 4:16]), r=[rdk], w=[rdk])
            P.op("dve", lambda e: e.tensor_tensor(out=s3.rearrange("p (a b) -> p a b", a=3), in0=rd[:, 4:16].rearrange("p (a b) -> p a b", a=3),
                                                  in1=gat[:, qb, :].rearrange("p (a b) -> p a b", a=3)[:, :, 4 * g:4 * g + 4], op=ALU.mult), r=[rdk, gatk], w=[s3k])
            for h in range(4):
                ah = acc[:, h * 64:(h + 1) * 64]
                P.op("dve", lambda e: e.tensor_scalar(out=ah, in0=pcs[:, h, 0:64], scalar1=s3[:, h:h + 1], scalar2=None, op0=ALU.mult), r=[pcs.k, s3k], w=[acck])
                P.op("dve", lambda e: e.scalar_tensor_tensor(out=ah, in0=self.pows[:, h, 0:64], scalar=s3[:, 8 + h:9 + h], in1=ah, op0=ALU.mult, op1=ALU.add),
                     r=[self.pows.k, s3k, acck], w=[acck])
                P.op("dve", lambda e: e.scalar_tensor_tensor(out=o[:, g * 256 + h * 64:g * 256 + (h + 1) * 64], in0=pos[:, h * 128:h * 128 + 64], scalar=s3[:, 4 + h:5 + h], in1=ah,
                                                             op0=ALU.mult, op1=ALU.add), r=[pos.k, s3k, acck], w=[ok_])

        def o_transpose(qb):
            o, ok_ = ob[qb % 2]
            pT = self.bank(self.pool_misc)
            pTb = pT[:].bitcast(BF16)
            for c in range(NCH):
                P.op("pe", lambda e: e.transpose(out=pTb[:, c * 128:(c + 1) * 128], in_=o[:, c * 128:(c + 1) * 128], identity=self.ident_b[:]), r=[ok_, self.ident_b.k], w=[pT.k])
            P.op("act", lambda e: e.activation(out=oT[:, :, qb * 128:(qb + 1) * 128], in_=pTb.rearrange("p (a b) -> p a b", a=8), func=AF.Copy), r=[pT.k], w=[oTk])

        def cjob(pidx):
            qb, g = pairs[pidx]
            j = mk_job("c", qb, g, 0, True, True, pidx)
            j["post"].append(lambda: select_chain(qb, g, pidx))
            return j

        jobs.append(cjob(0))
        pending_T = []
        for pidx, (qb, g) in enumerate(pairs):
            qbg = 4 * n + qb
            k0 = max(0, qbg - 4)
            wj = [mk_job("w", qb, g, kt, kt == k0, kt == qbg, pidx) for kt in range(k0, qbg + 1)]
            for f in pending_T:
                wj[min(2, len(wj) - 1)]["pre"].append(f)
            pending_T = []
            wj[-1]["post"].append(evac_win)
            jobs += wj
            if pidx + 1 < len(pairs):
                jobs.append(cjob(pidx + 1))
            sj = [mk_job("s", qb, g, kt, kt == 0, kt == qbg, pidx) for kt in range(qbg + 1)]
            sj[0]["pre"].append(lambda qb=qb, g=g: nsel_install(qb, g))
            sj[-1]["post"].append(lambda qb=qb, g=g, pidx=pidx: combine(qb, g, pidx))
            jobs += sj
            if g == 3:
                pending_T.append(lambda qb=qb: o_transpose(qb))
        for f in jobs[0]["pre"]:
            f()
        jobs[0]["st1"]()
        for i in range(len(jobs)):
            if i + 1 < len(jobs):
                for f in jobs[i + 1]["pre"]:
                    f()
                jobs[i + 1]["st1"]()
            jobs[i]["st2"]()
            for f in jobs[i]["post"]:
                f()
        for f in pending_T:
            f()

    def make_plan(self):
        plan = []
        for b in range(self.nb):
            for layer in range(self.n_layers):
                if layer < 2:
                    for n in range(NT):
                        plan += [("a_in", layer, c) for c in range(8)]
                        plan += [("a_out", layer, m) for m in range(8)]
                else:
                    if layer == 2:
                        for n in range(NT):
                            plan += [("kvk", i) for i in range(4)] + [("kvc", i) for i in range(4)] + [("kvv", i) for i in range(2)]
                        plan += [("cw2",)]
                    for n in range(NT):
                        plan += [("bg", layer - 2)] + [("bq", layer - 2, m) for m in range(8)] + [("bo", layer - 2, m) for m in range(8)]
                for t2 in range(2):
                    plan += [("f_in", layer, c) for c in range(FC)]
                    plan += [("f_out", layer, m) for m in range(8)]
        return plan

    def build(self):
        P = self.P
        self.convc = self.sb("convc", [128, 8, 3], F32)
        self.hst = self.sb("hst", [128, 8], F32)
        self.prologue()
        self.plan_weights(self.make_plan())
        for b in range(self.nb):
            self.mark("load%d" % b)
            self.load_x(b)
            for layer in range(self.n_layers):
                if layer < 2:
                    self.mark("a%d.%d" % (b, layer))
                    self.a_layer(layer)
                else:
                    if layer == 2:
                        self.mark("kv%d" % b)
                        self.kv_phase()
                    self.mark("b%d.%d" % (b, layer))
                    self.b_layer(layer - 2)
                self.mark("f%d.%d" % (b, layer))
                self.ffn_layer(layer)
            self.mark("store%d" % b)
            self.store_y(b)
        self.mark("end")
        P.wait_all("sp", self.out_trks)
        return self.nc


_CACHE = {}


def _prep_inputs(inputs):
    L = weight_layout()
    wall = pack_weights(inputs, L)
    vecs, bvecs = pack_vecs(inputs)
    cst = make_consts()
    return wall, vecs, bvecs, cst


def kernel(**inputs):
    inputs = {k: np.asarray(v) for k, v in inputs.items()}
    x = np.ascontiguousarray(inputs["x"], dtype=np.float32)
    B = x.shape[0]
    nb = B // NCORES
    wall, vecs, bvecs, cst = _prep_inputs(inputs)
    nc = Builder(nb).build()
    in_maps = []
    for c in range(NCORES):
        in_maps.append({"x": np.ascontiguousarray(x[c * nb:(c + 1) * nb]), "wall": wall, "vecs": vecs, "bvecs": bvecs, "cst": cst})
    res = run_bass_kernel_spmd(nc, in_maps, core_ids=list(range(NCORES)))
    out = np.concatenate([np.asarray(r["y"]).reshape(nb, S, D) for r in res.results], axis=0)
    return out.astype(np.float32)
```

```python
import numpy as np
import concourse.bass as bass
import concourse.mybir as mybir
from concourse.bass_utils import run_bass_kernel_spmd
from concourse.alu_op_type import AluOpType as ALU

F32 = mybir.dt.float32
BF16 = mybir.dt.bfloat16
AF = mybir.ActivationFunctionType

S = 2048
D = 1024
NCH = 8
TT = 512
NT = S // TT
FH = 2816
FC = 22
NCORES = 8
EPS = 1e-6
CONV_CH = 128 * 2048
SCALE = 0.125
NEGM = -30000.0


def _kt(W, c0, width=128):
    K = W.shape[0]
    return np.ascontiguousarray(W[:, c0:c0 + width].reshape(K // 128, 128, width).transpose(1, 0, 2)).reshape(128, -1)


class WLayout:
    def __init__(self):
        self.off = {}
        self.free = {}
        self.n = 0

    def add(self, name, free):
        self.off[name] = self.n
        self.free[name] = free
        self.n += 128 * free

    def total(self):
        return ((self.n + CONV_CH - 1) // CONV_CH) * CONV_CH


def weight_layout():
    L = WLayout()
    for l in range(2):
        for c in range(8):
            L.add(("a_in", l, c), 2304)
        for m in range(8):
            L.add(("a_out", l, m), 1024)
    for l in range(4):
        for c in range(FC):
            L.add(("f_in", l, c), 2048)
        for m in range(8):
            L.add(("f_out", l, m), FH)
    for i in range(4):
        L.add(("kvk", i), 1024)
    for i in range(4):
        L.add(("kvc", i), 1024)
    for i in range(2):
        L.add(("kvv", i), 2048)
    for i in range(2):
        L.add(("cw1", i), 8192)
    L.add(("cw2",), 256)
    for j in range(2):
        L.add(("bg", j), 384)
        for m in range(8):
            L.add(("bq", j, m), 1024)
        for m in range(8):
            L.add(("bo", j, m), 1024)
    return L


def pack_weights(inp, L):
    out = np.zeros(L.total(), np.float32)

    def put(name, arr):
        arr = np.asarray(arr, np.float32).reshape(128, -1)
        assert arr.shape[1] == L.free[name], (name, arr.shape)
        out[L.off[name]:L.off[name] + arr.size] = arr.reshape(-1)

    for l in range(2):
        Win = inp["a_w_in"][l]
        for c in range(8):
            put(("a_in", l, c), np.concatenate(
                [_kt(Win, c * 128), _kt(Win, 1024 + c * 128), inp["a_gate_w"][l][0, c], inp["a_gate_w"][l][1, c]], axis=1))
        for m in range(8):
            put(("a_out", l, m), _kt(inp["a_w_out"][l], m * 128))
    for l in range(4):
        W = inp["f_w_in"][l]
        for c in range(FC):
            put(("f_in", l, c), np.concatenate([_kt(W, c * 128), _kt(W, FH + c * 128)], axis=1))
        for m in range(8):
            put(("f_out", l, m), _kt(inp["f_w_out"][l], m * 128))
    kvw = inp["kv_w"]
    i = 0
    for jj in (2, 4):
        for cc in range(2):
            put(("kvk", i), _kt(kvw, jj * 256 + cc * 128))
            i += 1
    i = 0
    for jj in (0, 1):
        for cc in range(2):
            put(("kvc", i), _kt(kvw, jj * 256 + cc * 128))
            i += 1
    for i, jj in enumerate((3, 5)):
        put(("kvv", i), _kt(kvw, jj * 256, 256))
    for kv in range(2):
        t = np.zeros((128, 32, 256), np.float32)
        t[:64] = inp["cmp_w1"][kv].reshape(32, 64, 256).transpose(1, 0, 2)
        put(("cw1", kv), t)
    t = np.zeros((128, 2, 2, 64), np.float32)
    for kv in range(2):
        t[:, kv] = inp["cmp_w2"][kv].reshape(2, 128, 64).transpose(1, 0, 2)
    put(("cw2",), t)
    for j in range(2):
        put(("bg", j), _kt(inp["b_w_in"][j], 1024, 48))
        for m in range(8):
            put(("bq", j, m), _kt(inp["b_w_in"][j], m * 128))
        for m in range(8):
            put(("bo", j, m), _kt(inp["b_w_out"][j], m * 128))
    return out


VEC = {}
_nv = 0


def _vadd(name, n):
    global _nv
    VEC[name] = _nv
    _nv += n


for _l in range(2):
    _vadd(("a_norm", _l), 8)
    for _k in range(4):
        _vadd(("a_cw", _l, _k), 8)
    _vadd(("a_cb", _l), 8)
    _vadd(("a_gb", _l, 0), 8)
    _vadd(("a_gb", _l, 1), 8)
    _vadd(("a_lam", _l), 8)
for _l in range(4):
    _vadd(("f_norm", _l), 8)
_vadd(("kv_norm",), 8)
for _j in range(2):
    _vadd(("b_norm", _j), 8)
    _vadd(("q_norm", _j), 1)
for _i in range(3):
    _vadd(("k_norm", _i), 1)
for _i in range(2):
    _vadd(("c_b1", _i), 2)
    _vadd(("c_b2", _i), 1)
    _vadd(("c_pos", _i), 32)
NV = _nv
BV_GB = 0
BV_CB2V = 96
NBV = 160


def pack_vecs(inp):
    v = np.zeros((128, NV), np.float32)

    def fm(x):
        return np.asarray(x, np.float32).reshape(8, 128).T

    for l in range(2):
        v[:, VEC[("a_norm", l)]:][:, :8] = fm(inp["a_norm"][l])
        for k in range(4):
            v[:, VEC[("a_cw", l, k)]:][:, :8] = fm(inp["a_conv_w"][l][k])
        v[:, VEC[("a_cb", l)]:][:, :8] = fm(inp["a_conv_b"][l])
        v[:, VEC[("a_gb", l, 0)]:][:, :8] = fm(inp["a_gate_b"][l][0])
        v[:, VEC[("a_gb", l, 1)]:][:, :8] = fm(inp["a_gate_b"][l][1])
        v[:, VEC[("a_lam", l)]:][:, :8] = fm(inp["a_lambda"][l])
    for l in range(4):
        v[:, VEC[("f_norm", l)]:][:, :8] = fm(inp["f_norm"][l])
    v[:, VEC[("kv_norm",)]:][:, :8] = fm(inp["kv_norm"])
    for j in range(2):
        v[:, VEC[("b_norm", j)]:][:, :8] = fm(inp["b_norm"][j])
        v[:, VEC[("q_norm", j)]] = np.tile(np.asarray(inp["q_norm"][j], np.float32), 2)
    for i in range(3):
        v[:, VEC[("k_norm", i)]] = np.tile(np.asarray(inp["k_norm"][i], np.float32), 2)
    for i in range(2):
        v[:, VEC[("c_b1", i)]:][:, :2] = np.asarray(inp["cmp_b1"][i], np.float32).reshape(2, 128).T
        v[:, VEC[("c_b2", i)]] = np.tile(np.asarray(inp["cmp_b2"][i], np.float32), 2)
        v[:64, VEC[("c_pos", i)]:][:, :32] = np.asarray(inp["cmp_pos"][i], np.float32).T
    bv = np.zeros((128, NBV), np.float32)
    for j in range(2):
        bv[:, BV_GB + 48 * j: BV_GB + 48 * j + 48] = np.asarray(inp["b_gate_b"][j], np.float32)[None, :]
    bv[:, BV_CB2V:BV_CB2V + 64] = np.asarray(inp["cmp_b2"][1], np.float32)[None, :]
    return v, bv


CST = {}
_nc_ = 0


def _cadd(name, n):
    global _nc_
    CST[name] = _nc_
    _nc_ += n


_cadd("ident", 128)
_cadd("prot", 128)
_cadd("bones", 128)
_cadd("triL", 128)
_cadd("triU", 128)
_cadd("ovl", 32)
_cadd("cos", 2048)
_cadd("sin", 2048)
_cadd("cmask", 2048)
_cadd("eind", 2048)
_cadd("selb", 512)
NCST = _nc_


def make_consts():
    c = np.zeros((128, NCST), np.float32)
    p = np.arange(128)
    c[:, CST["ident"]:][:, :128] = np.eye(128)
    perm = (p // 64) * 64 + ((p % 64) + 32) % 64
    pr = np.zeros((128, 128), np.float32)
    pr[perm, p] = 1.0
    c[:, CST["prot"]:][:, :128] = pr
    c[:, CST["bones"]:][:, :128] = (p[:, None] // 64 == p[None, :] // 64)
    c[:, CST["triL"]:][:, :128] = (p[:, None] <= p[None, :])
    c[:, CST["triU"]:][:, :128] = (p[:, None] > p[None, :])
    cs = np.arange(127) * 16
    sl = np.arange(32) * 64
    ov = np.clip(np.minimum(cs[:, None] + 32, sl[None, :] + 64) - np.maximum(cs[:, None], sl[None, :]), 0, None) / 32.0
    c[:127, CST["ovl"]:][:, :32] = ov
    t = np.arange(2048, dtype=np.float64)
    fi = (p % 64) % 32
    freqs = (10000.0 ** (-(np.arange(32, dtype=np.float32) / np.float32(32)))).astype(np.float32)
    ang = (t[None, :].astype(np.float32) * freqs[fi][:, None]).astype(np.float32)
    c[:, CST["cos"]:][:, :2048] = np.cos(ang)
    sg = np.where((p % 64) < 32, -1.0, 1.0)[:, None]
    c[:, CST["sin"]:][:, :2048] = np.sin(ang) * sg
    cl = np.arange(127) * 16 + 31
    c[:127, CST["cmask"]:][:, :2048] = (cl[:, None] <= t[None, :])
    b = np.arange(32)
    c[64:96, CST["eind"]:][:, :2048] = (t[None, :].astype(np.int64) // 64 == b[:, None])
    sb = np.zeros((128, 16, 32), np.float32)
    for qb in range(16):
        tq = qb * 128 + p
        cur = (tq // 64)[:, None]
        causal = b[None, :] <= cur
        forced = (b[None, :] == 0) | (causal & (cur - b[None, :] < 2))
        sb[:, qb, :] = np.where(forced, 1e30, np.where(causal, 0.0, -1e30))
    c[:, CST["selb"]:][:, :512] = sb.reshape(128, 512)
    return c


class Trk:
    __slots__ = ("w", "r")

    def __init__(self):
        self.w = None
        self.r = {}


NDS = 24


class Prog:
    def __init__(self, nc):
        self.nc = nc
        self.eng = {"pe": nc.tensor, "act": nc.scalar, "dve": nc.vector, "pool": nc.gpsimd, "sp": nc.sync}
        self.sem = {e: nc.alloc_semaphore(name="s_" + e) for e in ("pe", "act", "dve", "pool")}
        self.cnt = {e: 0 for e in ("pe", "act", "dve", "pool")}
        self.seen = {e: {} for e in self.eng}
        self.dsem = [nc.alloc_semaphore(name="s_dma%d" % i) for i in range(NDS)]
        self.dn = 0
        self.ninst = 0

    def _semof(self, key):
        return self.sem[key] if isinstance(key, str) else self.dsem[key[1]]

    def _wait(self, e, key, val):
        if key == "pe" and e == "pe":
            return
        if self.seen[e].get(key, 0) >= val:
            return
        self.eng[e].wait_ge(self._semof(key), val)
        self.seen[e][key] = val

    def _deps(self, e, r, w):
        for t in r:
            if t.w is not None:
                self._wait(e, t.w[0], t.w[1])
        for t in w:
            if t.w is not None:
                self._wait(e, t.w[0], t.w[1])
            for k, v in t.r.items():
                self._wait(e, k, v)

    def op(self, e, fn, r=(), w=()):
        self._deps(e, r, w)
        inst = fn(self.eng[e])
        self.cnt[e] += 1
        self.ninst += 1
        inst.then_inc(self.sem[e], 1)
        v = self.cnt[e]
        for t in r:
            t.r[e] = v
        for t in w:
            t.w = (e, v)
            t.r = {}
        return inst

    def dma(self, q, out, in_, r=(), w=()):
        self._deps(q, r, w)
        i = self.dn % NDS
        gen = self.dn // NDS
        key = ("d", i)
        if gen > 0:
            self._wait(q, key, 16 * gen)
        inst = self.eng[q].dma_start(out=out, in_=in_)
        inst.then_inc(self.dsem[i], 16)
        self.dn += 1
        self.ninst += 1
        v = 16 * (gen + 1)
        for t in r:
            t.r[key] = v
        for t in w:
            t.w = (key, v)
            t.r = {}
        return (key, v)

    def wait_all(self, e, trks):
        self._deps(e, trks, trks)


class Buf:
    def __init__(self, t):
        self.t = t
        self.k = Trk()

    def __getitem__(self, idx):
        return self.t[idx]


class Builder:
    def __init__(self, nb, n_layers=4, debug_out=None):
        self.nb = nb
        self.n_layers = n_layers
        self.L = weight_layout()
        nc = bass.Bass("TRN2", target_bir_lowering=False)
        self.nc = nc
        self.P = Prog(nc)
        NW = self.L.total()
        self.x = nc.dram_tensor("x", [nb, S, D], F32, kind="ExternalInput").ap()
        self.wall = nc.dram_tensor("wall", [NW], F32, kind="ExternalInput").ap()
        self.vecs_d = nc.dram_tensor("vecs", [128, NV], F32, kind="ExternalInput").ap()
        self.bvecs_d = nc.dram_tensor("bvecs", [128, NBV], F32, kind="ExternalInput").ap()
        self.cst_d = nc.dram_tensor("cst", [128, NCST], F32, kind="ExternalInput").ap()
        self.y = nc.dram_tensor("y", [nb, S, D], F32, kind="ExternalOutput").ap()
        self.wbf = nc.dram_tensor("wbf", [NW], BF16, kind="Internal").ap()
        self.wchunk = [Trk() for _ in range(NW // CONV_CH)]
        self.out_trks = []
        self.marks = []

        def sb(name, shape, dt):
            return Buf(nc.alloc_sbuf_tensor(name, shape, dt))

        self.sb = sb
        self.hT = nc.alloc_sbuf_tensor("hT", [128, NCH, S], F32)
        self.hk = [[Trk() for _ in range(NT)] for _ in range(NCH)]
        self.vecs = sb("vecs_sb", [128, NV], F32)
        self.bvecs = sb("bvecs_sb", [128, NBV], F32)
        self.ident_f = sb("ident_f", [128, 128], F32)
        self.ident_b = sb("ident_b", [128, 128], BF16)
        self.ones_b = sb("ones_b", [128, 128], BF16)
        self.coef = sb("coef", [128, 2, 2, 8], F32)
        self.cos_b = sb("cos_b", [128, S], BF16)
        self.sin_b = sb("sin_b", [128, S], BF16)
        self.cmask_b = sb("cmask_b", [128, S], BF16)
        self.triL_b = sb("triL_b", [128, 128], BF16)
        self.triU_b = sb("triU_b", [128, 128], BF16)
        self.prot_b = sb("prot_b", [128, 128], BF16)
        self.bones_b = sb("bones_b", [128, 128], BF16)
        self.selb = sb("selb", [128, 16, 32], F32)
        self.KC = sb("KC", [128, 4, 128], BF16)
        self.VC = sb("VC", [128, 4, 98], BF16)
        self.posb = sb("posb", [128, 2, 32], BF16)
        self.cbias = sb("cbias", [128, 2, 2], F32)
        self.pocs = [sb("pocs%d" % i, [128, 4, 97], F32) for i in range(2)]
        self.pows = sb("pows", [128, 4, 65], F32)
        self.KSd = nc.dram_tensor("KSd", [2, 64, 4, S], BF16, kind="Internal").ap()
        self.Vd = nc.dram_tensor("Vd", [2, 128, 16, 4, 66], BF16, kind="Internal").ap()
        self.ksd_k = [[[Trk() for _ in range(NT)] for _ in range(4)] for _ in range(2)]
        self.vd_k = [[Trk() for _ in range(NT)] for _ in range(2)]
        self.ps = [Buf(nc.alloc_psum_tensor("ps%d" % i, [128, 512], F32)) for i in range(8)]
        self.ps_rr = 0
        self.pool_misc = [0, 1, 2, 3, 4, 5, 6, 7]
        self.NSLOT = 3
        self.ring = [sb("wring%d" % i, [128, FH], BF16) for i in range(self.NSLOT)]
        self.perm_ids = set(id(b) for b in self.ring)
        self.PHB = 100 * 1024
        self.ph = nc.alloc_sbuf_tensor("phase", [128, self.PHB // 4], F32)
        self.phk = {}

    def bank(self, pool=None):
        if pool is None:
            b = self.ps[self.ps_rr % 8]
        else:
            b = self.ps[pool[self.ps_rr % len(pool)]]
        self.ps_rr += 1
        return b

    def vec(self, name, c=0):
        i = VEC[name] + c
        return self.vecs[:, i:i + 1]

    def phase_view(self, byte_off, shape, dt):
        esz = 4 if dt == F32 else 2
        n = int(np.prod(shape[1:]))
        assert byte_off % 4 == 0 and byte_off + n * esz <= self.PHB, (byte_off, shape)
        if dt == F32:
            ap = self.ph[:, byte_off // 4: byte_off // 4 + n]
        else:
            ap = self.ph[:].bitcast(BF16)[:, byte_off // 2: byte_off // 2 + n]
        if len(shape) == 2:
            return ap
        names = " ".join("a%d" % i for i in range(len(shape) - 1))
        kw = {"a%d" % i: shape[i + 1] for i in range(len(shape) - 1)}
        return ap.rearrange("p (%s) -> p %s" % (names, names), **kw)

    def mark(self, name):
        self.marks.append((name, dict(self.P.cnt)))

    def new_phase(self, trks):
        merged = {}
        for t in self.phase_cur:
            if t.w is not None:
                merged[t.w[0]] = max(merged.get(t.w[0], 0), t.w[1])
            for k, v in t.r.items():
                merged[k] = max(merged.get(k, 0), v)
        for t in trks:
            t.w = None
            t.r = dict(merged)
        self.phase_cur = list(trks)

    def plan_weights(self, entries):
        self.wplan = list(entries)
        self.wplan_i = 0
        self.wq = []
        self.free_perm = list(self.ring)
        self.free_ext = []
        self.inuse = None
        self.serial = -1

    def begin_phase(self, ext_slots):
        self._release()
        self.serial += 1
        self.free_ext = list(ext_slots)
        self.ext_ids = set(id(b) for b in ext_slots)

    def _release(self):
        if self.inuse is not None:
            slot, ser = self.inuse
            if id(slot) in self.perm_ids:
                self.free_perm.append(slot)
            elif ser == self.serial:
                self.free_ext.append(slot)
            self.inuse = None

    def _topup(self):
        while self.wplan_i < len(self.wplan):
            name, ser = self.wplan[self.wplan_i]
            if ser == self.serial and self.free_ext:
                slot = self.free_ext.pop(0)
            elif self.free_perm:
                slot = self.free_perm.pop(0)
            else:
                break
            self.wplan_i += 1
            off = self.L.off[name]
            free = self.L.free[name]
            src = self.wbf[off:off + 128 * free].rearrange("(p f) -> p f", p=128)
            c0 = off // CONV_CH
            c1 = (off + 128 * free - 1) // CONV_CH
            self.P.dma("sp", slot[:, 0:free], src, r=[self.wchunk[c] for c in range(c0, c1 + 1)], w=[slot.k])
            self.wq.append((name, slot, ser))

    def next_w(self, name):
        self._release()
        self._topup()
        n, slot, ser = self.wq.pop(0)
        assert n == name and ser == self.serial, (n, name, ser, self.serial)
        self.inuse = (slot, ser)
        return slot

    def ext_slots(self, byte_off):
        out = []
        o = byte_off
        while o + 2 * FH <= self.PHB:
            b = Buf(self.phase_view(o, [128, FH], BF16))
            out.append(b)
            o += 2 * FH
        return out

    def prologue(self):
        P = self.P
        nc = self.nc
        P.dma("sp", self.vecs[:], self.vecs_d, w=[self.vecs.k])
        P.dma("sp", self.bvecs[:], self.bvecs_d, w=[self.bvecs.k])
        P.dma("sp", self.ident_f[:], self.cst_d[:, CST["ident"]:CST["ident"] + 128], w=[self.ident_f.k])
        P.op("dve", lambda e: e.tensor_copy(out=self.ident_b[:], in_=self.ident_f[:]), r=[self.ident_f.k], w=[self.ident_b.k])
        P.op("pool", lambda e: e.memset(self.ones_b[:], 1.0), w=[self.ones_b.k])
        tmp = self.sb("coef_tmp", [128, 16], F32)
        for l in range(2):
            lam = self.vecs[:, VEC[("a_lam", l)]:VEC[("a_lam", l)] + 8]
            P.op("act", lambda e: e.activation(out=tmp[:, l * 8:l * 8 + 8], in_=lam, func=AF.Exp, scale=-1.0), r=[self.vecs.k], w=[tmp.k])
            P.op("act", lambda e: e.activation(out=tmp[:, l * 8:l * 8 + 8], in_=tmp[:, l * 8:l * 8 + 8], func=AF.Ln, bias=1.0), r=[tmp.k], w=[tmp.k])
            P.op("dve", lambda e: e.tensor_scalar(out=self.coef[:, l, 0, :], in0=tmp[:, l * 8:l * 8 + 8], scalar1=-8.0, scalar2=None, op0=ALU.mult), r=[tmp.k], w=[self.coef.k])
            P.op("dve", lambda e: e.tensor_scalar(out=self.coef[:, l, 1, :], in0=tmp[:, l * 8:l * 8 + 8], scalar1=-16.0, scalar2=None, op0=ALU.mult), r=[tmp.k], w=[self.coef.k])
        stg = self.phase_view(0, [128, 2048], F32)
        stk = Trk()
        self.phase_cur = [stk]

        def ctab(name, n, dst, dstk, eng):
            P.dma("sp", stg[:, 0:n], self.cst_d[:, CST[name]:CST[name] + n], w=[stk])
            if eng == "act":
                P.op("act", lambda e: e.activation(out=dst, in_=stg[:, 0:n], func=AF.Copy), r=[stk], w=[dstk])
            else:
                P.op(eng, lambda e: e.tensor_copy(out=dst, in_=stg[:, 0:n]), r=[stk], w=[dstk])

        ctab("cos", 2048, self.cos_b[:], self.cos_b.k, "dve")
        ctab("sin", 2048, self.sin_b[:], self.sin_b.k, "act")
        ctab("cmask", 2048, self.cmask_b[:], self.cmask_b.k, "dve")
        ctab("triL", 128, self.triL_b[:], self.triL_b.k, "dve")
        ctab("triU", 128, self.triU_b[:], self.triU_b.k, "dve")
        ctab("prot", 128, self.prot_b[:], self.prot_b.k, "dve")
        ctab("bones", 128, self.bones_b[:], self.bones_b.k, "dve")
        P.dma("sp", self.selb[:].rearrange("p a b -> p (a b)"), self.cst_d[:, CST["selb"]:CST["selb"] + 512], w=[self.selb.k])
        P.op("pool", lambda e: e.memset(self.VC[:], 0.0), w=[self.VC.k])
        P.op("pool", lambda e: e.memset(self.KC[:], 0.0), w=[self.KC.k])
        P.op("pool", lambda e: e.memset(self.VC[:, :, 64:65], 1.0), w=[self.VC.k])
        P.dma("sp", stg[:, 0:32], self.cst_d[:, CST["ovl"]:CST["ovl"] + 32], w=[stk])
        for g in range(4):
            P.op("dve", lambda e: e.tensor_copy(out=self.VC[:, g, 65:97], in_=stg[:, 0:32]), r=[stk], w=[self.VC.k])
        for kv in range(2):
            o0 = VEC[("c_pos", kv)]
            P.op("dve", lambda e: e.tensor_copy(out=self.posb[:, kv, :], in_=self.vecs[:, o0:o0 + 32]), r=[self.vecs.k], w=[self.posb.k])
        nst = 3
        stf = [(self.phase_view(i * 12288, [128, 2048], F32), Trk()) for i in range(nst)]
        stb = [(self.phase_view(i * 12288 + 8192, [128, 2048], BF16), Trk()) for i in range(nst)]
        self.new_phase([k for _, k in stf] + [k for _, k in stb])
        nchunk = len(self.wchunk)
        engs = ["dve", "act", "pool"]
        for i in range(nchunk):
            sf, kf = stf[i % nst]
            sbb, kb = stb[i % nst]
            src = self.wall[i * CONV_CH:(i + 1) * CONV_CH].rearrange("(p f) -> p f", p=128)
            dst = self.wbf[i * CONV_CH:(i + 1) * CONV_CH].rearrange("(p f) -> p f", p=128)
            P.dma("sp", sf, src, w=[kf])
            e = engs[i % 3]
            if e == "act":
                P.op("act", lambda en: en.activation(out=sbb, in_=sf, func=AF.Copy), r=[kf], w=[kb])
            else:
                P.op(e, lambda en: en.tensor_copy(out=sbb, in_=sf), r=[kf], w=[kb])
            P.dma("sp", dst, sbb, r=[kb], w=[self.wchunk[i]])
        self.phase_cur = [k for _, k in stf] + [k for _, k in stb] + [stk]

    def load_x(self, b):
        P = self.P
        xin = [(self.phase_view(j * 4096, [128, 1024], F32), Trk()) for j in range(4)]
        self.new_phase([k for _, k in xin])
        for n in range(NT):
            for j in range(4):
                t0 = n * TT + j * 128
                P.dma("sp", xin[j][0], self.x[b, t0:t0 + 128, :], w=[xin[j][1]])
            for c in range(NCH):
                pb = self.bank()
                for j in range(4):
                    P.op("pe", lambda e: e.transpose(out=pb[:, j * 128:(j + 1) * 128], in_=xin[j][0][:, c * 128:(c + 1) * 128], identity=self.ident_f[:]),
                         r=[xin[j][1], self.ident_f.k], w=[pb.k])
                eng = "act" if c % 2 == 0 else "dve"
                dst = self.hT[:, c, n * TT:(n + 1) * TT]
                if eng == "act":
                    P.op("act", lambda e: e.activation(out=dst, in_=pb[:], func=AF.Copy), r=[pb.k], w=[self.hk[c][n]])
                else:
                    P.op("dve", lambda e: e.tensor_copy(out=dst, in_=pb[:]), r=[pb.k], w=[self.hk[c][n]])

    def store_y(self, b):
        P = self.P
        yo = [(self.phase_view(j * 4096, [128, 1024], F32), Trk()) for j in range(4)]
        self.new_phase([k for _, k in yo])
        for n in range(NT):
            for j in range(4):
                t0 = n * TT + j * 128
                for half in range(2):
                    pb = self.bank()
                    for cc in range(4):
                        c = half * 4 + cc
                        P.op("pe", lambda e: e.transpose(out=pb[:, cc * 128:(cc + 1) * 128], in_=self.hT[:, c, t0:t0 + 128], identity=self.ident_f[:]),
                             r=[self.hk[c][n], self.ident_f.k], w=[pb.k])
                    dst = yo[j][0][:, half * 512:(half + 1) * 512]
                    wl = [yo[j][1]]
                    if half == 0:
                        P.op("act", lambda e: e.activation(out=dst, in_=pb[:], func=AF.Copy), r=[pb.k], w=wl)
                    else:
                        P.op("dve", lambda e: e.tensor_copy(out=dst, in_=pb[:]), r=[pb.k], w=wl)
                ot = Trk()
                P.dma("sp", self.y[b, t0:t0 + 128, :], yo[j][0], r=[yo[j][1]], w=[ot])
                self.out_trks.append(ot)

    def norm_tile(self, n, gname, u_ap, u_k, sq_bufs, rstd_buf):
        P = self.P
        pb = self.bank()
        for c in range(NCH):
            sq, sk = sq_bufs[c % len(sq_bufs)]
            hsl = self.hT[:, c, n * TT:(n + 1) * TT]
            P.op("act", lambda e: e.activation(out=sq, in_=hsl, func=AF.Square), r=[self.hk[c][n]], w=[sk])
            P.op("pe", lambda e: e.matmul(pb[:], lhsT=self.ones_b[:], rhs=sq, start=(c == 0), stop=(c == NCH - 1)),
                 r=[sk, self.ones_b.k], w=[pb.k])
        rs, rk = rstd_buf
        P.op("act", lambda e: e.activation(out=rs, in_=pb[:], func=AF.Sqrt, scale=1.0 / D, bias=EPS), r=[pb.k], w=[rk])
        P.op("dve", lambda e: e.reciprocal(out=rs, in_=rs), r=[rk], w=[rk])
        for c in range(NCH):
            hsl = self.hT[:, c, n * TT:(n + 1) * TT]
            g = self.vec(gname, c)
            P.op("dve", lambda e: e.scalar_tensor_tensor(out=u_ap[:, c, :], in0=hsl, scalar=g, in1=rs, op0=ALU.mult, op1=ALU.mult),
                 r=[self.hk[c][n], rk, self.vecs.k], w=[u_k])

    def a_phase_setup(self):
        v = self.phase_view
        A = {}
        A["u"] = [(v(n * 8192, [128, 8, 512], BF16), Trk()) for n in range(NT)]
        A["m"] = [(v(32768 + n * 8192, [128, 8, 512], BF16), Trk()) for n in range(NT)]
        A["sq"] = [(v(65536 + i * 1024, [128, 512], BF16), Trk()) for i in range(2)]
        A["rstd"] = (v(67584, [128, 512], F32), Trk())
        A["xp"] = [(v(69632 + i * 2080, [128, 516], F32), Trk()) for i in range(2)]
        base = 73792
        sets = []
        for i in range(2):
            o = base + i * 12288
            xr = (v(o + 2048, [128, 512], F32), Trk())
            sets.append({
                "y": (v(o, [128, 512], BF16), Trk()),
                "xrb": (v(o + 1024, [128, 512], BF16), Trk()),
                "xr": xr,
                "r": (v(o + 4096, [128, 512], F32), Trk()),
                "i": (v(o + 6144, [128, 512], F32), Trk()),
                "a": (v(o + 8192, [128, 512], F32), Trk()),
                "a2": (v(o + 10240, [128, 512], F32), Trk()),
                "hr": xr,
            })
        A["sets"] = sets
        trks = [k for _, k in A["u"]] + [k for _, k in A["m"]] + [A["rstd"][1]] + [k for _, k in A["sq"]] + [k for _, k in A["xp"]]
        for st in sets:
            trks += [k for _, k in st.values()]
        self.new_phase(trks)
        self.begin_phase([])
        self.A = A

    def a_layer(self, l):
        P = self.P
        self.a_phase_setup()
        A = self.A
        P.op("pool", lambda e: e.memset(self.convc[:], 0.0), w=[self.convc.k])
        P.op("pool", lambda e: e.memset(self.hst[:], 0.0), w=[self.hst.k])
        for n in range(NT):
            self.norm_tile(n, ("a_norm", l), A["u"][n][0], A["u"][n][1], A["sq"], A["rstd"])
        it = 0
        for c in range(NCH):
            w = self.next_w(("a_in", l, c))
            for n in range(NT):
                u, uk = A["u"][n]
                m, mk = A["m"][n]
                pg = self.bank()
                pr = self.bank()
                for k in range(NCH):
                    P.op("pe", lambda e: e.matmul(pg[:], lhsT=w[:, k * 128:(k + 1) * 128], rhs=u[:, k, :], start=(k == 0), stop=(k == NCH - 1)),
                         r=[w.k, uk], w=[pg.k])
                for k in range(NCH):
                    P.op("pe", lambda e: e.matmul(pr[:], lhsT=w[:, 1024 + k * 128:1024 + (k + 1) * 128], rhs=u[:, k, :], start=(k == 0), stop=(k == NCH - 1)),
                         r=[w.k, uk], w=[pr.k])
                st = A["sets"][it % 2]
                xp, xpk = A["xp"][it % 2]
                it += 1
                y, yk = st["y"]
                xr, xrk = st["xr"]
                xrb, xrbk = st["xrb"]
                rr, rrk = st["r"]
                ii, iik = st["i"]
                aa, aak = st["a"]
                a2, a2k = st["a2"]
                hr, hrk = st["hr"]
                P.op("act", lambda e: e.activation(out=y, in_=pg[:], func=AF.Gelu_apprx_tanh), r=[pg.k], w=[yk])
                P.op("pool", lambda e: e.tensor_copy(out=xp[:, 0:3], in_=self.convc[:, c, :]), r=[self.convc.k], w=[xpk])
                P.op("act", lambda e: e.activation(out=xp[:, 3:515], in_=pr[:], func=AF.Copy), r=[pr.k], w=[xpk])
                P.op("dve", lambda e: e.tensor_scalar(out=xr, in0=xp[:, 0:512], scalar1=self.vec(("a_cw", l, 0), c), scalar2=self.vec(("a_cb", l), c),
                                                      op0=ALU.mult, op1=ALU.add), r=[xpk, self.vecs.k], w=[xrk])
                for kk in range(1, 4):
                    P.op("dve", lambda e: e.scalar_tensor_tensor(out=xr, in0=xp[:, kk:kk + 512], scalar=self.vec(("a_cw", l, kk), c), in1=xr,
                                                                 op0=ALU.mult, op1=ALU.add), r=[xpk, xrk, self.vecs.k], w=[xrk])
                P.op("pool", lambda e: e.tensor_copy(out=self.convc[:, c, :], in_=xp[:, 512:515]), r=[xpk], w=[self.convc.k])
                P.op("pool", lambda e: e.tensor_copy(out=xrb, in_=xr), r=[xrk], w=[xrbk])
                p1 = self.bank()
                p2 = self.bank()
                P.op("pe", lambda e: e.matmul(p1[:], lhsT=w[:, 2048:2176], rhs=xrb, start=True, stop=True), r=[w.k, xrbk], w=[p1.k])
                P.op("pe", lambda e: e.matmul(p2[:], lhsT=w[:, 2176:2304], rhs=xrb, start=True, stop=True), r=[w.k, xrbk], w=[p2.k])
                P.op("act", lambda e: e.activation(out=rr, in_=p1[:], func=AF.Sigmoid, bias=self.vec(("a_gb", l, 0), c)), r=[p1.k, self.vecs.k], w=[rrk])
                P.op("act", lambda e: e.activation(out=ii, in_=p2[:], func=AF.Sigmoid, bias=self.vec(("a_gb", l, 1), c)), r=[p2.k, self.vecs.k], w=[iik])
                P.op("act", lambda e: e.activation(out=aa, in_=rr, func=AF.Exp, scale=self.coef[:, l, 0, c:c + 1]), r=[rrk, self.coef.k], w=[aak])
                P.op("act", lambda e: e.activation(out=a2, in_=rr, func=AF.Exp, scale=self.coef[:, l, 1, c:c + 1]), r=[rrk, self.coef.k], w=[a2k])
                P.op("act", lambda e: e.activation(out=a2, in_=a2, func=AF.Sqrt, scale=-1.0, bias=1.0), r=[a2k], w=[a2k])
                P.op("dve", lambda e: e.tensor_tensor(out=ii, in0=ii, in1=xr, op=ALU.mult), r=[iik, xrk], w=[iik])
                P.op("dve", lambda e: e.tensor_tensor(out=ii, in0=ii, in1=a2, op=ALU.mult), r=[iik, a2k], w=[iik])
                P.op("dve", lambda e: e.tensor_tensor_scan(out=hr, data0=aa, data1=ii, initial=self.hst[:, c:c + 1], op0=ALU.mult, op1=ALU.add),
                     r=[aak, iik, self.hst.k], w=[hrk])
                P.op("pool", lambda e: e.tensor_copy(out=self.hst[:, c:c + 1], in_=hr[:, 511:512]), r=[hrk], w=[self.hst.k])
                P.op("pool", lambda e: e.tensor_tensor(out=m[:, c, :], in0=hr, in1=y, op=ALU.mult), r=[hrk, yk], w=[mk])
        for mm in range(NCH):
            w = self.next_w(("a_out", l, mm))
            for n in range(NT):
                m, mk = A["m"][n]
                po = self.bank()
                for k in range(NCH):
                    P.op("pe", lambda e: e.matmul(po[:], lhsT=w[:, k * 128:(k + 1) * 128], rhs=m[:, k, :], start=(k == 0), stop=(k == NCH - 1)),
                         r=[w.k, mk], w=[po.k])
                hsl = self.hT[:, mm, n * TT:(n + 1) * TT]
                P.op("dve", lambda e: e.tensor_tensor(out=hsl, in0=po[:], in1=hsl, op=ALU.add), r=[po.k, self.hk[mm][n]], w=[self.hk[mm][n]])

    def ffn_phase_setup(self):
        v = self.phase_view
        Fz = {}
        Fz["u"] = [(v(s * 8192, [128, 8, 512], BF16), Trk()) for s in range(2)]
        Fz["act"] = [[(v(16384 + (c * 2 + s) * 1024, [128, 512], BF16), Trk()) for s in range(2)] for c in range(FC)]
        Fz["sq"] = [(v(61440 + i * 1024, [128, 512], BF16), Trk()) for i in range(2)]
        Fz["rstd"] = (v(63488, [128, 512], F32), Trk())
        Fz["sg"] = [(v(65536 + i * 1024, [128, 512], BF16), Trk()) for i in range(4)]
        ext = self.ext_slots(69632)
        trks = [k for _, k in Fz["u"]] + [k for row in Fz["act"] for _, k in row] + [k for _, k in Fz["sq"]] + [Fz["rstd"][1]] + [k for _, k in Fz["sg"]] + [b.k for b in ext]
        self.new_phase(trks)
        self.begin_phase(ext)
        self.F = Fz

    def ffn_tile(self, L, t2):
        P = self.P
        Fz = self.F
        for s in range(2):
            self.norm_tile(2 * t2 + s, ("f_norm", L), Fz["u"][s][0], Fz["u"][s][1], Fz["sq"], Fz["rstd"])
        nsg = 0
        for c in range(FC):
            w = self.next_w(("f_in", L, c))
            for s in range(2):
                u, uk = Fz["u"][s]
                pg = self.bank()
                pu = self.bank()
                for k in range(NCH):
                    P.op("pe", lambda e: e.matmul(pg[:], lhsT=w[:, k * 128:(k + 1) * 128], rhs=u[:, k, :], start=(k == 0), stop=(k == NCH - 1)),
                         r=[w.k, uk], w=[pg.k])
                for k in range(NCH):
                    P.op("pe", lambda e: e.matmul(pu[:], lhsT=w[:, 1024 + k * 128:1024 + (k + 1) * 128], rhs=u[:, k, :], start=(k == 0), stop=(k == NCH - 1)),
                         r=[w.k, uk], w=[pu.k])
                sg, sgk = Fz["sg"][nsg % 4]
                nsg += 1
                a, ak = Fz["act"][c][s]
                P.op("act", lambda e: e.activation(out=sg, in_=pg[:], func=AF.Silu), r=[pg.k], w=[sgk])
                P.op("dve", lambda e: e.tensor_tensor(out=a, in0=sg, in1=pu[:], op=ALU.mult), r=[sgk, pu.k], w=[ak])
        for mm in range(NCH):
            w = self.next_w(("f_out", L, mm))
            for s in range(2):
                n = 2 * t2 + s
                po = self.bank()
                for c in range(FC):
                    a, ak = Fz["act"][c][s]
                    P.op("pe", lambda e: e.matmul(po[:], lhsT=w[:, c * 128:(c + 1) * 128], rhs=a, start=(c == 0), stop=(c == FC - 1)),
                         r=[w.k, ak], w=[po.k])
                hsl = self.hT[:, mm, n * TT:(n + 1) * TT]
                P.op("dve", lambda e: e.tensor_tensor(out=hsl, in0=po[:], in1=hsl, op=ALU.add), r=[po.k, self.hk[mm][n]], w=[self.hk[mm][n]])

    def ffn_layer(self, L):
        self.ffn_phase_setup()
        for t2 in range(2):
            self.ffn_tile(L, t2)

    def headnorm_rope(self, pk, R, C, gvec, cos_ap, sin_ap, T, bias=None):
        P = self.P
        sq, sqk = T["sq"]
        rs, rsk = T["rs"]
        qn, qnk = T["qn"]
        t1, t1k = T["t1"]
        t2, t2k = T["t2"]
        if bias is not None:
            xf, xfk = T["xf"]
            P.op("act", lambda e: e.activation(out=xf[0:R, 0:C], in_=pk[0:R, 0:C], func=AF.Identity, bias=bias), r=[pk.k, self.vecs.k], w=[xfk])
            src, srck = xf[0:R, 0:C], xfk
        else:
            src, srck = pk[0:R, 0:C], pk.k
        P.op("act", lambda e: e.activation(out=sq[0:R, 0:C], in_=src, func=AF.Square), r=[srck], w=[sqk])
        pss = self.bank(self.pool_misc)
        P.op("pe", lambda e: e.matmul(pss[0:R, 0:C], lhsT=self.bones_b[0:R, 0:R], rhs=sq[0:R, 0:C], start=True, stop=True), r=[sqk, self.bones_b.k], w=[pss.k])
        P.op("act", lambda e: e.activation(out=rs[0:R, 0:C], in_=pss[0:R, 0:C], func=AF.Sqrt, scale=1.0 / 64.0, bias=EPS), r=[pss.k], w=[rsk])
        P.op("dve", lambda e: e.reciprocal(out=rs[0:R, 0:C], in_=rs[0:R, 0:C]), r=[rsk], w=[rsk])
        P.op("dve", lambda e: e.scalar_tensor_tensor(out=qn[0:R, 0:C], in0=src, scalar=gvec, in1=rs[0:R, 0:C], op0=ALU.mult, op1=ALU.mult),
             r=[srck, rsk, self.vecs.k], w=[qnk])
        prt = self.bank(self.pool_misc)
        P.op("pe", lambda e: e.matmul(prt[0:R, 0:C], lhsT=self.prot_b[0:R, 0:R], rhs=qn[0:R, 0:C], start=True, stop=True), r=[qnk, self.prot_b.k], w=[prt.k])
        P.op("pool", lambda e: e.tensor_tensor(out=t1[0:R, 0:C], in0=qn[0:R, 0:C], in1=cos_ap, op=ALU.mult), r=[qnk, self.cos_b.k], w=[t1k])
        P.op("dve", lambda e: e.tensor_tensor(out=t2[0:R, 0:C], in0=prt[0:R, 0:C], in1=sin_ap, op=ALU.mult), r=[prt.k, self.sin_b.k], w=[t2k])

    def kv_phase(self):
        P = self.P
        v = self.phase_view
        U = [(v(n * 8192, [128, 8, 512], BF16), Trk()) for n in range(NT)]
        sqn = [(v(32768 + i * 1024, [128, 512], BF16), Trk()) for i in range(2)]
        rstd = (v(34816, [128, 512], F32), Trk())
        kcT, kcTk = v(36864, [128, 8, S], BF16), [Trk() for _ in range(8)]
        cw1, cw1k = v(0, [128, 2, 32, 256], BF16), Trk()
        T = {"sq": (v(69632, [128, 512], BF16), Trk()), "rs": (v(70656, [128, 512], F32), Trk()), "qn": (v(72704, [128, 512], BF16), Trk()),
             "t1": (v(73728, [128, 512], F32), Trk()), "t2": (v(75776, [128, 512], F32), Trk()), "xf": (v(77824, [128, 512], F32), Trk())}
        kout = [(v(79872 + i * 1024, [128, 512], BF16), Trk()) for i in range(2)]
        vst = [(v(81920 + i * 528, [128, 4, 66], BF16), Trk()) for i in range(2)]
        hid = [(v(83008 + i * 512, [128, 2, 128], BF16), Trk()) for i in range(2)]
        ext = self.ext_slots(84032)
        trks = [k for _, k in U] + [rstd[1], cw1k] + [k for _, k in sqn] + kcTk + [k for _, k in T.values()] + [k for _, k in kout] + [k for _, k in vst] \
            + [k for _, k in hid] + [b.k for b in ext]
        self.new_phase(trks)
        self.begin_phase(ext)
        self.pool_misc = [0, 1, 2, 3, 4, 5, 6, 7]
        for i in range(2):
            P.op("pool", lambda e: e.memset(vst[i][0][:, :, 64:66], 1.0), w=[vst[i][1]])
        for n in range(NT):
            self.norm_tile(n, ("kv_norm",), U[n][0], U[n][1], sqn, rstd)
        nko = 0
        nvs = 0
        for i in range(4):
            which, cc = i // 2, i % 2
            w = self.next_w(("kvk", i))
            for n in range(NT):
                tsl = slice(n * TT, (n + 1) * TT)
                u, uk = U[n]
                pk = self.bank()
                for k in range(NCH):
                    P.op("pe", lambda e: e.matmul(pk[:], lhsT=w[:, k * 128:(k + 1) * 128], rhs=u[:, k, :], start=(k == 0), stop=(k == NCH - 1)), r=[w.k, uk], w=[pk.k])
                self.headnorm_rope(pk, 128, 512, self.vec(("k_norm", 1 + which)), self.cos_b[:, tsl], self.sin_b[:, tsl], T)
                ko, kok = kout[nko % 2]
                nko += 1
                P.op("dve", lambda e: e.tensor_tensor(out=ko, in0=T["t1"][0], in1=T["t2"][0], op=ALU.add), r=[T["t1"][1], T["t2"][1]], w=[kok])
                for hh in range(2):
                    P.dma("sp", self.KSd[which, :, 2 * cc + hh, tsl], ko[hh * 64:(hh + 1) * 64, :], r=[kok], w=[self.ksd_k[which][2 * cc + hh][n]])
        for i in range(4):
            sel, cc = i // 2, i % 2
            w = self.next_w(("kvc", i))
            for n in range(NT):
                tsl = slice(n * TT, (n + 1) * TT)
                u, uk = U[n]
                for gg in range(2):
                    pc = self.bank()
                    for k in range(NCH):
                        P.op("pe", lambda e: e.matmul(pc[0:64, :], lhsT=w[:, k * 128 + gg * 64:k * 128 + gg * 64 + 64], rhs=u[:, k, :], start=(k == 0), stop=(k == NCH - 1)),
                             r=[w.k, uk], w=[pc.k])
                    idx = sel * 4 + 2 * cc + gg
                    P.op("act", lambda e: e.activation(out=kcT[0:64, idx, tsl], in_=pc[0:64, :], func=AF.Copy), r=[pc.k], w=[kcTk[idx]])
        for i in range(2):
            w = self.next_w(("kvv", i))
            for n in range(NT):
                u, uk = U[n]
                for jb in range(4):
                    pv = self.bank()
                    for k in range(NCH):
                        P.op("pe", lambda e: e.matmul(pv[:, 0:256], lhsT=u[:, k, jb * 128:(jb + 1) * 128], rhs=w[:, k * 256:(k + 1) * 256], start=(k == 0), stop=(k == NCH - 1)),
                             r=[w.k, uk], w=[pv.k])
                    vs, vsk = vst[nvs % 2]
                    nvs += 1
                    P.op("act", lambda e: e.activation(out=vs[:, :, 0:64], in_=pv[:, 0:256].rearrange("p (g d) -> p g d", g=4), func=AF.Copy), r=[pv.k], w=[vsk])
                    P.dma("sp", self.Vd[i, :, 4 * n + jb, :, :], vs, r=[vsk], w=[self.vd_k[i][n]])
        for kv in range(2):
            off = self.L.off[("cw1", kv)]
            src = self.wbf[off:off + 128 * 8192].rearrange("(p f) -> p f", p=128)
            c0, c1 = off // CONV_CH, (off + 128 * 8192 - 1) // CONV_CH
            P.dma("sp", cw1[0:64, kv].rearrange("p a b -> p (a b)"), src[0:64, :], r=[self.wchunk[c] for c in range(c0, c1 + 1)], w=[cw1k] + [k for _, k in U])
        w2 = self.next_w(("cw2",))
        for sel in range(2):
            pcv = self.bank()
            for cc in range(2):
                for l in range(32):
                    P.op("pe", lambda e: e.matmul(pcv[:, cc:cc + 1], lhsT=cw1[0:64, sel, l, cc * 128:(cc + 1) * 128], rhs=self.posb[0:64, sel, l:l + 1],
                                                  start=(cc == 0 and l == 0), stop=(cc == 1 and l == 31)), r=[cw1k, self.posb.k], w=[pcv.k])
            b1o = VEC[("c_b1", sel)]
            P.op("dve", lambda e: e.tensor_tensor(out=self.cbias[:, sel, :], in0=pcv[:, 0:2], in1=self.vecs[:, b1o:b1o + 2], op=ALU.add), r=[pcv.k, self.vecs.k], w=[self.cbias.k])
        nh = 0
        for sel in range(2):
            for g in range(4):
                hd, hdk = hid[nh % 2]
                nh += 1
                for cc in range(2):
                    ph = self.bank()
                    for l in range(32):
                        P.op("pe", lambda e: e.matmul(ph[:, 0:127], lhsT=cw1[0:64, sel, l, cc * 128:(cc + 1) * 128], rhs=kcT[0:64, sel * 4 + g, l:l + 16 * 126 + 1:16],
                                                      start=(l == 0), stop=(l == 31)), r=[cw1k, kcTk[sel * 4 + g]], w=[ph.k])
                    P.op("act", lambda e: e.activation(out=hd[:, cc, 0:127], in_=ph[:, 0:127], func=AF.Gelu_apprx_tanh, bias=self.cbias[:, sel, cc:cc + 1]),
                         r=[ph.k, self.cbias.k], w=[hdk])
                if sel == 0:
                    pk = self.bank()
                    for cc in range(2):
                        P.op("pe", lambda e: e.matmul(pk[0:64, 0:127], lhsT=w2[:, (0 * 2 + cc) * 64:(0 * 2 + cc) * 64 + 64], rhs=hd[:, cc, 0:127], start=(cc == 0), stop=(cc == 1)),
                             r=[w2.k, hdk], w=[pk.k])
                    self.headnorm_rope(pk, 64, 127, self.vecs[0:64, VEC[("k_norm", 0)]:VEC[("k_norm", 0)] + 1],
                                       self.cos_b[0:64, 31:31 + 16 * 126 + 1:16], self.sin_b[0:64, 31:31 + 16 * 126 + 1:16], T,
                                       bias=self.vecs[0:64, VEC[("c_b2", 0)]:VEC[("c_b2", 0)] + 1])
                    P.op("dve", lambda e: e.tensor_tensor(out=self.KC[0:64, g, 0:127], in0=T["t1"][0][0:64, 0:127], in1=T["t2"][0][0:64, 0:127], op=ALU.add),
                         r=[T["t1"][1], T["t2"][1]], w=[self.KC.k])
                else:
                    pv = self.bank()
                    for cc in range(2):
                        P.op("pe", lambda e: e.matmul(pv[0:127, 0:64], lhsT=hd[:, cc, 0:127], rhs=w2[:, (1 * 2 + cc) * 64:(1 * 2 + cc) * 64 + 64], start=(cc == 0), stop=(cc == 1)),
                             r=[w2.k, hdk], w=[pv.k])
                    P.op("dve", lambda e: e.tensor_tensor(out=self.VC[0:127, g, 0:64], in0=pv[0:127, 0:64], in1=self.bvecs[0:127, BV_CB2V:BV_CB2V + 64], op=ALU.add),
                         r=[pv.k, self.bvecs.k], w=[self.VC.k])

    def b_layer(self, j):
        P = self.P
        v = self.phase_view
        KS, KSk = v(0, [128, 4, S], BF16), [Trk() for _ in range(4)]
        KW, KWk = v(16384, [128, 4, S], BF16), [Trk() for _ in range(4)]
        VS, VSk = v(32768, [128, 16, 4, 66], BF16), Trk()
        VW, VWk = v(41216, [128, 16, 4, 66], BF16), Trk()
        u, uk = v(49664, [128, 8, 512], BF16), Trk()
        Q, Qk = v(57856, [128, 4, 16, 128], BF16), [Trk() for _ in range(4)]
        Nk = [[Trk() for _ in range(4)] for _ in range(4)]
        oT, oTk = v(74240, [128, 8, 512], BF16), Trk()
        ob = [(v(82432 + i * 2048, [128, 1024], BF16), Trk()) for i in range(2)]
        PT = [(v(86528 + i * 1024, [128, 512], BF16), Trk()) for i in range(4)]
        sqn = [(v(90624 + i * 1024, [128, 512], BF16), Trk()) for i in range(2)]
        rstd = (v(92672, [128, 512], F32), Trk())
        gat, gatk = v(94720, [128, 4, 48], F32), Trk()
        T = {"sq": sqn[0], "rs": rstd, "qn": (v(95488, [128, 512], BF16), Trk()),
             "t1": (v(96512, [128, 512], F32), Trk()), "t2": (v(98560, [128, 512], F32), Trk())}
        rd, rdk = v(100608, [128, 16], F32), Trk()
        s3, s3k = v(100672, [128, 12], F32), Trk()
        sc, sck = v(100736, [128, 32], F32), Trk()
        sc2, sc2k = v(100864, [128, 32], F32), Trk()
        nsl, nslk = v(100992, [128, 32], F32), Trk()
        m8, m8k = v(101120, [128, 16], F32), Trk()
        acc, acck = v(101184, [128, 256], F32), Trk()
        trks = KSk + KWk + [VSk, VWk, uk, oTk, gatk, rdk, s3k, sck, sc2k, nslk, m8k, acck] + Qk + [k for _, k in ob] + [k for _, k in PT] \
            + [k for _, k in sqn] + [rstd[1]] + [T["qn"][1], T["t1"][1], T["t2"][1]] + [k for row in Nk for k in row]
        self.new_phase(trks)
        self.begin_phase([])
        self.pool_misc = [6, 7]
        pool_st = [0, 1, 2]
        for g in range(4):
            P.dma("sp", KS[0:64, g, :], self.KSd[0, :, g, :], r=self.ksd_k[0][g], w=[KSk[g]])
            P.dma("sp", KW[0:64, g, :], self.KSd[1, :, g, :], r=self.ksd_k[1][g], w=[KWk[g]])
        P.dma("sp", VS.rearrange("p a b c -> p (a b c)"), self.Vd[0].rearrange("p a b c -> p (a b c)"), r=self.vd_k[0], w=[VSk])
        P.dma("sp", VW.rearrange("p a b c -> p (a b c)"), self.Vd[1].rearrange("p a b c -> p (a b c)"), r=self.vd_k[1], w=[VWk])
        est = self.ph[64:96, 57856 // 4:57856 // 4 + 2048]
        allq = Qk + [k for row in Nk for k in row]
        P.dma("sp", est, self.cst_d[64:96, CST["eind"]:CST["eind"] + 2048], w=allq)
        for g in range(4):
            P.op("dve", lambda e: e.tensor_copy(out=KS[64:96, g, :], in_=est), r=allq, w=[KSk[g]])
        gbo = BV_GB + 48 * j
        npt = 0
        nob = 0
        for n in range(NT):
            tsl = slice(n * TT, (n + 1) * TT)
            self.norm_tile(n, ("b_norm", j), u, uk, sqn, rstd)
            wg = self.next_w(("bg", j))
            for qb in range(4):
                pg = self.bank(self.pool_misc)
                for k in range(NCH):
                    P.op("pe", lambda e: e.matmul(pg[:, 0:48], lhsT=u[:, k, qb * 128:(qb + 1) * 128], rhs=wg[:, k * 48:(k + 1) * 48], start=(k == 0), stop=(k == NCH - 1)),
                         r=[wg.k, uk], w=[pg.k])
                P.op("dve", lambda e: e.tensor_tensor(out=gat[:, qb, :], in0=pg[:, 0:48], in1=self.bvecs[:, gbo:gbo + 48], op=ALU.add), r=[pg.k, self.bvecs.k], w=[gatk])
            P.op("act", lambda e: e.activation(out=gat, in_=gat, func=AF.Sigmoid), r=[gatk], w=[gatk])
            for m in range(NCH):
                w = self.next_w(("bq", j, m))
                pq = self.bank(pool_st)
                for k in range(NCH):
                    P.op("pe", lambda e: e.matmul(pq[:], lhsT=w[:, k * 128:(k + 1) * 128], rhs=u[:, k, :], start=(k == 0), stop=(k == NCH - 1)), r=[w.k, uk], w=[pq.k])
                self.headnorm_rope(pq, 128, 512, self.vec(("q_norm", j)), self.cos_b[:, tsl], self.sin_b[:, tsl], T)
                for hh in range(2):
                    h = 2 * m + hh
                    rsl = slice(hh * 64, (hh + 1) * 64)
                    eng = "dve" if hh == 0 else "pool"
                    P.op(eng, lambda e: e.tensor_tensor(out=Q[0:64, :, h, :], in0=T["t1"][0][rsl, :].rearrange("p (a b) -> p a b", a=4),
                                                        in1=T["t2"][0][rsl, :].rearrange("p (a b) -> p a b", a=4), op=ALU.add),
                         r=[T["t1"][1], T["t2"][1]], w=Qk)
            self.attn_tile(n, dict(KS=KS, KSk=KSk, KW=KW, KWk=KWk, VS=VS, VSk=VSk, VW=VW, VWk=VWk, Q=Q, Qk=Qk, Nk=Nk, oT=oT, oTk=oTk, ob=ob, PT=PT,
                                   gat=gat, gatk=gatk, rd=rd, rdk=rdk, s3=s3, s3k=s3k, sc=sc, sck=sck, sc2=sc2, sc2k=sc2k, nsl=nsl, nslk=nslk,
                                   m8=m8, m8k=m8k, acc=acc, acck=acck, pool_st=pool_st))
            for mm in range(NCH):
                w = self.next_w(("bo", j, mm))
                po = self.bank(pool_st)
                for k in range(NCH):
                    P.op("pe", lambda e: e.matmul(po[:], lhsT=w[:, k * 128:(k + 1) * 128], rhs=oT[:, k, :], start=(k == 0), stop=(k == NCH - 1)), r=[w.k, oTk], w=[po.k])
                hsl = self.hT[:, mm, tsl]
                P.op("dve", lambda e: e.tensor_tensor(out=hsl, in0=po[:], in1=hsl, op=ALU.add), r=[po.k, self.hk[mm][n]], w=[self.hk[mm][n]])
        self.pool_misc = [0, 1, 2, 3, 4, 5, 6, 7]

    def attn_tile(self, n, C):
        P = self.P
        KS, KSk, KW, KWk, VS, VSk, VW, VWk = C["KS"], C["KSk"], C["KW"], C["KWk"], C["VS"], C["VSk"], C["VW"], C["VWk"]
        Q, Qk, Nk, oT, oTk, ob, PT = C["Q"], C["Qk"], C["Nk"], C["oT"], C["oTk"], C["ob"], C["PT"]
        gat, gatk, rd, rdk, s3, s3k = C["gat"], C["gatk"], C["rd"], C["rdk"], C["s3"], C["s3k"]
        sc, sck, sc2, sc2k, nsl, nslk, m8, m8k, acc, acck = C["sc"], C["sck"], C["sc2"], C["sc2k"], C["nsl"], C["nslk"], C["m8"], C["m8k"], C["acc"], C["acck"]
        pool_st = C["pool_st"]
        poc, pos, pow_ = self.ps[3], self.ps[4], self.ps[5]
        st = {"npt": 0}
        pairs = [(qb, g) for qb in range(4) for g in range(4)]
        jobs = []

        def r4(ap):
            return ap.rearrange("p (a b) -> p a b", a=4)

        def mk_job(kind, qb, g, kt, first, last, pidx):
            qbg = 4 * n + qb
            job = {"pre": [], "post": []}
            box = {}

            def st1():
                pst = self.bank(pool_st)
                pt, ptk = PT[st["npt"] % 4]
                st["npt"] += 1
                box["pt"], box["ptk"] = pt, ptk
                Qsl = Q[0:64, qb, 4 * g:4 * g + 4, :]
                if kind == "c":
                    P.op("pe", lambda e: e.matmul(pst[0:127, :], lhsT=self.KC[0:64, g, 0:127], rhs=Qsl, start=True, stop=True), r=[self.KC.k, Qk[qb]], w=[pst.k])
                    P.op("act", lambda e: e.activation(out=pt[0:127, :], in_=pst[0:127, :], func=AF.Exp, scale=SCALE), r=[pst.k], w=[ptk])
                    cm = self.cmask_b[0:127, qbg * 128:(qbg + 1) * 128].unsqueeze(1).broadcast_to([127, 4, 128])
                    P.op("pool", lambda e: e.tensor_tensor(out=r4(pt[0:127, :]), in0=r4(pt[0:127, :]), in1=cm, op=ALU.mult), r=[ptk, self.cmask_b.k], w=[ptk])
                    return
                if kind == "s":
                    P.op("pe", lambda e: e.matmul(pst[:], lhsT=KS[0:96, g, kt * 128:(kt + 1) * 128], rhs=Q[0:96, qb, 4 * g:4 * g + 4, :], start=True, stop=True),
                         r=[KSk[g], Qk[qb], Nk[qb][g]], w=[pst.k])
                else:
                    P.op("pe", lambda e: e.matmul(pst[:], lhsT=KW[0:64, g, kt * 128:(kt + 1) * 128], rhs=Qsl, start=True, stop=True), r=[KWk[g], Qk[qb]], w=[pst.k])
                P.op("act", lambda e: e.activation(out=pt, in_=pst[:], func=AF.Exp, scale=SCALE), r=[pst.k], w=[ptk])
                msk = None
                if kt == qbg:
                    msk = self.triL_b
                elif kind == "w" and kt == qbg - 4:
                    msk = self.triU_b
                if msk is not None:
                    tm = msk[:].unsqueeze(1).broadcast_to([128, 4, 128])
                    P.op("pool", lambda e: e.tensor_tensor(out=r4(pt), in0=r4(pt), in1=tm, op=ALU.mult), r=[ptk, msk.k], w=[ptk])

            def st2():
                pt, ptk = box["pt"], box["ptk"]
                for h in range(4):
                    if kind == "c":
                        P.op("pe", lambda e: e.matmul(poc[:, h * 128:h * 128 + 97], lhsT=pt[0:127, h * 128:(h + 1) * 128], rhs=self.VC[0:127, g, 0:97],
                                                      start=(h == 0), stop=(h == 3), skip_group_check=True), r=[ptk, self.VC.k], w=[poc.k])
                    elif kind == "s":
                        P.op("pe", lambda e: e.matmul(pos[:, h * 128:h * 128 + 65], lhsT=pt[:, h * 128:(h + 1) * 128], rhs=VS[:, kt, g, 0:65],
                                                      start=(first and h == 0), stop=(last and h == 3), skip_group_check=True), r=[ptk, VSk], w=[pos.k])
                    else:
                        P.op("pe", lambda e: e.matmul(pow_[:, h * 128:h * 128 + 65], lhsT=pt[:, h * 128:(h + 1) * 128], rhs=VW[:, kt, g, 0:65],
                                                      start=(first and h == 0), stop=(last and h == 3), skip_group_check=True), r=[ptk, VWk], w=[pow_.k])

            job["st1"], job["st2"] = st1, st2
            return job

        def select_chain(qb, g, pidx):
            qbg = 4 * n + qb
            pcs = self.pocs[pidx % 2]
            P.op("act", lambda e: e.activation(out=pcs[:], in_=r4(poc[:])[:, :, 0:97], func=AF.Copy), r=[poc.k], w=[pcs.k])
            P.op("dve", lambda e: e.tensor_scalar(out=rd[:, 0:4], in0=pcs[:, :, 64], scalar1=1e-30, scalar2=None, op0=ALU.max), r=[pcs.k], w=[rdk])
            P.op("dve", lambda e: e.reciprocal(out=rd[:, 0:4], in_=rd[:, 0:4]), r=[rdk], w=[rdk])
            for h in range(4):
                src1 = self.selb[:, qbg, :] if h == 0 else sc
                P.op("dve", lambda e: e.scalar_tensor_tensor(out=sc, in0=pcs[:, h, 65:97], scalar=rd[:, h:h + 1], in1=src1, op0=ALU.mult, op1=ALU.add),
                     r=[pcs.k, rdk, sck, self.selb.k], w=[sck])
            P.op("dve", lambda e: e.max(out=m8[:, 0:8], in_=sc), r=[sck], w=[m8k])
            P.op("dve", lambda e: e.match_replace(out=sc2, in_to_replace=m8[:, 0:8], in_values=sc, imm_value=-3.0e38), r=[sck, m8k], w=[sc2k])
            P.op("dve", lambda e: e.max(out=m8[:, 8:16], in_=sc2), r=[sc2k], w=[m8k])
            P.op("dve", lambda e: e.tensor_scalar(out=nsl, in0=sc, scalar1=m8[:, 15:16], scalar2=NEGM, op0=ALU.is_lt, op1=ALU.mult), r=[sck, m8k], w=[nslk])

        def nsel_install(qb, g):
            pm = self.bank(self.pool_misc)
            P.op("pe", lambda e: e.transpose(out=pm[0:32, 0:128], in_=nsl, identity=self.ident_f[:]), r=[nslk, self.ident_f.k], w=[pm.k])
            P.op("act", lambda e: e.activation(out=Q[64:96, qb, 4 * g:4 * g + 4, :], in_=pm[0:32, 0:128].unsqueeze(1).broadcast_to([32, 4, 128]), func=AF.Copy),
                 r=[pm.k], w=[Nk[qb][g]])

        def evac_win():
            P.op("act", lambda e: e.activation(out=self.pows[:], in_=r4(pow_[:])[:, :, 0:65], func=AF.Copy), r=[pow_.k], w=[self.pows.k])

        def combine(qb, g, pidx):
            pcs = self.pocs[pidx % 2]
            o, ok_ = ob[qb % 2]
            P.op("dve", lambda e: e.tensor_scalar(out=rd[:, 4:8], in0=pcs[:, :, 64], scalar1=1e-30, scalar2=None, op0=ALU.max), r=[pcs.k], w=[rdk])
            P.op("dve", lambda e: e.tensor_scalar(out=rd[:, 8:12], in0=r4(pos[:])[:, :, 64], scalar1=1e-30, scalar2=None, op0=ALU.max), r=[pos.k], w=[rdk])
            P.op("dve", lambda e: e.tensor_scalar(out=rd[:, 12:16], in0=self.pows[:, :, 64], scalar1=1e-30, scalar2=None, op0=ALU.max), r=[self.pows.k], w=[rdk])
            P.op("dve", lambda e: e.reciprocal(out=rd[:, 4:16], in_=rd[:, 4:16]), r=[rdk], w=[rdk])
            P.op("dve", lambda e: e.tensor_tensor(out=s3.rearrange("p (a b) -> p a b", a=3), in0=rd[:, 4:16].rearrange("p (a b) -> p a b", a=3),
                                                  in1=gat[:, qb, :].rearrange("p (a b) -> p a b", a=3)[:, :, 4 * g:4 * g + 4], op=ALU.mult), r=[rdk, gatk], w=[s3k])
            for h in range(4):
                ah = acc[:, h * 64:(h + 1) * 64]
                P.op("dve", lambda e: e.tensor_scalar(out=ah, in0=pcs[:, h, 0:64], scalar1=s3[:, h:h + 1], scalar2=None, op0=ALU.mult), r=[pcs.k, s3k], w=[acck])
                P.op("dve", lambda e: e.scalar_tensor_tensor(out=ah, in0=self.pows[:, h, 0:64], scalar=s3[:, 8 + h:9 + h], in1=ah, op0=ALU.mult, op1=ALU.add),
                     r=[self.pows.k, s3k, acck], w=[acck])
                P.op("dve", lambda e: e.scalar_tensor_tensor(out=o[:, g * 256 + h * 64:g * 256 + (h + 1) * 64], in0=pos[:, h * 128:h * 128 + 64], scalar=s3[:, 4 + h:5 + h], in1=ah,
                                                             op0=ALU.mult, op1=ALU.add), r=[pos.k, s3k, acck], w=[ok_])

        def o_transpose(qb):
            o, ok_ = ob[qb % 2]
            pT = self.bank(self.pool_misc)
            pTb = pT[:].bitcast(BF16)
            for c in range(NCH):
                P.op("pe", lambda e: e.transpose(out=pTb[:, c * 128:(c + 1) * 128], in_=o[:, c * 128:(c + 1) * 128], identity=self.ident_b[:]), r=[ok_, self.ident_b.k], w=[pT.k])
            P.op("act", lambda e: e.activation(out=oT[:, :, qb * 128:(qb + 1) * 128], in_=pTb.rearrange("p (a b) -> p a b", a=8), func=AF.Copy), r=[pT.k], w=[oTk])

        def cjob(pidx):
            qb, g = pairs[pidx]
            j = mk_job("c", qb, g, 0, True, True, pidx)
            j["post"].append(lambda: select_chain(qb, g, pidx))
            return j

        jobs.append(cjob(0))
        pending_T = []
        for pidx, (qb, g) in enumerate(pairs):
            qbg = 4 * n + qb
            k0 = max(0, qbg - 4)
            wj = [mk_job("w", qb, g, kt, kt == k0, kt == qbg, pidx) for kt in range(k0, qbg + 1)]
            for f in pending_T:
                wj[min(2, len(wj) - 1)]["pre"].append(f)
            pending_T = []
            wj[-1]["post"].append(evac_win)
            jobs += wj
            if pidx + 1 < len(pairs):
                jobs.append(cjob(pidx + 1))
            sj = [mk_job("s", qb, g, kt, kt == 0, kt == qbg, pidx) for kt in range(qbg + 1)]
            sj[0]["pre"].append(lambda qb=qb, g=g: nsel_install(qb, g))
            sj[-1]["post"].append(lambda qb=qb, g=g, pidx=pidx: combine(qb, g, pidx))
            jobs += sj
            if g == 3:
                pending_T.append(lambda qb=qb: o_transpose(qb))
        for f in jobs[0]["pre"]:
            f()
        jobs[0]["st1"]()
        for i in range(len(jobs)):
            if i + 1 < len(jobs):
                for f in jobs[i + 1]["pre"]:
                    f()
                jobs[i + 1]["st1"]()
            jobs[i]["st2"]()
            for f in jobs[i]["post"]:
                f()
        for f in pending_T:
            f()

    def make_plan(self):
        plan = []
        ser = -1
        for b in range(self.nb):
            for layer in range(self.n_layers):
                if layer < 2:
                    ser += 1
                    plan += [(("a_in", layer, c), ser) for c in range(8)]
                    plan += [(("a_out", layer, m), ser) for m in range(8)]
                else:
                    if layer == 2:
                        ser += 1
                        plan += [(("kvk", i), ser) for i in range(4)] + [(("kvc", i), ser) for i in range(4)] + [(("kvv", i), ser) for i in range(2)]
                        plan += [(("cw2",), ser)]
                    ser += 1
                    for n in range(NT):
                        plan += [(("bg", layer - 2), ser)] + [(("bq", layer - 2, m), ser) for m in range(8)] + [(("bo", layer - 2, m), ser) for m in range(8)]
                ser += 1
                for t2 in range(2):
                    plan += [(("f_in", layer, c), ser) for c in range(FC)]
                    plan += [(("f_out", layer, m), ser) for m in range(8)]
        return plan

    def build(self):
        P = self.P
        self.convc = self.sb("convc", [128, 8, 3], F32)
        self.hst = self.sb("hst", [128, 8], F32)
        self.prologue()
        self.plan_weights(self.make_plan())
        for b in range(self.nb):
            self.mark("load%d" % b)
            self.load_x(b)
            for layer in range(self.n_layers):
                if layer < 2:
                    self.mark("a%d.%d" % (b, layer))
                    self.a_layer(layer)
                else:
                    if layer == 2:
                        self.mark("kv%d" % b)
                        self.kv_phase()
                    self.mark("b%d.%d" % (b, layer))
                    self.b_layer(layer - 2)
                self.mark("f%d.%d" % (b, layer))
                self.ffn_layer(layer)
            self.mark("store%d" % b)
            self.store_y(b)
        self.mark("end")
        P.wait_all("sp", self.out_trks)
        return self.nc


_CACHE = {}


def _prep_inputs(inputs):
    L = weight_layout()
    wall = pack_weights(inputs, L)
    vecs, bvecs = pack_vecs(inputs)
    cst = make_consts()
    return wall, vecs, bvecs, cst


def kernel(**inputs):
    inputs = {k: np.asarray(v) for k, v in inputs.items()}
    x = np.ascontiguousarray(inputs["x"], dtype=np.float32)
    B = x.shape[0]
    nb = B // NCORES
    wall, vecs, bvecs, cst = _prep_inputs(inputs)
    nc = Builder(nb).build()
    in_maps = []
    for c in range(NCORES):
        in_maps.append({"x": np.ascontiguousarray(x[c * nb:(c + 1) * nb]), "wall": wall, "vecs": vecs, "bvecs": bvecs, "cst": cst})
    res = run_bass_kernel_spmd(nc, in_maps, core_ids=list(range(NCORES)))
    out = np.concatenate([np.asarray(r["y"]).reshape(nb, S, D) for r in res.results], axis=0)
    return out.astype(np.float32)
```

```python
import numpy as np
import concourse.bass as bass
import concourse.mybir as mybir
from concourse.bass_utils import run_bass_kernel_spmd
from concourse.alu_op_type import AluOpType as ALU

F32 = mybir.dt.float32
BF16 = mybir.dt.bfloat16
AF = mybir.ActivationFunctionType

S = 2048
D = 1024
NCH = 8
TT = 512
NT = S // TT
FH = 2816
FC = 22
NCORES = 8
EPS = 1e-6
CONV_CH = 128 * 2048
SCALE = 0.125
NEGM = -30000.0


def _kt(W, c0, width=128):
    K = W.shape[0]
    return np.ascontiguousarray(W[:, c0:c0 + width].reshape(K // 128, 128, width).transpose(1, 0, 2)).reshape(128, -1)


class WLayout:
    def __init__(self):
        self.off = {}
        self.free = {}
        self.n = 0

    def add(self, name, free):
        self.off[name] = self.n
        self.free[name] = free
        self.n += 128 * free

    def total(self):
        return ((self.n + CONV_CH - 1) // CONV_CH) * CONV_CH


def weight_layout():
    L = WLayout()
    for l in range(2):
        for c in range(8):
            L.add(("a_in", l, c), 2304)
        for m in range(8):
            L.add(("a_out", l, m), 1024)
    for l in range(4):
        for c in range(FC):
            L.add(("f_in", l, c), 2048)
        for m in range(8):
            L.add(("f_out", l, m), FH)
    for i in range(4):
        L.add(("kvk", i), 1024)
    for i in range(4):
        L.add(("kvc", i), 1024)
    for i in range(2):
        L.add(("kvv", i), 2048)
    for i in range(2):
        L.add(("cw1", i), 8192)
    L.add(("cw2",), 256)
    for j in range(2):
        L.add(("bg", j), 384)
        for m in range(8):
            L.add(("bq", j, m), 1024)
        for m in range(8):
            L.add(("bo", j, m), 1024)
    return L


def pack_weights(inp, L):
    out = np.zeros(L.total(), np.float32)

    def put(name, arr):
        arr = np.asarray(arr, np.float32).reshape(128, -1)
        assert arr.shape[1] == L.free[name], (name, arr.shape)
        out[L.off[name]:L.off[name] + arr.size] = arr.reshape(-1)

    for l in range(2):
        Win = inp["a_w_in"][l]
        for c in range(8):
            put(("a_in", l, c), np.concatenate(
                [_kt(Win, c * 128), _kt(Win, 1024 + c * 128), inp["a_gate_w"][l][0, c], inp["a_gate_w"][l][1, c]], axis=1))
        for m in range(8):
            put(("a_out", l, m), _kt(inp["a_w_out"][l], m * 128))
    for l in range(4):
        W = inp["f_w_in"][l]
        for c in range(FC):
            put(("f_in", l, c), np.concatenate([_kt(W, c * 128), _kt(W, FH + c * 128)], axis=1))
        for m in range(8):
            put(("f_out", l, m), _kt(inp["f_w_out"][l], m * 128))
    kvw = inp["kv_w"]
    i = 0
    for jj in (2, 4):
        for cc in range(2):
            put(("kvk", i), _kt(kvw, jj * 256 + cc * 128))
            i += 1
    i = 0
    for jj in (0, 1):
        for cc in range(2):
            put(("kvc", i), _kt(kvw, jj * 256 + cc * 128))
            i += 1
    for i, jj in enumerate((3, 5)):
        put(("kvv", i), _kt(kvw, jj * 256, 256))
    for kv in range(2):
        t = np.zeros((128, 32, 256), np.float32)
        t[:64] = inp["cmp_w1"][kv].reshape(32, 64, 256).transpose(1, 0, 2)
        put(("cw1", kv), t)
    t = np.zeros((128, 2, 2, 64), np.float32)
    for kv in range(2):
        t[:, kv] = inp["cmp_w2"][kv].reshape(2, 128, 64).transpose(1, 0, 2)
    put(("cw2",), t)
    for j in range(2):
        put(("bg", j), _kt(inp["b_w_in"][j], 1024, 48))
        for m in range(8):
            put(("bq", j, m), _kt(inp["b_w_in"][j], m * 128))
        for m in range(8):
            put(("bo", j, m), _kt(inp["b_w_out"][j], m * 128))
    return out


VEC = {}
_nv = 0


def _vadd(name, n):
    global _nv
    VEC[name] = _nv
    _nv += n


for _l in range(2):
    _vadd(("a_norm", _l), 8)
    for _k in range(4):
        _vadd(("a_cw", _l, _k), 8)
    _vadd(("a_cb", _l), 8)
    _vadd(("a_gb", _l, 0), 8)
    _vadd(("a_gb", _l, 1), 8)
    _vadd(("a_lam", _l), 8)
for _l in range(4):
    _vadd(("f_norm", _l), 8)
_vadd(("kv_norm",), 8)
for _j in range(2):
    _vadd(("b_norm", _j), 8)
    _vadd(("q_norm", _j), 1)
for _i in range(3):
    _vadd(("k_norm", _i), 1)
for _i in range(2):
    _vadd(("c_b1", _i), 2)
    _vadd(("c_b2", _i), 1)
    _vadd(("c_pos", _i), 32)
NV = _nv
BV_GB = 0
BV_CB2V = 96
NBV = 160


def pack_vecs(inp):
    v = np.zeros((128, NV), np.float32)

    def fm(x):
        return np.asarray(x, np.float32).reshape(8, 128).T

    for l in range(2):
        v[:, VEC[("a_norm", l)]:][:, :8] = fm(inp["a_norm"][l])
        for k in range(4):
            v[:, VEC[("a_cw", l, k)]:][:, :8] = fm(inp["a_conv_w"][l][k])
        v[:, VEC[("a_cb", l)]:][:, :8] = fm(inp["a_conv_b"][l])
        v[:, VEC[("a_gb", l, 0)]:][:, :8] = fm(inp["a_gate_b"][l][0])
        v[:, VEC[("a_gb", l, 1)]:][:, :8] = fm(inp["a_gate_b"][l][1])
        v[:, VEC[("a_lam", l)]:][:, :8] = fm(inp["a_lambda"][l])
    for l in range(4):
        v[:, VEC[("f_norm", l)]:][:, :8] = fm(inp["f_norm"][l])
    v[:, VEC[("kv_norm",)]:][:, :8] = fm(inp["kv_norm"])
    for j in range(2):
        v[:, VEC[("b_norm", j)]:][:, :8] = fm(inp["b_norm"][j])
        v[:, VEC[("q_norm", j)]] = np.tile(np.asarray(inp["q_norm"][j], np.float32), 2)
    for i in range(3):
        v[:, VEC[("k_norm", i)]] = np.tile(np.asarray(inp["k_norm"][i], np.float32), 2)
    for i in range(2):
        v[:, VEC[("c_b1", i)]:][:, :2] = np.asarray(inp["cmp_b1"][i], np.float32).reshape(2, 128).T
        v[:, VEC[("c_b2", i)]] = np.tile(np.asarray(inp["cmp_b2"][i], np.float32), 2)
        v[:64, VEC[("c_pos", i)]:][:, :32] = np.asarray(inp["cmp_pos"][i], np.float32).T
    bv = np.zeros((128, NBV), np.float32)
    for j in range(2):
        bv[:, BV_GB + 48 * j: BV_GB + 48 * j + 48] = np.asarray(inp["b_gate_b"][j], np.float32)[None, :]
    bv[:, BV_CB2V:BV_CB2V + 64] = np.asarray(inp["cmp_b2"][1], np.float32)[None, :]
    return v, bv


CST = {}
_nc_ = 0


def _cadd(name, n):
    global _nc_
    CST[name] = _nc_
    _nc_ += n


_cadd("ident", 128)
_cadd("prot", 128)
_cadd("bones", 128)
_cadd("triL", 128)
_cadd("triU", 128)
_cadd("ovl", 32)
_cadd("cos", 2048)
_cadd("sin", 2048)
_cadd("cmask", 2048)
_cadd("eind", 2048)
_cadd("selb", 512)
NCST = _nc_


def make_consts():
    c = np.zeros((128, NCST), np.float32)
    p = np.arange(128)
    c[:, CST["ident"]:][:, :128] = np.eye(128)
    perm = (p // 64) * 64 + ((p % 64) + 32) % 64
    pr = np.zeros((128, 128), np.float32)
    pr[perm, p] = 1.0
    c[:, CST["prot"]:][:, :128] = pr
    c[:, CST["bones"]:][:, :128] = (p[:, None] // 64 == p[None, :] // 64)
    c[:, CST["triL"]:][:, :128] = (p[:, None] <= p[None, :])
    c[:, CST["triU"]:][:, :128] = (p[:, None] > p[None, :])
    cs = np.arange(127) * 16
    sl = np.arange(32) * 64
    ov = np.clip(np.minimum(cs[:, None] + 32, sl[None, :] + 64) - np.maximum(cs[:, None], sl[None, :]), 0, None) / 32.0
    c[:127, CST["ovl"]:][:, :32] = ov
    t = np.arange(2048, dtype=np.float64)
    fi = (p % 64) % 32
    freqs = (10000.0 ** (-(np.arange(32, dtype=np.float32) / np.float32(32)))).astype(np.float32)
    ang = (t[None, :].astype(np.float32) * freqs[fi][:, None]).astype(np.float32)
    c[:, CST["cos"]:][:, :2048] = np.cos(ang)
    sg = np.where((p % 64) < 32, -1.0, 1.0)[:, None]
    c[:, CST["sin"]:][:, :2048] = np.sin(ang) * sg
    cl = np.arange(127) * 16 + 31
    c[:127, CST["cmask"]:][:, :2048] = (cl[:, None] <= t[None, :])
    b = np.arange(32)
    c[64:96, CST["eind"]:][:, :2048] = (t[None, :].astype(np.int64) // 64 == b[:, None])
    sb = np.zeros((128, 16, 32), np.float32)
    for qb in range(16):
        tq = qb * 128 + p
        cur = (tq // 64)[:, None]
        causal = b[None, :] <= cur
        forced = (b[None, :] == 0) | (causal & (cur - b[None, :] < 2))
        sb[:, qb, :] = np.where(forced, 1e30, np.where(causal, 0.0, -1e30))
    c[:, CST["selb"]:][:, :512] = sb.reshape(128, 512)
    return c


class Trk:
    __slots__ = ("w", "r")

    def __init__(self):
        self.w = None
        self.r = {}


NDS = 24


class Prog:
    def __init__(self, nc):
        self.nc = nc
        self.eng = {"pe": nc.tensor, "act": nc.scalar, "dve": nc.vector, "pool": nc.gpsimd, "sp": nc.sync}
        self.sem = {e: nc.alloc_semaphore(name="s_" + e) for e in ("pe", "act", "dve", "pool")}
        self.cnt = {e: 0 for e in ("pe", "act", "dve", "pool")}
        self.seen = {e: {} for e in self.eng}
        self.dsem = [nc.alloc_semaphore(name="s_dma%d" % i) for i in range(NDS)]
        self.dn = 0
        self.ninst = 0

    def _semof(self, key):
        return self.sem[key] if isinstance(key, str) else self.dsem[key[1]]

    def _wait(self, e, key, val):
        if key == "pe" and e == "pe":
            return
        if self.seen[e].get(key, 0) >= val:
            return
        self.eng[e].wait_ge(self._semof(key), val)
        self.seen[e][key] = val

    def _deps(self, e, r, w):
        for t in r:
            if t.w is not None:
                self._wait(e, t.w[0], t.w[1])
        for t in w:
            if t.w is not None:
                self._wait(e, t.w[0], t.w[1])
            for k, v in t.r.items():
                self._wait(e, k, v)

    def op(self, e, fn, r=(), w=()):
        self._deps(e, r, w)
        inst = fn(self.eng[e])
        self.cnt[e] += 1
        self.ninst += 1
        inst.then_inc(self.sem[e], 1)
        v = self.cnt[e]
        for t in r:
            t.r[e] = v
        for t in w:
            t.w = (e, v)
            t.r = {}
        return inst

    def dma(self, q, out, in_, r=(), w=()):
        self._deps(q, r, w)
        i = self.dn % NDS
        gen = self.dn // NDS
        key = ("d", i)
        if gen > 0:
            self._wait(q, key, 16 * gen)
        inst = self.eng[q].dma_start(out=out, in_=in_)
        inst.then_inc(self.dsem[i], 16)
        self.dn += 1
        self.ninst += 1
        v = 16 * (gen + 1)
        for t in r:
            t.r[key] = v
        for t in w:
            t.w = (key, v)
            t.r = {}
        return (key, v)

    def wait_all(self, e, trks):
        self._deps(e, trks, trks)


class Buf:
    def __init__(self, t):
        self.t = t
        self.k = Trk()

    def __getitem__(self, idx):
        return self.t[idx]


class Builder:
    def __init__(self, nb, n_layers=4, debug_out=None):
        self.nb = nb
        self.n_layers = n_layers
        self.L = weight_layout()
        nc = bass.Bass("TRN2", target_bir_lowering=False)
        self.nc = nc
        self.P = Prog(nc)
        NW = self.L.total()
        self.x = nc.dram_tensor("x", [nb, S, D], F32, kind="ExternalInput").ap()
        self.wall = nc.dram_tensor("wall", [NW], F32, kind="ExternalInput").ap()
        self.vecs_d = nc.dram_tensor("vecs", [128, NV], F32, kind="ExternalInput").ap()
        self.bvecs_d = nc.dram_tensor("bvecs", [128, NBV], F32, kind="ExternalInput").ap()
        self.cst_d = nc.dram_tensor("cst", [128, NCST], F32, kind="ExternalInput").ap()
        self.y = nc.dram_tensor("y", [nb, S, D], F32, kind="ExternalOutput").ap()
        self.wbf = nc.dram_tensor("wbf", [NW], BF16, kind="Internal").ap()
        self.wchunk = [Trk() for _ in range(NW // CONV_CH)]
        self.out_trks = []
        self.marks = []

        def sb(name, shape, dt):
            return Buf(nc.alloc_sbuf_tensor(name, shape, dt))

        self.sb = sb
        self.hT = nc.alloc_sbuf_tensor("hT", [128, NCH, S], F32)
        self.hk = [[Trk() for _ in range(NT)] for _ in range(NCH)]
        self.vecs = sb("vecs_sb", [128, NV], F32)
        self.bvecs = sb("bvecs_sb", [128, NBV], F32)
        self.ident_f = sb("ident_f", [128, 128], F32)
        self.ident_b = sb("ident_b", [128, 128], BF16)
        self.ones_b = sb("ones_b", [128, 128], BF16)
        self.coef = sb("coef", [128, 2, 2, 8], F32)
        self.cos_b = sb("cos_b", [128, S], BF16)
        self.sin_b = sb("sin_b", [128, S], BF16)
        self.cmask_b = sb("cmask_b", [128, S], BF16)
        self.triL_b = sb("triL_b", [128, 128], BF16)
        self.triU_b = sb("triU_b", [128, 128], BF16)
        self.prot_b = sb("prot_b", [128, 128], BF16)
        self.bones_b = sb("bones_b", [128, 128], BF16)
        self.selb = sb("selb", [128, 16, 32], F32)
        self.KC = sb("KC", [128, 4, 128], BF16)
        self.VC = sb("VC", [128, 4, 98], BF16)
        self.posb = sb("posb", [128, 2, 32], BF16)
        self.cbias = sb("cbias", [128, 2, 2], F32)
        self.pocs = [sb("pocs%d" % i, [128, 4, 97], F32) for i in range(2)]
        self.pows = sb("pows", [128, 4, 65], F32)
        self.KSd = nc.dram_tensor("KSd", [2, 64, 4, S], BF16, kind="Internal").ap()
        self.Vd = nc.dram_tensor("Vd", [2, 128, 16, 4, 66], BF16, kind="Internal").ap()
        self.ksd_k = [[[Trk() for _ in range(NT)] for _ in range(4)] for _ in range(2)]
        self.vd_k = [[Trk() for _ in range(NT)] for _ in range(2)]
        self.ps = [Buf(nc.alloc_psum_tensor("ps%d" % i, [128, 512], F32)) for i in range(8)]
        self.ps_rr = 0
        self.pool_misc = [0, 1, 2, 3, 4, 5, 6, 7]
        self.NSLOT = 3
        self.ring = [sb("wring%d" % i, [128, FH], BF16) for i in range(self.NSLOT)]
        self.perm_ids = set(id(b) for b in self.ring)
        self.PHB = 100 * 1024
        self.ph = nc.alloc_sbuf_tensor("phase", [128, self.PHB // 4], F32)
        self.phk = {}

    def bank(self, pool=None):
        if pool is None:
            b = self.ps[self.ps_rr % 8]
        else:
            b = self.ps[pool[self.ps_rr % len(pool)]]
        self.ps_rr += 1
        return b

    def vec(self, name, c=0):
        i = VEC[name] + c
        return self.vecs[:, i:i + 1]

    def phase_view(self, byte_off, shape, dt):
        esz = 4 if dt == F32 else 2
        n = int(np.prod(shape[1:]))
        assert byte_off % 4 == 0 and byte_off + n * esz <= self.PHB, (byte_off, shape)
        if dt == F32:
            ap = self.ph[:, byte_off // 4: byte_off // 4 + n]
        else:
            ap = self.ph[:].bitcast(BF16)[:, byte_off // 2: byte_off // 2 + n]
        if len(shape) == 2:
            return ap
        names = " ".join("a%d" % i for i in range(len(shape) - 1))
        kw = {"a%d" % i: shape[i + 1] for i in range(len(shape) - 1)}
        return ap.rearrange("p (%s) -> p %s" % (names, names), **kw)

    def mark(self, name):
        self.marks.append((name, dict(self.P.cnt)))

    def new_phase(self, trks):
        merged = {}
        for t in self.phase_cur:
            if t.w is not None:
                merged[t.w[0]] = max(merged.get(t.w[0], 0), t.w[1])
            for k, v in t.r.items():
                merged[k] = max(merged.get(k, 0), v)
        for t in trks:
            t.w = None
            t.r = dict(merged)
        self.phase_cur = list(trks)

    def plan_weights(self, entries):
        self.wplan = list(entries)
        self.wplan_i = 0
        self.wq = []
        self.free_perm = list(self.ring)
        self.free_ext = []
        self.inuse = None
        self.serial = -1

    def begin_phase(self, ext_slots):
        self._release()
        self.serial += 1
        self.free_ext = list(ext_slots)
        self.ext_ids = set(id(b) for b in ext_slots)

    def _release(self):
        if self.inuse is not None:
            slot, ser = self.inuse
            if id(slot) in self.perm_ids:
                self.free_perm.append(slot)
            elif ser == self.serial:
                self.free_ext.append(slot)
            self.inuse = None

    def _topup(self):
        while self.wplan_i < len(self.wplan):
            name, ser = self.wplan[self.wplan_i]
            if ser == self.serial and self.free_ext:
                slot = self.free_ext.pop(0)
            elif self.free_perm:
                slot = self.free_perm.pop(0)
            else:
                break
            self.wplan_i += 1
            off = self.L.off[name]
            free = self.L.free[name]
            src = self.wbf[off:off + 128 * free].rearrange("(p f) -> p f", p=128)
            c0 = off // CONV_CH
            c1 = (off + 128 * free - 1) // CONV_CH
            self.P.dma("sp", slot[:, 0:free], src, r=[self.wchunk[c] for c in range(c0, c1 + 1)], w=[slot.k])
            self.wq.append((name, slot, ser))

    def next_w(self, name, hold=False):
        self._release()
        self._topup()
        n, slot, ser = self.wq.pop(0)
        assert n == name and ser == self.serial, (n, name, ser, self.serial)
        if not hold:
            self.inuse = (slot, ser)
        return slot

    def release_slot(self, slot):
        if id(slot) in self.perm_ids:
            self.free_perm.append(slot)
        elif id(slot) in self.ext_ids:
            self.free_ext.append(slot)

    def ext_slots(self, byte_off):
        out = []
        o = byte_off
        while o + 2 * FH <= self.PHB:
            b = Buf(self.phase_view(o, [128, FH], BF16))
            out.append(b)
            o += 2 * FH
        return out

    def prologue(self):
        P = self.P
        nc = self.nc
        P.dma("sp", self.vecs[:], self.vecs_d, w=[self.vecs.k])
        P.dma("sp", self.bvecs[:], self.bvecs_d, w=[self.bvecs.k])
        P.dma("sp", self.ident_f[:], self.cst_d[:, CST["ident"]:CST["ident"] + 128], w=[self.ident_f.k])
        P.op("dve", lambda e: e.tensor_copy(out=self.ident_b[:], in_=self.ident_f[:]), r=[self.ident_f.k], w=[self.ident_b.k])
        P.op("pool", lambda e: e.memset(self.ones_b[:], 1.0), w=[self.ones_b.k])
        tmp = self.sb("coef_tmp", [128, 16], F32)
        for l in range(2):
            lam = self.vecs[:, VEC[("a_lam", l)]:VEC[("a_lam", l)] + 8]
            P.op("act", lambda e: e.activation(out=tmp[:, l * 8:l * 8 + 8], in_=lam, func=AF.Exp, scale=-1.0), r=[self.vecs.k], w=[tmp.k])
            P.op("act", lambda e: e.activation(out=tmp[:, l * 8:l * 8 + 8], in_=tmp[:, l * 8:l * 8 + 8], func=AF.Ln, bias=1.0), r=[tmp.k], w=[tmp.k])
            P.op("dve", lambda e: e.tensor_scalar(out=self.coef[:, l, 0, :], in0=tmp[:, l * 8:l * 8 + 8], scalar1=-8.0, scalar2=None, op0=ALU.mult), r=[tmp.k], w=[self.coef.k])
            P.op("dve", lambda e: e.tensor_scalar(out=self.coef[:, l, 1, :], in0=tmp[:, l * 8:l * 8 + 8], scalar1=-16.0, scalar2=None, op0=ALU.mult), r=[tmp.k], w=[self.coef.k])
        stg = self.phase_view(0, [128, 2048], F32)
        stk = Trk()
        self.phase_cur = [stk]

        def ctab(name, n, dst, dstk, eng):
            P.dma("sp", stg[:, 0:n], self.cst_d[:, CST[name]:CST[name] + n], w=[stk])
            if eng == "act":
                P.op("act", lambda e: e.activation(out=dst, in_=stg[:, 0:n], func=AF.Copy), r=[stk], w=[dstk])
            else:
                P.op(eng, lambda e: e.tensor_copy(out=dst, in_=stg[:, 0:n]), r=[stk], w=[dstk])

        ctab("cos", 2048, self.cos_b[:], self.cos_b.k, "dve")
        ctab("sin", 2048, self.sin_b[:], self.sin_b.k, "act")
        ctab("cmask", 2048, self.cmask_b[:], self.cmask_b.k, "dve")
        ctab("triL", 128, self.triL_b[:], self.triL_b.k, "dve")
        ctab("triU", 128, self.triU_b[:], self.triU_b.k, "dve")
        ctab("prot", 128, self.prot_b[:], self.prot_b.k, "dve")
        ctab("bones", 128, self.bones_b[:], self.bones_b.k, "dve")
        P.dma("sp", self.selb[:].rearrange("p a b -> p (a b)"), self.cst_d[:, CST["selb"]:CST["selb"] + 512], w=[self.selb.k])
        P.op("pool", lambda e: e.memset(self.VC[:], 0.0), w=[self.VC.k])
        P.op("pool", lambda e: e.memset(self.KC[:], 0.0), w=[self.KC.k])
        P.op("pool", lambda e: e.memset(self.VC[:, :, 64:65], 1.0), w=[self.VC.k])
        P.dma("sp", stg[:, 0:32], self.cst_d[:, CST["ovl"]:CST["ovl"] + 32], w=[stk])
        for g in range(4):
            P.op("dve", lambda e: e.tensor_copy(out=self.VC[:, g, 65:97], in_=stg[:, 0:32]), r=[stk], w=[self.VC.k])
        for kv in range(2):
            o0 = VEC[("c_pos", kv)]
            P.op("dve", lambda e: e.tensor_copy(out=self.posb[:, kv, :], in_=self.vecs[:, o0:o0 + 32]), r=[self.vecs.k], w=[self.posb.k])
        nst = 3
        stf = [(self.phase_view(i * 12288, [128, 2048], F32), Trk()) for i in range(nst)]
        stb = [(self.phase_view(i * 12288 + 8192, [128, 2048], BF16), Trk()) for i in range(nst)]
        self.new_phase([k for _, k in stf] + [k for _, k in stb])
        nchunk = len(self.wchunk)
        engs = ["dve", "act", "pool"]
        for i in range(nchunk):
            sf, kf = stf[i % nst]
            sbb, kb = stb[i % nst]
            src = self.wall[i * CONV_CH:(i + 1) * CONV_CH].rearrange("(p f) -> p f", p=128)
            dst = self.wbf[i * CONV_CH:(i + 1) * CONV_CH].rearrange("(p f) -> p f", p=128)
            P.dma("sp", sf, src, w=[kf])
            e = engs[i % 3]
            if e == "act":
                P.op("act", lambda en: en.activation(out=sbb, in_=sf, func=AF.Copy), r=[kf], w=[kb])
            else:
                P.op(e, lambda en: en.tensor_copy(out=sbb, in_=sf), r=[kf], w=[kb])
            P.dma("sp", dst, sbb, r=[kb], w=[self.wchunk[i]])
        self.phase_cur = [k for _, k in stf] + [k for _, k in stb] + [stk]

    def load_x(self, b):
        P = self.P
        xin = [(self.phase_view(j * 4096, [128, 1024], F32), Trk()) for j in range(4)]
        self.new_phase([k for _, k in xin])
        for n in range(NT):
            for j in range(4):
                t0 = n * TT + j * 128
                P.dma("sp", xin[j][0], self.x[b, t0:t0 + 128, :], w=[xin[j][1]])
            for c in range(NCH):
                pb = self.bank()
                for j in range(4):
                    P.op("pe", lambda e: e.transpose(out=pb[:, j * 128:(j + 1) * 128], in_=xin[j][0][:, c * 128:(c + 1) * 128], identity=self.ident_f[:]),
                         r=[xin[j][1], self.ident_f.k], w=[pb.k])
                eng = "act" if c % 2 == 0 else "dve"
                dst = self.hT[:, c, n * TT:(n + 1) * TT]
                if eng == "act":
                    P.op("act", lambda e: e.activation(out=dst, in_=pb[:], func=AF.Copy), r=[pb.k], w=[self.hk[c][n]])
                else:
                    P.op("dve", lambda e: e.tensor_copy(out=dst, in_=pb[:]), r=[pb.k], w=[self.hk[c][n]])

    def store_y(self, b):
        P = self.P
        yo = [(self.phase_view(j * 4096, [128, 1024], F32), Trk()) for j in range(4)]
        self.new_phase([k for _, k in yo])
        for n in range(NT):
            for j in range(4):
                t0 = n * TT + j * 128
                for half in range(2):
                    pb = self.bank()
                    for cc in range(4):
                        c = half * 4 + cc
                        P.op("pe", lambda e: e.transpose(out=pb[:, cc * 128:(cc + 1) * 128], in_=self.hT[:, c, t0:t0 + 128], identity=self.ident_f[:]),
                             r=[self.hk[c][n], self.ident_f.k], w=[pb.k])
                    dst = yo[j][0][:, half * 512:(half + 1) * 512]
                    wl = [yo[j][1]]
                    if half == 0:
                        P.op("act", lambda e: e.activation(out=dst, in_=pb[:], func=AF.Copy), r=[pb.k], w=wl)
                    else:
                        P.op("dve", lambda e: e.tensor_copy(out=dst, in_=pb[:]), r=[pb.k], w=wl)
                ot = Trk()
                P.dma("sp", self.y[b, t0:t0 + 128, :], yo[j][0], r=[yo[j][1]], w=[ot])
                self.out_trks.append(ot)

    def norm_tile(self, n, gname, u_ap, u_k, sq_bufs, rstd_buf):
        P = self.P
        pb = self.bank()
        for c in range(NCH):
            sq, sk = sq_bufs[c % len(sq_bufs)]
            hsl = self.hT[:, c, n * TT:(n + 1) * TT]
            P.op("act", lambda e: e.activation(out=sq, in_=hsl, func=AF.Square), r=[self.hk[c][n]], w=[sk])
            P.op("pe", lambda e: e.matmul(pb[:], lhsT=self.ones_b[:], rhs=sq, start=(c == 0), stop=(c == NCH - 1)),
                 r=[sk, self.ones_b.k], w=[pb.k])
        rs, rk = rstd_buf
        P.op("act", lambda e: e.activation(out=rs, in_=pb[:], func=AF.Sqrt, scale=1.0 / D, bias=EPS), r=[pb.k], w=[rk])
        P.op("dve", lambda e: e.reciprocal(out=rs, in_=rs), r=[rk], w=[rk])
        for c in range(NCH):
            hsl = self.hT[:, c, n * TT:(n + 1) * TT]
            g = self.vec(gname, c)
            P.op("dve", lambda e: e.scalar_tensor_tensor(out=u_ap[:, c, :], in0=hsl, scalar=g, in1=rs, op0=ALU.mult, op1=ALU.mult),
                 r=[self.hk[c][n], rk, self.vecs.k], w=[u_k])

    def a_phase_setup(self):
        v = self.phase_view
        A = {}
        A["u"] = [(v(n * 8192, [128, 8, 512], BF16), Trk()) for n in range(NT)]
        A["m"] = [(v(32768 + n * 8192, [128, 8, 512], BF16), Trk()) for n in range(NT)]
        A["sq"] = [(v(65536 + i * 1024, [128, 512], BF16), Trk()) for i in range(2)]
        A["rstd"] = (v(67584, [128, 512], F32), Trk())
        A["xp"] = [(v(69632 + i * 2080, [128, 516], F32), Trk()) for i in range(2)]
        base = 73792
        s1 = []
        for i in range(3):
            o = base + i * 4096
            xr = (v(o + 2048, [128, 512], F32), Trk())
            s1.append({"y": (v(o, [128, 512], BF16), Trk()), "xrb": (v(o + 1024, [128, 512], BF16), Trk()), "xr": xr, "hr": xr})
        s2 = []
        for i in range(2):
            o = base + 12288 + i * 6144
            rr = (v(o, [128, 512], F32), Trk())
            s2.append({"r": rr, "a2": rr, "i": (v(o + 2048, [128, 512], F32), Trk()), "a": (v(o + 4096, [128, 512], F32), Trk())})
        sets = s1 + s2
        A["s1"] = s1
        A["s2"] = s2
        A["sets"] = sets
        trks = [k for _, k in A["u"]] + [k for _, k in A["m"]] + [A["rstd"][1]] + [k for _, k in A["sq"]] + [k for _, k in A["xp"]]
        for st in sets:
            trks += [k for _, k in st.values()]
        self.new_phase(list(dict((id(t), t) for t in trks).values()))
        self.begin_phase([])
        self.A = A

    def a_layer(self, l):
        P = self.P
        self.a_phase_setup()
        A = self.A
        P.op("pool", lambda e: e.memset(self.convc[:], 0.0), w=[self.convc.k])
        P.op("pool", lambda e: e.memset(self.hst[:], 0.0), w=[self.hst.k])
        for n in range(NT):
            self.norm_tile(n, ("a_norm", l), A["u"][n][0], A["u"][n][1], A["sq"], A["rstd"])
        items = [(c, n) for c in range(NCH) for n in range(NT)]
        wslot = {}

        def stage1(it):
            c, n = items[it]
            if n == 0:
                wslot[c] = self.next_w(("a_in", l, c), hold=True)
            w = wslot[c]
            u, uk = A["u"][n]
            pg = self.bank()
            pr = self.bank()
            for k in range(NCH):
                P.op("pe", lambda e: e.matmul(pg[:], lhsT=w[:, k * 128:(k + 1) * 128], rhs=u[:, k, :], start=(k == 0), stop=(k == NCH - 1)),
                     r=[w.k, uk], w=[pg.k])
            for k in range(NCH):
                P.op("pe", lambda e: e.matmul(pr[:], lhsT=w[:, 1024 + k * 128:1024 + (k + 1) * 128], rhs=u[:, k, :], start=(k == 0), stop=(k == NCH - 1)),
                     r=[w.k, uk], w=[pr.k])
            st = A["s1"][it % 3]
            xp, xpk = A["xp"][it % 2]
            y, yk = st["y"]
            xr, xrk = st["xr"]
            xrb, xrbk = st["xrb"]
            P.op("act", lambda e: e.activation(out=y, in_=pg[:], func=AF.Gelu_apprx_tanh), r=[pg.k], w=[yk])
            P.op("pool", lambda e: e.tensor_copy(out=xp[:, 0:3], in_=self.convc[:, c, :]), r=[self.convc.k], w=[xpk])
            P.op("act", lambda e: e.activation(out=xp[:, 3:515], in_=pr[:], func=AF.Copy), r=[pr.k], w=[xpk])
            P.op("dve", lambda e: e.tensor_scalar(out=xr, in0=xp[:, 0:512], scalar1=self.vec(("a_cw", l, 0), c), scalar2=self.vec(("a_cb", l), c),
                                                  op0=ALU.mult, op1=ALU.add), r=[xpk, self.vecs.k], w=[xrk])
            for kk in range(1, 4):
                P.op("dve", lambda e: e.scalar_tensor_tensor(out=xr, in0=xp[:, kk:kk + 512], scalar=self.vec(("a_cw", l, kk), c), in1=xr,
                                                             op0=ALU.mult, op1=ALU.add), r=[xpk, xrk, self.vecs.k], w=[xrk])
            P.op("pool", lambda e: e.tensor_copy(out=self.convc[:, c, :], in_=xp[:, 512:515]), r=[xpk], w=[self.convc.k])
            P.op("pool", lambda e: e.tensor_copy(out=xrb, in_=xr), r=[xrk], w=[xrbk])

        def stage2(it):
            c, n = items[it]
            w = wslot[c]
            m, mk = A["m"][n]
            st = A["s1"][it % 3]
            y, yk = st["y"]
            xr, xrk = st["xr"]
            xrb, xrbk = st["xrb"]
            hr, hrk = st["hr"]
            t2 = A["s2"][it % 2]
            rr, rrk = t2["r"]
            ii, iik = t2["i"]
            aa, aak = t2["a"]
            p1 = self.bank()
            p2 = self.bank()
            P.op("pe", lambda e: e.matmul(p1[:], lhsT=w[:, 2048:2176], rhs=xrb, start=True, stop=True), r=[w.k, xrbk], w=[p1.k])
            P.op("pe", lambda e: e.matmul(p2[:], lhsT=w[:, 2176:2304], rhs=xrb, start=True, stop=True), r=[w.k, xrbk], w=[p2.k])
            if n == NT - 1:
                self.release_slot(w)
            P.op("act", lambda e: e.activation(out=rr, in_=p1[:], func=AF.Sigmoid, bias=self.vec(("a_gb", l, 0), c)), r=[p1.k, self.vecs.k], w=[rrk])
            P.op("act", lambda e: e.activation(out=ii, in_=p2[:], func=AF.Sigmoid, bias=self.vec(("a_gb", l, 1), c)), r=[p2.k, self.vecs.k], w=[iik])
            P.op("act", lambda e: e.activation(out=aa, in_=rr, func=AF.Exp, scale=self.coef[:, l, 0, c:c + 1]), r=[rrk, self.coef.k], w=[aak])
            P.op("act", lambda e: e.activation(out=rr, in_=rr, func=AF.Exp, scale=self.coef[:, l, 1, c:c + 1]), r=[rrk, self.coef.k], w=[rrk])
            P.op("act", lambda e: e.activation(out=rr, in_=rr, func=AF.Sqrt, scale=-1.0, bias=1.0), r=[rrk], w=[rrk])
            P.op("dve", lambda e: e.tensor_tensor(out=ii, in0=ii, in1=xr, op=ALU.mult), r=[iik, xrk], w=[iik])
            P.op("dve", lambda e: e.tensor_tensor(out=ii, in0=ii, in1=rr, op=ALU.mult), r=[iik, rrk], w=[iik])
            P.op("dve", lambda e: e.tensor_tensor_scan(out=hr, data0=aa, data1=ii, initial=self.hst[:, c:c + 1], op0=ALU.mult, op1=ALU.add),
                 r=[aak, iik, self.hst.k], w=[hrk])
            P.op("pool", lambda e: e.tensor_copy(out=self.hst[:, c:c + 1], in_=hr[:, 511:512]), r=[hrk], w=[self.hst.k])
            P.op("pool", lambda e: e.tensor_tensor(out=m[:, c, :], in0=hr, in1=y, op=ALU.mult), r=[hrk, yk], w=[mk])

        LA = 2
        for it in range(min(LA, len(items))):
            stage1(it)
        for it in range(len(items)):
            if it + LA < len(items):
                stage1(it + LA)
            stage2(it)
        for mm in range(NCH):
            w = self.next_w(("a_out", l, mm))
            for n in range(NT):
                m, mk = A["m"][n]
                po = self.bank()
                for k in range(NCH):
                    P.op("pe", lambda e: e.matmul(po[:], lhsT=w[:, k * 128:(k + 1) * 128], rhs=m[:, k, :], start=(k == 0), stop=(k == NCH - 1)),
                         r=[w.k, mk], w=[po.k])
                hsl = self.hT[:, mm, n * TT:(n + 1) * TT]
                P.op("dve", lambda e: e.tensor_tensor(out=hsl, in0=po[:], in1=hsl, op=ALU.add), r=[po.k, self.hk[mm][n]], w=[self.hk[mm][n]])

    def ffn_phase_setup(self):
        v = self.phase_view
        Fz = {}
        Fz["u"] = [(v(s * 8192, [128, 8, 512], BF16), Trk()) for s in range(2)]
        Fz["act"] = [[(v(16384 + (c * 2 + s) * 1024, [128, 512], BF16), Trk()) for s in range(2)] for c in range(FC)]
        Fz["sq"] = [(v(61440 + i * 1024, [128, 512], BF16), Trk()) for i in range(2)]
        Fz["rstd"] = (v(63488, [128, 512], F32), Trk())
        Fz["sg"] = [(v(65536 + i * 1024, [128, 512], BF16), Trk()) for i in range(4)]
        ext = self.ext_slots(69632)
        trks = [k for _, k in Fz["u"]] + [k for row in Fz["act"] for _, k in row] + [k for _, k in Fz["sq"]] + [Fz["rstd"][1]] + [k for _, k in Fz["sg"]] + [b.k for b in ext]
        self.new_phase(trks)
        self.begin_phase(ext)
        self.F = Fz

    def ffn_tile(self, L, t2):
        P = self.P
        Fz = self.F
        for s in range(2):
            self.norm_tile(2 * t2 + s, ("f_norm", L), Fz["u"][s][0], Fz["u"][s][1], Fz["sq"], Fz["rstd"])
        nsg = 0
        for c in range(FC):
            w = self.next_w(("f_in", L, c))
            for s in range(2):
                u, uk = Fz["u"][s]
                pg = self.bank()
                pu = self.bank()
                for k in range(NCH):
                    P.op("pe", lambda e: e.matmul(pg[:], lhsT=w[:, k * 128:(k + 1) * 128], rhs=u[:, k, :], start=(k == 0), stop=(k == NCH - 1)),
                         r=[w.k, uk], w=[pg.k])
                for k in range(NCH):
                    P.op("pe", lambda e: e.matmul(pu[:], lhsT=w[:, 1024 + k * 128:1024 + (k + 1) * 128], rhs=u[:, k, :], start=(k == 0), stop=(k == NCH - 1)),
                         r=[w.k, uk], w=[pu.k])
                sg, sgk = Fz["sg"][nsg % 4]
                nsg += 1
                a, ak = Fz["act"][c][s]
                P.op("act", lambda e: e.activation(out=sg, in_=pg[:], func=AF.Silu), r=[pg.k], w=[sgk])
                P.op("dve", lambda e: e.tensor_tensor(out=a, in0=sg, in1=pu[:], op=ALU.mult), r=[sgk, pu.k], w=[ak])
        for mm in range(NCH):
            w = self.next_w(("f_out", L, mm))
            for s in range(2):
                n = 2 * t2 + s
                po = self.bank()
                for c in range(FC):
                    a, ak = Fz["act"][c][s]
                    P.op("pe", lambda e: e.matmul(po[:], lhsT=w[:, c * 128:(c + 1) * 128], rhs=a, start=(c == 0), stop=(c == FC - 1)),
                         r=[w.k, ak], w=[po.k])
                hsl = self.hT[:, mm, n * TT:(n + 1) * TT]
                P.op("dve", lambda e: e.tensor_tensor(out=hsl, in0=po[:], in1=hsl, op=ALU.add), r=[po.k, self.hk[mm][n]], w=[self.hk[mm][n]])

    def ffn_layer(self, L):
        self.ffn_phase_setup()
        for t2 in range(2):
            self.ffn_tile(L, t2)

    def headnorm_rope(self, pk, R, C, gvec, cos_ap, sin_ap, T, bias=None):
        P = self.P
        sq, sqk = T["sq"]
        rs, rsk = T["rs"]
        qn, qnk = T["qn"]
        t1, t1k = T["t1"]
        t2, t2k = T["t2"]
        if bias is not None:
            xf, xfk = T["xf"]
            P.op("act", lambda e: e.activation(out=xf[0:R, 0:C], in_=pk[0:R, 0:C], func=AF.Identity, bias=bias), r=[pk.k, self.vecs.k], w=[xfk])
            src, srck = xf[0:R, 0:C], xfk
        else:
            src, srck = pk[0:R, 0:C], pk.k
        P.op("act", lambda e: e.activation(out=sq[0:R, 0:C], in_=src, func=AF.Square), r=[srck], w=[sqk])
        pss = self.bank(self.pool_misc)
        P.op("pe", lambda e: e.matmul(pss[0:R, 0:C], lhsT=self.bones_b[0:R, 0:R], rhs=sq[0:R, 0:C], start=True, stop=True), r=[sqk, self.bones_b.k], w=[pss.k])
        P.op("act", lambda e: e.activation(out=rs[0:R, 0:C], in_=pss[0:R, 0:C], func=AF.Sqrt, scale=1.0 / 64.0, bias=EPS), r=[pss.k], w=[rsk])
        P.op("dve", lambda e: e.reciprocal(out=rs[0:R, 0:C], in_=rs[0:R, 0:C]), r=[rsk], w=[rsk])
        P.op("dve", lambda e: e.scalar_tensor_tensor(out=qn[0:R, 0:C], in0=src, scalar=gvec, in1=rs[0:R, 0:C], op0=ALU.mult, op1=ALU.mult),
             r=[srck, rsk, self.vecs.k], w=[qnk])
        prt = self.bank(self.pool_misc)
        P.op("pe", lambda e: e.matmul(prt[0:R, 0:C], lhsT=self.prot_b[0:R, 0:R], rhs=qn[0:R, 0:C], start=True, stop=True), r=[qnk, self.prot_b.k], w=[prt.k])
        P.op("pool", lambda e: e.tensor_tensor(out=t1[0:R, 0:C], in0=qn[0:R, 0:C], in1=cos_ap, op=ALU.mult), r=[qnk, self.cos_b.k], w=[t1k])
        P.op("dve", lambda e: e.tensor_tensor(out=t2[0:R, 0:C], in0=prt[0:R, 0:C], in1=sin_ap, op=ALU.mult), r=[prt.k, self.sin_b.k], w=[t2k])

    def kv_phase(self):
        P = self.P
        v = self.phase_view
        U = [(v(n * 8192, [128, 8, 512], BF16), Trk()) for n in range(NT)]
        sqn = [(v(32768 + i * 1024, [128, 512], BF16), Trk()) for i in range(2)]
        rstd = (v(34816, [128, 512], F32), Trk())
        kcT, kcTk = v(36864, [128, 8, S], BF16), [Trk() for _ in range(8)]
        cw1, cw1k = v(0, [128, 2, 32, 256], BF16), Trk()
        T = {"sq": (v(69632, [128, 512], BF16), Trk()), "rs": (v(70656, [128, 512], F32), Trk()), "qn": (v(72704, [128, 512], BF16), Trk()),
             "t1": (v(73728, [128, 512], F32), Trk()), "t2": (v(75776, [128, 512], F32), Trk()), "xf": (v(77824, [128, 512], F32), Trk())}
        kout = [(v(79872 + i * 1024, [128, 512], BF16), Trk()) for i in range(2)]
        vst = [(v(81920 + i * 528, [128, 4, 66], BF16), Trk()) for i in range(2)]
        hid = [(v(83008 + i * 512, [128, 2, 128], BF16), Trk()) for i in range(2)]
        ext = self.ext_slots(84032)
        trks = [k for _, k in U] + [rstd[1], cw1k] + [k for _, k in sqn] + kcTk + [k for _, k in T.values()] + [k for _, k in kout] + [k for _, k in vst] \
            + [k for _, k in hid] + [b.k for b in ext]
        self.new_phase(trks)
        self.begin_phase(ext)
        self.pool_misc = [0, 1, 2, 3, 4, 5, 6, 7]
        for i in range(2):
            P.op("pool", lambda e: e.memset(vst[i][0][:, :, 64:66], 1.0), w=[vst[i][1]])
        for n in range(NT):
            self.norm_tile(n, ("kv_norm",), U[n][0], U[n][1], sqn, rstd)
        nko = 0
        nvs = 0
        for i in range(4):
            which, cc = i // 2, i % 2
            w = self.next_w(("kvk", i))
            for n in range(NT):
                tsl = slice(n * TT, (n + 1) * TT)
                u, uk = U[n]
                pk = self.bank()
                for k in range(NCH):
                    P.op("pe", lambda e: e.matmul(pk[:], lhsT=w[:, k * 128:(k + 1) * 128], rhs=u[:, k, :], start=(k == 0), stop=(k == NCH - 1)), r=[w.k, uk], w=[pk.k])
                self.headnorm_rope(pk, 128, 512, self.vec(("k_norm", 1 + which)), self.cos_b[:, tsl], self.sin_b[:, tsl], T)
                ko, kok = kout[nko % 2]
                nko += 1
                P.op("dve", lambda e: e.tensor_tensor(out=ko, in0=T["t1"][0], in1=T["t2"][0], op=ALU.add), r=[T["t1"][1], T["t2"][1]], w=[kok])
                for hh in range(2):
                    P.dma("sp", self.KSd[which, :, 2 * cc + hh, tsl], ko[hh * 64:(hh + 1) * 64, :], r=[kok], w=[self.ksd_k[which][2 * cc + hh][n]])
        for i in range(4):
            sel, cc = i // 2, i % 2
            w = self.next_w(("kvc", i))
            for n in range(NT):
                tsl = slice(n * TT, (n + 1) * TT)
                u, uk = U[n]
                for gg in range(2):
                    pc = self.bank()
                    for k in range(NCH):
                        P.op("pe", lambda e: e.matmul(pc[0:64, :], lhsT=w[:, k * 128 + gg * 64:k * 128 + gg * 64 + 64], rhs=u[:, k, :], start=(k == 0), stop=(k == NCH - 1)),
                             r=[w.k, uk], w=[pc.k])
                    idx = sel * 4 + 2 * cc + gg
                    P.op("act", lambda e: e.activation(out=kcT[0:64, idx, tsl], in_=pc[0:64, :], func=AF.Copy), r=[pc.k], w=[kcTk[idx]])
        for i in range(2):
            w = self.next_w(("kvv", i))
            for n in range(NT):
                u, uk = U[n]
                for jb in range(4):
                    pv = self.bank()
                    for k in range(NCH):
                        P.op("pe", lambda e: e.matmul(pv[:, 0:256], lhsT=u[:, k, jb * 128:(jb + 1) * 128], rhs=w[:, k * 256:(k + 1) * 256], start=(k == 0), stop=(k == NCH - 1)),
                             r=[w.k, uk], w=[pv.k])
                    vs, vsk = vst[nvs % 2]
                    nvs += 1
                    P.op("act", lambda e: e.activation(out=vs[:, :, 0:64], in_=pv[:, 0:256].rearrange("p (g d) -> p g d", g=4), func=AF.Copy), r=[pv.k], w=[vsk])
                    P.dma("sp", self.Vd[i, :, 4 * n + jb, :, :], vs, r=[vsk], w=[self.vd_k[i][n]])
        for kv in range(2):
            off = self.L.off[("cw1", kv)]
            src = self.wbf[off:off + 128 * 8192].rearrange("(p f) -> p f", p=128)
            c0, c1 = off // CONV_CH, (off + 128 * 8192 - 1) // CONV_CH
            P.dma("sp", cw1[0:64, kv].rearrange("p a b -> p (a b)"), src[0:64, :], r=[self.wchunk[c] for c in range(c0, c1 + 1)], w=[cw1k] + [k for _, k in U])
        w2 = self.next_w(("cw2",))
        for sel in range(2):
            pcv = self.bank()
            for cc in range(2):
                for l in range(32):
                    P.op("pe", lambda e: e.matmul(pcv[:, cc:cc + 1], lhsT=cw1[0:64, sel, l, cc * 128:(cc + 1) * 128], rhs=self.posb[0:64, sel, l:l + 1],
                                                  start=(cc == 0 and l == 0), stop=(cc == 1 and l == 31)), r=[cw1k, self.posb.k], w=[pcv.k])
            b1o = VEC[("c_b1", sel)]
            P.op("dve", lambda e: e.tensor_tensor(out=self.cbias[:, sel, :], in0=pcv[:, 0:2], in1=self.vecs[:, b1o:b1o + 2], op=ALU.add), r=[pcv.k, self.vecs.k], w=[self.cbias.k])
        nh = 0
        for sel in range(2):
            for g in range(4):
                hd, hdk = hid[nh % 2]
                nh += 1
                for cc in range(2):
                    ph = self.bank()
                    for l in range(32):
                        P.op("pe", lambda e: e.matmul(ph[:, 0:127], lhsT=cw1[0:64, sel, l, cc * 128:(cc + 1) * 128], rhs=kcT[0:64, sel * 4 + g, l:l + 16 * 126 + 1:16],
                                                      start=(l == 0), stop=(l == 31)), r=[cw1k, kcTk[sel * 4 + g]], w=[ph.k])
                    P.op("act", lambda e: e.activation(out=hd[:, cc, 0:127], in_=ph[:, 0:127], func=AF.Gelu_apprx_tanh, bias=self.cbias[:, sel, cc:cc + 1]),
                         r=[ph.k, self.cbias.k], w=[hdk])
                if sel == 0:
                    pk = self.bank()
                    for cc in range(2):
                        P.op("pe", lambda e: e.matmul(pk[0:64, 0:127], lhsT=w2[:, (0 * 2 + cc) * 64:(0 * 2 + cc) * 64 + 64], rhs=hd[:, cc, 0:127], start=(cc == 0), stop=(cc == 1)),
                             r=[w2.k, hdk], w=[pk.k])
                    self.headnorm_rope(pk, 64, 127, self.vecs[0:64, VEC[("k_norm", 0)]:VEC[("k_norm", 0)] + 1],
                                       self.cos_b[0:64, 31:31 + 16 * 126 + 1:16], self.sin_b[0:64, 31:31 + 16 * 126 + 1:16], T,
                                       bias=self.vecs[0:64, VEC[("c_b2", 0)]:VEC[("c_b2", 0)] + 1])
                    P.op("dve", lambda e: e.tensor_tensor(out=self.KC[0:64, g, 0:127], in0=T["t1"][0][0:64, 0:127], in1=T["t2"][0][0:64, 0:127], op=ALU.add),
                         r=[T["t1"][1], T["t2"][1]], w=[self.KC.k])
                else:
                    pv = self.bank()
                    for cc in range(2):
                        P.op("pe", lambda e: e.matmul(pv[0:127, 0:64], lhsT=hd[:, cc, 0:127], rhs=w2[:, (1 * 2 + cc) * 64:(1 * 2 + cc) * 64 + 64], start=(cc == 0), stop=(cc == 1)),
                             r=[w2.k, hdk], w=[pv.k])
                    P.op("dve", lambda e: e.tensor_tensor(out=self.VC[0:127, g, 0:64], in0=pv[0:127, 0:64], in1=self.bvecs[0:127, BV_CB2V:BV_CB2V + 64], op=ALU.add),
                         r=[pv.k, self.bvecs.k], w=[self.VC.k])

    def b_layer(self, j):
        P = self.P
        v = self.phase_view
        KS, KSk = v(0, [128, 4, S], BF16), [Trk() for _ in range(4)]
        KW, KWk = v(16384, [128, 4, S], BF16), [Trk() for _ in range(4)]
        VS, VSk = v(32768, [128, 16, 4, 66], BF16), Trk()
        VW, VWk = v(41216, [128, 16, 4, 66], BF16), Trk()
        u, uk = v(49664, [128, 8, 512], BF16), Trk()
        Q, Qk = v(57856, [128, 4, 16, 128], BF16), [Trk() for _ in range(4)]
        Nk = [[Trk() for _ in range(4)] for _ in range(4)]
        oT, oTk = v(74240, [128, 8, 512], BF16), Trk()
        ob = [(v(82432 + i * 2048, [128, 1024], BF16), Trk()) for i in range(2)]
        PT = [(v(86528 + i * 1024, [128, 512], BF16), Trk()) for i in range(4)]
        sqn = [(v(90624 + i * 1024, [128, 512], BF16), Trk()) for i in range(2)]
        rstd = (v(92672, [128, 512], F32), Trk())
        gat, gatk = v(94720, [128, 4, 48], F32), Trk()
        T = {"sq": sqn[0], "rs": rstd, "qn": (v(95488, [128, 512], BF16), Trk()),
             "t1": (v(96512, [128, 512], F32), Trk()), "t2": (v(98560, [128, 512], F32), Trk())}
        rd, rdk = v(100608, [128, 16], F32), Trk()
        s3, s3k = v(100672, [128, 12], F32), Trk()
        sc, sck = v(100736, [128, 32], F32), Trk()
        sc2, sc2k = v(100864, [128, 32], F32), Trk()
        nsl, nslk = v(100992, [128, 32], F32), Trk()
        m8, m8k = v(101120, [128, 16], F32), Trk()
        acc, acck = v(101184, [128, 256], F32), Trk()
        trks = KSk + KWk + [VSk, VWk, uk, oTk, gatk, rdk, s3k, sck, sc2k, nslk, m8k, acck] + Qk + [k for _, k in ob] + [k for _, k in PT] \
            + [k for _, k in sqn] + [rstd[1]] + [T["qn"][1], T["t1"][1], T["t2"][1]] + [k for row in Nk for k in row]
        self.new_phase(trks)
        self.begin_phase([])
        self.pool_misc = [6, 7]
        pool_st = [0, 1, 2]
        for g in range(4):
            P.dma("sp", KS[0:64, g, :], self.KSd[0, :, g, :], r=self.ksd_k[0][g], w=[KSk[g]])
            P.dma("sp", KW[0:64, g, :], self.KSd[1, :, g, :], r=self.ksd_k[1][g], w=[KWk[g]])
        P.dma("sp", VS.rearrange("p a b c -> p (a b c)"), self.Vd[0].rearrange("p a b c -> p (a b c)"), r=self.vd_k[0], w=[VSk])
        P.dma("sp", VW.rearrange("p a b c -> p (a b c)"), self.Vd[1].rearrange("p a b c -> p (a b c)"), r=self.vd_k[1], w=[VWk])
        est = self.ph[64:96, 57856 // 4:57856 // 4 + 2048]
        allq = Qk + [k for row in Nk for k in row]
        P.dma("sp", est, self.cst_d[64:96, CST["eind"]:CST["eind"] + 2048], w=allq)
        for g in range(4):
            P.op("dve", lambda e: e.tensor_copy(out=KS[64:96, g, :], in_=est), r=allq, w=[KSk[g]])
            P.op("pool", lambda e: e.memset(KW[64:96, g, :], 0.0), w=[KWk[g]])
        P.op("pool", lambda e: e.memset(Q[64:96].rearrange("p a b c -> p (a b c)"), 0.0), w=allq)
        gbo = BV_GB + 48 * j
        npt = 0
        nob = 0
        for n in range(NT):
            tsl = slice(n * TT, (n + 1) * TT)
            self.norm_tile(n, ("b_norm", j), u, uk, sqn, rstd)
            wg = self.next_w(("bg", j))
            for qb in range(4):
                pg = self.bank(self.pool_misc)
                for k in range(NCH):
                    P.op("pe", lambda e: e.matmul(pg[:, 0:48], lhsT=u[:, k, qb * 128:(qb + 1) * 128], rhs=wg[:, k * 48:(k + 1) * 48], start=(k == 0), stop=(k == NCH - 1)),
                         r=[wg.k, uk], w=[pg.k])
                P.op("dve", lambda e: e.tensor_tensor(out=gat[:, qb, :], in0=pg[:, 0:48], in1=self.bvecs[:, gbo:gbo + 48], op=ALU.add), r=[pg.k, self.bvecs.k], w=[gatk])
            P.op("act", lambda e: e.activation(out=gat, in_=gat, func=AF.Sigmoid), r=[gatk], w=[gatk])
            for m in range(NCH):
                w = self.next_w(("bq", j, m))
                pq = self.bank(pool_st)
                for k in range(NCH):
                    P.op("pe", lambda e: e.matmul(pq[:], lhsT=w[:, k * 128:(k + 1) * 128], rhs=u[:, k, :], start=(k == 0), stop=(k == NCH - 1)), r=[w.k, uk], w=[pq.k])
                self.headnorm_rope(pq, 128, 512, self.vec(("q_norm", j)), self.cos_b[:, tsl], self.sin_b[:, tsl], T)
                for hh in range(2):
                    h = 2 * m + hh
                    rsl = slice(hh * 64, (hh + 1) * 64)
                    eng = "dve" if hh == 0 else "pool"
                    P.op(eng, lambda e: e.tensor_tensor(out=Q[0:64, :, h, :], in0=T["t1"][0][rsl, :].rearrange("p (a b) -> p a b", a=4),
                                                        in1=T["t2"][0][rsl, :].rearrange("p (a b) -> p a b", a=4), op=ALU.add),
                         r=[T["t1"][1], T["t2"][1]], w=Qk)
            self.attn_tile(n, dict(KS=KS, KSk=KSk, KW=KW, KWk=KWk, VS=VS, VSk=VSk, VW=VW, VWk=VWk, Q=Q, Qk=Qk, Nk=Nk, oT=oT, oTk=oTk, ob=ob, PT=PT,
                                   gat=gat, gatk=gatk, rd=rd, rdk=rdk, s3=s3, s3k=s3k, sc=sc, sck=sck, sc2=sc2, sc2k=sc2k, nsl=nsl, nslk=nslk,
                                   m8=m8, m8k=m8k, acc=acc, acck=acck, pool_st=pool_st))
            for mm in range(NCH):
                w = self.next_w(("bo", j, mm))
                po = self.bank(pool_st)
                for k in range(NCH):
                    P.op("pe", lambda e: e.matmul(po[:], lhsT=w[:, k * 128:(k + 1) * 128], rhs=oT[:, k, :], start=(k == 0), stop=(k == NCH - 1)), r=[w.k, oTk], w=[po.k])
                hsl = self.hT[:, mm, tsl]
                P.op("dve", lambda e: e.tensor_tensor(out=hsl, in0=po[:], in1=hsl, op=ALU.add), r=[po.k, self.hk[mm][n]], w=[self.hk[mm][n]])
        self.pool_misc = [0, 1, 2, 3, 4, 5, 6, 7]

    def attn_tile(self, n, C):
        P = self.P
        KS, KSk, KW, KWk, VS, VSk, VW, VWk = C["KS"], C["KSk"], C["KW"], C["KWk"], C["VS"], C["VSk"], C["VW"], C["VWk"]
        Q, Qk, Nk, oT, oTk, ob, PT = C["Q"], C["Qk"], C["Nk"], C["oT"], C["oTk"], C["ob"], C["PT"]
        gat, gatk, rd, rdk, s3, s3k = C["gat"], C["gatk"], C["rd"], C["rdk"], C["s3"], C["s3k"]
        sc, sck, sc2, sc2k, nsl, nslk, m8, m8k, acc, acck = C["sc"], C["sck"], C["sc2"], C["sc2k"], C["nsl"], C["nslk"], C["m8"], C["m8k"], C["acc"], C["acck"]
        pool_st = C["pool_st"]
        poc, pos, pow_ = self.ps[3], self.ps[4], self.ps[5]
        st = {"npt": 0}
        pairs = [(qb, g) for qb in range(4) for g in range(4)]
        jobs = []

        def r4(ap):
            return ap.rearrange("p (a b) -> p a b", a=4)

        def mk_job(kind, qb, g, kt, first, last, pidx):
            qbg = 4 * n + qb
            job = {"pre": [], "post": []}
            box = {}

            def st1():
                pst = self.bank(pool_st)
                pt, ptk = PT[st["npt"] % 4]
                st["npt"] += 1
                box["pt"], box["ptk"] = pt, ptk
                Q96 = Q[0:96, qb, 4 * g:4 * g + 4, :]
                if kind == "c":
                    P.op("pe", lambda e: e.matmul(pst[0:127, :], lhsT=self.KC[0:96, g, 0:127], rhs=Q96, start=True, stop=True), r=[self.KC.k, Qk[qb]], w=[pst.k])
                    P.op("act", lambda e: e.activation(out=pt[0:127, :], in_=pst[0:127, :], func=AF.Exp, scale=SCALE), r=[pst.k], w=[ptk])
                    cm = self.cmask_b[0:127, qbg * 128:(qbg + 1) * 128].unsqueeze(1).broadcast_to([127, 4, 128])
                    P.op("pool", lambda e: e.tensor_tensor(out=r4(pt[0:127, :]), in0=r4(pt[0:127, :]), in1=cm, op=ALU.mult), r=[ptk, self.cmask_b.k], w=[ptk])
                    return
                if kind == "s":
                    P.op("pe", lambda e: e.matmul(pst[:], lhsT=KS[0:96, g, kt * 128:(kt + 1) * 128], rhs=Q96, start=True, stop=True),
                         r=[KSk[g], Qk[qb], Nk[qb][g]], w=[pst.k])
                else:
                    P.op("pe", lambda e: e.matmul(pst[:], lhsT=KW[0:96, g, kt * 128:(kt + 1) * 128], rhs=Q96, start=True, stop=True), r=[KWk[g], Qk[qb]], w=[pst.k])
                P.op("act", lambda e: e.activation(out=pt, in_=pst[:], func=AF.Exp, scale=SCALE), r=[pst.k], w=[ptk])
                msk = None
                if kt == qbg:
                    msk = self.triL_b
                elif kind == "w" and kt == qbg - 4:
                    msk = self.triU_b
                if msk is not None:
                    tm = msk[:].unsqueeze(1).broadcast_to([128, 4, 128])
                    P.op("pool", lambda e: e.tensor_tensor(out=r4(pt), in0=r4(pt), in1=tm, op=ALU.mult), r=[ptk, msk.k], w=[ptk])

            def st2():
                pt, ptk = box["pt"], box["ptk"]
                for h in range(4):
                    if kind == "c":
                        P.op("pe", lambda e: e.matmul(poc[:, h * 128:h * 128 + 97], lhsT=pt[0:127, h * 128:(h + 1) * 128], rhs=self.VC[0:127, g, 0:97],
                                                      start=(h == 0), stop=(h == 3), skip_group_check=True), r=[ptk, self.VC.k], w=[poc.k])
                    elif kind == "s":
                        P.op("pe", lambda e: e.matmul(pos[:, h * 128:h * 128 + 65], lhsT=pt[:, h * 128:(h + 1) * 128], rhs=VS[:, kt, g, 0:65],
                                                      start=(first and h == 0), stop=(last and h == 3), skip_group_check=True), r=[ptk, VSk], w=[pos.k])
                    else:
                        P.op("pe", lambda e: e.matmul(pow_[:, h * 128:h * 128 + 65], lhsT=pt[:, h * 128:(h + 1) * 128], rhs=VW[:, kt, g, 0:65],
                                                      start=(first and h == 0), stop=(last and h == 3), skip_group_check=True), r=[ptk, VWk], w=[pow_.k])

            job["st1"], job["st2"] = st1, st2
            return job

        def select_chain(qb, g, pidx):
            qbg = 4 * n + qb
            pcs = self.pocs[pidx % 2]
            P.op("act", lambda e: e.activation(out=pcs[:], in_=r4(poc[:])[:, :, 0:97], func=AF.Copy), r=[poc.k], w=[pcs.k])
            P.op("dve", lambda e: e.tensor_scalar(out=rd[:, 0:4], in0=pcs[:, :, 64], scalar1=1e-30, scalar2=None, op0=ALU.max), r=[pcs.k], w=[rdk])
            P.op("dve", lambda e: e.reciprocal(out=rd[:, 0:4], in_=rd[:, 0:4]), r=[rdk], w=[rdk])
            for h in range(4):
                src1 = self.selb[:, qbg, :] if h == 0 else sc
                P.op("dve", lambda e: e.scalar_tensor_tensor(out=sc, in0=pcs[:, h, 65:97], scalar=rd[:, h:h + 1], in1=src1, op0=ALU.mult, op1=ALU.add),
                     r=[pcs.k, rdk, sck, self.selb.k], w=[sck])
            P.op("dve", lambda e: e.max(out=m8[:, 0:8], in_=sc), r=[sck], w=[m8k])
            P.op("dve", lambda e: e.match_replace(out=sc2, in_to_replace=m8[:, 0:8], in_values=sc, imm_value=-3.0e38), r=[sck, m8k], w=[sc2k])
            P.op("dve", lambda e: e.max(out=m8[:, 8:16], in_=sc2), r=[sc2k], w=[m8k])
            P.op("dve", lambda e: e.tensor_scalar(out=nsl, in0=sc, scalar1=m8[:, 15:16], scalar2=NEGM, op0=ALU.is_lt, op1=ALU.mult), r=[sck, m8k], w=[nslk])

        def nsel_install(qb, g):
            pm = self.bank(self.pool_misc)
            P.op("pe", lambda e: e.transpose(out=pm[0:32, 0:128], in_=nsl, identity=self.ident_f[:]), r=[nslk, self.ident_f.k], w=[pm.k])
            P.op("act", lambda e: e.activation(out=Q[64:96, qb, 4 * g:4 * g + 4, :], in_=pm[0:32, 0:128].unsqueeze(1).broadcast_to([32, 4, 128]), func=AF.Copy),
                 r=[pm.k], w=[Nk[qb][g]])

        def evac_win():
            P.op("act", lambda e: e.activation(out=self.pows[:], in_=r4(pow_[:])[:, :, 0:65], func=AF.Copy), r=[pow_.k], w=[self.pows.k])

        def combine(qb, g, pidx):
            pcs = self.pocs[pidx % 2]
            o, ok_ = ob[qb % 2]
            P.op("dve", lambda e: e.tensor_scalar(out=rd[:, 4:8], in0=pcs[:, :, 64], scalar1=1e-30, scalar2=None, op0=ALU.max), r=[pcs.k], w=[rdk])
            P.op("dve", lambda e: e.tensor_scalar(out=rd[:, 8:12], in0=r4(pos[:])[:, :, 64], scalar1=1e-30, scalar2=None, op0=ALU.max), r=[pos.k], w=[rdk])
            P.op("dve", lambda e: e.tensor_scalar(out=rd[:, 12:16], in0=self.pows[:, :, 64], scalar1=1e-30, scalar2=None, op0=ALU.max), r=[self.pows.k], w=[rdk])
            P.op("dve", lambda e: e.reciprocal(out=rd[:, 4:16], in_=rd[:, 4:16]), r=[rdk], w=[rdk])
            P.op("dve", lambda e: e.tensor_tensor(out=s3.rearrange("p (a b) -> p a b", a=3), in0=rd[:, 4:16].rearrange("p (a b) -> p a b", a=3),
                                                  in1=gat[:, qb, :].rearrange("p (a b) -> p a b", a=3)[:, :, 4 * g:4 * g + 4], op=ALU.mult), r=[rdk, gatk], w=[s3k])
            for h in range(4):
                ah = acc[:, h * 64:(h + 1) * 64]
                P.op("dve", lambda e: e.tensor_scalar(out=ah, in0=pcs[:, h, 0:64], scalar1=s3[:, h:h + 1], scalar2=None, op0=ALU.mult), r=[pcs.k, s3k], w=[acck])
                P.op("dve", lambda e: e.scalar_tensor_tensor(out=ah, in0=self.pows[:, h, 0:64], scalar=s3[:, 8 + h:9 + h], in1=ah, op0=ALU.mult, op1=ALU.add),
                     r=[self.pows.k, s3k, acck], w=[acck])
                P.op("dve", lambda e: e.scalar_tensor_tensor(out=o[:, g * 256 + h * 64:g * 256 + (h + 1) * 64], in0=pos[:, h * 128:h * 128 + 64], scalar=s3[:, 4 + h:5 + h], in1=ah,
                                                             op0=ALU.mult, op1=ALU.add), r=[pos.k, s3k, acck], w=[ok_])

        def o_transpose(qb):
            o, ok_ = ob[qb % 2]
            pT = self.bank(self.pool_misc)
            pTb = pT[:].bitcast(BF16)
            for c in range(NCH):
                P.op("pe", lambda e: e.transpose(out=pTb[:, c * 128:(c + 1) * 128], in_=o[:, c * 128:(c + 1) * 128], identity=self.ident_b[:]), r=[ok_, self.ident_b.k], w=[pT.k])
            P.op("act", lambda e: e.activation(out=oT[:, :, qb * 128:(qb + 1) * 128], in_=pTb.rearrange("p (a b) -> p a b", a=8), func=AF.Copy), r=[pT.k], w=[oTk])

        def cjob(pidx):
            qb, g = pairs[pidx]
            j = mk_job("c", qb, g, 0, True, True, pidx)
            j["post"].append(lambda: select_chain(qb, g, pidx))
            return j

        jobs.append(cjob(0))
        pending_T = []
        for pidx, (qb, g) in enumerate(pairs):
            qbg = 4 * n + qb
            k0 = max(0, qbg - 4)
            wj = [mk_job("w", qb, g, kt, kt == k0, kt == qbg, pidx) for kt in range(k0, qbg + 1)]
            for f in pending_T:
                wj[min(2, len(wj) - 1)]["post"].append(f)
            pending_T = []
            wj[-1]["post"].append(evac_win)
            jobs += wj
            if pidx + 1 < len(pairs):
                jobs.append(cjob(pidx + 1))
            sj = [mk_job("s", qb, g, kt, kt == 0, kt == qbg, pidx) for kt in range(qbg + 1)]
            sj[0]["pre"].append(lambda qb=qb, g=g: nsel_install(qb, g))
            sj[-1]["post"].append(lambda qb=qb, g=g, pidx=pidx: combine(qb, g, pidx))
            jobs += sj
            if g == 3:
                pending_T.append(lambda qb=qb: o_transpose(qb))
        LA = 2
        for j in range(min(LA, len(jobs))):
            for f in jobs[j]["pre"]:
                f()
            jobs[j]["st1"]()
        for i in range(len(jobs)):
            if i + LA < len(jobs):
                for f in jobs[i + LA]["pre"]:
                    f()
                jobs[i + LA]["st1"]()
            jobs[i]["st2"]()
            for f in jobs[i]["post"]:
                f()
        for f in pending_T:
            f()

    def make_plan(self):
        plan = []
        ser = -1
        for b in range(self.nb):
            for layer in range(self.n_layers):
                if layer < 2:
                    ser += 1
                    plan += [(("a_in", layer, c), ser) for c in range(8)]
                    plan += [(("a_out", layer, m), ser) for m in range(8)]
                else:
                    if layer == 2:
                        ser += 1
                        plan += [(("kvk", i), ser) for i in range(4)] + [(("kvc", i), ser) for i in range(4)] + [(("kvv", i), ser) for i in range(2)]
                        plan += [(("cw2",), ser)]
                    ser += 1
                    for n in range(NT):
                        plan += [(("bg", layer - 2), ser)] + [(("bq", layer - 2, m), ser) for m in range(8)] + [(("bo", layer - 2, m), ser) for m in range(8)]
                ser += 1
                for t2 in range(2):
                    plan += [(("f_in", layer, c), ser) for c in range(FC)]
                    plan += [(("f_out", layer, m), ser) for m in range(8)]
        return plan

    def build(self):
        P = self.P
        self.convc = self.sb("convc", [128, 8, 3], F32)
        self.hst = self.sb("hst", [128, 8], F32)
        self.prologue()
        self.plan_weights(self.make_plan())
        for b in range(self.nb):
            self.mark("load%d" % b)
            self.load_x(b)
            for layer in range(self.n_layers):
                if layer < 2:
                    self.mark("a%d.%d" % (b, layer))
                    self.a_layer(layer)
                else:
                    if layer == 2:
                        self.mark("kv%d" % b)
                        self.kv_phase()
                    self.mark("b%d.%d" % (b, layer))
                    self.b_layer(layer - 2)
                self.mark("f%d.%d" % (b, layer))
                self.ffn_layer(layer)
            self.mark("store%d" % b)
            self.store_y(b)
        self.mark("end")
        P.wait_all("sp", self.out_trks)
        return self.nc


_CACHE = {}


def _prep_inputs(inputs):
    L = weight_layout()
    wall = pack_weights(inputs, L)
    vecs, bvecs = pack_vecs(inputs)
    cst = make_consts()
    return wall, vecs, bvecs, cst


def kernel(**inputs):
    inputs = {k: np.asarray(v) for k, v in inputs.items()}
    x = np.ascontiguousarray(inputs["x"], dtype=np.float32)
    B = x.shape[0]
    nb = B // NCORES
    wall, vecs, bvecs, cst = _prep_inputs(inputs)
    nc = Builder(nb).build()
    in_maps = []
    for c in range(NCORES):
        in_maps.append({"x": np.ascontiguousarray(x[c * nb:(c + 1) * nb]), "wall": wall, "vecs": vecs, "bvecs": bvecs, "cst": cst})
    res = run_bass_kernel_spmd(nc, in_maps, core_ids=list(range(NCORES)))
    out = np.concatenate([np.asarray(r["y"]).reshape(nb, S, D) for r in res.results], axis=0)
    return out.astype(np.float32)
```

```python
import numpy as np
import concourse.bass as bass
import concourse.mybir as mybir
from concourse.bass_utils import run_bass_kernel_spmd
from concourse.alu_op_type import AluOpType as ALU

F32 = mybir.dt.float32
BF16 = mybir.dt.bfloat16
AF = mybir.ActivationFunctionType

S = 2048
D = 1024
NCH = 8
TT = 512
NT = S // TT
FH = 2816
FC = 22
NCORES = 8
EPS = 1e-6
CONV_CH = 128 * 2048
SCALE = 0.125
NEGM = -30000.0


def _kt(W, c0, width=128):
    K = W.shape[0]
    return np.ascontiguousarray(W[:, c0:c0 + width].reshape(K // 128, 128, width).transpose(1, 0, 2)).reshape(128, -1)


class WLayout:
    def __init__(self):
        self.off = {}
        self.free = {}
        self.n = 0

    def add(self, name, free):
        self.off[name] = self.n
        self.free[name] = free
        self.n += 128 * free

    def total(self):
        return ((self.n + CONV_CH - 1) // CONV_CH) * CONV_CH


def weight_layout():
    L = WLayout()
    L.phase_end = []

    def a(l):
        for c in range(8):
            L.add(("a_in", l, c), 2304)
        for m in range(8):
            L.add(("a_out", l, m), 1024)
        L.phase_end.append(L.n)

    def f(l):
        for c in range(FC):
            L.add(("f_in", l, c), 2048)
        for m in range(8):
            L.add(("f_out", l, m), FH)
        L.phase_end.append(L.n)

    def b(j):
        L.add(("bg", j), 384)
        for m in range(8):
            L.add(("bq", j, m), 1024)
        for m in range(8):
            L.add(("bo", j, m), 1024)
        L.phase_end.append(L.n)

    a(0)
    f(0)
    a(1)
    f(1)
    for i in range(4):
        L.add(("kvk", i), 1024)
    for i in range(4):
        L.add(("kvc", i), 1024)
    for i in range(2):
        L.add(("kvv", i), 2048)
    for i in range(2):
        L.add(("cw1", i), 8192)
    L.add(("cw2",), 256)
    L.phase_end.append(L.n)
    b(0)
    f(2)
    b(1)
    f(3)
    return L


def pack_weights(inp, L):
    out = np.zeros(L.total(), np.float32)

    def put(name, arr):
        arr = np.asarray(arr, np.float32).reshape(128, -1)
        assert arr.shape[1] == L.free[name], (name, arr.shape)
        out[L.off[name]:L.off[name] + arr.size] = arr.reshape(-1)

    for l in range(2):
        Win = inp["a_w_in"][l]
        for c in range(8):
            put(("a_in", l, c), np.concatenate(
                [_kt(Win, c * 128), _kt(Win, 1024 + c * 128), inp["a_gate_w"][l][0, c], inp["a_gate_w"][l][1, c]], axis=1))
        for m in range(8):
            put(("a_out", l, m), _kt(inp["a_w_out"][l], m * 128))
    for l in range(4):
        W = inp["f_w_in"][l]
        for c in range(FC):
            put(("f_in", l, c), np.concatenate([_kt(W, c * 128), _kt(W, FH + c * 128)], axis=1))
        for m in range(8):
            put(("f_out", l, m), _kt(inp["f_w_out"][l], m * 128))
    kvw = inp["kv_w"]
    i = 0
    for jj in (2, 4):
        for cc in range(2):
            put(("kvk", i), _kt(kvw, jj * 256 + cc * 128))
            i += 1
    i = 0
    for jj in (0, 1):
        for cc in range(2):
            put(("kvc", i), _kt(kvw, jj * 256 + cc * 128))
            i += 1
    for i, jj in enumerate((3, 5)):
        put(("kvv", i), _kt(kvw, jj * 256, 256))
    for kv in range(2):
        t = np.zeros((128, 32, 256), np.float32)
        t[:64] = inp["cmp_w1"][kv].reshape(32, 64, 256).transpose(1, 0, 2)
        put(("cw1", kv), t)
    t = np.zeros((128, 2, 2, 64), np.float32)
    for kv in range(2):
        t[:, kv] = inp["cmp_w2"][kv].reshape(2, 128, 64).transpose(1, 0, 2)
    put(("cw2",), t)
    for j in range(2):
        put(("bg", j), _kt(inp["b_w_in"][j], 1024, 48))
        for m in range(8):
            put(("bq", j, m), _kt(inp["b_w_in"][j], m * 128))
        for m in range(8):
            put(("bo", j, m), _kt(inp["b_w_out"][j], m * 128))
    return out


VEC = {}
_nv = 0


def _vadd(name, n):
    global _nv
    VEC[name] = _nv
    _nv += n


for _l in range(2):
    _vadd(("a_norm", _l), 8)
    for _k in range(4):
        _vadd(("a_cw", _l, _k), 8)
    _vadd(("a_cb", _l), 8)
    _vadd(("a_gb", _l, 0), 8)
    _vadd(("a_gb", _l, 1), 8)
    _vadd(("a_lam", _l), 8)
for _l in range(4):
    _vadd(("f_norm", _l), 8)
_vadd(("kv_norm",), 8)
for _j in range(2):
    _vadd(("b_norm", _j), 8)
    _vadd(("q_norm", _j), 1)
for _i in range(3):
    _vadd(("k_norm", _i), 1)
for _i in range(2):
    _vadd(("c_b1", _i), 2)
    _vadd(("c_b2", _i), 1)
    _vadd(("c_pos", _i), 32)
NV = _nv
BV_GB = 0
BV_CB2V = 96
NBV = 160


def pack_vecs(inp):
    v = np.zeros((128, NV), np.float32)

    def fm(x):
        return np.asarray(x, np.float32).reshape(8, 128).T

    for l in range(2):
        v[:, VEC[("a_norm", l)]:][:, :8] = fm(inp["a_norm"][l])
        for k in range(4):
            v[:, VEC[("a_cw", l, k)]:][:, :8] = fm(inp["a_conv_w"][l][k])
        v[:, VEC[("a_cb", l)]:][:, :8] = fm(inp["a_conv_b"][l])
        v[:, VEC[("a_gb", l, 0)]:][:, :8] = fm(inp["a_gate_b"][l][0])
        v[:, VEC[("a_gb", l, 1)]:][:, :8] = fm(inp["a_gate_b"][l][1])
        v[:, VEC[("a_lam", l)]:][:, :8] = fm(inp["a_lambda"][l])
    for l in range(4):
        v[:, VEC[("f_norm", l)]:][:, :8] = fm(inp["f_norm"][l])
    v[:, VEC[("kv_norm",)]:][:, :8] = fm(inp["kv_norm"])
    for j in range(2):
        v[:, VEC[("b_norm", j)]:][:, :8] = fm(inp["b_norm"][j])
        v[:, VEC[("q_norm", j)]] = np.tile(np.asarray(inp["q_norm"][j], np.float32), 2)
    for i in range(3):
        v[:, VEC[("k_norm", i)]] = np.tile(np.asarray(inp["k_norm"][i], np.float32), 2)
    for i in range(2):
        v[:, VEC[("c_b1", i)]:][:, :2] = np.asarray(inp["cmp_b1"][i], np.float32).reshape(2, 128).T
        v[:, VEC[("c_b2", i)]] = np.tile(np.asarray(inp["cmp_b2"][i], np.float32), 2)
        v[:64, VEC[("c_pos", i)]:][:, :32] = np.asarray(inp["cmp_pos"][i], np.float32).T
    bv = np.zeros((128, NBV), np.float32)
    for j in range(2):
        bv[:, BV_GB + 48 * j: BV_GB + 48 * j + 48] = np.asarray(inp["b_gate_b"][j], np.float32)[None, :]
    bv[:, BV_CB2V:BV_CB2V + 64] = np.asarray(inp["cmp_b2"][1], np.float32)[None, :]
    return v, bv


CST = {}
_nc_ = 0


def _cadd(name, n):
    global _nc_
    CST[name] = _nc_
    _nc_ += n


_cadd("ident", 128)
_cadd("prot", 128)
_cadd("bones", 128)
_cadd("triL", 128)
_cadd("triU", 128)
_cadd("ovl", 32)
_cadd("cos", 2048)
_cadd("sin", 2048)
_cadd("cmask", 2048)
_cadd("eind", 2048)
_cadd("selb", 512)
NCST = _nc_


def make_consts():
    c = np.zeros((128, NCST), np.float32)
    p = np.arange(128)
    c[:, CST["ident"]:][:, :128] = np.eye(128)
    perm = (p // 64) * 64 + ((p % 64) + 32) % 64
    pr = np.zeros((128, 128), np.float32)
    pr[perm, p] = 1.0
    c[:, CST["prot"]:][:, :128] = pr
    c[:, CST["bones"]:][:, :128] = (p[:, None] // 64 == p[None, :] // 64)
    c[:, CST["triL"]:][:, :128] = (p[:, None] <= p[None, :])
    c[:, CST["triU"]:][:, :128] = (p[:, None] > p[None, :])
    cs = np.arange(127) * 16
    sl = np.arange(32) * 64
    ov = np.clip(np.minimum(cs[:, None] + 32, sl[None, :] + 64) - np.maximum(cs[:, None], sl[None, :]), 0, None) / 32.0
    c[:127, CST["ovl"]:][:, :32] = ov
    t = np.arange(2048, dtype=np.float64)
    fi = (p % 64) % 32
    freqs = (10000.0 ** (-(np.arange(32, dtype=np.float32) / np.float32(32)))).astype(np.float32)
    ang = (t[None, :].astype(np.float32) * freqs[fi][:, None]).astype(np.float32)
    c[:, CST["cos"]:][:, :2048] = np.cos(ang)
    sg = np.where((p % 64) < 32, -1.0, 1.0)[:, None]
    c[:, CST["sin"]:][:, :2048] = np.sin(ang) * sg
    cl = np.arange(127) * 16 + 31
    c[:127, CST["cmask"]:][:, :2048] = (cl[:, None] <= t[None, :])
    b = np.arange(32)
    c[64:96, CST["eind"]:][:, :2048] = (t[None, :].astype(np.int64) // 64 == b[:, None])
    sb = np.zeros((128, 16, 32), np.float32)
    for qb in range(16):
        tq = qb * 128 + p
        cur = (tq // 64)[:, None]
        causal = b[None, :] <= cur
        forced = (b[None, :] == 0) | (causal & (cur - b[None, :] < 2))
        sb[:, qb, :] = np.where(forced, 1e30, np.where(causal, 0.0, -1e30))
    c[:, CST["selb"]:][:, :512] = sb.reshape(128, 512)
    return c


class Trk:
    __slots__ = ("w", "r")

    def __init__(self):
        self.w = None
        self.r = {}


NDS = 24


class Prog:
    def __init__(self, nc):
        self.nc = nc
        self.eng = {"pe": nc.tensor, "act": nc.scalar, "dve": nc.vector, "pool": nc.gpsimd, "sp": nc.sync}
        self.sem = {e: nc.alloc_semaphore(name="s_" + e) for e in ("pe", "act", "dve", "pool")}
        self.cnt = {e: 0 for e in ("pe", "act", "dve", "pool")}
        self.seen = {e: {} for e in self.eng}
        self.dsem = [nc.alloc_semaphore(name="s_dma%d" % i) for i in range(NDS)]
        self.dn = 0
        self.ninst = 0

    def _semof(self, key):
        return self.sem[key] if isinstance(key, str) else self.dsem[key[1]]

    def _wait(self, e, key, val):
        if key == "pe" and e == "pe":
            return
        if self.seen[e].get(key, 0) >= val:
            return
        self.eng[e].wait_ge(self._semof(key), val)
        self.seen[e][key] = val

    def _deps(self, e, r, w):
        for t in r:
            if t.w is not None:
                self._wait(e, t.w[0], t.w[1])
        for t in w:
            if t.w is not None:
                self._wait(e, t.w[0], t.w[1])
            for k, v in t.r.items():
                self._wait(e, k, v)

    def op(self, e, fn, r=(), w=()):
        self._deps(e, r, w)
        inst = fn(self.eng[e])
        self.cnt[e] += 1
        self.ninst += 1
        inst.then_inc(self.sem[e], 1)
        v = self.cnt[e]
        for t in r:
            t.r[e] = v
        for t in w:
            t.w = (e, v)
            t.r = {}
        return inst

    def dma(self, q, out, in_, r=(), w=()):
        self._deps(q, r, w)
        i = self.dn % NDS
        gen = self.dn // NDS
        key = ("d", i)
        if gen > 0:
            self._wait(q, key, 16 * gen)
        inst = self.eng[q].dma_start(out=out, in_=in_)
        inst.then_inc(self.dsem[i], 16)
        self.dn += 1
        self.ninst += 1
        v = 16 * (gen + 1)
        for t in r:
            t.r[key] = v
        for t in w:
            t.w = (key, v)
            t.r = {}
        return (key, v)

    def wait_all(self, e, trks):
        self._deps(e, trks, trks)


class Buf:
    def __init__(self, t):
        self.t = t
        self.k = Trk()

    def __getitem__(self, idx):
        return self.t[idx]


class Builder:
    def __init__(self, nb, n_layers=4, debug_out=None):
        self.nb = nb
        self.n_layers = n_layers
        self.L = weight_layout()
        nc = bass.Bass("TRN2", target_bir_lowering=False)
        self.nc = nc
        self.P = Prog(nc)
        NW = self.L.total()
        self.x = nc.dram_tensor("x", [nb, S, D], F32, kind="ExternalInput").ap()
        self.wall = nc.dram_tensor("wall", [NW], F32, kind="ExternalInput").ap()
        self.vecs_d = nc.dram_tensor("vecs", [128, NV], F32, kind="ExternalInput").ap()
        self.bvecs_d = nc.dram_tensor("bvecs", [128, NBV], F32, kind="ExternalInput").ap()
        self.cst_d = nc.dram_tensor("cst", [128, NCST], F32, kind="ExternalInput").ap()
        self.y = nc.dram_tensor("y", [nb, S, D], F32, kind="ExternalOutput").ap()
        self.wbf = nc.dram_tensor("wbf", [NW], BF16, kind="Internal").ap()
        self.wchunk = [Trk() for _ in range(NW // CONV_CH)]
        self.out_trks = []
        self.marks = []

        def sb(name, shape, dt):
            return Buf(nc.alloc_sbuf_tensor(name, shape, dt))

        self.sb = sb
        self.hT = nc.alloc_sbuf_tensor("hT", [128, NCH, S], F32)
        self.hk = [[Trk() for _ in range(NT)] for _ in range(NCH)]
        self.vecs = sb("vecs_sb", [128, NV], F32)
        self.bvecs = sb("bvecs_sb", [128, NBV], F32)
        self.ident_f = sb("ident_f", [128, 128], F32)
        self.ident_b = sb("ident_b", [128, 128], BF16)
        self.ones_b = sb("ones_b", [128, 128], BF16)
        self.coef = sb("coef", [128, 2, 2, 8], F32)
        self.cos_b = sb("cos_b", [128, S], BF16)
        self.sin_b = sb("sin_b", [128, S], BF16)
        self.cmask_b = sb("cmask_b", [128, S], BF16)
        self.triL_b = sb("triL_b", [128, 128], BF16)
        self.triU_b = sb("triU_b", [128, 128], BF16)
        self.prot_b = sb("prot_b", [128, 128], BF16)
        self.bones_b = sb("bones_b", [128, 128], BF16)
        self.selb = sb("selb", [128, 16, 32], F32)
        self.KC = sb("KC", [128, 4, 128], BF16)
        self.VC = sb("VC", [128, 4, 98], BF16)
        self.posb = sb("posb", [128, 2, 32], BF16)
        self.cbias = sb("cbias", [128, 2, 2], F32)
        self.pocs = [sb("pocs%d" % i, [128, 4, 97], F32) for i in range(2)]
        self.pows = sb("pows", [128, 4, 65], F32)
        self.KSd = nc.dram_tensor("KSd", [2, 64, 4, S], BF16, kind="Internal").ap()
        self.Vd = nc.dram_tensor("Vd", [2, 128, 16, 4, 66], BF16, kind="Internal").ap()
        self.ksd_k = [[[Trk() for _ in range(NT)] for _ in range(4)] for _ in range(2)]
        self.vd_k = [[Trk() for _ in range(NT)] for _ in range(2)]
        self.ps = [Buf(nc.alloc_psum_tensor("ps%d" % i, [128, 512], F32)) for i in range(8)]
        self.ps_rr = 0
        self.pool_misc = [0, 1, 2, 3, 4, 5, 6, 7]
        self.NSLOT = 3
        self.ring = [sb("wring%d" % i, [128, FH], BF16) for i in range(self.NSLOT)]
        self.perm_ids = set(id(b) for b in self.ring)
        self.PHB = 100 * 1024
        self.ph = nc.alloc_sbuf_tensor("phase", [128, self.PHB // 4], F32)
        self.phk = {}

    def bank(self, pool=None):
        if pool is None:
            b = self.ps[self.ps_rr % 8]
        else:
            b = self.ps[pool[self.ps_rr % len(pool)]]
        self.ps_rr += 1
        return b

    def vec(self, name, c=0):
        i = VEC[name] + c
        return self.vecs[:, i:i + 1]

    def phase_view(self, byte_off, shape, dt):
        esz = 4 if dt == F32 else 2
        n = int(np.prod(shape[1:]))
        assert byte_off % 4 == 0 and byte_off + n * esz <= self.PHB, (byte_off, shape)
        if dt == F32:
            ap = self.ph[:, byte_off // 4: byte_off // 4 + n]
        else:
            ap = self.ph[:].bitcast(BF16)[:, byte_off // 2: byte_off // 2 + n]
        if len(shape) == 2:
            return ap
        names = " ".join("a%d" % i for i in range(len(shape) - 1))
        kw = {"a%d" % i: shape[i + 1] for i in range(len(shape) - 1)}
        return ap.rearrange("p (%s) -> p %s" % (names, names), **kw)

    def mark(self, name):
        self.marks.append((name, dict(self.P.cnt)))

    def new_phase(self, trks):
        merged = {}
        for t in self.phase_cur:
            if t.w is not None:
                merged[t.w[0]] = max(merged.get(t.w[0], 0), t.w[1])
            for k, v in t.r.items():
                merged[k] = max(merged.get(k, 0), v)
        for t in trks:
            t.w = None
            t.r = dict(merged)
        self.phase_cur = list(trks)

    def plan_weights(self, entries):
        self.wplan = list(entries)
        self.wplan_i = 0
        self.wq = []
        self.free_perm = list(self.ring)
        self.free_ext = []
        self.inuse = None
        self.serial = -1

    def begin_phase(self, ext_slots):
        self._release()
        self.serial += 1
        self.free_ext = list(ext_slots)
        self.ext_ids = set(id(b) for b in ext_slots)

    def _release(self):
        if self.inuse is not None:
            slot, ser = self.inuse
            if id(slot) in self.perm_ids:
                self.free_perm.append(slot)
            elif ser == self.serial:
                self.free_ext.append(slot)
            self.inuse = None

    def _topup(self):
        while self.wplan_i < len(self.wplan):
            name, ser = self.wplan[self.wplan_i]
            if ser == self.serial and self.free_ext:
                slot = self.free_ext.pop(0)
            elif self.free_perm:
                slot = self.free_perm.pop(0)
            else:
                break
            self.wplan_i += 1
            off = self.L.off[name]
            free = self.L.free[name]
            src = self.wbf[off:off + 128 * free].rearrange("(p f) -> p f", p=128)
            c0 = off // CONV_CH
            c1 = (off + 128 * free - 1) // CONV_CH
            self.P.dma("sp", slot[:, 0:free], src, r=[self.wchunk[c] for c in range(c0, c1 + 1)], w=[slot.k])
            self.wq.append((name, slot, ser))

    def next_w(self, name, hold=False):
        self._release()
        self._topup()
        n, slot, ser = self.wq.pop(0)
        assert n == name and ser == self.serial, (n, name, ser, self.serial)
        if not hold:
            self.inuse = (slot, ser)
        return slot

    def release_slot(self, slot):
        if id(slot) in self.perm_ids:
            self.free_perm.append(slot)
        elif id(slot) in self.ext_ids:
            self.free_ext.append(slot)

    def ext_slots(self, byte_off):
        out = []
        o = byte_off
        while o + 2 * FH <= self.PHB:
            b = Buf(self.phase_view(o, [128, FH], BF16))
            out.append(b)
            o += 2 * FH
        return out

    def prologue(self):
        P = self.P
        nc = self.nc
        P.dma("sp", self.vecs[:], self.vecs_d, w=[self.vecs.k])
        P.dma("sp", self.bvecs[:], self.bvecs_d, w=[self.bvecs.k])
        P.dma("sp", self.ident_f[:], self.cst_d[:, CST["ident"]:CST["ident"] + 128], w=[self.ident_f.k])
        P.op("dve", lambda e: e.tensor_copy(out=self.ident_b[:], in_=self.ident_f[:]), r=[self.ident_f.k], w=[self.ident_b.k])
        P.op("pool", lambda e: e.memset(self.ones_b[:], 1.0), w=[self.ones_b.k])
        tmp = self.sb("coef_tmp", [128, 16], F32)
        for l in range(2):
            lam = self.vecs[:, VEC[("a_lam", l)]:VEC[("a_lam", l)] + 8]
            P.op("act", lambda e: e.activation(out=tmp[:, l * 8:l * 8 + 8], in_=lam, func=AF.Exp, scale=-1.0), r=[self.vecs.k], w=[tmp.k])
            P.op("act", lambda e: e.activation(out=tmp[:, l * 8:l * 8 + 8], in_=tmp[:, l * 8:l * 8 + 8], func=AF.Ln, bias=1.0), r=[tmp.k], w=[tmp.k])
            P.op("dve", lambda e: e.tensor_scalar(out=self.coef[:, l, 0, :], in0=tmp[:, l * 8:l * 8 + 8], scalar1=-8.0, scalar2=None, op0=ALU.mult), r=[tmp.k], w=[self.coef.k])
            P.op("dve", lambda e: e.tensor_scalar(out=self.coef[:, l, 1, :], in0=tmp[:, l * 8:l * 8 + 8], scalar1=-16.0, scalar2=None, op0=ALU.mult), r=[tmp.k], w=[self.coef.k])
        stg = self.phase_view(0, [128, 2048], F32)
        stk = Trk()
        self.phase_cur = [stk]
        self.conv_done = 0

        def ctab(name, n, dst, dstk, eng):
            P.dma("sp", stg[:, 0:n], self.cst_d[:, CST[name]:CST[name] + n], w=[stk])
            if eng == "act":
                P.op("act", lambda e: e.activation(out=dst, in_=stg[:, 0:n], func=AF.Copy), r=[stk], w=[dstk])
            else:
                P.op(eng, lambda e: e.tensor_copy(out=dst, in_=stg[:, 0:n]), r=[stk], w=[dstk])

        ctab("cos", 2048, self.cos_b[:], self.cos_b.k, "dve")
        ctab("sin", 2048, self.sin_b[:], self.sin_b.k, "act")
        ctab("cmask", 2048, self.cmask_b[:], self.cmask_b.k, "dve")
        ctab("triL", 128, self.triL_b[:], self.triL_b.k, "dve")
        ctab("triU", 128, self.triU_b[:], self.triU_b.k, "dve")
        ctab("prot", 128, self.prot_b[:], self.prot_b.k, "dve")
        ctab("bones", 128, self.bones_b[:], self.bones_b.k, "dve")
        P.dma("sp", self.selb[:].rearrange("p a b -> p (a b)"), self.cst_d[:, CST["selb"]:CST["selb"] + 512], w=[self.selb.k])
        P.op("pool", lambda e: e.memset(self.VC[:], 0.0), w=[self.VC.k])
        P.op("pool", lambda e: e.memset(self.KC[:], 0.0), w=[self.KC.k])
        P.op("pool", lambda e: e.memset(self.VC[:, :, 64:65], 1.0), w=[self.VC.k])
        P.dma("sp", stg[:, 0:32], self.cst_d[:, CST["ovl"]:CST["ovl"] + 32], w=[stk])
        for g in range(4):
            P.op("dve", lambda e: e.tensor_copy(out=self.VC[:, g, 65:97], in_=stg[:, 0:32]), r=[stk], w=[self.VC.k])
        for kv in range(2):
            o0 = VEC[("c_pos", kv)]
            P.op("dve", lambda e: e.tensor_copy(out=self.posb[:, kv, :], in_=self.vecs[:, o0:o0 + 32]), r=[self.vecs.k], w=[self.posb.k])
        self.conv_done = 0

    def convert_upto(self, nchunks):
        nchunks = min(nchunks, len(self.wchunk))
        for i in range(self.conv_done, nchunks):
            src = self.wall[i * CONV_CH:(i + 1) * CONV_CH].rearrange("(p f) -> p f", p=128)
            dst = self.wbf[i * CONV_CH:(i + 1) * CONV_CH].rearrange("(p f) -> p f", p=128)
            self.P.dma("pool", dst, src, w=[self.wchunk[i]])
        self.conv_done = max(self.conv_done, nchunks)

    def convert_phase(self, ph):
        ends = self.L.phase_end
        ph = min(ph, len(ends) - 1)
        self.convert_upto((ends[ph] + CONV_CH - 1) // CONV_CH)

    def load_x(self, b):
        P = self.P
        xin = [(self.phase_view(j * 4096, [128, 1024], F32), Trk()) for j in range(4)]
        self.new_phase([k for _, k in xin])
        for n in range(NT):
            for j in range(4):
                t0 = n * TT + j * 128
                P.dma("sp", xin[j][0], self.x[b, t0:t0 + 128, :], w=[xin[j][1]])
            for c in range(NCH):
                pb = self.bank()
                for j in range(4):
                    P.op("pe", lambda e: e.transpose(out=pb[:, j * 128:(j + 1) * 128], in_=xin[j][0][:, c * 128:(c + 1) * 128], identity=self.ident_f[:]),
                         r=[xin[j][1], self.ident_f.k], w=[pb.k])
                eng = "act" if c % 2 == 0 else "dve"
                dst = self.hT[:, c, n * TT:(n + 1) * TT]
                if eng == "act":
                    P.op("act", lambda e: e.activation(out=dst, in_=pb[:], func=AF.Copy), r=[pb.k], w=[self.hk[c][n]])
                else:
                    P.op("dve", lambda e: e.tensor_copy(out=dst, in_=pb[:]), r=[pb.k], w=[self.hk[c][n]])

    def store_y(self, b):
        P = self.P
        yo = [(self.phase_view(j * 4096, [128, 1024], F32), Trk()) for j in range(4)]
        self.new_phase([k for _, k in yo])
        for n in range(NT):
            for j in range(4):
                t0 = n * TT + j * 128
                for half in range(2):
                    pb = self.bank()
                    for cc in range(4):
                        c = half * 4 + cc
                        P.op("pe", lambda e: e.transpose(out=pb[:, cc * 128:(cc + 1) * 128], in_=self.hT[:, c, t0:t0 + 128], identity=self.ident_f[:]),
                             r=[self.hk[c][n], self.ident_f.k], w=[pb.k])
                    dst = yo[j][0][:, half * 512:(half + 1) * 512]
                    wl = [yo[j][1]]
                    if half == 0:
                        P.op("act", lambda e: e.activation(out=dst, in_=pb[:], func=AF.Copy), r=[pb.k], w=wl)
                    else:
                        P.op("dve", lambda e: e.tensor_copy(out=dst, in_=pb[:]), r=[pb.k], w=wl)
                ot = Trk()
                P.dma("sp", self.y[b, t0:t0 + 128, :], yo[j][0], r=[yo[j][1]], w=[ot])
                self.out_trks.append(ot)

    def norm_tile(self, n, gname, u_ap, u_k, sq_bufs, rstd_buf):
        P = self.P
        pb = self.bank()
        for c in range(NCH):
            sq, sk = sq_bufs[c % len(sq_bufs)]
            hsl = self.hT[:, c, n * TT:(n + 1) * TT]
            P.op("act", lambda e: e.activation(out=sq, in_=hsl, func=AF.Square), r=[self.hk[c][n]], w=[sk])
            P.op("pe", lambda e: e.matmul(pb[:], lhsT=self.ones_b[:], rhs=sq, start=(c == 0), stop=(c == NCH - 1)),
                 r=[sk, self.ones_b.k], w=[pb.k])
        rs, rk = rstd_buf
        P.op("act", lambda e: e.activation(out=rs, in_=pb[:], func=AF.Sqrt, scale=1.0 / D, bias=EPS), r=[pb.k], w=[rk])
        P.op("dve", lambda e: e.reciprocal(out=rs, in_=rs), r=[rk], w=[rk])
        for c in range(NCH):
            hsl = self.hT[:, c, n * TT:(n + 1) * TT]
            g = self.vec(gname, c)
            P.op("dve", lambda e: e.scalar_tensor_tensor(out=u_ap[:, c, :], in0=hsl, scalar=g, in1=rs, op0=ALU.mult, op1=ALU.mult),
                 r=[self.hk[c][n], rk, self.vecs.k], w=[u_k])

    def a_phase_setup(self):
        v = self.phase_view
        A = {}
        A["u"] = [(v(n * 8192, [128, 8, 512], BF16), Trk()) for n in range(NT)]
        A["m"] = [(v(32768 + n * 8192, [128, 8, 512], BF16), Trk()) for n in range(NT)]
        A["sq"] = [(v(65536 + i * 1024, [128, 512], BF16), Trk()) for i in range(2)]
        A["rstd"] = (v(67584, [128, 512], F32), Trk())
        A["xp"] = [(v(69632 + i * 2080, [128, 516], F32), Trk()) for i in range(2)]
        base = 73792
        s1 = []
        for i in range(3):
            o = base + i * 4096
            xr = (v(o + 2048, [128, 512], F32), Trk())
            s1.append({"y": (v(o, [128, 512], BF16), Trk()), "xrb": (v(o + 1024, [128, 512], BF16), Trk()), "xr": xr, "hr": xr})
        s2 = []
        for i in range(2):
            o = base + 12288 + i * 6144
            rr = (v(o, [128, 512], F32), Trk())
            s2.append({"r": rr, "a2": rr, "i": (v(o + 2048, [128, 512], F32), Trk()), "a": (v(o + 4096, [128, 512], F32), Trk())})
        sets = s1 + s2
        A["s1"] = s1
        A["s2"] = s2
        A["sets"] = sets
        trks = [k for _, k in A["u"]] + [k for _, k in A["m"]] + [A["rstd"][1]] + [k for _, k in A["sq"]] + [k for _, k in A["xp"]]
        for st in sets:
            trks += [k for _, k in st.values()]
        self.new_phase(list(dict((id(t), t) for t in trks).values()))
        self.begin_phase([])
        self.A = A

    def a_layer(self, l):
        P = self.P
        self.a_phase_setup()
        A = self.A
        P.op("pool", lambda e: e.memset(self.convc[:], 0.0), w=[self.convc.k])
        P.op("pool", lambda e: e.memset(self.hst[:], 0.0), w=[self.hst.k])
        for n in range(NT):
            self.norm_tile(n, ("a_norm", l), A["u"][n][0], A["u"][n][1], A["sq"], A["rstd"])
        items = [(c, n) for c in range(NCH) for n in range(NT)]
        wslot = {}

        def stage1(it):
            c, n = items[it]
            if n == 0:
                wslot[c] = self.next_w(("a_in", l, c), hold=True)
            w = wslot[c]
            u, uk = A["u"][n]
            pg = self.bank()
            pr = self.bank()
            for k in range(NCH):
                P.op("pe", lambda e: e.matmul(pg[:], lhsT=w[:, k * 128:(k + 1) * 128], rhs=u[:, k, :], start=(k == 0), stop=(k == NCH - 1)),
                     r=[w.k, uk], w=[pg.k])
            for k in range(NCH):
                P.op("pe", lambda e: e.matmul(pr[:], lhsT=w[:, 1024 + k * 128:1024 + (k + 1) * 128], rhs=u[:, k, :], start=(k == 0), stop=(k == NCH - 1)),
                     r=[w.k, uk], w=[pr.k])
            st = A["s1"][it % 3]
            xp, xpk = A["xp"][it % 2]
            y, yk = st["y"]
            xr, xrk = st["xr"]
            xrb, xrbk = st["xrb"]
            P.op("act", lambda e: e.activation(out=y, in_=pg[:], func=AF.Gelu_apprx_tanh), r=[pg.k], w=[yk])
            P.op("pool", lambda e: e.tensor_copy(out=xp[:, 0:3], in_=self.convc[:, c, :]), r=[self.convc.k], w=[xpk])
            P.op("act", lambda e: e.activation(out=xp[:, 3:515], in_=pr[:], func=AF.Copy), r=[pr.k], w=[xpk])
            P.op("dve", lambda e: e.tensor_scalar(out=xr, in0=xp[:, 0:512], scalar1=self.vec(("a_cw", l, 0), c), scalar2=self.vec(("a_cb", l), c),
                                                  op0=ALU.mult, op1=ALU.add), r=[xpk, self.vecs.k], w=[xrk])
            for kk in range(1, 4):
                P.op("dve", lambda e: e.scalar_tensor_tensor(out=xr, in0=xp[:, kk:kk + 512], scalar=self.vec(("a_cw", l, kk), c), in1=xr,
                                                             op0=ALU.mult, op1=ALU.add), r=[xpk, xrk, self.vecs.k], w=[xrk])
            P.op("pool", lambda e: e.tensor_copy(out=self.convc[:, c, :], in_=xp[:, 512:515]), r=[xpk], w=[self.convc.k])
            P.op("pool", lambda e: e.tensor_copy(out=xrb, in_=xr), r=[xrk], w=[xrbk])

        def stage2(it):
            c, n = items[it]
            w = wslot[c]
            m, mk = A["m"][n]
            st = A["s1"][it % 3]
            y, yk = st["y"]
            xr, xrk = st["xr"]
            xrb, xrbk = st["xrb"]
            hr, hrk = st["hr"]
            t2 = A["s2"][it % 2]
            rr, rrk = t2["r"]
            ii, iik = t2["i"]
            aa, aak = t2["a"]
            p1 = self.bank()
            p2 = self.bank()
            P.op("pe", lambda e: e.matmul(p1[:], lhsT=w[:, 2048:2176], rhs=xrb, start=True, stop=True), r=[w.k, xrbk], w=[p1.k])
            P.op("pe", lambda e: e.matmul(p2[:], lhsT=w[:, 2176:2304], rhs=xrb, start=True, stop=True), r=[w.k, xrbk], w=[p2.k])
            if n == NT - 1:
                self.release_slot(w)
            P.op("act", lambda e: e.activation(out=rr, in_=p1[:], func=AF.Sigmoid, bias=self.vec(("a_gb", l, 0), c)), r=[p1.k, self.vecs.k], w=[rrk])
            P.op("act", lambda e: e.activation(out=ii, in_=p2[:], func=AF.Sigmoid, bias=self.vec(("a_gb", l, 1), c)), r=[p2.k, self.vecs.k], w=[iik])
            P.op("act", lambda e: e.activation(out=aa, in_=rr, func=AF.Exp, scale=self.coef[:, l, 0, c:c + 1]), r=[rrk, self.coef.k], w=[aak])
            P.op("act", lambda e: e.activation(out=rr, in_=rr, func=AF.Exp, scale=self.coef[:, l, 1, c:c + 1]), r=[rrk, self.coef.k], w=[rrk])
            P.op("act", lambda e: e.activation(out=rr, in_=rr, func=AF.Sqrt, scale=-1.0, bias=1.0), r=[rrk], w=[rrk])
            P.op("dve", lambda e: e.tensor_tensor(out=ii, in0=ii, in1=xr, op=ALU.mult), r=[iik, xrk], w=[iik])
            P.op("dve", lambda e: e.tensor_tensor(out=ii, in0=ii, in1=rr, op=ALU.mult), r=[iik, rrk], w=[iik])
            P.op("dve", lambda e: e.tensor_tensor_scan(out=hr, data0=aa, data1=ii, initial=self.hst[:, c:c + 1], op0=ALU.mult, op1=ALU.add),
                 r=[aak, iik, self.hst.k], w=[hrk])
            P.op("pool", lambda e: e.tensor_copy(out=self.hst[:, c:c + 1], in_=hr[:, 511:512]), r=[hrk], w=[self.hst.k])
            P.op("pool", lambda e: e.tensor_tensor(out=m[:, c, :], in0=hr, in1=y, op=ALU.mult), r=[hrk, yk], w=[mk])

        LA = 2
        for it in range(min(LA, len(items))):
            stage1(it)
        for it in range(len(items)):
            if it + LA < len(items):
                stage1(it + LA)
            stage2(it)
        for mm in range(NCH):
            w = self.next_w(("a_out", l, mm))
            for n in range(NT):
                m, mk = A["m"][n]
                po = self.bank()
                for k in range(NCH):
                    P.op("pe", lambda e: e.matmul(po[:], lhsT=w[:, k * 128:(k + 1) * 128], rhs=m[:, k, :], start=(k == 0), stop=(k == NCH - 1)),
                         r=[w.k, mk], w=[po.k])
                hsl = self.hT[:, mm, n * TT:(n + 1) * TT]
                P.op("dve", lambda e: e.tensor_tensor(out=hsl, in0=po[:], in1=hsl, op=ALU.add), r=[po.k, self.hk[mm][n]], w=[self.hk[mm][n]])

    def ffn_phase_setup(self):
        v = self.phase_view
        Fz = {}
        Fz["u"] = [(v(s * 8192, [128, 8, 512], BF16), Trk()) for s in range(2)]
        Fz["act"] = [[(v(16384 + (c * 2 + s) * 1024, [128, 512], BF16), Trk()) for s in range(2)] for c in range(FC)]
        Fz["sq"] = [(v(61440 + i * 1024, [128, 512], BF16), Trk()) for i in range(2)]
        Fz["rstd"] = (v(63488, [128, 512], F32), Trk())
        Fz["sg"] = [(v(65536 + i * 1024, [128, 512], BF16), Trk()) for i in range(4)]
        ext = self.ext_slots(69632)
        trks = [k for _, k in Fz["u"]] + [k for row in Fz["act"] for _, k in row] + [k for _, k in Fz["sq"]] + [Fz["rstd"][1]] + [k for _, k in Fz["sg"]] + [b.k for b in ext]
        self.new_phase(trks)
        self.begin_phase(ext)
        self.F = Fz

    def ffn_tile(self, L, t2):
        P = self.P
        Fz = self.F
        for s in range(2):
            self.norm_tile(2 * t2 + s, ("f_norm", L), Fz["u"][s][0], Fz["u"][s][1], Fz["sq"], Fz["rstd"])
        nsg = 0
        for c in range(FC):
            w = self.next_w(("f_in", L, c))
            for s in range(2):
                u, uk = Fz["u"][s]
                pg = self.bank()
                pu = self.bank()
                for k in range(NCH):
                    P.op("pe", lambda e: e.matmul(pg[:], lhsT=w[:, k * 128:(k + 1) * 128], rhs=u[:, k, :], start=(k == 0), stop=(k == NCH - 1)),
                         r=[w.k, uk], w=[pg.k])
                for k in range(NCH):
                    P.op("pe", lambda e: e.matmul(pu[:], lhsT=w[:, 1024 + k * 128:1024 + (k + 1) * 128], rhs=u[:, k, :], start=(k == 0), stop=(k == NCH - 1)),
                         r=[w.k, uk], w=[pu.k])
                sg, sgk = Fz["sg"][nsg % 4]
                nsg += 1
                a, ak = Fz["act"][c][s]
                P.op("act", lambda e: e.activation(out=sg, in_=pg[:], func=AF.Silu), r=[pg.k], w=[sgk])
                P.op("dve", lambda e: e.tensor_tensor(out=a, in0=sg, in1=pu[:], op=ALU.mult), r=[sgk, pu.k], w=[ak])
        for mm in range(NCH):
            w = self.next_w(("f_out", L, mm))
            for s in range(2):
                n = 2 * t2 + s
                po = self.bank()
                for c in range(FC):
                    a, ak = Fz["act"][c][s]
                    P.op("pe", lambda e: e.matmul(po[:], lhsT=w[:, c * 128:(c + 1) * 128], rhs=a, start=(c == 0), stop=(c == FC - 1)),
                         r=[w.k, ak], w=[po.k])
                hsl = self.hT[:, mm, n * TT:(n + 1) * TT]
                P.op("dve", lambda e: e.tensor_tensor(out=hsl, in0=po[:], in1=hsl, op=ALU.add), r=[po.k, self.hk[mm][n]], w=[self.hk[mm][n]])

    def ffn_layer(self, L):
        self.ffn_phase_setup()
        for t2 in range(2):
            self.ffn_tile(L, t2)

    def headnorm_rope(self, pk, R, C, gvec, cos_ap, sin_ap, T, bias=None):
        P = self.P
        sq, sqk = T["sq"]
        rs, rsk = T["rs"]
        qn, qnk = T["qn"]
        t1, t1k = T["t1"]
        t2, t2k = T["t2"]
        if bias is not None:
            xf, xfk = T["xf"]
            P.op("act", lambda e: e.activation(out=xf[0:R, 0:C], in_=pk[0:R, 0:C], func=AF.Identity, bias=bias), r=[pk.k, self.vecs.k], w=[xfk])
            src, srck = xf[0:R, 0:C], xfk
        else:
            src, srck = pk[0:R, 0:C], pk.k
        P.op("act", lambda e: e.activation(out=sq[0:R, 0:C], in_=src, func=AF.Square), r=[srck], w=[sqk])
        pss = self.bank(self.pool_misc)
        P.op("pe", lambda e: e.matmul(pss[0:R, 0:C], lhsT=self.bones_b[0:R, 0:R], rhs=sq[0:R, 0:C], start=True, stop=True), r=[sqk, self.bones_b.k], w=[pss.k])
        P.op("act", lambda e: e.activation(out=rs[0:R, 0:C], in_=pss[0:R, 0:C], func=AF.Sqrt, scale=1.0 / 64.0, bias=EPS), r=[pss.k], w=[rsk])
        P.op("dve", lambda e: e.reciprocal(out=rs[0:R, 0:C], in_=rs[0:R, 0:C]), r=[rsk], w=[rsk])
        P.op("dve", lambda e: e.scalar_tensor_tensor(out=qn[0:R, 0:C], in0=src, scalar=gvec, in1=rs[0:R, 0:C], op0=ALU.mult, op1=ALU.mult),
             r=[srck, rsk, self.vecs.k], w=[qnk])
        prt = self.bank(self.pool_misc)
        P.op("pe", lambda e: e.matmul(prt[0:R, 0:C], lhsT=self.prot_b[0:R, 0:R], rhs=qn[0:R, 0:C], start=True, stop=True), r=[qnk, self.prot_b.k], w=[prt.k])
        P.op("pool", lambda e: e.tensor_tensor(out=t1[0:R, 0:C], in0=qn[0:R, 0:C], in1=cos_ap, op=ALU.mult), r=[qnk, self.cos_b.k], w=[t1k])
        P.op("dve", lambda e: e.tensor_tensor(out=t2[0:R, 0:C], in0=prt[0:R, 0:C], in1=sin_ap, op=ALU.mult), r=[prt.k, self.sin_b.k], w=[t2k])

    def kv_phase(self):
        P = self.P
        v = self.phase_view
        U = [(v(n * 8192, [128, 8, 512], BF16), Trk()) for n in range(NT)]
        sqn = [(v(32768 + i * 1024, [128, 512], BF16), Trk()) for i in range(2)]
        rstd = (v(34816, [128, 512], F32), Trk())
        kcT, kcTk = v(36864, [128, 8, S], BF16), [Trk() for _ in range(8)]
        cw1, cw1k = v(0, [128, 2, 32, 256], BF16), Trk()
        T = {"sq": (v(69632, [128, 512], BF16), Trk()), "rs": (v(70656, [128, 512], F32), Trk()), "qn": (v(72704, [128, 512], BF16), Trk()),
             "t1": (v(73728, [128, 512], F32), Trk()), "t2": (v(75776, [128, 512], F32), Trk()), "xf": (v(77824, [128, 512], F32), Trk())}
        kout = [(v(79872 + i * 1024, [128, 512], BF16), Trk()) for i in range(2)]
        vst = [(v(81920 + i * 528, [128, 4, 66], BF16), Trk()) for i in range(2)]
        hid = [(v(83008 + i * 512, [128, 2, 128], BF16), Trk()) for i in range(2)]
        ext = self.ext_slots(84032)
        trks = [k for _, k in U] + [rstd[1], cw1k] + [k for _, k in sqn] + kcTk + [k for _, k in T.values()] + [k for _, k in kout] + [k for _, k in vst] \
            + [k for _, k in hid] + [b.k for b in ext]
        self.new_phase(trks)
        self.begin_phase(ext)
        self.pool_misc = [0, 1, 2, 3, 4, 5, 6, 7]
        for i in range(2):
            P.op("pool", lambda e: e.memset(vst[i][0][:, :, 64:66], 1.0), w=[vst[i][1]])
        for n in range(NT):
            self.norm_tile(n, ("kv_norm",), U[n][0], U[n][1], sqn, rstd)
        nko = 0
        nvs = 0
        for i in range(4):
            which, cc = i // 2, i % 2
            w = self.next_w(("kvk", i))
            for n in range(NT):
                tsl = slice(n * TT, (n + 1) * TT)
                u, uk = U[n]
                pk = self.bank()
                for k in range(NCH):
                    P.op("pe", lambda e: e.matmul(pk[:], lhsT=w[:, k * 128:(k + 1) * 128], rhs=u[:, k, :], start=(k == 0), stop=(k == NCH - 1)), r=[w.k, uk], w=[pk.k])
                self.headnorm_rope(pk, 128, 512, self.vec(("k_norm", 1 + which)), self.cos_b[:, tsl], self.sin_b[:, tsl], T)
                ko, kok = kout[nko % 2]
                nko += 1
                P.op("dve", lambda e: e.tensor_tensor(out=ko, in0=T["t1"][0], in1=T["t2"][0], op=ALU.add), r=[T["t1"][1], T["t2"][1]], w=[kok])
                for hh in range(2):
                    P.dma("sp", self.KSd[which, :, 2 * cc + hh, tsl], ko[hh * 64:(hh + 1) * 64, :], r=[kok], w=[self.ksd_k[which][2 * cc + hh][n]])
        for i in range(4):
            sel, cc = i // 2, i % 2
            w = self.next_w(("kvc", i))
            for n in range(NT):
                tsl = slice(n * TT, (n + 1) * TT)
                u, uk = U[n]
                for gg in range(2):
                    pc = self.bank()
                    for k in range(NCH):
                        P.op("pe", lambda e: e.matmul(pc[0:64, :], lhsT=w[:, k * 128 + gg * 64:k * 128 + gg * 64 + 64], rhs=u[:, k, :], start=(k == 0), stop=(k == NCH - 1)),
                             r=[w.k, uk], w=[pc.k])
                    idx = sel * 4 + 2 * cc + gg
                    P.op("act", lambda e: e.activation(out=kcT[0:64, idx, tsl], in_=pc[0:64, :], func=AF.Copy), r=[pc.k], w=[kcTk[idx]])
        for i in range(2):
            w = self.next_w(("kvv", i))
            for n in range(NT):
                u, uk = U[n]
                for jb in range(4):
                    pv = self.bank()
                    for k in range(NCH):
                        P.op("pe", lambda e: e.matmul(pv[:, 0:256], lhsT=u[:, k, jb * 128:(jb + 1) * 128], rhs=w[:, k * 256:(k + 1) * 256], start=(k == 0), stop=(k == NCH - 1)),
                             r=[w.k, uk], w=[pv.k])
                    vs, vsk = vst[nvs % 2]
                    nvs += 1
                    P.op("act", lambda e: e.activation(out=vs[:, :, 0:64], in_=pv[:, 0:256].rearrange("p (g d) -> p g d", g=4), func=AF.Copy), r=[pv.k], w=[vsk])
                    P.dma("sp", self.Vd[i, :, 4 * n + jb, :, :], vs, r=[vsk], w=[self.vd_k[i][n]])
        for kv in range(2):
            off = self.L.off[("cw1", kv)]
            src = self.wbf[off:off + 128 * 8192].rearrange("(p f) -> p f", p=128)
            c0, c1 = off // CONV_CH, (off + 128 * 8192 - 1) // CONV_CH
            P.dma("sp", cw1[0:64, kv].rearrange("p a b -> p (a b)"), src[0:64, :], r=[self.wchunk[c] for c in range(c0, c1 + 1)], w=[cw1k] + [k for _, k in U])
        w2 = self.next_w(("cw2",))
        for sel in range(2):
            pcv = self.bank()
            for cc in range(2):
                for l in range(32):
                    P.op("pe", lambda e: e.matmul(pcv[:, cc:cc + 1], lhsT=cw1[0:64, sel, l, cc * 128:(cc + 1) * 128], rhs=self.posb[0:64, sel, l:l + 1],
                                                  start=(cc == 0 and l == 0), stop=(cc == 1 and l == 31)), r=[cw1k, self.posb.k], w=[pcv.k])
            b1o = VEC[("c_b1", sel)]
            P.op("dve", lambda e: e.tensor_tensor(out=self.cbias[:, sel, :], in0=pcv[:, 0:2], in1=self.vecs[:, b1o:b1o + 2], op=ALU.add), r=[pcv.k, self.vecs.k], w=[self.cbias.k])
        nh = 0
        for sel in range(2):
            for g in range(4):
                hd, hdk = hid[nh % 2]
                nh += 1
                for cc in range(2):
                    ph = self.bank()
                    for l in range(32):
                        P.op("pe", lambda e: e.matmul(ph[:, 0:127], lhsT=cw1[0:64, sel, l, cc * 128:(cc + 1) * 128], rhs=kcT[0:64, sel * 4 + g, l:l + 16 * 126 + 1:16],
                                                      start=(l == 0), stop=(l == 31)), r=[cw1k, kcTk[sel * 4 + g]], w=[ph.k])
                    P.op("act", lambda e: e.activation(out=hd[:, cc, 0:127], in_=ph[:, 0:127], func=AF.Gelu_apprx_tanh, bias=self.cbias[:, sel, cc:cc + 1]),
                         r=[ph.k, self.cbias.k], w=[hdk])
                if sel == 0:
                    pk = self.bank()
                    for cc in range(2):
                        P.op("pe", lambda e: e.matmul(pk[0:64, 0:127], lhsT=w2[:, (0 * 2 + cc) * 64:(0 * 2 + cc) * 64 + 64], rhs=hd[:, cc, 0:127], start=(cc == 0), stop=(cc == 1)),
                             r=[w2.k, hdk], w=[pk.k])
                    self.headnorm_rope(pk, 64, 127, self.vecs[0:64, VEC[("k_norm", 0)]:VEC[("k_norm", 0)] + 1],
                                       self.cos_b[0:64, 31:31 + 16 * 126 + 1:16], self.sin_b[0:64, 31:31 + 16 * 126 + 1:16], T,
                                       bias=self.vecs[0:64, VEC[("c_b2", 0)]:VEC[("c_b2", 0)] + 1])
                    P.op("dve", lambda e: e.tensor_tensor(out=self.KC[0:64, g, 0:127], in0=T["t1"][0][0:64, 0:127], in1=T["t2"][0][0:64, 0:127], op=ALU.add),
                         r=[T["t1"][1], T["t2"][1]], w=[self.KC.k])
                else:
                    pv = self.bank()
                    for cc in range(2):
                        P.op("pe", lambda e: e.matmul(pv[0:127, 0:64], lhsT=hd[:, cc, 0:127], rhs=w2[:, (1 * 2 + cc) * 64:(1 * 2 + cc) * 64 + 64], start=(cc == 0), stop=(cc == 1)),
                             r=[w2.k, hdk], w=[pv.k])
                    P.op("dve", lambda e: e.tensor_tensor(out=self.VC[0:127, g, 0:64], in0=pv[0:127, 0:64], in1=self.bvecs[0:127, BV_CB2V:BV_CB2V + 64], op=ALU.add),
                         r=[pv.k, self.bvecs.k], w=[self.VC.k])

    def b_layer(self, j):
        P = self.P
        v = self.phase_view
        KS, KSk = v(0, [128, 4, S], BF16), [Trk() for _ in range(4)]
        KW, KWk = v(16384, [128, 4, S], BF16), [Trk() for _ in range(4)]
        VS, VSk = v(32768, [128, 16, 4, 66], BF16), Trk()
        VW, VWk = v(41216, [128, 16, 4, 66], BF16), Trk()
        u, uk = v(49664, [128, 8, 512], BF16), Trk()
        Q, Qk = v(57856, [128, 4, 16, 128], BF16), [Trk() for _ in range(4)]
        Nk = [[Trk() for _ in range(4)] for _ in range(4)]
        oT, oTk = v(74240, [128, 8, 512], BF16), Trk()
        ob = [(v(82432 + i * 2048, [128, 1024], BF16), Trk()) for i in range(2)]
        PT = [(v(86528 + i * 1024, [128, 512], BF16), Trk()) for i in range(4)]
        sqn = [(v(90624 + i * 1024, [128, 512], BF16), Trk()) for i in range(2)]
        rstd = (v(92672, [128, 512], F32), Trk())
        gat, gatk = v(94720, [128, 4, 48], F32), Trk()
        T = {"sq": sqn[0], "rs": rstd, "qn": (v(95488, [128, 512], BF16), Trk()),
             "t1": (v(96512, [128, 512], F32), Trk()), "t2": (v(98560, [128, 512], F32), Trk())}
        rd, rdk = v(100608, [128, 16], F32), Trk()
        s3, s3k = v(100672, [128, 12], F32), Trk()
        sc, sck = v(100736, [128, 32], F32), Trk()
        sc2, sc2k = v(100864, [128, 32], F32), Trk()
        nsl, nslk = v(100992, [128, 32], F32), Trk()
        m8, m8k = v(101120, [128, 16], F32), Trk()
        acc, acck = v(101184, [128, 256], F32), Trk()
        trks = KSk + KWk + [VSk, VWk, uk, oTk, gatk, rdk, s3k, sck, sc2k, nslk, m8k, acck] + Qk + [k for _, k in ob] + [k for _, k in PT] \
            + [k for _, k in sqn] + [rstd[1]] + [T["qn"][1], T["t1"][1], T["t2"][1]] + [k for row in Nk for k in row]
        self.new_phase(trks)
        self.begin_phase([])
        self.pool_misc = [6, 7]
        pool_st = [0, 1, 2]
        for g in range(4):
            P.dma("sp", KS[0:64, g, :], self.KSd[0, :, g, :], r=self.ksd_k[0][g], w=[KSk[g]])
            P.dma("sp", KW[0:64, g, :], self.KSd[1, :, g, :], r=self.ksd_k[1][g], w=[KWk[g]])
        P.dma("sp", VS.rearrange("p a b c -> p (a b c)"), self.Vd[0].rearrange("p a b c -> p (a b c)"), r=self.vd_k[0], w=[VSk])
        P.dma("sp", VW.rearrange("p a b c -> p (a b c)"), self.Vd[1].rearrange("p a b c -> p (a b c)"), r=self.vd_k[1], w=[VWk])
        est = self.ph[64:96, 57856 // 4:57856 // 4 + 2048]
        allq = Qk + [k for row in Nk for k in row]
        P.dma("sp", est, self.cst_d[64:96, CST["eind"]:CST["eind"] + 2048], w=allq)
        for g in range(4):
            P.op("dve", lambda e: e.tensor_copy(out=KS[64:96, g, :], in_=est), r=allq, w=[KSk[g]])
            P.op("pool", lambda e: e.memset(KW[64:96, g, :], 0.0), w=[KWk[g]])
        P.op("pool", lambda e: e.memset(Q[64:96].rearrange("p a b c -> p (a b c)"), 0.0), w=allq)
        gbo = BV_GB + 48 * j
        npt = 0
        nob = 0
        for n in range(NT):
            tsl = slice(n * TT, (n + 1) * TT)
            self.norm_tile(n, ("b_norm", j), u, uk, sqn, rstd)
            wg = self.next_w(("bg", j))
            for qb in range(4):
                pg = self.bank(self.pool_misc)
                for k in range(NCH):
                    P.op("pe", lambda e: e.matmul(pg[:, 0:48], lhsT=u[:, k, qb * 128:(qb + 1) * 128], rhs=wg[:, k * 48:(k + 1) * 48], start=(k == 0), stop=(k == NCH - 1)),
                         r=[wg.k, uk], w=[pg.k])
                P.op("dve", lambda e: e.tensor_tensor(out=gat[:, qb, :], in0=pg[:, 0:48], in1=self.bvecs[:, gbo:gbo + 48], op=ALU.add), r=[pg.k, self.bvecs.k], w=[gatk])
            P.op("act", lambda e: e.activation(out=gat, in_=gat, func=AF.Sigmoid), r=[gatk], w=[gatk])
            for m in range(NCH):
                w = self.next_w(("bq", j, m))
                pq = self.bank(pool_st)
                for k in range(NCH):
                    P.op("pe", lambda e: e.matmul(pq[:], lhsT=w[:, k * 128:(k + 1) * 128], rhs=u[:, k, :], start=(k == 0), stop=(k == NCH - 1)), r=[w.k, uk], w=[pq.k])
                self.headnorm_rope(pq, 128, 512, self.vec(("q_norm", j)), self.cos_b[:, tsl], self.sin_b[:, tsl], T)
                for hh in range(2):
                    h = 2 * m + hh
                    rsl = slice(hh * 64, (hh + 1) * 64)
                    eng = "dve" if hh == 0 else "pool"
                    P.op(eng, lambda e: e.tensor_tensor(out=Q[0:64, :, h, :], in0=T["t1"][0][rsl, :].rearrange("p (a b) -> p a b", a=4),
                                                        in1=T["t2"][0][rsl, :].rearrange("p (a b) -> p a b", a=4), op=ALU.add),
                         r=[T["t1"][1], T["t2"][1]], w=Qk)
            self.attn_tile(n, dict(KS=KS, KSk=KSk, KW=KW, KWk=KWk, VS=VS, VSk=VSk, VW=VW, VWk=VWk, Q=Q, Qk=Qk, Nk=Nk, oT=oT, oTk=oTk, ob=ob, PT=PT,
                                   gat=gat, gatk=gatk, rd=rd, rdk=rdk, s3=s3, s3k=s3k, sc=sc, sck=sck, sc2=sc2, sc2k=sc2k, nsl=nsl, nslk=nslk,
                                   m8=m8, m8k=m8k, acc=acc, acck=acck, pool_st=pool_st))
            for mm in range(NCH):
                w = self.next_w(("bo", j, mm))
                po = self.bank(pool_st)
                for k in range(NCH):
                    P.op("pe", lambda e: e.matmul(po[:], lhsT=w[:, k * 128:(k + 1) * 128], rhs=oT[:, k, :], start=(k == 0), stop=(k == NCH - 1)), r=[w.k, oTk], w=[po.k])
                hsl = self.hT[:, mm, tsl]
                P.op("dve", lambda e: e.tensor_tensor(out=hsl, in0=po[:], in1=hsl, op=ALU.add), r=[po.k, self.hk[mm][n]], w=[self.hk[mm][n]])
        self.pool_misc = [0, 1, 2, 3, 4, 5, 6, 7]

    def attn_tile(self, n, C):
        P = self.P
        KS, KSk, KW, KWk, VS, VSk, VW, VWk = C["KS"], C["KSk"], C["KW"], C["KWk"], C["VS"], C["VSk"], C["VW"], C["VWk"]
        Q, Qk, Nk, oT, oTk, ob, PT = C["Q"], C["Qk"], C["Nk"], C["oT"], C["oTk"], C["ob"], C["PT"]
        gat, gatk, rd, rdk, s3, s3k = C["gat"], C["gatk"], C["rd"], C["rdk"], C["s3"], C["s3k"]
        sc, sck, sc2, sc2k, nsl, nslk, m8, m8k, acc, acck = C["sc"], C["sck"], C["sc2"], C["sc2k"], C["nsl"], C["nslk"], C["m8"], C["m8k"], C["acc"], C["acck"]
        pool_st = C["pool_st"]
        poc, pos, pow_ = self.ps[3], self.ps[4], self.ps[5]
        st = {"npt": 0}
        pairs = [(qb, g) for qb in range(4) for g in range(4)]
        jobs = []

        def r4(ap):
            return ap.rearrange("p (a b) -> p a b", a=4)

        def mk_job(kind, qb, g, kt, first, last, pidx):
            qbg = 4 * n + qb
            job = {"pre": [], "post": []}
            box = {}

            def st1():
                pst = self.bank(pool_st)
                pt, ptk = PT[st["npt"] % 4]
                st["npt"] += 1
                box["pt"], box["ptk"] = pt, ptk
                Q96 = Q[0:96, qb, 4 * g:4 * g + 4, :]
                if kind == "c":
                    P.op("pe", lambda e: e.matmul(pst[0:127, :], lhsT=self.KC[0:96, g, 0:127], rhs=Q96, start=True, stop=True), r=[self.KC.k, Qk[qb]], w=[pst.k])
                    P.op("act", lambda e: e.activation(out=pt[0:127, :], in_=pst[0:127, :], func=AF.Exp, scale=SCALE), r=[pst.k], w=[ptk])
                    cm = self.cmask_b[0:127, qbg * 128:(qbg + 1) * 128].unsqueeze(1).broadcast_to([127, 4, 128])
                    P.op("pool", lambda e: e.tensor_tensor(out=r4(pt[0:127, :]), in0=r4(pt[0:127, :]), in1=cm, op=ALU.mult), r=[ptk, self.cmask_b.k], w=[ptk])
                    return
                if kind == "s":
                    P.op("pe", lambda e: e.matmul(pst[:], lhsT=KS[0:96, g, kt * 128:(kt + 1) * 128], rhs=Q96, start=True, stop=True),
                         r=[KSk[g], Qk[qb], Nk[qb][g]], w=[pst.k])
                else:
                    P.op("pe", lambda e: e.matmul(pst[:], lhsT=KW[0:96, g, kt * 128:(kt + 1) * 128], rhs=Q96, start=True, stop=True), r=[KWk[g], Qk[qb]], w=[pst.k])
                P.op("act", lambda e: e.activation(out=pt, in_=pst[:], func=AF.Exp, scale=SCALE), r=[pst.k], w=[ptk])
                msk = None
                if kt == qbg:
                    msk = self.triL_b
                elif kind == "w" and kt == qbg - 4:
                    msk = self.triU_b
                if msk is not None:
                    tm = msk[:].unsqueeze(1).broadcast_to([128, 4, 128])
                    P.op("pool", lambda e: e.tensor_tensor(out=r4(pt), in0=r4(pt), in1=tm, op=ALU.mult), r=[ptk, msk.k], w=[ptk])

            def st2():
                pt, ptk = box["pt"], box["ptk"]
                for h in range(4):
                    if kind == "c":
                        P.op("pe", lambda e: e.matmul(poc[:, h * 128:h * 128 + 97], lhsT=pt[0:127, h * 128:(h + 1) * 128], rhs=self.VC[0:127, g, 0:97],
                                                      start=(h == 0), stop=(h == 3), skip_group_check=True), r=[ptk, self.VC.k], w=[poc.k])
                    elif kind == "s":
                        P.op("pe", lambda e: e.matmul(pos[:, h * 128:h * 128 + 65], lhsT=pt[:, h * 128:(h + 1) * 128], rhs=VS[:, kt, g, 0:65],
                                                      start=(first and h == 0), stop=(last and h == 3), skip_group_check=True), r=[ptk, VSk], w=[pos.k])
                    else:
                        P.op("pe", lambda e: e.matmul(pow_[:, h * 128:h * 128 + 65], lhsT=pt[:, h * 128:(h + 1) * 128], rhs=VW[:, kt, g, 0:65],
                                                      start=(first and h == 0), stop=(last and h == 3), skip_group_check=True), r=[ptk, VWk], w=[pow_.k])

            job["st1"], job["st2"] = st1, st2
            return job

        def select_chain(qb, g, pidx):
            qbg = 4 * n + qb
            pcs = self.pocs[pidx % 2]
            P.op("act", lambda e: e.activation(out=pcs[:], in_=r4(poc[:])[:, :, 0:97], func=AF.Copy), r=[poc.k], w=[pcs.k])
            P.op("dve", lambda e: e.tensor_scalar(out=rd[:, 0:4], in0=pcs[:, :, 64], scalar1=1e-30, scalar2=None, op0=ALU.max), r=[pcs.k], w=[rdk])
            P.op("dve", lambda e: e.reciprocal(out=rd[:, 0:4], in_=rd[:, 0:4]), r=[rdk], w=[rdk])
            for h in range(4):
                src1 = self.selb[:, qbg, :] if h == 0 else sc
                P.op("dve", lambda e: e.scalar_tensor_tensor(out=sc, in0=pcs[:, h, 65:97], scalar=rd[:, h:h + 1], in1=src1, op0=ALU.mult, op1=ALU.add),
                     r=[pcs.k, rdk, sck, self.selb.k], w=[sck])
            P.op("dve", lambda e: e.max(out=m8[:, 0:8], in_=sc), r=[sck], w=[m8k])
            P.op("dve", lambda e: e.match_replace(out=sc2, in_to_replace=m8[:, 0:8], in_values=sc, imm_value=-3.0e38), r=[sck, m8k], w=[sc2k])
            P.op("dve", lambda e: e.max(out=m8[:, 8:16], in_=sc2), r=[sc2k], w=[m8k])
            P.op("dve", lambda e: e.tensor_scalar(out=nsl, in0=sc, scalar1=m8[:, 15:16], scalar2=NEGM, op0=ALU.is_lt, op1=ALU.mult), r=[sck, m8k], w=[nslk])

        def nsel_install(qb, g):
            pm = self.bank(self.pool_misc)
            P.op("pe", lambda e: e.transpose(out=pm[0:32, 0:128], in_=nsl, identity=self.ident_f[:]), r=[nslk, self.ident_f.k], w=[pm.k])
            P.op("act", lambda e: e.activation(out=Q[64:96, qb, 4 * g:4 * g + 4, :], in_=pm[0:32, 0:128].unsqueeze(1).broadcast_to([32, 4, 128]), func=AF.Copy),
                 r=[pm.k], w=[Nk[qb][g]])

        def evac_win():
            P.op("act", lambda e: e.activation(out=self.pows[:], in_=r4(pow_[:])[:, :, 0:65], func=AF.Copy), r=[pow_.k], w=[self.pows.k])

        def combine(qb, g, pidx):
            pcs = self.pocs[pidx % 2]
            o, ok_ = ob[qb % 2]
            P.op("dve", lambda e: e.tensor_scalar(out=rd[:, 4:8], in0=pcs[:, :, 64], scalar1=1e-30, scalar2=None, op0=ALU.max), r=[pcs.k], w=[rdk])
            P.op("dve", lambda e: e.tensor_scalar(out=rd[:, 8:12], in0=r4(pos[:])[:, :, 64], scalar1=1e-30, scalar2=None, op0=ALU.max), r=[pos.k], w=[rdk])
            P.op("dve", lambda e: e.tensor_scalar(out=rd[:, 12:16], in0=self.pows[:, :, 64], scalar1=1e-30, scalar2=None, op0=ALU.max), r=[self.pows.k], w=[rdk])
            P.op("dve", lambda e: e.reciprocal(out=rd[:, 4:16], in_=rd[:, 4:16]), r=[rdk], w=[rdk])
            P.op("dve", lambda e: e.tensor_tensor(out=s3.rearrange("p (a b) -> p a b", a=3), in0=rd[:, 4:16].rearrange("p (a b) -> p a b", a=3),
                                                  in1=gat[:, qb, :].rearrange("p (a b) -> p a b", a=3)[:, :, 4 * g:4 * g + 4], op=ALU.mult), r=[rdk, gatk], w=[s3k])
            for h in range(4):
                ah = acc[:, h * 64:(h + 1) * 64]
                P.op("dve", lambda e: e.tensor_scalar(out=ah, in0=pcs[:, h, 0:64], scalar1=s3[:, h:h + 1], scalar2=None, op0=ALU.mult), r=[pcs.k, s3k], w=[acck])
                P.op("dve", lambda e: e.scalar_tensor_tensor(out=ah, in0=self.pows[:, h, 0:64], scalar=s3[:, 8 + h:9 + h], in1=ah, op0=ALU.mult, op1=ALU.add),
                     r=[self.pows.k, s3k, acck], w=[acck])
                P.op("dve", lambda e: e.scalar_tensor_tensor(out=o[:, g * 256 + h * 64:g * 256 + (h + 1) * 64], in0=pos[:, h * 128:h * 128 + 64], scalar=s3[:, 4 + h:5 + h], in1=ah,
                                                             op0=ALU.mult, op1=ALU.add), r=[pos.k, s3k, acck], w=[ok_])

        def o_transpose(qb):
            o, ok_ = ob[qb % 2]
            pT = self.bank(self.pool_misc)
            pTb = pT[:].bitcast(BF16)
            for c in range(NCH):
                P.op("pe", lambda e: e.transpose(out=pTb[:, c * 128:(c + 1) * 128], in_=o[:, c * 128:(c + 1) * 128], identity=self.ident_b[:]), r=[ok_, self.ident_b.k], w=[pT.k])
            P.op("act", lambda e: e.activation(out=oT[:, :, qb * 128:(qb + 1) * 128], in_=pTb.rearrange("p (a b) -> p a b", a=8), func=AF.Copy), r=[pT.k], w=[oTk])

        def cjob(pidx):
            qb, g = pairs[pidx]
            j = mk_job("c", qb, g, 0, True, True, pidx)
            j["post"].append(lambda: select_chain(qb, g, pidx))
            return j

        jobs.append(cjob(0))
        pending_T = []
        for pidx, (qb, g) in enumerate(pairs):
            qbg = 4 * n + qb
            k0 = max(0, qbg - 4)
            wj = [mk_job("w", qb, g, kt, kt == k0, kt == qbg, pidx) for kt in range(k0, qbg + 1)]
            for f in pending_T:
                wj[min(2, len(wj) - 1)]["post"].append(f)
            pending_T = []
            wj[-1]["post"].append(evac_win)
            jobs += wj
            if pidx + 1 < len(pairs):
                jobs.append(cjob(pidx + 1))
            sj = [mk_job("s", qb, g, kt, kt == 0, kt == qbg, pidx) for kt in range(qbg + 1)]
            sj[0]["pre"].append(lambda qb=qb, g=g: nsel_install(qb, g))
            sj[-1]["post"].append(lambda qb=qb, g=g, pidx=pidx: combine(qb, g, pidx))
            jobs += sj
            if g == 3:
                pending_T.append(lambda qb=qb: o_transpose(qb))
        LA = 2
        for j in range(min(LA, len(jobs))):
            for f in jobs[j]["pre"]:
                f()
            jobs[j]["st1"]()
        for i in range(len(jobs)):
            if i + LA < len(jobs):
                for f in jobs[i + LA]["pre"]:
                    f()
                jobs[i + LA]["st1"]()
            jobs[i]["st2"]()
            for f in jobs[i]["post"]:
                f()
        for f in pending_T:
            f()

    def make_plan(self):
        plan = []
        ser = -1
        for b in range(self.nb):
            for layer in range(self.n_layers):
                if layer < 2:
                    ser += 1
                    plan += [(("a_in", layer, c), ser) for c in range(8)]
                    plan += [(("a_out", layer, m), ser) for m in range(8)]
                else:
                    if layer == 2:
                        ser += 1
                        plan += [(("kvk", i), ser) for i in range(4)] + [(("kvc", i), ser) for i in range(4)] + [(("kvv", i), ser) for i in range(2)]
                        plan += [(("cw2",), ser)]
                    ser += 1
                    for n in range(NT):
                        plan += [(("bg", layer - 2), ser)] + [(("bq", layer - 2, m), ser) for m in range(8)] + [(("bo", layer - 2, m), ser) for m in range(8)]
                ser += 1
                for t2 in range(2):
                    plan += [(("f_in", layer, c), ser) for c in range(FC)]
                    plan += [(("f_out", layer, m), ser) for m in range(8)]
        return plan

    def build(self):
        P = self.P
        self.convc = self.sb("convc", [128, 8, 3], F32)
        self.hst = self.sb("hst", [128, 8], F32)
        self.prologue()
        self.plan_weights(self.make_plan())
        phase_no = {"a0": 0, "f0": 1, "a1": 2, "f1": 3, "kv": 4, "b0": 5, "f2": 6, "b1": 7, "f3": 8}
        for b in range(self.nb):
            def pre(tag):
                if b == 0:
                    self.convert_phase(phase_no[tag] + 1)
            if b == 0:
                self.convert_phase(0)
            self.mark("load%d" % b)
            self.load_x(b)
            for layer in range(self.n_layers):
                if layer < 2:
                    pre("a%d" % layer)
                    self.mark("a%d.%d" % (b, layer))
                    self.a_layer(layer)
                else:
                    if layer == 2:
                        pre("kv")
                        self.mark("kv%d" % b)
                        self.kv_phase()
                    pre("b%d" % (layer - 2))
                    self.mark("b%d.%d" % (b, layer))
                    self.b_layer(layer - 2)
                pre("f%d" % layer)
                self.mark("f%d.%d" % (b, layer))
                self.ffn_layer(layer)
            self.mark("store%d" % b)
            self.store_y(b)
        self.mark("end")
        P.wait_all("sp", self.out_trks)
        return self.nc


_CACHE = {}


def _prep_inputs(inputs):
    L = weight_layout()
    wall = pack_weights(inputs, L)
    vecs, bvecs = pack_vecs(inputs)
    cst = make_consts()
    return wall, vecs, bvecs, cst


def kernel(**inputs):
    inputs = {k: np.asarray(v) for k, v in inputs.items()}
    x = np.ascontiguousarray(inputs["x"], dtype=np.float32)
    B = x.shape[0]
    nb = B // NCORES
    wall, vecs, bvecs, cst = _prep_inputs(inputs)
    nc = Builder(nb).build()
    in_maps = []
    for c in range(NCORES):
        in_maps.append({"x": np.ascontiguousarray(x[c * nb:(c + 1) * nb]), "wall": wall, "vecs": vecs, "bvecs": bvecs, "cst": cst})
    res = run_bass_kernel_spmd(nc, in_maps, core_ids=list(range(NCORES)))
    out = np.concatenate([np.asarray(r["y"]).reshape(nb, S, D) for r in res.results], axis=0)
    return out.astype(np.float32)
```

```python
import numpy as np
import concourse.bass as bass
import concourse.mybir as mybir
from concourse.bass_utils import run_bass_kernel_spmd
from concourse.alu_op_type import AluOpType as ALU

F32 = mybir.dt.float32
BF16 = mybir.dt.bfloat16
AF = mybir.ActivationFunctionType

S = 2048
D = 1024
NCH = 8
TT = 512
NT = S // TT
FH = 2816
FC = 22
NCORES = 8
EPS = 1e-6
CONV_CH = 128 * 2048
SCALE = 0.125
NEGM = -30000.0


def _kt(W, c0, width=128):
    K = W.shape[0]
    return np.ascontiguousarray(W[:, c0:c0 + width].reshape(K // 128, 128, width).transpose(1, 0, 2)).reshape(128, -1)


class WLayout:
    def __init__(self):
        self.off = {}
        self.free = {}
        self.n = 0

    def add(self, name, free):
        self.off[name] = self.n
        self.free[name] = free
        self.n += 128 * free

    def total(self):
        return ((self.n + CONV_CH - 1) // CONV_CH) * CONV_CH


def weight_layout():
    L = WLayout()
    L.phase_end = []

    def a(l):
        for c in range(8):
            L.add(("a_in", l, c), 2304)
        for m in range(8):
            L.add(("a_out", l, m), 1024)
        L.phase_end.append(L.n)

    def f(l):
        for c in range(FC):
            L.add(("f_in", l, c), 2048)
        for m in range(8):
            L.add(("f_out", l, m), FH)
        L.phase_end.append(L.n)

    def b(j):
        L.add(("bg", j), 384)
        for m in range(8):
            L.add(("bq", j, m), 1024)
        for m in range(8):
            L.add(("bo", j, m), 1024)
        L.phase_end.append(L.n)

    a(0)
    f(0)
    a(1)
    f(1)
    for i in range(4):
        L.add(("kvk", i), 1024)
    for i in range(4):
        L.add(("kvc", i), 1024)
    for i in range(2):
        L.add(("kvv", i), 2048)
    for i in range(2):
        L.add(("cw1", i), 8192)
    L.add(("cw2",), 256)
    L.phase_end.append(L.n)
    b(0)
    f(2)
    b(1)
    f(3)
    return L


def pack_weights(inp, L):
    out = np.zeros(L.total(), np.float32)

    def put(name, arr):
        arr = np.asarray(arr, np.float32).reshape(128, -1)
        assert arr.shape[1] == L.free[name], (name, arr.shape)
        out[L.off[name]:L.off[name] + arr.size] = arr.reshape(-1)

    for l in range(2):
        Win = inp["a_w_in"][l]
        for c in range(8):
            put(("a_in", l, c), np.concatenate(
                [_kt(Win, c * 128), _kt(Win, 1024 + c * 128), inp["a_gate_w"][l][0, c], inp["a_gate_w"][l][1, c]], axis=1))
        for m in range(8):
            put(("a_out", l, m), _kt(inp["a_w_out"][l], m * 128))
    for l in range(4):
        W = inp["f_w_in"][l]
        for c in range(FC):
            put(("f_in", l, c), np.concatenate([_kt(W, c * 128), _kt(W, FH + c * 128)], axis=1))
        for m in range(8):
            put(("f_out", l, m), _kt(inp["f_w_out"][l], m * 128))
    kvw = inp["kv_w"]
    i = 0
    for jj in (2, 4):
        for cc in range(2):
            put(("kvk", i), _kt(kvw, jj * 256 + cc * 128))
            i += 1
    i = 0
    for jj in (0, 1):
        for cc in range(2):
            put(("kvc", i), _kt(kvw, jj * 256 + cc * 128))
            i += 1
    for i, jj in enumerate((3, 5)):
        put(("kvv", i), _kt(kvw, jj * 256, 256))
    for kv in range(2):
        t = np.zeros((128, 32, 256), np.float32)
        t[:64] = inp["cmp_w1"][kv].reshape(32, 64, 256).transpose(1, 0, 2)
        put(("cw1", kv), t)
    t = np.zeros((128, 2, 2, 64), np.float32)
    for kv in range(2):
        t[:, kv] = inp["cmp_w2"][kv].reshape(2, 128, 64).transpose(1, 0, 2)
    put(("cw2",), t)
    for j in range(2):
        put(("bg", j), _kt(inp["b_w_in"][j], 1024, 48))
        for m in range(8):
            put(("bq", j, m), _kt(inp["b_w_in"][j], m * 128))
        for m in range(8):
            put(("bo", j, m), _kt(inp["b_w_out"][j], m * 128))
    return out


VEC = {}
_nv = 0


def _vadd(name, n):
    global _nv
    VEC[name] = _nv
    _nv += n


for _l in range(2):
    _vadd(("a_norm", _l), 8)
    for _k in range(4):
        _vadd(("a_cw", _l, _k), 8)
    _vadd(("a_cb", _l), 8)
    _vadd(("a_gb", _l, 0), 8)
    _vadd(("a_gb", _l, 1), 8)
    _vadd(("a_lam", _l), 8)
for _l in range(4):
    _vadd(("f_norm", _l), 8)
_vadd(("kv_norm",), 8)
for _j in range(2):
    _vadd(("b_norm", _j), 8)
    _vadd(("q_norm", _j), 1)
for _i in range(3):
    _vadd(("k_norm", _i), 1)
for _i in range(2):
    _vadd(("c_b1", _i), 2)
    _vadd(("c_b2", _i), 1)
    _vadd(("c_pos", _i), 32)
NV = _nv
BV_GB = 0
BV_CB2V = 96
NBV = 160


def pack_vecs(inp):
    v = np.zeros((128, NV), np.float32)

    def fm(x):
        return np.asarray(x, np.float32).reshape(8, 128).T

    for l in range(2):
        v[:, VEC[("a_norm", l)]:][:, :8] = fm(inp["a_norm"][l])
        for k in range(4):
            v[:, VEC[("a_cw", l, k)]:][:, :8] = fm(inp["a_conv_w"][l][k])
        v[:, VEC[("a_cb", l)]:][:, :8] = fm(inp["a_conv_b"][l])
        v[:, VEC[("a_gb", l, 0)]:][:, :8] = fm(inp["a_gate_b"][l][0])
        v[:, VEC[("a_gb", l, 1)]:][:, :8] = fm(inp["a_gate_b"][l][1])
        v[:, VEC[("a_lam", l)]:][:, :8] = fm(inp["a_lambda"][l])
    for l in range(4):
        v[:, VEC[("f_norm", l)]:][:, :8] = fm(inp["f_norm"][l])
    v[:, VEC[("kv_norm",)]:][:, :8] = fm(inp["kv_norm"])
    for j in range(2):
        v[:, VEC[("b_norm", j)]:][:, :8] = fm(inp["b_norm"][j])
        v[:, VEC[("q_norm", j)]] = np.tile(np.asarray(inp["q_norm"][j], np.float32), 2)
    for i in range(3):
        v[:, VEC[("k_norm", i)]] = np.tile(np.asarray(inp["k_norm"][i], np.float32), 2)
    for i in range(2):
        v[:, VEC[("c_b1", i)]:][:, :2] = np.asarray(inp["cmp_b1"][i], np.float32).reshape(2, 128).T
        v[:, VEC[("c_b2", i)]] = np.tile(np.asarray(inp["cmp_b2"][i], np.float32), 2)
        v[:64, VEC[("c_pos", i)]:][:, :32] = np.asarray(inp["cmp_pos"][i], np.float32).T
    bv = np.zeros((128, NBV), np.float32)
    for j in range(2):
        bv[:, BV_GB + 48 * j: BV_GB + 48 * j + 48] = np.asarray(inp["b_gate_b"][j], np.float32)[None, :]
    bv[:, BV_CB2V:BV_CB2V + 64] = np.asarray(inp["cmp_b2"][1], np.float32)[None, :]
    return v, bv


CST = {}
_nc_ = 0


def _cadd(name, n):
    global _nc_
    CST[name] = _nc_
    _nc_ += n


_cadd("ident", 128)
_cadd("prot", 128)
_cadd("bones", 128)
_cadd("triL", 128)
_cadd("triU", 128)
_cadd("ovl", 32)
_cadd("cos", 2048)
_cadd("sin", 2048)
_cadd("cmask", 2048)
_cadd("eind", 2048)
_cadd("selb", 512)
NCST = _nc_


def make_consts():
    c = np.zeros((128, NCST), np.float32)
    p = np.arange(128)
    c[:, CST["ident"]:][:, :128] = np.eye(128)
    perm = (p // 64) * 64 + ((p % 64) + 32) % 64
    pr = np.zeros((128, 128), np.float32)
    pr[perm, p] = 1.0
    c[:, CST["prot"]:][:, :128] = pr
    c[:, CST["bones"]:][:, :128] = (p[:, None] // 64 == p[None, :] // 64)
    c[:, CST["triL"]:][:, :128] = (p[:, None] <= p[None, :])
    c[:, CST["triU"]:][:, :128] = (p[:, None] > p[None, :])
    cs = np.arange(127) * 16
    sl = np.arange(32) * 64
    ov = np.clip(np.minimum(cs[:, None] + 32, sl[None, :] + 64) - np.maximum(cs[:, None], sl[None, :]), 0, None) / 32.0
    c[:127, CST["ovl"]:][:, :32] = ov
    t = np.arange(2048, dtype=np.float64)
    fi = (p % 64) % 32
    freqs = (10000.0 ** (-(np.arange(32, dtype=np.float32) / np.float32(32)))).astype(np.float32)
    ang = (t[None, :].astype(np.float32) * freqs[fi][:, None]).astype(np.float32)
    c[:, CST["cos"]:][:, :2048] = np.cos(ang)
    sg = np.where((p % 64) < 32, -1.0, 1.0)[:, None]
    c[:, CST["sin"]:][:, :2048] = np.sin(ang) * sg
    cl = np.arange(127) * 16 + 31
    c[:127, CST["cmask"]:][:, :2048] = (cl[:, None] <= t[None, :])
    b = np.arange(32)
    c[64:96, CST["eind"]:][:, :2048] = (t[None, :].astype(np.int64) // 64 == b[:, None])
    sb = np.zeros((128, 16, 32), np.float32)
    for qb in range(16):
        tq = qb * 128 + p
        cur = (tq // 64)[:, None]
        causal = b[None, :] <= cur
        forced = (b[None, :] == 0) | (causal & (cur - b[None, :] < 2))
        sb[:, qb, :] = np.where(forced, 1e30, np.where(causal, 0.0, -1e30))
    c[:, CST["selb"]:][:, :512] = sb.reshape(128, 512)
    return c


class Trk:
    __slots__ = ("w", "r")

    def __init__(self):
        self.w = None
        self.r = {}


NDS = 24


class Prog:
    def __init__(self, nc):
        self.nc = nc
        self.eng = {"pe": nc.tensor, "act": nc.scalar, "dve": nc.vector, "pool": nc.gpsimd, "sp": nc.sync}
        self.sem = {e: nc.alloc_semaphore(name="s_" + e) for e in ("pe", "act", "dve", "pool")}
        self.cnt = {e: 0 for e in ("pe", "act", "dve", "pool")}
        self.seen = {e: {} for e in self.eng}
        self.dsem = [nc.alloc_semaphore(name="s_dma%d" % i) for i in range(NDS)]
        self.dn = 0
        self.ninst = 0

    def _semof(self, key):
        return self.sem[key] if isinstance(key, str) else self.dsem[key[1]]

    def _wait(self, e, key, val):
        if key == "pe" and e == "pe":
            return
        if self.seen[e].get(key, 0) >= val:
            return
        self.eng[e].wait_ge(self._semof(key), val)
        self.seen[e][key] = val

    def _deps(self, e, r, w):
        for t in r:
            if t.w is not None:
                self._wait(e, t.w[0], t.w[1])
        for t in w:
            if t.w is not None:
                self._wait(e, t.w[0], t.w[1])
            for k, v in t.r.items():
                self._wait(e, k, v)

    def op(self, e, fn, r=(), w=()):
        self._deps(e, r, w)
        inst = fn(self.eng[e])
        self.cnt[e] += 1
        self.ninst += 1
        inst.then_inc(self.sem[e], 1)
        v = self.cnt[e]
        for t in r:
            t.r[e] = v
        for t in w:
            t.w = (e, v)
            t.r = {}
        return inst

    def dma(self, q, out, in_, r=(), w=()):
        self._deps(q, r, w)
        i = self.dn % NDS
        gen = self.dn // NDS
        key = ("d", i)
        if gen > 0:
            self._wait(q, key, 16 * gen)
        inst = self.eng[q].dma_start(out=out, in_=in_)
        inst.then_inc(self.dsem[i], 16)
        self.dn += 1
        self.ninst += 1
        v = 16 * (gen + 1)
        for t in r:
            t.r[key] = v
        for t in w:
            t.w = (key, v)
            t.r = {}
        return (key, v)

    def wait_all(self, e, trks):
        self._deps(e, trks, trks)


class Buf:
    def __init__(self, t):
        self.t = t
        self.k = Trk()

    def __getitem__(self, idx):
        return self.t[idx]


class Builder:
    def __init__(self, nb, n_layers=4, debug_out=None):
        self.nb = nb
        self.n_layers = n_layers
        self.L = weight_layout()
        nc = bass.Bass("TRN2", target_bir_lowering=False)
        self.nc = nc
        self.P = Prog(nc)
        NW = self.L.total()
        self.x = nc.dram_tensor("x", [nb, S, D], F32, kind="ExternalInput").ap()
        self.wall = nc.dram_tensor("wall", [NW], F32, kind="ExternalInput").ap()
        self.vecs_d = nc.dram_tensor("vecs", [128, NV], F32, kind="ExternalInput").ap()
        self.bvecs_d = nc.dram_tensor("bvecs", [128, NBV], F32, kind="ExternalInput").ap()
        self.cst_d = nc.dram_tensor("cst", [128, NCST], F32, kind="ExternalInput").ap()
        self.y = nc.dram_tensor("y", [nb, S, D], F32, kind="ExternalOutput").ap()
        self.wbf = nc.dram_tensor("wbf", [NW], BF16, kind="Internal").ap()
        self.wchunk = [Trk() for _ in range(NW // CONV_CH)]
        self.out_trks = []
        self.marks = []

        def sb(name, shape, dt):
            return Buf(nc.alloc_sbuf_tensor(name, shape, dt))

        self.sb = sb
        self.hT = nc.alloc_sbuf_tensor("hT", [128, NCH, S], F32)
        self.hk = [[Trk() for _ in range(NT)] for _ in range(NCH)]
        self.vecs = sb("vecs_sb", [128, NV], F32)
        self.bvecs = sb("bvecs_sb", [128, NBV], F32)
        self.ident_f = sb("ident_f", [128, 128], F32)
        self.ident_b = sb("ident_b", [128, 128], BF16)
        self.ones_b = sb("ones_b", [128, 128], BF16)
        self.coef = sb("coef", [128, 2, 2, 8], F32)
        self.hgb = sb("hgb", [128, 2, 2, 8], F32)
        self.lnhalf = sb("lnhalf", [128, 2], F32)
        self.cos_b = sb("cos_b", [128, S], BF16)
        self.sin_b = sb("sin_b", [128, S], BF16)
        self.cmask_b = sb("cmask_b", [128, S], BF16)
        self.triL_b = sb("triL_b", [128, 128], BF16)
        self.triU_b = sb("triU_b", [128, 128], BF16)
        self.prot_b = sb("prot_b", [128, 128], BF16)
        self.bones_b = sb("bones_b", [128, 128], BF16)
        self.selb = sb("selb", [128, 16, 32], F32)
        self.KC = sb("KC", [128, 4, 128], BF16)
        self.VC = sb("VC", [128, 4, 98], BF16)
        self.posb = sb("posb", [128, 2, 32], BF16)
        self.cbias = sb("cbias", [128, 2, 2], F32)
        self.pocs = [sb("pocs%d" % i, [128, 4, 97], F32) for i in range(2)]
        self.pows = sb("pows", [128, 4, 65], F32)
        self.KSd = nc.dram_tensor("KSd", [2, 64, 4, S], BF16, kind="Internal").ap()
        self.Vd = nc.dram_tensor("Vd", [2, 128, 16, 4, 66], BF16, kind="Internal").ap()
        self.ksd_k = [[[Trk() for _ in range(NT)] for _ in range(4)] for _ in range(2)]
        self.vd_k = [[Trk() for _ in range(NT)] for _ in range(2)]
        self.ps = [Buf(nc.alloc_psum_tensor("ps%d" % i, [128, 512], F32)) for i in range(8)]
        self.ps_rr = 0
        self.pool_misc = [0, 1, 2, 3, 4, 5, 6, 7]
        self.NSLOT = 3
        self.ring = [sb("wring%d" % i, [128, FH], BF16) for i in range(self.NSLOT)]
        self.perm_ids = set(id(b) for b in self.ring)
        self.PHB = 100 * 1024
        self.ph = nc.alloc_sbuf_tensor("phase", [128, self.PHB // 4], F32)
        self.phk = {}

    def bank(self, pool=None):
        if pool is None:
            b = self.ps[self.ps_rr % 8]
        else:
            b = self.ps[pool[self.ps_rr % len(pool)]]
        self.ps_rr += 1
        return b

    def vec(self, name, c=0):
        i = VEC[name] + c
        return self.vecs[:, i:i + 1]

    def phase_view(self, byte_off, shape, dt):
        esz = 4 if dt == F32 else 2
        n = int(np.prod(shape[1:]))
        assert byte_off % 4 == 0 and byte_off + n * esz <= self.PHB, (byte_off, shape)
        if dt == F32:
            ap = self.ph[:, byte_off // 4: byte_off // 4 + n]
        else:
            ap = self.ph[:].bitcast(BF16)[:, byte_off // 2: byte_off // 2 + n]
        if len(shape) == 2:
            return ap
        names = " ".join("a%d" % i for i in range(len(shape) - 1))
        kw = {"a%d" % i: shape[i + 1] for i in range(len(shape) - 1)}
        return ap.rearrange("p (%s) -> p %s" % (names, names), **kw)

    def mark(self, name):
        self.marks.append((name, dict(self.P.cnt)))

    def new_phase(self, trks):
        merged = {}
        for t in self.phase_cur:
            if t.w is not None:
                merged[t.w[0]] = max(merged.get(t.w[0], 0), t.w[1])
            for k, v in t.r.items():
                merged[k] = max(merged.get(k, 0), v)
        for t in trks:
            t.w = None
            t.r = dict(merged)
        self.phase_cur = list(trks)

    def plan_weights(self, entries):
        self.wplan = list(entries)
        self.wplan_i = 0
        self.wq = []
        self.free_perm = list(self.ring)
        self.free_ext = []
        self.inuse = None
        self.serial = -1

    def begin_phase(self, ext_slots):
        self._release()
        self.serial += 1
        self.free_ext = list(ext_slots)
        self.ext_ids = set(id(b) for b in ext_slots)

    def _release(self):
        if self.inuse is not None:
            slot, ser = self.inuse
            if id(slot) in self.perm_ids:
                self.free_perm.append(slot)
            elif ser == self.serial:
                self.free_ext.append(slot)
            self.inuse = None

    def _topup(self):
        while self.wplan_i < len(self.wplan):
            name, ser = self.wplan[self.wplan_i]
            if ser == self.serial and self.free_ext:
                slot = self.free_ext.pop(0)
            elif self.free_perm:
                slot = self.free_perm.pop(0)
            else:
                break
            self.wplan_i += 1
            off = self.L.off[name]
            free = self.L.free[name]
            src = self.wbf[off:off + 128 * free].rearrange("(p f) -> p f", p=128)
            c0 = off // CONV_CH
            c1 = (off + 128 * free - 1) // CONV_CH
            self.P.dma("sp", slot[:, 0:free], src, r=[self.wchunk[c] for c in range(c0, c1 + 1)], w=[slot.k])
            self.wq.append((name, slot, ser))

    def next_w(self, name, hold=False):
        self._release()
        self._topup()
        n, slot, ser = self.wq.pop(0)
        assert n == name and ser == self.serial, (n, name, ser, self.serial)
        if not hold:
            self.inuse = (slot, ser)
        return slot

    def release_slot(self, slot):
        if id(slot) in self.perm_ids:
            self.free_perm.append(slot)
        elif id(slot) in self.ext_ids:
            self.free_ext.append(slot)

    def ext_slots(self, byte_off):
        out = []
        o = byte_off
        while o + 2 * FH <= self.PHB:
            b = Buf(self.phase_view(o, [128, FH], BF16))
            out.append(b)
            o += 2 * FH
        return out

    def prologue(self):
        P = self.P
        nc = self.nc
        P.dma("sp", self.vecs[:], self.vecs_d, w=[self.vecs.k])
        P.dma("sp", self.bvecs[:], self.bvecs_d, w=[self.bvecs.k])
        P.dma("sp", self.ident_f[:], self.cst_d[:, CST["ident"]:CST["ident"] + 128], w=[self.ident_f.k])
        P.op("dve", lambda e: e.tensor_copy(out=self.ident_b[:], in_=self.ident_f[:]), r=[self.ident_f.k], w=[self.ident_b.k])
        P.op("pool", lambda e: e.memset(self.ones_b[:], 1.0), w=[self.ones_b.k])
        tmp = self.sb("coef_tmp", [128, 16], F32)
        for l in range(2):
            lam = self.vecs[:, VEC[("a_lam", l)]:VEC[("a_lam", l)] + 8]
            P.op("act", lambda e: e.activation(out=tmp[:, l * 8:l * 8 + 8], in_=lam, func=AF.Exp, scale=-1.0), r=[self.vecs.k], w=[tmp.k])
            P.op("act", lambda e: e.activation(out=tmp[:, l * 8:l * 8 + 8], in_=tmp[:, l * 8:l * 8 + 8], func=AF.Ln, bias=1.0), r=[tmp.k], w=[tmp.k])
            P.op("dve", lambda e: e.tensor_scalar(out=self.coef[:, l, 0, :], in0=tmp[:, l * 8:l * 8 + 8], scalar1=-4.0, scalar2=None, op0=ALU.mult), r=[tmp.k], w=[self.coef.k])
            P.op("dve", lambda e: e.tensor_scalar(out=self.coef[:, l, 1, :], in0=tmp[:, l * 8:l * 8 + 8], scalar1=-8.0, scalar2=None, op0=ALU.mult), r=[tmp.k], w=[self.coef.k])
            for kk in range(2):
                gbo = VEC[("a_gb", l, kk)]
                P.op("dve", lambda e: e.tensor_scalar(out=self.hgb[:, l, kk, :], in0=self.vecs[:, gbo:gbo + 8], scalar1=0.5, scalar2=None, op0=ALU.mult), r=[self.vecs.k], w=[self.hgb.k])
        P.op("pool", lambda e: e.memset(self.lnhalf[:], -0.6931471805599453), w=[self.lnhalf.k])
        stg = self.phase_view(0, [128, 2048], F32)
        stk = Trk()
        self.phase_cur = [stk]
        self.conv_done = 0

        def ctab(name, n, dst, dstk, eng):
            P.dma("sp", stg[:, 0:n], self.cst_d[:, CST[name]:CST[name] + n], w=[stk])
            if eng == "act":
                P.op("act", lambda e: e.activation(out=dst, in_=stg[:, 0:n], func=AF.Copy), r=[stk], w=[dstk])
            else:
                P.op(eng, lambda e: e.tensor_copy(out=dst, in_=stg[:, 0:n]), r=[stk], w=[dstk])

        ctab("cos", 2048, self.cos_b[:], self.cos_b.k, "dve")
        ctab("sin", 2048, self.sin_b[:], self.sin_b.k, "act")
        ctab("cmask", 2048, self.cmask_b[:], self.cmask_b.k, "dve")
        ctab("triL", 128, self.triL_b[:], self.triL_b.k, "dve")
        ctab("triU", 128, self.triU_b[:], self.triU_b.k, "dve")
        ctab("prot", 128, self.prot_b[:], self.prot_b.k, "dve")
        ctab("bones", 128, self.bones_b[:], self.bones_b.k, "dve")
        P.dma("sp", self.selb[:].rearrange("p a b -> p (a b)"), self.cst_d[:, CST["selb"]:CST["selb"] + 512], w=[self.selb.k])
        P.op("pool", lambda e: e.memset(self.VC[:], 0.0), w=[self.VC.k])
        P.op("pool", lambda e: e.memset(self.KC[:], 0.0), w=[self.KC.k])
        P.op("pool", lambda e: e.memset(self.VC[:, :, 64:65], 1.0), w=[self.VC.k])
        P.dma("sp", stg[:, 0:32], self.cst_d[:, CST["ovl"]:CST["ovl"] + 32], w=[stk])
        for g in range(4):
            P.op("dve", lambda e: e.tensor_copy(out=self.VC[:, g, 65:97], in_=stg[:, 0:32]), r=[stk], w=[self.VC.k])
        for kv in range(2):
            o0 = VEC[("c_pos", kv)]
            P.op("dve", lambda e: e.tensor_copy(out=self.posb[:, kv, :], in_=self.vecs[:, o0:o0 + 32]), r=[self.vecs.k], w=[self.posb.k])
        self.conv_done = 0

    def convert_upto(self, nchunks):
        nchunks = min(nchunks, len(self.wchunk))
        for i in range(self.conv_done, nchunks):
            src = self.wall[i * CONV_CH:(i + 1) * CONV_CH].rearrange("(p f) -> p f", p=128)
            dst = self.wbf[i * CONV_CH:(i + 1) * CONV_CH].rearrange("(p f) -> p f", p=128)
            self.P.dma("pool", dst, src, w=[self.wchunk[i]])
        self.conv_done = max(self.conv_done, nchunks)

    def convert_phase(self, ph):
        ends = self.L.phase_end
        ph = min(ph, len(ends) - 1)
        self.convert_upto((ends[ph] + CONV_CH - 1) // CONV_CH)

    def load_x(self, b):
        P = self.P
        xin = [(self.phase_view(j * 4096, [128, 1024], F32), Trk()) for j in range(4)]
        self.new_phase([k for _, k in xin])
        for n in range(NT):
            for j in range(4):
                t0 = n * TT + j * 128
                P.dma("sp", xin[j][0], self.x[b, t0:t0 + 128, :], w=[xin[j][1]])
            for c in range(NCH):
                pb = self.bank()
                for j in range(4):
                    P.op("pe", lambda e: e.transpose(out=pb[:, j * 128:(j + 1) * 128], in_=xin[j][0][:, c * 128:(c + 1) * 128], identity=self.ident_f[:]),
                         r=[xin[j][1], self.ident_f.k], w=[pb.k])
                eng = "act" if c % 2 == 0 else "dve"
                dst = self.hT[:, c, n * TT:(n + 1) * TT]
                if eng == "act":
                    P.op("act", lambda e: e.activation(out=dst, in_=pb[:], func=AF.Copy), r=[pb.k], w=[self.hk[c][n]])
                else:
                    P.op("dve", lambda e: e.tensor_copy(out=dst, in_=pb[:]), r=[pb.k], w=[self.hk[c][n]])

    def store_y(self, b):
        P = self.P
        yo = [(self.phase_view(j * 4096, [128, 1024], F32), Trk()) for j in range(4)]
        self.new_phase([k for _, k in yo])
        for n in range(NT):
            for j in range(4):
                t0 = n * TT + j * 128
                for half in range(2):
                    pb = self.bank()
                    for cc in range(4):
                        c = half * 4 + cc
                        P.op("pe", lambda e: e.transpose(out=pb[:, cc * 128:(cc + 1) * 128], in_=self.hT[:, c, t0:t0 + 128], identity=self.ident_f[:]),
                             r=[self.hk[c][n], self.ident_f.k], w=[pb.k])
                    dst = yo[j][0][:, half * 512:(half + 1) * 512]
                    wl = [yo[j][1]]
                    if half == 0:
                        P.op("act", lambda e: e.activation(out=dst, in_=pb[:], func=AF.Copy), r=[pb.k], w=wl)
                    else:
                        P.op("dve", lambda e: e.tensor_copy(out=dst, in_=pb[:]), r=[pb.k], w=wl)
                ot = Trk()
                P.dma("sp", self.y[b, t0:t0 + 128, :], yo[j][0], r=[yo[j][1]], w=[ot])
                self.out_trks.append(ot)

    def norm_tile(self, n, gname, u_ap, u_k, sq_bufs, rstd_buf):
        P = self.P
        pb = self.bank()
        for c in range(NCH):
            sq, sk = sq_bufs[c % len(sq_bufs)]
            hsl = self.hT[:, c, n * TT:(n + 1) * TT]
            P.op("act", lambda e: e.activation(out=sq, in_=hsl, func=AF.Square), r=[self.hk[c][n]], w=[sk])
            P.op("pe", lambda e: e.matmul(pb[:], lhsT=self.ones_b[:], rhs=sq, start=(c == 0), stop=(c == NCH - 1)),
                 r=[sk, self.ones_b.k], w=[pb.k])
        rs, rk = rstd_buf
        P.op("act", lambda e: e.activation(out=rs, in_=pb[:], func=AF.Ln, scale=1.0 / D, bias=EPS), r=[pb.k], w=[rk])
        P.op("act", lambda e: e.activation(out=rs, in_=rs, func=AF.Exp, scale=-0.5), r=[rk], w=[rk])
        for c in range(NCH):
            hsl = self.hT[:, c, n * TT:(n + 1) * TT]
            g = self.vec(gname, c)
            P.op("dve", lambda e: e.scalar_tensor_tensor(out=u_ap[:, c, :], in0=hsl, scalar=g, in1=rs, op0=ALU.mult, op1=ALU.mult),
                 r=[self.hk[c][n], rk, self.vecs.k], w=[u_k])

    def a_phase_setup(self):
        v = self.phase_view
        A = {}
        A["u"] = [(v(n * 8192, [128, 8, 512], BF16), Trk()) for n in range(NT)]
        A["m"] = [(v(32768 + n * 8192, [128, 8, 512], BF16), Trk()) for n in range(NT)]
        A["sq"] = [(v(65536 + i * 1024, [128, 512], BF16), Trk()) for i in range(2)]
        A["rstd"] = (v(67584, [128, 512], F32), Trk())
        A["xp"] = [(v(69632 + i * 2080, [128, 516], F32), Trk()) for i in range(2)]
        base = 73792
        s1 = []
        for i in range(3):
            o = base + i * 4096
            xr = (v(o + 2048, [128, 512], F32), Trk())
            s1.append({"y": (v(o, [128, 512], BF16), Trk()), "xrb": (v(o + 1024, [128, 512], BF16), Trk()), "xr": xr, "hr": xr})
        s2 = []
        for i in range(2):
            o = base + 12288 + i * 6144
            rr = (v(o, [128, 512], F32), Trk())
            s2.append({"r": rr, "a2": rr, "i": (v(o + 2048, [128, 512], F32), Trk()), "a": (v(o + 4096, [128, 512], F32), Trk())})
        sets = s1 + s2
        A["s1"] = s1
        A["s2"] = s2
        A["sets"] = sets
        trks = [k for _, k in A["u"]] + [k for _, k in A["m"]] + [A["rstd"][1]] + [k for _, k in A["sq"]] + [k for _, k in A["xp"]]
        for st in sets:
            trks += [k for _, k in st.values()]
        self.new_phase(list(dict((id(t), t) for t in trks).values()))
        self.begin_phase([])
        self.A = A

    def a_layer(self, l):
        P = self.P
        self.a_phase_setup()
        A = self.A
        P.op("pool", lambda e: e.memset(self.convc[:], 0.0), w=[self.convc.k])
        P.op("pool", lambda e: e.memset(self.hst[:], 0.0), w=[self.hst.k])
        for n in range(NT):
            self.norm_tile(n, ("a_norm", l), A["u"][n][0], A["u"][n][1], A["sq"], A["rstd"])
        items = [(c, n) for c in range(NCH) for n in range(NT)]
        wslot = {}

        def stage1(it):
            c, n = items[it]
            if n == 0:
                wslot[c] = self.next_w(("a_in", l, c), hold=True)
            w = wslot[c]
            u, uk = A["u"][n]
            pg = self.bank()
            pr = self.bank()
            for k in range(NCH):
                P.op("pe", lambda e: e.matmul(pg[:], lhsT=w[:, k * 128:(k + 1) * 128], rhs=u[:, k, :], start=(k == 0), stop=(k == NCH - 1)),
                     r=[w.k, uk], w=[pg.k])
            for k in range(NCH):
                P.op("pe", lambda e: e.matmul(pr[:], lhsT=w[:, 1024 + k * 128:1024 + (k + 1) * 128], rhs=u[:, k, :], start=(k == 0), stop=(k == NCH - 1)),
                     r=[w.k, uk], w=[pr.k])
            st = A["s1"][it % 3]
            xp, xpk = A["xp"][it % 2]
            y, yk = st["y"]
            xr, xrk = st["xr"]
            xrb, xrbk = st["xrb"]
            P.op("act", lambda e: e.activation(out=y, in_=pg[:], func=AF.Gelu_apprx_tanh), r=[pg.k], w=[yk])
            P.op("pool", lambda e: e.tensor_copy(out=xp[:, 0:3], in_=self.convc[:, c, :]), r=[self.convc.k], w=[xpk])
            P.op("act", lambda e: e.activation(out=xp[:, 3:515], in_=pr[:], func=AF.Copy), r=[pr.k], w=[xpk])
            P.op("dve", lambda e: e.tensor_scalar(out=xr, in0=xp[:, 0:512], scalar1=self.vec(("a_cw", l, 0), c), scalar2=self.vec(("a_cb", l), c),
                                                  op0=ALU.mult, op1=ALU.add), r=[xpk, self.vecs.k], w=[xrk])
            for kk in range(1, 4):
                P.op("dve", lambda e: e.scalar_tensor_tensor(out=xr, in0=xp[:, kk:kk + 512], scalar=self.vec(("a_cw", l, kk), c), in1=xr,
                                                             op0=ALU.mult, op1=ALU.add), r=[xpk, xrk, self.vecs.k], w=[xrk])
            P.op("pool", lambda e: e.tensor_copy(out=self.convc[:, c, :], in_=xp[:, 512:515]), r=[xpk], w=[self.convc.k])
            P.op("pool", lambda e: e.tensor_copy(out=xrb, in_=xr), r=[xrk], w=[xrbk])

        def stage2(it):
            c, n = items[it]
            w = wslot[c]
            m, mk = A["m"][n]
            st = A["s1"][it % 3]
            y, yk = st["y"]
            xr, xrk = st["xr"]
            xrb, xrbk = st["xrb"]
            hr, hrk = st["hr"]
            t2 = A["s2"][it % 2]
            rr, rrk = t2["r"]
            ii, iik = t2["i"]
            aa, aak = t2["a"]
            p1 = self.bank()
            p2 = self.bank()
            P.op("pe", lambda e: e.matmul(p1[:], lhsT=w[:, 2048:2176], rhs=xrb, start=True, stop=True), r=[w.k, xrbk], w=[p1.k])
            P.op("pe", lambda e: e.matmul(p2[:], lhsT=w[:, 2176:2304], rhs=xrb, start=True, stop=True), r=[w.k, xrbk], w=[p2.k])
            if n == NT - 1:
                self.release_slot(w)
            P.op("act", lambda e: e.activation(out=rr, in_=p1[:], func=AF.Tanh, scale=0.5, bias=self.hgb[:, l, 0, c:c + 1]), r=[p1.k, self.hgb.k], w=[rrk])
            P.op("act", lambda e: e.activation(out=ii, in_=p2[:], func=AF.Tanh, scale=0.5, bias=self.hgb[:, l, 1, c:c + 1]), r=[p2.k, self.hgb.k], w=[iik])
            P.op("act", lambda e: e.activation(out=aa, in_=rr, func=AF.Exp, scale=self.coef[:, l, 0, c:c + 1], bias=self.coef[:, l, 0, c:c + 1]), r=[rrk, self.coef.k], w=[aak])
            P.op("act", lambda e: e.activation(out=rr, in_=rr, func=AF.Exp, scale=self.coef[:, l, 1, c:c + 1], bias=self.coef[:, l, 1, c:c + 1]), r=[rrk, self.coef.k], w=[rrk])
            P.op("act", lambda e: e.activation(out=rr, in_=rr, func=AF.Ln, scale=-0.999999, bias=1.0), r=[rrk], w=[rrk])
            P.op("act", lambda e: e.activation(out=rr, in_=rr, func=AF.Exp, scale=0.5, bias=self.lnhalf[:, 0:1]), r=[rrk, self.lnhalf.k], w=[rrk])
            P.op("dve", lambda e: e.scalar_tensor_tensor(out=ii, in0=ii, scalar=1.0, in1=xr, op0=ALU.add, op1=ALU.mult), r=[iik, xrk], w=[iik])
            P.op("dve", lambda e: e.tensor_tensor(out=ii, in0=ii, in1=rr, op=ALU.mult), r=[iik, rrk], w=[iik])
            P.op("dve", lambda e: e.tensor_tensor_scan(out=hr, data0=aa, data1=ii, initial=self.hst[:, c:c + 1], op0=ALU.mult, op1=ALU.add),
                 r=[aak, iik, self.hst.k], w=[hrk])
            P.op("pool", lambda e: e.tensor_copy(out=self.hst[:, c:c + 1], in_=hr[:, 511:512]), r=[hrk], w=[self.hst.k])
            P.op("pool", lambda e: e.tensor_tensor(out=m[:, c, :], in0=hr, in1=y, op=ALU.mult), r=[hrk, yk], w=[mk])

        LA = 2
        for it in range(min(LA, len(items))):
            stage1(it)
        for it in range(len(items)):
            if it + LA < len(items):
                stage1(it + LA)
            stage2(it)
        for mm in range(NCH):
            w = self.next_w(("a_out", l, mm))
            for n in range(NT):
                m, mk = A["m"][n]
                po = self.bank()
                for k in range(NCH):
                    P.op("pe", lambda e: e.matmul(po[:], lhsT=w[:, k * 128:(k + 1) * 128], rhs=m[:, k, :], start=(k == 0), stop=(k == NCH - 1)),
                         r=[w.k, mk], w=[po.k])
                hsl = self.hT[:, mm, n * TT:(n + 1) * TT]
                P.op("dve", lambda e: e.tensor_tensor(out=hsl, in0=po[:], in1=hsl, op=ALU.add), r=[po.k, self.hk[mm][n]], w=[self.hk[mm][n]])

    def ffn_phase_setup(self):
        v = self.phase_view
        Fz = {}
        Fz["u"] = [(v(s * 8192, [128, 8, 512], BF16), Trk()) for s in range(2)]
        Fz["act"] = [[(v(16384 + (c * 2 + s) * 1024, [128, 512], BF16), Trk()) for s in range(2)] for c in range(FC)]
        Fz["sq"] = [(v(61440 + i * 1024, [128, 512], BF16), Trk()) for i in range(2)]
        Fz["rstd"] = (v(63488, [128, 512], F32), Trk())
        Fz["sg"] = [(v(65536 + i * 1024, [128, 512], BF16), Trk()) for i in range(4)]
        ext = self.ext_slots(69632)
        trks = [k for _, k in Fz["u"]] + [k for row in Fz["act"] for _, k in row] + [k for _, k in Fz["sq"]] + [Fz["rstd"][1]] + [k for _, k in Fz["sg"]] + [b.k for b in ext]
        self.new_phase(trks)
        self.begin_phase(ext)
        self.F = Fz

    def ffn_tile(self, L, t2):
        P = self.P
        Fz = self.F
        for s in range(2):
            self.norm_tile(2 * t2 + s, ("f_norm", L), Fz["u"][s][0], Fz["u"][s][1], Fz["sq"], Fz["rstd"])
        nsg = 0
        for c in range(FC):
            w = self.next_w(("f_in", L, c))
            for s in range(2):
                u, uk = Fz["u"][s]
                pg = self.bank()
                pu = self.bank()
                for k in range(NCH):
                    P.op("pe", lambda e: e.matmul(pg[:], lhsT=w[:, k * 128:(k + 1) * 128], rhs=u[:, k, :], start=(k == 0), stop=(k == NCH - 1)),
                         r=[w.k, uk], w=[pg.k])
                for k in range(NCH):
                    P.op("pe", lambda e: e.matmul(pu[:], lhsT=w[:, 1024 + k * 128:1024 + (k + 1) * 128], rhs=u[:, k, :], start=(k == 0), stop=(k == NCH - 1)),
                         r=[w.k, uk], w=[pu.k])
                sg, sgk = Fz["sg"][nsg % 4]
                nsg += 1
                a, ak = Fz["act"][c][s]
                P.op("act", lambda e: e.activation(out=sg, in_=pg[:], func=AF.Silu), r=[pg.k], w=[sgk])
                P.op("dve", lambda e: e.tensor_tensor(out=a, in0=sg, in1=pu[:], op=ALU.mult), r=[sgk, pu.k], w=[ak])
        for mm in range(NCH):
            w = self.next_w(("f_out", L, mm))
            for s in range(2):
                n = 2 * t2 + s
                po = self.bank()
                for c in range(FC):
                    a, ak = Fz["act"][c][s]
                    P.op("pe", lambda e: e.matmul(po[:], lhsT=w[:, c * 128:(c + 1) * 128], rhs=a, start=(c == 0), stop=(c == FC - 1)),
                         r=[w.k, ak], w=[po.k])
                hsl = self.hT[:, mm, n * TT:(n + 1) * TT]
                P.op("dve", lambda e: e.tensor_tensor(out=hsl, in0=po[:], in1=hsl, op=ALU.add), r=[po.k, self.hk[mm][n]], w=[self.hk[mm][n]])

    def ffn_layer(self, L):
        self.ffn_phase_setup()
        for t2 in range(2):
            self.ffn_tile(L, t2)

    def headnorm_rope(self, pk, R, C, gvec, cos_ap, sin_ap, T, bias=None):
        P = self.P
        sq, sqk = T["sq"]
        rs, rsk = T["rs"]
        qn, qnk = T["qn"]
        t1, t1k = T["t1"]
        t2, t2k = T["t2"]
        if bias is not None:
            xf, xfk = T["xf"]
            P.op("act", lambda e: e.activation(out=xf[0:R, 0:C], in_=pk[0:R, 0:C], func=AF.Identity, bias=bias), r=[pk.k, self.vecs.k], w=[xfk])
            src, srck = xf[0:R, 0:C], xfk
        else:
            src, srck = pk[0:R, 0:C], pk.k
        P.op("act", lambda e: e.activation(out=sq[0:R, 0:C], in_=src, func=AF.Square), r=[srck], w=[sqk])
        pss = self.bank(self.pool_misc)
        P.op("pe", lambda e: e.matmul(pss[0:R, 0:C], lhsT=self.bones_b[0:R, 0:R], rhs=sq[0:R, 0:C], start=True, stop=True), r=[sqk, self.bones_b.k], w=[pss.k])
        P.op("act", lambda e: e.activation(out=rs[0:R, 0:C], in_=pss[0:R, 0:C], func=AF.Ln, scale=1.0 / 64.0, bias=EPS), r=[pss.k], w=[rsk])
        P.op("act", lambda e: e.activation(out=rs[0:R, 0:C], in_=rs[0:R, 0:C], func=AF.Exp, scale=-0.5), r=[rsk], w=[rsk])
        P.op("dve", lambda e: e.scalar_tensor_tensor(out=qn[0:R, 0:C], in0=src, scalar=gvec, in1=rs[0:R, 0:C], op0=ALU.mult, op1=ALU.mult),
             r=[srck, rsk, self.vecs.k], w=[qnk])
        prt = self.bank(self.pool_misc)
        P.op("pe", lambda e: e.matmul(prt[0:R, 0:C], lhsT=self.prot_b[0:R, 0:R], rhs=qn[0:R, 0:C], start=True, stop=True), r=[qnk, self.prot_b.k], w=[prt.k])
        P.op("pool", lambda e: e.tensor_tensor(out=t1[0:R, 0:C], in0=qn[0:R, 0:C], in1=cos_ap, op=ALU.mult), r=[qnk, self.cos_b.k], w=[t1k])
        P.op("dve", lambda e: e.tensor_tensor(out=t2[0:R, 0:C], in0=prt[0:R, 0:C], in1=sin_ap, op=ALU.mult), r=[prt.k, self.sin_b.k], w=[t2k])

    def kv_phase(self):
        P = self.P
        v = self.phase_view
        U = [(v(n * 8192, [128, 8, 512], BF16), Trk()) for n in range(NT)]
        sqn = [(v(32768 + i * 1024, [128, 512], BF16), Trk()) for i in range(2)]
        rstd = (v(34816, [128, 512], F32), Trk())
        kcT, kcTk = v(36864, [128, 8, S], BF16), [Trk() for _ in range(8)]
        cw1, cw1k = v(0, [128, 2, 32, 256], BF16), Trk()
        T = {"sq": (v(69632, [128, 512], BF16), Trk()), "rs": (v(70656, [128, 512], F32), Trk()), "qn": (v(72704, [128, 512], BF16), Trk()),
             "t1": (v(73728, [128, 512], F32), Trk()), "t2": (v(75776, [128, 512], F32), Trk()), "xf": (v(77824, [128, 512], F32), Trk())}
        kout = [(v(79872 + i * 1024, [128, 512], BF16), Trk()) for i in range(2)]
        vst = [(v(81920 + i * 528, [128, 4, 66], BF16), Trk()) for i in range(2)]
        hid = [(v(83008 + i * 512, [128, 2, 128], BF16), Trk()) for i in range(2)]
        ext = self.ext_slots(84032)
        trks = [k for _, k in U] + [rstd[1], cw1k] + [k for _, k in sqn] + kcTk + [k for _, k in T.values()] + [k for _, k in kout] + [k for _, k in vst] \
            + [k for _, k in hid] + [b.k for b in ext]
        self.new_phase(trks)
        self.begin_phase(ext)
        self.pool_misc = [0, 1, 2, 3, 4, 5, 6, 7]
        for i in range(2):
            P.op("pool", lambda e: e.memset(vst[i][0][:, :, 64:66], 1.0), w=[vst[i][1]])
        for n in range(NT):
            self.norm_tile(n, ("kv_norm",), U[n][0], U[n][1], sqn, rstd)
        nko = 0
        nvs = 0
        for i in range(4):
            which, cc = i // 2, i % 2
            w = self.next_w(("kvk", i))
            for n in range(NT):
                tsl = slice(n * TT, (n + 1) * TT)
                u, uk = U[n]
                pk = self.bank()
                for k in range(NCH):
                    P.op("pe", lambda e: e.matmul(pk[:], lhsT=w[:, k * 128:(k + 1) * 128], rhs=u[:, k, :], start=(k == 0), stop=(k == NCH - 1)), r=[w.k, uk], w=[pk.k])
                self.headnorm_rope(pk, 128, 512, self.vec(("k_norm", 1 + which)), self.cos_b[:, tsl], self.sin_b[:, tsl], T)
                ko, kok = kout[nko % 2]
                nko += 1
                P.op("dve", lambda e: e.tensor_tensor(out=ko, in0=T["t1"][0], in1=T["t2"][0], op=ALU.add), r=[T["t1"][1], T["t2"][1]], w=[kok])
                for hh in range(2):
                    P.dma("sp", self.KSd[which, :, 2 * cc + hh, tsl], ko[hh * 64:(hh + 1) * 64, :], r=[kok], w=[self.ksd_k[which][2 * cc + hh][n]])
        for i in range(4):
            sel, cc = i // 2, i % 2
            w = self.next_w(("kvc", i))
            for n in range(NT):
                tsl = slice(n * TT, (n + 1) * TT)
                u, uk = U[n]
                for gg in range(2):
                    pc = self.bank()
                    for k in range(NCH):
                        P.op("pe", lambda e: e.matmul(pc[0:64, :], lhsT=w[:, k * 128 + gg * 64:k * 128 + gg * 64 + 64], rhs=u[:, k, :], start=(k == 0), stop=(k == NCH - 1)),
                             r=[w.k, uk], w=[pc.k])
                    idx = sel * 4 + 2 * cc + gg
                    P.op("act", lambda e: e.activation(out=kcT[0:64, idx, tsl], in_=pc[0:64, :], func=AF.Copy), r=[pc.k], w=[kcTk[idx]])
        for i in range(2):
            w = self.next_w(("kvv", i))
            for n in range(NT):
                u, uk = U[n]
                for jb in range(4):
                    pv = self.bank()
                    for k in range(NCH):
                        P.op("pe", lambda e: e.matmul(pv[:, 0:256], lhsT=u[:, k, jb * 128:(jb + 1) * 128], rhs=w[:, k * 256:(k + 1) * 256], start=(k == 0), stop=(k == NCH - 1)),
                             r=[w.k, uk], w=[pv.k])
                    vs, vsk = vst[nvs % 2]
                    nvs += 1
                    P.op("act", lambda e: e.activation(out=vs[:, :, 0:64], in_=pv[:, 0:256].rearrange("p (g d) -> p g d", g=4), func=AF.Copy), r=[pv.k], w=[vsk])
                    P.dma("sp", self.Vd[i, :, 4 * n + jb, :, :], vs, r=[vsk], w=[self.vd_k[i][n]])
        for kv in range(2):
            off = self.L.off[("cw1", kv)]
            src = self.wbf[off:off + 128 * 8192].rearrange("(p f) -> p f", p=128)
            c0, c1 = off // CONV_CH, (off + 128 * 8192 - 1) // CONV_CH
            P.dma("sp", cw1[0:64, kv].rearrange("p a b -> p (a b)"), src[0:64, :], r=[self.wchunk[c] for c in range(c0, c1 + 1)], w=[cw1k] + [k for _, k in U])
        w2 = self.next_w(("cw2",))
        for sel in range(2):
            pcv = self.bank()
            for cc in range(2):
                for l in range(32):
                    P.op("pe", lambda e: e.matmul(pcv[:, cc:cc + 1], lhsT=cw1[0:64, sel, l, cc * 128:(cc + 1) * 128], rhs=self.posb[0:64, sel, l:l + 1],
                                                  start=(cc == 0 and l == 0), stop=(cc == 1 and l == 31)), r=[cw1k, self.posb.k], w=[pcv.k])
            b1o = VEC[("c_b1", sel)]
            P.op("dve", lambda e: e.tensor_tensor(out=self.cbias[:, sel, :], in0=pcv[:, 0:2], in1=self.vecs[:, b1o:b1o + 2], op=ALU.add), r=[pcv.k, self.vecs.k], w=[self.cbias.k])
        nh = 0
        for sel in range(2):
            for g in range(4):
                hd, hdk = hid[nh % 2]
                nh += 1
                for cc in range(2):
                    ph = self.bank()
                    for l in range(32):
                        P.op("pe", lambda e: e.matmul(ph[:, 0:127], lhsT=cw1[0:64, sel, l, cc * 128:(cc + 1) * 128], rhs=kcT[0:64, sel * 4 + g, l:l + 16 * 126 + 1:16],
                                                      start=(l == 0), stop=(l == 31)), r=[cw1k, kcTk[sel * 4 + g]], w=[ph.k])
                    P.op("act", lambda e: e.activation(out=hd[:, cc, 0:127], in_=ph[:, 0:127], func=AF.Gelu_apprx_tanh, bias=self.cbias[:, sel, cc:cc + 1]),
                         r=[ph.k, self.cbias.k], w=[hdk])
                if sel == 0:
                    pk = self.bank()
                    for cc in range(2):
                        P.op("pe", lambda e: e.matmul(pk[0:64, 0:127], lhsT=w2[:, (0 * 2 + cc) * 64:(0 * 2 + cc) * 64 + 64], rhs=hd[:, cc, 0:127], start=(cc == 0), stop=(cc == 1)),
                             r=[w2.k, hdk], w=[pk.k])
                    self.headnorm_rope(pk, 64, 127, self.vecs[0:64, VEC[("k_norm", 0)]:VEC[("k_norm", 0)] + 1],
                                       self.cos_b[0:64, 31:31 + 16 * 126 + 1:16], self.sin_b[0:64, 31:31 + 16 * 126 + 1:16], T,
                                       bias=self.vecs[0:64, VEC[("c_b2", 0)]:VEC[("c_b2", 0)] + 1])
                    P.op("dve", lambda e: e.tensor_tensor(out=self.KC[0:64, g, 0:127], in0=T["t1"][0][0:64, 0:127], in1=T["t2"][0][0:64, 0:127], op=ALU.add),
                         r=[T["t1"][1], T["t2"][1]], w=[self.KC.k])
                else:
                    pv = self.bank()
                    for cc in range(2):
                        P.op("pe", lambda e: e.matmul(pv[0:127, 0:64], lhsT=hd[:, cc, 0:127], rhs=w2[:, (1 * 2 + cc) * 64:(1 * 2 + cc) * 64 + 64], start=(cc == 0), stop=(cc == 1)),
                             r=[w2.k, hdk], w=[pv.k])
                    P.op("dve", lambda e: e.tensor_tensor(out=self.VC[0:127, g, 0:64], in0=pv[0:127, 0:64], in1=self.bvecs[0:127, BV_CB2V:BV_CB2V + 64], op=ALU.add),
                         r=[pv.k, self.bvecs.k], w=[self.VC.k])

    def b_layer(self, j):
        P = self.P
        v = self.phase_view
        KS, KSk = v(0, [128, 4, S], BF16), [Trk() for _ in range(4)]
        KW, KWk = v(16384, [128, 4, S], BF16), [Trk() for _ in range(4)]
        VS, VSk = v(32768, [128, 16, 4, 66], BF16), Trk()
        VW, VWk = v(41216, [128, 16, 4, 66], BF16), Trk()
        u, uk = v(49664, [128, 8, 512], BF16), Trk()
        Q, Qk = v(57856, [128, 4, 16, 128], BF16), [Trk() for _ in range(4)]
        Nk = [[Trk() for _ in range(4)] for _ in range(4)]
        oT, oTk = v(74240, [128, 8, 512], BF16), Trk()
        ob = [(v(82432 + i * 2048, [128, 1024], BF16), Trk()) for i in range(2)]
        PT = [(v(86528 + i * 1024, [128, 512], BF16), Trk()) for i in range(4)]
        sqn = [(v(90624 + i * 1024, [128, 512], BF16), Trk()) for i in range(2)]
        rstd = (v(92672, [128, 512], F32), Trk())
        gat, gatk = v(94720, [128, 4, 48], F32), Trk()
        T = {"sq": sqn[0], "rs": rstd, "qn": (v(95488, [128, 512], BF16), Trk()),
             "t1": (v(96512, [128, 512], F32), Trk()), "t2": (v(98560, [128, 512], F32), Trk())}
        rd, rdk = v(100608, [128, 16], F32), Trk()
        s3, s3k = v(100672, [128, 12], F32), Trk()
        sc, sck = v(100736, [128, 32], F32), Trk()
        sc2, sc2k = v(100864, [128, 32], F32), Trk()
        nsl, nslk = v(100992, [128, 32], F32), Trk()
        m8, m8k = v(101120, [128, 16], F32), Trk()
        acc, acck = v(101184, [128, 256], F32), Trk()
        trks = KSk + KWk + [VSk, VWk, uk, oTk, gatk, rdk, s3k, sck, sc2k, nslk, m8k, acck] + Qk + [k for _, k in ob] + [k for _, k in PT] \
            + [k for _, k in sqn] + [rstd[1]] + [T["qn"][1], T["t1"][1], T["t2"][1]] + [k for row in Nk for k in row]
        self.new_phase(trks)
        self.begin_phase([])
        self.pool_misc = [6, 7]
        pool_st = [0, 1, 2]
        for g in range(4):
            P.dma("sp", KS[0:64, g, :], self.KSd[0, :, g, :], r=self.ksd_k[0][g], w=[KSk[g]])
            P.dma("sp", KW[0:64, g, :], self.KSd[1, :, g, :], r=self.ksd_k[1][g], w=[KWk[g]])
        P.dma("sp", VS.rearrange("p a b c -> p (a b c)"), self.Vd[0].rearrange("p a b c -> p (a b c)"), r=self.vd_k[0], w=[VSk])
        P.dma("sp", VW.rearrange("p a b c -> p (a b c)"), self.Vd[1].rearrange("p a b c -> p (a b c)"), r=self.vd_k[1], w=[VWk])
        est = self.ph[64:96, 57856 // 4:57856 // 4 + 2048]
        allq = Qk + [k for row in Nk for k in row]
        P.dma("sp", est, self.cst_d[64:96, CST["eind"]:CST["eind"] + 2048], w=allq)
        for g in range(4):
            P.op("dve", lambda e: e.tensor_copy(out=KS[64:96, g, :], in_=est), r=allq, w=[KSk[g]])
            P.op("pool", lambda e: e.memset(KW[64:96, g, :], 0.0), w=[KWk[g]])
        P.op("pool", lambda e: e.memset(Q[64:96].rearrange("p a b c -> p (a b c)"), 0.0), w=allq)
        gbo = BV_GB + 48 * j
        npt = 0
        nob = 0
        for n in range(NT):
            tsl = slice(n * TT, (n + 1) * TT)
            self.norm_tile(n, ("b_norm", j), u, uk, sqn, rstd)
            wg = self.next_w(("bg", j))
            for qb in range(4):
                pg = self.bank(self.pool_misc)
                for k in range(NCH):
                    P.op("pe", lambda e: e.matmul(pg[:, 0:48], lhsT=u[:, k, qb * 128:(qb + 1) * 128], rhs=wg[:, k * 48:(k + 1) * 48], start=(k == 0), stop=(k == NCH - 1)),
                         r=[wg.k, uk], w=[pg.k])
                P.op("dve", lambda e: e.tensor_tensor(out=gat[:, qb, :], in0=pg[:, 0:48], in1=self.bvecs[:, gbo:gbo + 48], op=ALU.add), r=[pg.k, self.bvecs.k], w=[gatk])
            P.op("act", lambda e: e.activation(out=gat, in_=gat, func=AF.Tanh, scale=0.5), r=[gatk], w=[gatk])
            P.op("dve", lambda e: e.tensor_scalar(out=gat, in0=gat, scalar1=0.5, scalar2=0.5, op0=ALU.mult, op1=ALU.add), r=[gatk], w=[gatk])
            for m in range(NCH):
                w = self.next_w(("bq", j, m))
                pq = self.bank(pool_st)
                for k in range(NCH):
                    P.op("pe", lambda e: e.matmul(pq[:], lhsT=w[:, k * 128:(k + 1) * 128], rhs=u[:, k, :], start=(k == 0), stop=(k == NCH - 1)), r=[w.k, uk], w=[pq.k])
                self.headnorm_rope(pq, 128, 512, self.vec(("q_norm", j)), self.cos_b[:, tsl], self.sin_b[:, tsl], T)
                for hh in range(2):
                    h = 2 * m + hh
                    rsl = slice(hh * 64, (hh + 1) * 64)
                    eng = "dve" if hh == 0 else "pool"
                    P.op(eng, lambda e: e.tensor_tensor(out=Q[0:64, :, h, :], in0=T["t1"][0][rsl, :].rearrange("p (a b) -> p a b", a=4),
                                                        in1=T["t2"][0][rsl, :].rearrange("p (a b) -> p a b", a=4), op=ALU.add),
                         r=[T["t1"][1], T["t2"][1]], w=Qk)
            self.attn_tile(n, dict(KS=KS, KSk=KSk, KW=KW, KWk=KWk, VS=VS, VSk=VSk, VW=VW, VWk=VWk, Q=Q, Qk=Qk, Nk=Nk, oT=oT, oTk=oTk, ob=ob, PT=PT,
                                   gat=gat, gatk=gatk, rd=rd, rdk=rdk, s3=s3, s3k=s3k, sc=sc, sck=sck, sc2=sc2, sc2k=sc2k, nsl=nsl, nslk=nslk,
                                   m8=m8, m8k=m8k, acc=acc, acck=acck, pool_st=pool_st))
            for mm in range(NCH):
                w = self.next_w(("bo", j, mm))
                po = self.bank(pool_st)
                for k in range(NCH):
                    P.op("pe", lambda e: e.matmul(po[:], lhsT=w[:, k * 128:(k + 1) * 128], rhs=oT[:, k, :], start=(k == 0), stop=(k == NCH - 1)), r=[w.k, oTk], w=[po.k])
                hsl = self.hT[:, mm, tsl]
                P.op("dve", lambda e: e.tensor_tensor(out=hsl, in0=po[:], in1=hsl, op=ALU.add), r=[po.k, self.hk[mm][n]], w=[self.hk[mm][n]])
        self.pool_misc = [0, 1, 2, 3, 4, 5, 6, 7]

    def attn_tile(self, n, C):
        P = self.P
        KS, KSk, KW, KWk, VS, VSk, VW, VWk = C["KS"], C["KSk"], C["KW"], C["KWk"], C["VS"], C["VSk"], C["VW"], C["VWk"]
        Q, Qk, Nk, oT, oTk, ob, PT = C["Q"], C["Qk"], C["Nk"], C["oT"], C["oTk"], C["ob"], C["PT"]
        gat, gatk, rd, rdk, s3, s3k = C["gat"], C["gatk"], C["rd"], C["rdk"], C["s3"], C["s3k"]
        sc, sck, sc2, sc2k, nsl, nslk, m8, m8k, acc, acck = C["sc"], C["sck"], C["sc2"], C["sc2k"], C["nsl"], C["nslk"], C["m8"], C["m8k"], C["acc"], C["acck"]
        pool_st = C["pool_st"]
        poc, pos, pow_ = self.ps[3], self.ps[4], self.ps[5]
        st = {"npt": 0}
        pairs = [(qb, g) for qb in range(4) for g in range(4)]
        jobs = []

        def r4(ap):
            return ap.rearrange("p (a b) -> p a b", a=4)

        def mk_job(kind, qb, g, kt, first, last, pidx):
            qbg = 4 * n + qb
            job = {"pre": [], "post": []}
            box = {}

            def st1():
                pst = self.bank(pool_st)
                pt, ptk = PT[st["npt"] % 4]
                st["npt"] += 1
                box["pt"], box["ptk"] = pt, ptk
                Q96 = Q[0:96, qb, 4 * g:4 * g + 4, :]
                if kind == "c":
                    P.op("pe", lambda e: e.matmul(pst[0:127, :], lhsT=self.KC[0:96, g, 0:127], rhs=Q96, start=True, stop=True), r=[self.KC.k, Qk[qb]], w=[pst.k])
                    P.op("act", lambda e: e.activation(out=pt[0:127, :], in_=pst[0:127, :], func=AF.Exp, scale=SCALE), r=[pst.k], w=[ptk])
                    cm = self.cmask_b[0:127, qbg * 128:(qbg + 1) * 128].unsqueeze(1).broadcast_to([127, 4, 128])
                    P.op("pool", lambda e: e.tensor_tensor(out=r4(pt[0:127, :]), in0=r4(pt[0:127, :]), in1=cm, op=ALU.mult), r=[ptk, self.cmask_b.k], w=[ptk])
                    return
                if kind == "s":
                    P.op("pe", lambda e: e.matmul(pst[:], lhsT=KS[0:96, g, kt * 128:(kt + 1) * 128], rhs=Q96, start=True, stop=True),
                         r=[KSk[g], Qk[qb], Nk[qb][g]], w=[pst.k])
                else:
                    P.op("pe", lambda e: e.matmul(pst[:], lhsT=KW[0:96, g, kt * 128:(kt + 1) * 128], rhs=Q96, start=True, stop=True), r=[KWk[g], Qk[qb]], w=[pst.k])
                P.op("act", lambda e: e.activation(out=pt, in_=pst[:], func=AF.Exp, scale=SCALE), r=[pst.k], w=[ptk])
                msk = None
                if kt == qbg:
                    msk = self.triL_b
                elif kind == "w" and kt == qbg - 4:
                    msk = self.triU_b
                if msk is not None:
                    tm = msk[:].unsqueeze(1).broadcast_to([128, 4, 128])
                    P.op("pool", lambda e: e.tensor_tensor(out=r4(pt), in0=r4(pt), in1=tm, op=ALU.mult), r=[ptk, msk.k], w=[ptk])

            def st2():
                pt, ptk = box["pt"], box["ptk"]
                for h in range(4):
                    if kind == "c":
                        P.op("pe", lambda e: e.matmul(poc[:, h * 128:h * 128 + 97], lhsT=pt[0:127, h * 128:(h + 1) * 128], rhs=self.VC[0:127, g, 0:97],
                                                      start=(h == 0), stop=(h == 3), skip_group_check=True), r=[ptk, self.VC.k], w=[poc.k])
                    elif kind == "s":
                        P.op("pe", lambda e: e.matmul(pos[:, h * 128:h * 128 + 65], lhsT=pt[:, h * 128:(h + 1) * 128], rhs=VS[:, kt, g, 0:65],
                                                      start=(first and h == 0), stop=(last and h == 3), skip_group_check=True), r=[ptk, VSk], w=[pos.k])
                    else:
                        P.op("pe", lambda e: e.matmul(pow_[:, h * 128:h * 128 + 65], lhsT=pt[:, h * 128:(h + 1) * 128], rhs=VW[:, kt, g, 0:65],
                                                      start=(first and h == 0), stop=(last and h == 3), skip_group_check=True), r=[ptk, VWk], w=[pow_.k])

            job["st1"], job["st2"] = st1, st2
            return job

        def select_chain(qb, g, pidx):
            qbg = 4 * n + qb
            pcs = self.pocs[pidx % 2]
            P.op("act", lambda e: e.activation(out=pcs[:], in_=r4(poc[:])[:, :, 0:97], func=AF.Copy), r=[poc.k], w=[pcs.k])
            P.op("dve", lambda e: e.tensor_scalar(out=rd[:, 0:4], in0=pcs[:, :, 64], scalar1=1e-30, scalar2=None, op0=ALU.max), r=[pcs.k], w=[rdk])
            P.op("dve", lambda e: e.reciprocal(out=rd[:, 0:4], in_=rd[:, 0:4]), r=[rdk], w=[rdk])
            for h in range(4):
                src1 = self.selb[:, qbg, :] if h == 0 else sc
                P.op("dve", lambda e: e.scalar_tensor_tensor(out=sc, in0=pcs[:, h, 65:97], scalar=rd[:, h:h + 1], in1=src1, op0=ALU.mult, op1=ALU.add),
                     r=[pcs.k, rdk, sck, self.selb.k], w=[sck])
            P.op("dve", lambda e: e.max(out=m8[:, 0:8], in_=sc), r=[sck], w=[m8k])
            P.op("dve", lambda e: e.match_replace(out=sc2, in_to_replace=m8[:, 0:8], in_values=sc, imm_value=-3.0e38), r=[sck, m8k], w=[sc2k])
            P.op("dve", lambda e: e.max(out=m8[:, 8:16], in_=sc2), r=[sc2k], w=[m8k])
            P.op("dve", lambda e: e.tensor_scalar(out=nsl, in0=sc, scalar1=m8[:, 15:16], scalar2=NEGM, op0=ALU.is_lt, op1=ALU.mult), r=[sck, m8k], w=[nslk])

        def nsel_install(qb, g):
            pm = self.bank(self.pool_misc)
            P.op("pe", lambda e: e.transpose(out=pm[0:32, 0:128], in_=nsl, identity=self.ident_f[:]), r=[nslk, self.ident_f.k], w=[pm.k])
            P.op("act", lambda e: e.activation(out=Q[64:96, qb, 4 * g:4 * g + 4, :], in_=pm[0:32, 0:128].unsqueeze(1).broadcast_to([32, 4, 128]), func=AF.Copy),
                 r=[pm.k], w=[Nk[qb][g]])

        def evac_win():
            P.op("act", lambda e: e.activation(out=self.pows[:], in_=r4(pow_[:])[:, :, 0:65], func=AF.Copy), r=[pow_.k], w=[self.pows.k])

        def combine(qb, g, pidx):
            pcs = self.pocs[pidx % 2]
            o, ok_ = ob[qb % 2]
            P.op("dve", lambda e: e.tensor_scalar(out=rd[:, 4:8], in0=pcs[:, :, 64], scalar1=1e-30, scalar2=None, op0=ALU.max), r=[pcs.k], w=[rdk])
            P.op("dve", lambda e: e.tensor_scalar(out=rd[:, 8:12], in0=r4(pos[:])[:, :, 64], scalar1=1e-30, scalar2=None, op0=ALU.max), r=[pos.k], w=[rdk])
            P.op("dve", lambda e: e.tensor_scalar(out=rd[:, 12:16], in0=self.pows[:, :, 64], scalar1=1e-30, scalar2=None, op0=ALU.max), r=[self.pows.k], w=[rdk])
            P.op("dve", lambda e: e.reciprocal(out=rd[:, 4:16], in_=rd[:, 4:16]), r=[rdk], w=[rdk])
            P.op("dve", lambda e: e.tensor_tensor(out=s3.rearrange("p (a b) -> p a b", a=3), in0=rd[:, 4:16].rearrange("p (a b) -> p a b", a=3),
                                                  in1=gat[:, qb, :].rearrange("p (a b) -> p a b", a=3)[:, :, 4 * g:4 * g + 4], op=ALU.mult), r=[rdk, gatk], w=[s3k])
            for h in range(4):
                ah = acc[:, h * 64:(h + 1) * 64]
                P.op("dve", lambda e: e.tensor_scalar(out=ah, in0=pcs[:, h, 0:64], scalar1=s3[:, h:h + 1], scalar2=None, op0=ALU.mult), r=[pcs.k, s3k], w=[acck])
                P.op("dve", lambda e: e.scalar_tensor_tensor(out=ah, in0=self.pows[:, h, 0:64], scalar=s3[:, 8 + h:9 + h], in1=ah, op0=ALU.mult, op1=ALU.add),
                     r=[self.pows.k, s3k, acck], w=[acck])
                P.op("dve", lambda e: e.scalar_tensor_tensor(out=o[:, g * 256 + h * 64:g * 256 + (h + 1) * 64], in0=pos[:, h * 128:h * 128 + 64], scalar=s3[:, 4 + h:5 + h], in1=ah,
                                                             op0=ALU.mult, op1=ALU.add), r=[pos.k, s3k, acck], w=[ok_])

        def o_transpose(qb):
            o, ok_ = ob[qb % 2]
            pT = self.bank(self.pool_misc)
            pTb = pT[:].bitcast(BF16)
            for c in range(NCH):
                P.op("pe", lambda e: e.transpose(out=pTb[:, c * 128:(c + 1) * 128], in_=o[:, c * 128:(c + 1) * 128], identity=self.ident_b[:]), r=[ok_, self.ident_b.k], w=[pT.k])
            P.op("act", lambda e: e.activation(out=oT[:, :, qb * 128:(qb + 1) * 128], in_=pTb.rearrange("p (a b) -> p a b", a=8), func=AF.Copy), r=[pT.k], w=[oTk])

        def cjob(pidx):
            qb, g = pairs[pidx]
            j = mk_job("c", qb, g, 0, True, True, pidx)
            j["post"].append(lambda: select_chain(qb, g, pidx))
            return j

        jobs.append(cjob(0))
        pending_T = []
        for pidx, (qb, g) in enumerate(pairs):
            qbg = 4 * n + qb
            k0 = max(0, qbg - 4)
            wj = [mk_job("w", qb, g, kt, kt == k0, kt == qbg, pidx) for kt in range(k0, qbg + 1)]
            for f in pending_T:
                wj[min(2, len(wj) - 1)]["post"].append(f)
            pending_T = []
            wj[-1]["post"].append(evac_win)
            jobs += wj
            if pidx + 1 < len(pairs):
                jobs.append(cjob(pidx + 1))
            sj = [mk_job("s", qb, g, kt, kt == 0, kt == qbg, pidx) for kt in range(qbg + 1)]
            sj[0]["pre"].append(lambda qb=qb, g=g: nsel_install(qb, g))
            sj[-1]["post"].append(lambda qb=qb, g=g, pidx=pidx: combine(qb, g, pidx))
            jobs += sj
            if g == 3:
                pending_T.append(lambda qb=qb: o_transpose(qb))
        LA = 2
        for j in range(min(LA, len(jobs))):
            for f in jobs[j]["pre"]:
                f()
            jobs[j]["st1"]()
        for i in range(len(jobs)):
            if i + LA < len(jobs):
                for f in jobs[i + LA]["pre"]:
                    f()
                jobs[i + LA]["st1"]()
            jobs[i]["st2"]()
            for f in jobs[i]["post"]:
                f()
        for f in pending_T:
            f()

    def make_plan(self):
        plan = []
        ser = -1
        for b in range(self.nb):
            for layer in range(self.n_layers):
                if layer < 2:
                    ser += 1
                    plan += [(("a_in", layer, c), ser) for c in range(8)]
                    plan += [(("a_out", layer, m), ser) for m in range(8)]
                else:
                    if layer == 2:
                        ser += 1
                        plan += [(("kvk", i), ser) for i in range(4)] + [(("kvc", i), ser) for i in range(4)] + [(("kvv", i), ser) for i in range(2)]
                        plan += [(("cw2",), ser)]
                    ser += 1
                    for n in range(NT):
                        plan += [(("bg", layer - 2), ser)] + [(("bq", layer - 2, m), ser) for m in range(8)] + [(("bo", layer - 2, m), ser) for m in range(8)]
                ser += 1
                for t2 in range(2):
                    plan += [(("f_in", layer, c), ser) for c in range(FC)]
                    plan += [(("f_out", layer, m), ser) for m in range(8)]
        return plan

    def build(self):
        P = self.P
        self.convc = self.sb("convc", [128, 8, 3], F32)
        self.hst = self.sb("hst", [128, 8], F32)
        self.prologue()
        self.plan_weights(self.make_plan())
        phase_no = {"a0": 0, "f0": 1, "a1": 2, "f1": 3, "kv": 4, "b0": 5, "f2": 6, "b1": 7, "f3": 8}
        for b in range(self.nb):
            def pre(tag):
                if b == 0:
                    self.convert_phase(phase_no[tag] + 1)
            if b == 0:
                self.convert_phase(0)
            self.mark("load%d" % b)
            self.load_x(b)
            for layer in range(self.n_layers):
                if layer < 2:
                    pre("a%d" % layer)
                    self.mark("a%d.%d" % (b, layer))
                    self.a_layer(layer)
                else:
                    if layer == 2:
                        pre("kv")
                        self.mark("kv%d" % b)
                        self.kv_phase()
                    pre("b%d" % (layer - 2))
                    self.mark("b%d.%d" % (b, layer))
                    self.b_layer(layer - 2)
                pre("f%d" % layer)
                self.mark("f%d.%d" % (b, layer))
                self.ffn_layer(layer)
            self.mark("store%d" % b)
            self.store_y(b)
        self.mark("end")
        P.wait_all("sp", self.out_trks)
        return self.nc


_CACHE = {}


def _prep_inputs(inputs):
    L = weight_layout()
    wall = pack_weights(inputs, L)
    vecs, bvecs = pack_vecs(inputs)
    cst = make_consts()
    return wall, vecs, bvecs, cst


def kernel(**inputs):
    inputs = {k: np.asarray(v) for k, v in inputs.items()}
    x = np.ascontiguousarray(inputs["x"], dtype=np.float32)
    B = x.shape[0]
    nb = B // NCORES
    wall, vecs, bvecs, cst = _prep_inputs(inputs)
    nc = Builder(nb).build()
    in_maps = []
    for c in range(NCORES):
        in_maps.append({"x": np.ascontiguousarray(x[c * nb:(c + 1) * nb]), "wall": wall, "vecs": vecs, "bvecs": bvecs, "cst": cst})
    res = run_bass_kernel_spmd(nc, in_maps, core_ids=list(range(NCORES)))
    out = np.concatenate([np.asarray(r["y"]).reshape(nb, S, D) for r in res.results], axis=0)
    return out.astype(np.float32)
```

```python
import numpy as np
import concourse.bass as bass
import concourse.mybir as mybir
from concourse.bass_utils import run_bass_kernel_spmd
from concourse.alu_op_type import AluOpType as ALU

F32 = mybir.dt.float32
BF16 = mybir.dt.bfloat16
AF = mybir.ActivationFunctionType

S = 2048
D = 1024
NCH = 8
TT = 512
NT = S // TT
FH = 2816
FC = 22
NCORES = 8
EPS = 1e-6
CONV_CH = 128 * 2048
SLOT = 2304
SCALE = 0.125
NEGM = -30000.0


def _kt(W, c0, width=128):
    K = W.shape[0]
    return np.ascontiguousarray(W[:, c0:c0 + width].reshape(K // 128, 128, width).transpose(1, 0, 2)).reshape(128, -1)


class WLayout:
    def __init__(self):
        self.off = {}
        self.free = {}
        self.n = 0

    def add(self, name, free):
        self.off[name] = self.n
        self.free[name] = free
        self.n += 128 * free

    def total(self):
        return ((self.n + CONV_CH - 1) // CONV_CH) * CONV_CH


def weight_layout():
    L = WLayout()
    L.phase_end = []

    def a(l):
        for c in range(8):
            L.add(("a_in", l, c), 2304)
        for m in range(8):
            L.add(("a_out", l, m), 1024)
        L.phase_end.append(L.n)

    def f(l):
        for c in range(FC):
            L.add(("f_in", l, c), 2048)
        for m in range(8):
            for hf in range(2):
                L.add(("f_out", l, m, hf), FH // 2)
        L.phase_end.append(L.n)

    def b(j):
        L.add(("bg", j), 384)
        for m in range(8):
            L.add(("bq", j, m), 1024)
        for m in range(8):
            L.add(("bo", j, m), 1024)
        L.phase_end.append(L.n)

    a(0)
    f(0)
    a(1)
    f(1)
    for i in range(4):
        L.add(("kvk", i), 1024)
    for i in range(4):
        L.add(("kvc", i), 1024)
    for i in range(2):
        L.add(("kvv", i), 2048)
    for i in range(2):
        L.add(("cw1", i), 8192)
    L.add(("cw2",), 256)
    L.phase_end.append(L.n)
    b(0)
    f(2)
    b(1)
    f(3)
    return L


def pack_weights(inp, L):
    out = np.zeros(L.total(), np.float32)

    def put(name, arr):
        arr = np.asarray(arr, np.float32).reshape(128, -1)
        assert arr.shape[1] == L.free[name], (name, arr.shape)
        out[L.off[name]:L.off[name] + arr.size] = arr.reshape(-1)

    for l in range(2):
        Win = inp["a_w_in"][l]
        for c in range(8):
            put(("a_in", l, c), np.concatenate(
                [_kt(Win, c * 128), _kt(Win, 1024 + c * 128), inp["a_gate_w"][l][0, c], inp["a_gate_w"][l][1, c]], axis=1))
        for m in range(8):
            put(("a_out", l, m), _kt(inp["a_w_out"][l], m * 128))
    for l in range(4):
        W = inp["f_w_in"][l]
        for c in range(FC):
            put(("f_in", l, c), np.concatenate([_kt(W, c * 128), _kt(W, FH + c * 128)], axis=1))
        for m in range(8):
            t = _kt(inp["f_w_out"][l], m * 128)
            for hf in range(2):
                put(("f_out", l, m, hf), t[:, hf * (FH // 2):(hf + 1) * (FH // 2)])
    kvw = inp["kv_w"]
    i = 0
    for jj in (2, 4):
        for cc in range(2):
            put(("kvk", i), _kt(kvw, jj * 256 + cc * 128))
            i += 1
    i = 0
    for jj in (0, 1):
        for cc in range(2):
            put(("kvc", i), _kt(kvw, jj * 256 + cc * 128))
            i += 1
    for i, jj in enumerate((3, 5)):
        put(("kvv", i), _kt(kvw, jj * 256, 256))
    for kv in range(2):
        t = np.zeros((128, 32, 256), np.float32)
        t[:64] = inp["cmp_w1"][kv].reshape(32, 64, 256).transpose(1, 0, 2)
        put(("cw1", kv), t)
    t = np.zeros((128, 2, 2, 64), np.float32)
    for kv in range(2):
        t[:, kv] = inp["cmp_w2"][kv].reshape(2, 128, 64).transpose(1, 0, 2)
    put(("cw2",), t)
    for j in range(2):
        put(("bg", j), _kt(inp["b_w_in"][j], 1024, 48))
        for m in range(8):
            put(("bq", j, m), _kt(inp["b_w_in"][j], m * 128))
        for m in range(8):
            put(("bo", j, m), _kt(inp["b_w_out"][j], m * 128))
    return out


VEC = {}
_nv = 0


def _vadd(name, n):
    global _nv
    VEC[name] = _nv
    _nv += n


for _l in range(2):
    _vadd(("a_norm", _l), 8)
    for _k in range(4):
        _vadd(("a_cw", _l, _k), 8)
    _vadd(("a_cb", _l), 8)
    _vadd(("a_gb", _l, 0), 8)
    _vadd(("a_gb", _l, 1), 8)
    _vadd(("a_lam", _l), 8)
for _l in range(4):
    _vadd(("f_norm", _l), 8)
_vadd(("kv_norm",), 8)
for _j in range(2):
    _vadd(("b_norm", _j), 8)
    _vadd(("q_norm", _j), 1)
for _i in range(3):
    _vadd(("k_norm", _i), 1)
for _i in range(2):
    _vadd(("c_b1", _i), 2)
    _vadd(("c_b2", _i), 1)
    _vadd(("c_pos", _i), 32)
NV = _nv
BV_GB = 0
BV_CB2V = 96
NBV = 160


def pack_vecs(inp):
    v = np.zeros((128, NV), np.float32)

    def fm(x):
        return np.asarray(x, np.float32).reshape(8, 128).T

    for l in range(2):
        v[:, VEC[("a_norm", l)]:][:, :8] = fm(inp["a_norm"][l])
        for k in range(4):
            v[:, VEC[("a_cw", l, k)]:][:, :8] = fm(inp["a_conv_w"][l][k])
        v[:, VEC[("a_cb", l)]:][:, :8] = fm(inp["a_conv_b"][l])
        v[:, VEC[("a_gb", l, 0)]:][:, :8] = fm(inp["a_gate_b"][l][0])
        v[:, VEC[("a_gb", l, 1)]:][:, :8] = fm(inp["a_gate_b"][l][1])
        v[:, VEC[("a_lam", l)]:][:, :8] = fm(inp["a_lambda"][l])
    for l in range(4):
        v[:, VEC[("f_norm", l)]:][:, :8] = fm(inp["f_norm"][l])
    v[:, VEC[("kv_norm",)]:][:, :8] = fm(inp["kv_norm"])
    for j in range(2):
        v[:, VEC[("b_norm", j)]:][:, :8] = fm(inp["b_norm"][j])
        v[:, VEC[("q_norm", j)]] = np.tile(np.asarray(inp["q_norm"][j], np.float32), 2)
    for i in range(3):
        v[:, VEC[("k_norm", i)]] = np.tile(np.asarray(inp["k_norm"][i], np.float32), 2)
    for i in range(2):
        v[:, VEC[("c_b1", i)]:][:, :2] = np.asarray(inp["cmp_b1"][i], np.float32).reshape(2, 128).T
        v[:, VEC[("c_b2", i)]] = np.tile(np.asarray(inp["cmp_b2"][i], np.float32), 2)
        v[:64, VEC[("c_pos", i)]:][:, :32] = np.asarray(inp["cmp_pos"][i], np.float32).T
    bv = np.zeros((128, NBV), np.float32)
    for j in range(2):
        bv[:, BV_GB + 48 * j: BV_GB + 48 * j + 48] = np.asarray(inp["b_gate_b"][j], np.float32)[None, :]
    bv[:, BV_CB2V:BV_CB2V + 64] = np.asarray(inp["cmp_b2"][1], np.float32)[None, :]
    return v, bv


CST = {}
_nc_ = 0


def _cadd(name, n):
    global _nc_
    CST[name] = _nc_
    _nc_ += n


_cadd("ident", 128)
_cadd("prot", 128)
_cadd("bones", 128)
_cadd("triL", 128)
_cadd("triU", 128)
_cadd("ovl", 32)
_cadd("cos", 2048)
_cadd("sin", 2048)
_cadd("cmask", 2048)
_cadd("eind", 2048)
_cadd("selb", 512)
NCST = _nc_


def make_consts():
    c = np.zeros((128, NCST), np.float32)
    p = np.arange(128)
    c[:, CST["ident"]:][:, :128] = np.eye(128)
    perm = (p // 64) * 64 + ((p % 64) + 32) % 64
    pr = np.zeros((128, 128), np.float32)
    pr[perm, p] = 1.0
    c[:, CST["prot"]:][:, :128] = pr
    c[:, CST["bones"]:][:, :128] = (p[:, None] // 64 == p[None, :] // 64)
    c[:, CST["triL"]:][:, :128] = (p[:, None] <= p[None, :])
    c[:, CST["triU"]:][:, :128] = (p[:, None] > p[None, :])
    cs = np.arange(127) * 16
    sl = np.arange(32) * 64
    ov = np.clip(np.minimum(cs[:, None] + 32, sl[None, :] + 64) - np.maximum(cs[:, None], sl[None, :]), 0, None) / 32.0
    c[:127, CST["ovl"]:][:, :32] = ov
    t = np.arange(2048, dtype=np.float64)
    fi = (p % 64) % 32
    freqs = (10000.0 ** (-(np.arange(32, dtype=np.float32) / np.float32(32)))).astype(np.float32)
    ang = (t[None, :].astype(np.float32) * freqs[fi][:, None]).astype(np.float32)
    c[:, CST["cos"]:][:, :2048] = np.cos(ang)
    sg = np.where((p % 64) < 32, -1.0, 1.0)[:, None]
    c[:, CST["sin"]:][:, :2048] = np.sin(ang) * sg
    cl = np.arange(127) * 16 + 31
    c[:127, CST["cmask"]:][:, :2048] = (cl[:, None] <= t[None, :])
    b = np.arange(32)
    c[64:96, CST["eind"]:][:, :2048] = (t[None, :].astype(np.int64) // 64 == b[:, None])
    sb = np.zeros((128, 16, 32), np.float32)
    for qb in range(16):
        tq = qb * 128 + p
        cur = (tq // 64)[:, None]
        causal = b[None, :] <= cur
        forced = (b[None, :] == 0) | (causal & (cur - b[None, :] < 2))
        sb[:, qb, :] = np.where(forced, 1e30, np.where(causal, 0.0, -1e30))
    c[:, CST["selb"]:][:, :512] = sb.reshape(128, 512)
    return c


class Trk:
    __slots__ = ("w", "r")

    def __init__(self):
        self.w = None
        self.r = {}


NDS = 24


class Prog:
    def __init__(self, nc):
        self.nc = nc
        self.eng = {"pe": nc.tensor, "act": nc.scalar, "dve": nc.vector, "pool": nc.gpsimd, "sp": nc.sync}
        self.sem = {e: nc.alloc_semaphore(name="s_" + e) for e in ("pe", "act", "dve", "pool")}
        self.cnt = {e: 0 for e in ("pe", "act", "dve", "pool")}
        self.seen = {e: {} for e in self.eng}
        self.dsem = [nc.alloc_semaphore(name="s_dma%d" % i) for i in range(NDS)]
        self.dn = 0
        self.ninst = 0

    def _semof(self, key):
        return self.sem[key] if isinstance(key, str) else self.dsem[key[1]]

    def _wait(self, e, key, val):
        if key == "pe" and e == "pe":
            return
        if self.seen[e].get(key, 0) >= val:
            return
        self.eng[e].wait_ge(self._semof(key), val)
        self.seen[e][key] = val

    def _deps(self, e, r, w):
        for t in r:
            if t.w is not None:
                self._wait(e, t.w[0], t.w[1])
        for t in w:
            if t.w is not None:
                self._wait(e, t.w[0], t.w[1])
            for k, v in t.r.items():
                self._wait(e, k, v)

    def op(self, e, fn, r=(), w=()):
        self._deps(e, r, w)
        inst = fn(self.eng[e])
        self.cnt[e] += 1
        self.ninst += 1
        inst.then_inc(self.sem[e], 1)
        v = self.cnt[e]
        for t in r:
            t.r[e] = v
        for t in w:
            t.w = (e, v)
            t.r = {}
        return inst

    def dma(self, q, out, in_, r=(), w=()):
        self._deps(q, r, w)
        i = self.dn % NDS
        gen = self.dn // NDS
        key = ("d", i)
        if gen > 0:
            self._wait(q, key, 16 * gen)
        inst = self.eng[q].dma_start(out=out, in_=in_)
        inst.then_inc(self.dsem[i], 16)
        self.dn += 1
        self.ninst += 1
        v = 16 * (gen + 1)
        for t in r:
            t.r[key] = v
        for t in w:
            t.w = (key, v)
            t.r = {}
        return (key, v)

    def wait_all(self, e, trks):
        self._deps(e, trks, trks)


class Buf:
    def __init__(self, t):
        self.t = t
        self.k = Trk()

    def __getitem__(self, idx):
        return self.t[idx]


class Builder:
    def __init__(self, nb, n_layers=4, debug_out=None):
        self.nb = nb
        self.n_layers = n_layers
        self.L = weight_layout()
        nc = bass.Bass("TRN2", target_bir_lowering=False)
        self.nc = nc
        self.P = Prog(nc)
        NW = self.L.total()
        self.x = nc.dram_tensor("x", [nb, S, D], F32, kind="ExternalInput").ap()
        self.wall = nc.dram_tensor("wall", [NW], F32, kind="ExternalInput").ap()
        self.vecs_d = nc.dram_tensor("vecs", [128, NV], F32, kind="ExternalInput").ap()
        self.bvecs_d = nc.dram_tensor("bvecs", [128, NBV], F32, kind="ExternalInput").ap()
        self.cst_d = nc.dram_tensor("cst", [128, NCST], F32, kind="ExternalInput").ap()
        self.y = nc.dram_tensor("y", [nb, S, D], F32, kind="ExternalOutput").ap()
        self.wbf = nc.dram_tensor("wbf", [NW], BF16, kind="Internal").ap()
        self.wchunk = [Trk() for _ in range(NW // CONV_CH)]
        self.out_trks = []
        self.marks = []

        def sb(name, shape, dt):
            return Buf(nc.alloc_sbuf_tensor(name, shape, dt))

        self.sb = sb
        self.hT = nc.alloc_sbuf_tensor("hT", [128, NCH, S], F32)
        self.hk = [[Trk() for _ in range(NT)] for _ in range(NCH)]
        self.vecs = sb("vecs_sb", [128, NV], F32)
        self.bvecs = sb("bvecs_sb", [128, NBV], F32)
        self.ident_f = sb("ident_f", [128, 128], F32)
        self.ident_b = sb("ident_b", [128, 128], BF16)
        self.ones_b = sb("ones_b", [128, 128], BF16)
        self.coef = sb("coef", [128, 2, 2, 8], F32)
        self.hgb = sb("hgb", [128, 2, 2, 8], F32)
        self.lnhalf = sb("lnhalf", [128, 2], F32)
        self.cos_b = sb("cos_b", [128, S], BF16)
        self.sin_b = sb("sin_b", [128, S], BF16)
        self.cmask_b = sb("cmask_b", [128, S], BF16)
        self.triL_b = sb("triL_b", [128, 128], BF16)
        self.triU_b = sb("triU_b", [128, 128], BF16)
        self.prot_b = sb("prot_b", [128, 128], BF16)
        self.bones_b = sb("bones_b", [128, 128], BF16)
        self.selb = sb("selb", [128, 16, 32], BF16)
        self.Qnext = sb("Qnext", [128, 8, 512], BF16)
        self.KC = sb("KC", [128, 4, 128], BF16)
        self.VC = sb("VC", [128, 4, 98], BF16)
        self.posb = sb("posb", [128, 2, 32], BF16)
        self.cbias = sb("cbias", [128, 2, 2], F32)
        self.pocs = [sb("pocs%d" % i, [128, 4, 97], F32) for i in range(2)]
        self.pows = sb("pows", [128, 4, 65], F32)
        self.KSd = nc.dram_tensor("KSd", [2, 64, 4, S], BF16, kind="Internal").ap()
        self.Vd = nc.dram_tensor("Vd", [2, 128, 16, 4, 66], BF16, kind="Internal").ap()
        self.ksd_k = [[[Trk() for _ in range(NT)] for _ in range(4)] for _ in range(2)]
        self.vd_k = [[Trk() for _ in range(NT)] for _ in range(2)]
        self.ps = [Buf(nc.alloc_psum_tensor("ps%d" % i, [128, 512], F32)) for i in range(8)]
        self.ps_rr = 0
        self.pool_misc = [0, 1, 2, 3, 4, 5, 6, 7]
        self.NSLOT = 3
        self.ring = [sb("wring%d" % i, [128, SLOT], BF16) for i in range(self.NSLOT)]
        self.perm_ids = set(id(b) for b in self.ring)
        self.PHB = 100352
        self.ph = nc.alloc_sbuf_tensor("phase", [128, self.PHB // 4], F32)
        self.phk = {}

    def bank(self, pool=None):
        if pool is None:
            b = self.ps[self.ps_rr % 8]
        else:
            b = self.ps[pool[self.ps_rr % len(pool)]]
        self.ps_rr += 1
        return b

    def vec(self, name, c=0):
        i = VEC[name] + c
        return self.vecs[:, i:i + 1]

    def phase_view(self, byte_off, shape, dt):
        esz = 4 if dt == F32 else 2
        n = int(np.prod(shape[1:]))
        assert byte_off % 4 == 0 and byte_off + n * esz <= self.PHB, (byte_off, shape)
        if dt == F32:
            ap = self.ph[:, byte_off // 4: byte_off // 4 + n]
        else:
            ap = self.ph[:].bitcast(BF16)[:, byte_off // 2: byte_off // 2 + n]
        if len(shape) == 2:
            return ap
        names = " ".join("a%d" % i for i in range(len(shape) - 1))
        kw = {"a%d" % i: shape[i + 1] for i in range(len(shape) - 1)}
        return ap.rearrange("p (%s) -> p %s" % (names, names), **kw)

    def mark(self, name):
        self.marks.append((name, dict(self.P.cnt)))

    def new_phase(self, trks):
        merged = {}
        for t in self.phase_cur:
            if t.w is not None:
                merged[t.w[0]] = max(merged.get(t.w[0], 0), t.w[1])
            for k, v in t.r.items():
                merged[k] = max(merged.get(k, 0), v)
        for t in trks:
            t.w = None
            t.r = dict(merged)
        self.phase_cur = list(trks)

    def plan_weights(self, entries):
        self.wplan = list(entries)
        self.wplan_i = 0
        self.wq = []
        self.free_perm = list(self.ring)
        self.free_ext = []
        self.inuse = None
        self.serial = -1

    def begin_phase(self, ext_slots):
        self._release()
        self.serial += 1
        self.free_ext = list(ext_slots)
        self.ext_ids = set(id(b) for b in ext_slots)

    def _release(self):
        if self.inuse is not None:
            slot, ser = self.inuse
            if id(slot) in self.perm_ids:
                self.free_perm.append(slot)
            elif ser == self.serial:
                self.free_ext.append(slot)
            self.inuse = None

    def _topup(self):
        while self.wplan_i < len(self.wplan):
            name, ser = self.wplan[self.wplan_i]
            if ser == self.serial and self.free_ext:
                slot = self.free_ext.pop(0)
            elif self.free_perm:
                slot = self.free_perm.pop(0)
            else:
                break
            self.wplan_i += 1
            off = self.L.off[name]
            free = self.L.free[name]
            src = self.wbf[off:off + 128 * free].rearrange("(p f) -> p f", p=128)
            c0 = off // CONV_CH
            c1 = (off + 128 * free - 1) // CONV_CH
            self.P.dma("sp", slot[:, 0:free], src, r=[self.wchunk[c] for c in range(c0, c1 + 1)], w=[slot.k])
            self.wq.append((name, slot, ser))

    def next_w(self, name, hold=False):
        self._release()
        self._topup()
        n, slot, ser = self.wq.pop(0)
        assert n == name and ser == self.serial, (n, name, ser, self.serial)
        if not hold:
            self.inuse = (slot, ser)
        return slot

    def release_slot(self, slot):
        if id(slot) in self.perm_ids:
            self.free_perm.append(slot)
        elif id(slot) in self.ext_ids:
            self.free_ext.append(slot)

    def ext_slots(self, byte_off):
        out = []
        o = byte_off
        while o + 2 * SLOT <= self.PHB:
            b = Buf(self.phase_view(o, [128, SLOT], BF16))
            out.append(b)
            o += 2 * SLOT
        return out

    def prologue(self):
        P = self.P
        nc = self.nc
        P.dma("sp", self.vecs[:], self.vecs_d, w=[self.vecs.k])
        P.dma("sp", self.bvecs[:], self.bvecs_d, w=[self.bvecs.k])
        P.dma("sp", self.ident_f[:], self.cst_d[:, CST["ident"]:CST["ident"] + 128], w=[self.ident_f.k])
        P.op("dve", lambda e: e.tensor_copy(out=self.ident_b[:], in_=self.ident_f[:]), r=[self.ident_f.k], w=[self.ident_b.k])
        P.op("pool", lambda e: e.memset(self.ones_b[:], 1.0), w=[self.ones_b.k])
        tmp = self.sb("coef_tmp", [128, 16], F32)
        for l in range(2):
            lam = self.vecs[:, VEC[("a_lam", l)]:VEC[("a_lam", l)] + 8]
            P.op("act", lambda e: e.activation(out=tmp[:, l * 8:l * 8 + 8], in_=lam, func=AF.Exp, scale=-1.0), r=[self.vecs.k], w=[tmp.k])
            P.op("act", lambda e: e.activation(out=tmp[:, l * 8:l * 8 + 8], in_=tmp[:, l * 8:l * 8 + 8], func=AF.Ln, bias=1.0), r=[tmp.k], w=[tmp.k])
            P.op("dve", lambda e: e.tensor_scalar(out=self.coef[:, l, 0, :], in0=tmp[:, l * 8:l * 8 + 8], scalar1=-4.0, scalar2=None, op0=ALU.mult), r=[tmp.k], w=[self.coef.k])
            P.op("dve", lambda e: e.tensor_scalar(out=self.coef[:, l, 1, :], in0=tmp[:, l * 8:l * 8 + 8], scalar1=-8.0, scalar2=None, op0=ALU.mult), r=[tmp.k], w=[self.coef.k])
            for kk in range(2):
                gbo = VEC[("a_gb", l, kk)]
                P.op("dve", lambda e: e.tensor_scalar(out=self.hgb[:, l, kk, :], in0=self.vecs[:, gbo:gbo + 8], scalar1=0.5, scalar2=None, op0=ALU.mult), r=[self.vecs.k], w=[self.hgb.k])
        P.op("pool", lambda e: e.memset(self.lnhalf[:], -0.6931471805599453), w=[self.lnhalf.k])
        stg = self.phase_view(0, [128, 2048], F32)
        stk = Trk()
        self.phase_cur = [stk]
        self.conv_done = 0

        def ctab(name, n, dst, dstk, eng):
            P.dma("sp", stg[:, 0:n], self.cst_d[:, CST[name]:CST[name] + n], w=[stk])
            if eng == "act":
                P.op("act", lambda e: e.activation(out=dst, in_=stg[:, 0:n], func=AF.Copy), r=[stk], w=[dstk])
            else:
                P.op(eng, lambda e: e.tensor_copy(out=dst, in_=stg[:, 0:n]), r=[stk], w=[dstk])

        ctab("cos", 2048, self.cos_b[:], self.cos_b.k, "dve")
        ctab("sin", 2048, self.sin_b[:], self.sin_b.k, "act")
        ctab("cmask", 2048, self.cmask_b[:], self.cmask_b.k, "dve")
        ctab("triL", 128, self.triL_b[:], self.triL_b.k, "dve")
        ctab("triU", 128, self.triU_b[:], self.triU_b.k, "dve")
        ctab("prot", 128, self.prot_b[:], self.prot_b.k, "dve")
        ctab("bones", 128, self.bones_b[:], self.bones_b.k, "dve")
        ctab("selb", 512, self.selb[:].rearrange("p a b -> p (a b)"), self.selb.k, "dve")
        P.op("pool", lambda e: e.memset(self.VC[:], 0.0), w=[self.VC.k])
        P.op("pool", lambda e: e.memset(self.KC[:], 0.0), w=[self.KC.k])
        P.op("pool", lambda e: e.memset(self.VC[:, :, 64:65], 1.0), w=[self.VC.k])
        P.dma("sp", stg[:, 0:32], self.cst_d[:, CST["ovl"]:CST["ovl"] + 32], w=[stk])
        for g in range(4):
            P.op("dve", lambda e: e.tensor_copy(out=self.VC[:, g, 65:97], in_=stg[:, 0:32]), r=[stk], w=[self.VC.k])
        for kv in range(2):
            o0 = VEC[("c_pos", kv)]
            P.op("dve", lambda e: e.tensor_copy(out=self.posb[:, kv, :], in_=self.vecs[:, o0:o0 + 32]), r=[self.vecs.k], w=[self.posb.k])
        self.conv_done = 0

    def convert_upto(self, nchunks):
        nchunks = min(nchunks, len(self.wchunk))
        for i in range(self.conv_done, nchunks):
            src = self.wall[i * CONV_CH:(i + 1) * CONV_CH].rearrange("(p f) -> p f", p=128)
            dst = self.wbf[i * CONV_CH:(i + 1) * CONV_CH].rearrange("(p f) -> p f", p=128)
            self.P.dma("pool", dst, src, w=[self.wchunk[i]])
        self.conv_done = max(self.conv_done, nchunks)

    def convert_phase(self, ph):
        ends = self.L.phase_end
        ph = min(ph, len(ends) - 1)
        self.convert_upto((ends[ph] + CONV_CH - 1) // CONV_CH)

    def load_x(self, b):
        P = self.P
        xin = [(self.phase_view(j * 4096, [128, 1024], F32), Trk()) for j in range(4)]
        self.new_phase([k for _, k in xin])
        for n in range(NT):
            for j in range(4):
                t0 = n * TT + j * 128
                P.dma("sp", xin[j][0], self.x[b, t0:t0 + 128, :], w=[xin[j][1]])
            for c in range(NCH):
                pb = self.bank()
                for j in range(4):
                    P.op("pe", lambda e: e.transpose(out=pb[:, j * 128:(j + 1) * 128], in_=xin[j][0][:, c * 128:(c + 1) * 128], identity=self.ident_f[:]),
                         r=[xin[j][1], self.ident_f.k], w=[pb.k])
                eng = "act" if c % 2 == 0 else "dve"
                dst = self.hT[:, c, n * TT:(n + 1) * TT]
                if eng == "act":
                    P.op("act", lambda e: e.activation(out=dst, in_=pb[:], func=AF.Copy), r=[pb.k], w=[self.hk[c][n]])
                else:
                    P.op("dve", lambda e: e.tensor_copy(out=dst, in_=pb[:]), r=[pb.k], w=[self.hk[c][n]])

    def store_y(self, b):
        P = self.P
        yo = [(self.phase_view(j * 4096, [128, 1024], F32), Trk()) for j in range(4)]
        self.new_phase([k for _, k in yo])
        for n in range(NT):
            for j in range(4):
                t0 = n * TT + j * 128
                for half in range(2):
                    pb = self.bank()
                    for cc in range(4):
                        c = half * 4 + cc
                        P.op("pe", lambda e: e.transpose(out=pb[:, cc * 128:(cc + 1) * 128], in_=self.hT[:, c, t0:t0 + 128], identity=self.ident_f[:]),
                             r=[self.hk[c][n], self.ident_f.k], w=[pb.k])
                    dst = yo[j][0][:, half * 512:(half + 1) * 512]
                    wl = [yo[j][1]]
                    if half == 0:
                        P.op("act", lambda e: e.activation(out=dst, in_=pb[:], func=AF.Copy), r=[pb.k], w=wl)
                    else:
                        P.op("dve", lambda e: e.tensor_copy(out=dst, in_=pb[:]), r=[pb.k], w=wl)
                ot = Trk()
                P.dma("sp", self.y[b, t0:t0 + 128, :], yo[j][0], r=[yo[j][1]], w=[ot])
                self.out_trks.append(ot)

    def norm_tile(self, n, gname, u_ap, u_k, sq_bufs, rstd_buf, pool=None):
        P = self.P
        pb = self.bank(pool)
        for c in range(NCH):
            sq, sk = sq_bufs[c % len(sq_bufs)]
            hsl = self.hT[:, c, n * TT:(n + 1) * TT]
            P.op("act", lambda e: e.activation(out=sq, in_=hsl, func=AF.Square), r=[self.hk[c][n]], w=[sk])
            P.op("pe", lambda e: e.matmul(pb[:], lhsT=self.ones_b[:], rhs=sq, start=(c == 0), stop=(c == NCH - 1)),
                 r=[sk, self.ones_b.k], w=[pb.k])
        rs, rk = rstd_buf
        P.op("act", lambda e: e.activation(out=rs, in_=pb[:], func=AF.Ln, scale=1.0 / D, bias=EPS), r=[pb.k], w=[rk])
        P.op("act", lambda e: e.activation(out=rs, in_=rs, func=AF.Exp, scale=-0.5), r=[rk], w=[rk])
        for c in range(NCH):
            hsl = self.hT[:, c, n * TT:(n + 1) * TT]
            g = self.vec(gname, c)
            P.op("dve", lambda e: e.scalar_tensor_tensor(out=u_ap[:, c, :], in0=hsl, scalar=g, in1=rs, op0=ALU.mult, op1=ALU.mult),
                 r=[self.hk[c][n], rk, self.vecs.k], w=[u_k])

    def a_phase_setup(self):
        v = self.phase_view
        A = {}
        A["u"] = [(v(n * 8192, [128, 8, 512], BF16), Trk()) for n in range(NT)]
        A["m"] = [(v(32768 + n * 8192, [128, 8, 512], BF16), Trk()) for n in range(NT)]
        A["sq"] = [(v(65536 + i * 1024, [128, 512], BF16), Trk()) for i in range(2)]
        A["rstd"] = (v(67584, [128, 512], F32), Trk())
        A["xp"] = [(v(69632 + i * 2080, [128, 516], F32), Trk()) for i in range(2)]
        base = 73792
        s1 = []
        for i in range(3):
            o = base + i * 4096
            xr = (v(o + 2048, [128, 512], F32), Trk())
            s1.append({"y": (v(o, [128, 512], BF16), Trk()), "xrb": (v(o + 1024, [128, 512], BF16), Trk()), "xr": xr, "hr": xr})
        s2 = []
        for i in range(2):
            o = base + 12288 + i * 6144
            rr = (v(o, [128, 512], F32), Trk())
            s2.append({"r": rr, "a2": rr, "i": (v(o + 2048, [128, 512], F32), Trk()), "a": (v(o + 4096, [128, 512], F32), Trk())})
        sets = s1 + s2
        A["s1"] = s1
        A["s2"] = s2
        A["sets"] = sets
        trks = [k for _, k in A["u"]] + [k for _, k in A["m"]] + [A["rstd"][1]] + [k for _, k in A["sq"]] + [k for _, k in A["xp"]]
        for st in sets:
            trks += [k for _, k in st.values()]
        self.new_phase(list(dict((id(t), t) for t in trks).values()))
        self.begin_phase([])
        self.A = A

    def a_layer(self, l):
        P = self.P
        self.a_phase_setup()
        A = self.A
        P.op("pool", lambda e: e.memset(self.convc[:], 0.0), w=[self.convc.k])
        P.op("pool", lambda e: e.memset(self.hst[:], 0.0), w=[self.hst.k])
        for n in range(NT):
            self.norm_tile(n, ("a_norm", l), A["u"][n][0], A["u"][n][1], A["sq"], A["rstd"])
        items = [(c, n) for c in range(NCH) for n in range(NT)]
        wslot = {}

        def stage1(it):
            c, n = items[it]
            if n == 0:
                wslot[c] = self.next_w(("a_in", l, c), hold=True)
            w = wslot[c]
            u, uk = A["u"][n]
            pg = self.bank()
            pr = self.bank()
            for k in range(NCH):
                P.op("pe", lambda e: e.matmul(pg[:], lhsT=w[:, k * 128:(k + 1) * 128], rhs=u[:, k, :], start=(k == 0), stop=(k == NCH - 1)),
                     r=[w.k, uk], w=[pg.k])
            for k in range(NCH):
                P.op("pe", lambda e: e.matmul(pr[:], lhsT=w[:, 1024 + k * 128:1024 + (k + 1) * 128], rhs=u[:, k, :], start=(k == 0), stop=(k == NCH - 1)),
                     r=[w.k, uk], w=[pr.k])
            st = A["s1"][it % 3]
            xp, xpk = A["xp"][it % 2]
            y, yk = st["y"]
            xr, xrk = st["xr"]
            xrb, xrbk = st["xrb"]
            P.op("act", lambda e: e.activation(out=y, in_=pg[:], func=AF.Gelu_apprx_tanh), r=[pg.k], w=[yk])
            P.op("pool", lambda e: e.tensor_copy(out=xp[:, 0:3], in_=self.convc[:, c, :]), r=[self.convc.k], w=[xpk])
            P.op("act", lambda e: e.activation(out=xp[:, 3:515], in_=pr[:], func=AF.Copy), r=[pr.k], w=[xpk])
            P.op("dve", lambda e: e.tensor_scalar(out=xr, in0=xp[:, 0:512], scalar1=self.vec(("a_cw", l, 0), c), scalar2=self.vec(("a_cb", l), c),
                                                  op0=ALU.mult, op1=ALU.add), r=[xpk, self.vecs.k], w=[xrk])
            for kk in range(1, 4):
                P.op("dve", lambda e: e.scalar_tensor_tensor(out=xr, in0=xp[:, kk:kk + 512], scalar=self.vec(("a_cw", l, kk), c), in1=xr,
                                                             op0=ALU.mult, op1=ALU.add), r=[xpk, xrk, self.vecs.k], w=[xrk])
            P.op("pool", lambda e: e.tensor_copy(out=self.convc[:, c, :], in_=xp[:, 512:515]), r=[xpk], w=[self.convc.k])
            P.op("pool", lambda e: e.tensor_copy(out=xrb, in_=xr), r=[xrk], w=[xrbk])

        def stage2(it):
            c, n = items[it]
            w = wslot[c]
            m, mk = A["m"][n]
            st = A["s1"][it % 3]
            y, yk = st["y"]
            xr, xrk = st["xr"]
            xrb, xrbk = st["xrb"]
            hr, hrk = st["hr"]
            t2 = A["s2"][it % 2]
            rr, rrk = t2["r"]
            ii, iik = t2["i"]
            aa, aak = t2["a"]
            p1 = self.bank()
            p2 = self.bank()
            P.op("pe", lambda e: e.matmul(p1[:], lhsT=w[:, 2048:2176], rhs=xrb, start=True, stop=True), r=[w.k, xrbk], w=[p1.k])
            P.op("pe", lambda e: e.matmul(p2[:], lhsT=w[:, 2176:2304], rhs=xrb, start=True, stop=True), r=[w.k, xrbk], w=[p2.k])
            if n == NT - 1:
                self.release_slot(w)
            P.op("act", lambda e: e.activation(out=rr, in_=p1[:], func=AF.Tanh, scale=0.5, bias=self.hgb[:, l, 0, c:c + 1]), r=[p1.k, self.hgb.k], w=[rrk])
            P.op("act", lambda e: e.activation(out=ii, in_=p2[:], func=AF.Tanh, scale=0.5, bias=self.hgb[:, l, 1, c:c + 1]), r=[p2.k, self.hgb.k], w=[iik])
            P.op("act", lambda e: e.activation(out=aa, in_=rr, func=AF.Exp, scale=self.coef[:, l, 0, c:c + 1], bias=self.coef[:, l, 0, c:c + 1]), r=[rrk, self.coef.k], w=[aak])
            P.op("act", lambda e: e.activation(out=rr, in_=rr, func=AF.Exp, scale=self.coef[:, l, 1, c:c + 1], bias=self.coef[:, l, 1, c:c + 1]), r=[rrk, self.coef.k], w=[rrk])
            P.op("act", lambda e: e.activation(out=rr, in_=rr, func=AF.Ln, scale=-0.999999, bias=1.0), r=[rrk], w=[rrk])
            P.op("act", lambda e: e.activation(out=rr, in_=rr, func=AF.Exp, scale=0.5, bias=self.lnhalf[:, 0:1]), r=[rrk, self.lnhalf.k], w=[rrk])
            P.op("dve", lambda e: e.scalar_tensor_tensor(out=ii, in0=ii, scalar=1.0, in1=xr, op0=ALU.add, op1=ALU.mult), r=[iik, xrk], w=[iik])
            P.op("dve", lambda e: e.tensor_tensor(out=ii, in0=ii, in1=rr, op=ALU.mult), r=[iik, rrk], w=[iik])
            P.op("dve", lambda e: e.tensor_tensor_scan(out=hr, data0=aa, data1=ii, initial=self.hst[:, c:c + 1], op0=ALU.mult, op1=ALU.add),
                 r=[aak, iik, self.hst.k], w=[hrk])
            P.op("pool", lambda e: e.tensor_copy(out=self.hst[:, c:c + 1], in_=hr[:, 511:512]), r=[hrk], w=[self.hst.k])
            P.op("pool", lambda e: e.tensor_tensor(out=m[:, c, :], in0=hr, in1=y, op=ALU.mult), r=[hrk, yk], w=[mk])

        LA = 2
        for it in range(min(LA, len(items))):
            stage1(it)
        for it in range(len(items)):
            if it + LA < len(items):
                stage1(it + LA)
            stage2(it)
        for mm in range(NCH):
            w = self.next_w(("a_out", l, mm))
            for n in range(NT):
                m, mk = A["m"][n]
                po = self.bank()
                for k in range(NCH):
                    P.op("pe", lambda e: e.matmul(po[:], lhsT=w[:, k * 128:(k + 1) * 128], rhs=m[:, k, :], start=(k == 0), stop=(k == NCH - 1)),
                         r=[w.k, mk], w=[po.k])
                hsl = self.hT[:, mm, n * TT:(n + 1) * TT]
                P.op("dve", lambda e: e.tensor_tensor(out=hsl, in0=po[:], in1=hsl, op=ALU.add), r=[po.k, self.hk[mm][n]], w=[self.hk[mm][n]])

    def ffn_phase_setup(self):
        v = self.phase_view
        Fz = {}
        Fz["u"] = [(v(s * 8192, [128, 8, 512], BF16), Trk()) for s in range(2)]
        Fz["act"] = [[(v(16384 + (c * 2 + s) * 1024, [128, 512], BF16), Trk()) for s in range(2)] for c in range(FC)]
        Fz["sq"] = [(v(61440 + i * 1024, [128, 512], BF16), Trk()) for i in range(2)]
        Fz["rstd"] = (v(63488, [128, 512], F32), Trk())
        Fz["sg"] = [(v(65536 + i * 1024, [128, 512], BF16), Trk()) for i in range(4)]
        ext = self.ext_slots(69632)
        trks = [k for _, k in Fz["u"]] + [k for row in Fz["act"] for _, k in row] + [k for _, k in Fz["sq"]] + [Fz["rstd"][1]] + [k for _, k in Fz["sg"]] + [b.k for b in ext]
        self.new_phase(trks)
        self.begin_phase(ext)
        self.F = Fz

    def ffn_tile(self, L, t2):
        P = self.P
        Fz = self.F
        for s in range(2):
            self.norm_tile(2 * t2 + s, ("f_norm", L), Fz["u"][s][0], Fz["u"][s][1], Fz["sq"], Fz["rstd"])
        nsg = 0
        for c in range(FC):
            w = self.next_w(("f_in", L, c))
            for s in range(2):
                u, uk = Fz["u"][s]
                pg = self.bank()
                pu = self.bank()
                for k in range(NCH):
                    P.op("pe", lambda e: e.matmul(pg[:], lhsT=w[:, k * 128:(k + 1) * 128], rhs=u[:, k, :], start=(k == 0), stop=(k == NCH - 1)),
                         r=[w.k, uk], w=[pg.k])
                for k in range(NCH):
                    P.op("pe", lambda e: e.matmul(pu[:], lhsT=w[:, 1024 + k * 128:1024 + (k + 1) * 128], rhs=u[:, k, :], start=(k == 0), stop=(k == NCH - 1)),
                         r=[w.k, uk], w=[pu.k])
                sg, sgk = Fz["sg"][nsg % 4]
                nsg += 1
                a, ak = Fz["act"][c][s]
                P.op("act", lambda e: e.activation(out=sg, in_=pg[:], func=AF.Silu), r=[pg.k], w=[sgk])
                P.op("dve", lambda e: e.tensor_tensor(out=a, in0=sg, in1=pu[:], op=ALU.mult), r=[sgk, pu.k], w=[ak])
        HC = FC // 2
        for mm in range(NCH):
            pos_ = [self.bank(), self.bank()]
            for hf in range(2):
                w = self.next_w(("f_out", L, mm, hf))
                for s in range(2):
                    po = pos_[s]
                    for cc in range(HC):
                        c = hf * HC + cc
                        a, ak = Fz["act"][c][s]
                        P.op("pe", lambda e: e.matmul(po[:], lhsT=w[:, cc * 128:(cc + 1) * 128], rhs=a, start=(c == 0), stop=(c == FC - 1)),
                             r=[w.k, ak], w=[po.k])
            for s in range(2):
                n = 2 * t2 + s
                po = pos_[s]
                hsl = self.hT[:, mm, n * TT:(n + 1) * TT]
                P.op("dve", lambda e: e.tensor_tensor(out=hsl, in0=po[:], in1=hsl, op=ALU.add), r=[po.k, self.hk[mm][n]], w=[self.hk[mm][n]])

    def ffn_layer(self, L):
        self.ffn_phase_setup()
        for t2 in range(2):
            self.ffn_tile(L, t2)

    def headnorm_rope(self, pk, R, C, gvec, cos_ap, sin_ap, T, bias=None):
        P = self.P
        sq, sqk = T["sq"]
        rs, rsk = T["rs"]
        qn, qnk = T["qn"]
        t1, t1k = T["t1"]
        t2, t2k = T["t2"]
        if bias is not None:
            xf, xfk = T["xf"]
            P.op("act", lambda e: e.activation(out=xf[0:R, 0:C], in_=pk[0:R, 0:C], func=AF.Identity, bias=bias), r=[pk.k, self.vecs.k], w=[xfk])
            src, srck = xf[0:R, 0:C], xfk
        else:
            src, srck = pk[0:R, 0:C], pk.k
        P.op("act", lambda e: e.activation(out=sq[0:R, 0:C], in_=src, func=AF.Square), r=[srck], w=[sqk])
        pss = self.bank(self.pool_misc)
        P.op("pe", lambda e: e.matmul(pss[0:R, 0:C], lhsT=self.bones_b[0:R, 0:R], rhs=sq[0:R, 0:C], start=True, stop=True), r=[sqk, self.bones_b.k], w=[pss.k])
        P.op("act", lambda e: e.activation(out=rs[0:R, 0:C], in_=pss[0:R, 0:C], func=AF.Ln, scale=1.0 / 64.0, bias=EPS), r=[pss.k], w=[rsk])
        P.op("act", lambda e: e.activation(out=rs[0:R, 0:C], in_=rs[0:R, 0:C], func=AF.Exp, scale=-0.5), r=[rsk], w=[rsk])
        P.op("dve", lambda e: e.scalar_tensor_tensor(out=qn[0:R, 0:C], in0=src, scalar=gvec, in1=rs[0:R, 0:C], op0=ALU.mult, op1=ALU.mult),
             r=[srck, rsk, self.vecs.k], w=[qnk])
        prt = self.bank(self.pool_misc)
        P.op("pe", lambda e: e.matmul(prt[0:R, 0:C], lhsT=self.prot_b[0:R, 0:R], rhs=qn[0:R, 0:C], start=True, stop=True), r=[qnk, self.prot_b.k], w=[prt.k])
        P.op("pool", lambda e: e.tensor_tensor(out=t1[0:R, 0:C], in0=qn[0:R, 0:C], in1=cos_ap, op=ALU.mult), r=[qnk, self.cos_b.k], w=[t1k])
        P.op("dve", lambda e: e.tensor_tensor(out=t2[0:R, 0:C], in0=prt[0:R, 0:C], in1=sin_ap, op=ALU.mult), r=[prt.k, self.sin_b.k], w=[t2k])

    def kv_phase(self):
        P = self.P
        v = self.phase_view
        U = [(v(n * 8192, [128, 8, 512], BF16), Trk()) for n in range(NT)]
        sqn = [(v(32768 + i * 1024, [128, 512], BF16), Trk()) for i in range(2)]
        rstd = (v(34816, [128, 512], F32), Trk())
        kcT, kcTk = v(36864, [128, 8, S], BF16), [Trk() for _ in range(8)]
        cw1, cw1k = v(0, [128, 2, 32, 256], BF16), Trk()
        T = {"sq": (v(69632, [128, 512], BF16), Trk()), "rs": (v(70656, [128, 512], F32), Trk()), "qn": (v(72704, [128, 512], BF16), Trk()),
             "t1": (v(73728, [128, 512], F32), Trk()), "t2": (v(75776, [128, 512], F32), Trk()), "xf": (v(77824, [128, 512], F32), Trk())}
        kout = [(v(79872 + i * 1024, [128, 512], BF16), Trk()) for i in range(2)]
        vst = [(v(81920 + i * 528, [128, 4, 66], BF16), Trk()) for i in range(2)]
        hid = [(v(83008 + i * 512, [128, 2, 128], BF16), Trk()) for i in range(2)]
        ext = self.ext_slots(84032)
        trks = [k for _, k in U] + [rstd[1], cw1k] + [k for _, k in sqn] + kcTk + [k for _, k in T.values()] + [k for _, k in kout] + [k for _, k in vst] \
            + [k for _, k in hid] + [b.k for b in ext]
        self.new_phase(trks)
        self.begin_phase(ext)
        self.pool_misc = [0, 1, 2, 3, 4, 5, 6, 7]
        for i in range(2):
            P.op("pool", lambda e: e.memset(vst[i][0][:, :, 64:66], 1.0), w=[vst[i][1]])
        for n in range(NT):
            self.norm_tile(n, ("kv_norm",), U[n][0], U[n][1], sqn, rstd)
        nko = 0
        nvs = 0
        for i in range(4):
            which, cc = i // 2, i % 2
            w = self.next_w(("kvk", i))
            for n in range(NT):
                tsl = slice(n * TT, (n + 1) * TT)
                u, uk = U[n]
                pk = self.bank()
                for k in range(NCH):
                    P.op("pe", lambda e: e.matmul(pk[:], lhsT=w[:, k * 128:(k + 1) * 128], rhs=u[:, k, :], start=(k == 0), stop=(k == NCH - 1)), r=[w.k, uk], w=[pk.k])
                self.headnorm_rope(pk, 128, 512, self.vec(("k_norm", 1 + which)), self.cos_b[:, tsl], self.sin_b[:, tsl], T)
                ko, kok = kout[nko % 2]
                nko += 1
                P.op("dve", lambda e: e.tensor_tensor(out=ko, in0=T["t1"][0], in1=T["t2"][0], op=ALU.add), r=[T["t1"][1], T["t2"][1]], w=[kok])
                for hh in range(2):
                    P.dma("sp", self.KSd[which, :, 2 * cc + hh, tsl], ko[hh * 64:(hh + 1) * 64, :], r=[kok], w=[self.ksd_k[which][2 * cc + hh][n]])
        for i in range(4):
            sel, cc = i // 2, i % 2
            w = self.next_w(("kvc", i))
            for n in range(NT):
                tsl = slice(n * TT, (n + 1) * TT)
                u, uk = U[n]
                for gg in range(2):
                    pc = self.bank()
                    for k in range(NCH):
                        P.op("pe", lambda e: e.matmul(pc[0:64, :], lhsT=w[:, k * 128 + gg * 64:k * 128 + gg * 64 + 64], rhs=u[:, k, :], start=(k == 0), stop=(k == NCH - 1)),
                             r=[w.k, uk], w=[pc.k])
                    idx = sel * 4 + 2 * cc + gg
                    P.op("act", lambda e: e.activation(out=kcT[0:64, idx, tsl], in_=pc[0:64, :], func=AF.Copy), r=[pc.k], w=[kcTk[idx]])
        for i in range(2):
            w = self.next_w(("kvv", i))
            for n in range(NT):
                u, uk = U[n]
                for jb in range(4):
                    pv = self.bank()
                    for k in range(NCH):
                        P.op("pe", lambda e: e.matmul(pv[:, 0:256], lhsT=u[:, k, jb * 128:(jb + 1) * 128], rhs=w[:, k * 256:(k + 1) * 256], start=(k == 0), stop=(k == NCH - 1)),
                             r=[w.k, uk], w=[pv.k])
                    vs, vsk = vst[nvs % 2]
                    nvs += 1
                    P.op("act", lambda e: e.activation(out=vs[:, :, 0:64], in_=pv[:, 0:256].rearrange("p (g d) -> p g d", g=4), func=AF.Copy), r=[pv.k], w=[vsk])
                    P.dma("sp", self.Vd[i, :, 4 * n + jb, :, :], vs, r=[vsk], w=[self.vd_k[i][n]])
        for kv in range(2):
            off = self.L.off[("cw1", kv)]
            src = self.wbf[off:off + 128 * 8192].rearrange("(p f) -> p f", p=128)
            c0, c1 = off // CONV_CH, (off + 128 * 8192 - 1) // CONV_CH
            P.dma("sp", cw1[0:64, kv].rearrange("p a b -> p (a b)"), src[0:64, :], r=[self.wchunk[c] for c in range(c0, c1 + 1)], w=[cw1k] + [k for _, k in U])
        w2 = self.next_w(("cw2",))
        for sel in range(2):
            pcv = self.bank()
            for cc in range(2):
                for l in range(32):
                    P.op("pe", lambda e: e.matmul(pcv[:, cc:cc + 1], lhsT=cw1[0:64, sel, l, cc * 128:(cc + 1) * 128], rhs=self.posb[0:64, sel, l:l + 1],
                                                  start=(cc == 0 and l == 0), stop=(cc == 1 and l == 31)), r=[cw1k, self.posb.k], w=[pcv.k])
            b1o = VEC[("c_b1", sel)]
            P.op("dve", lambda e: e.tensor_tensor(out=self.cbias[:, sel, :], in0=pcv[:, 0:2], in1=self.vecs[:, b1o:b1o + 2], op=ALU.add), r=[pcv.k, self.vecs.k], w=[self.cbias.k])
        nh = 0
        for sel in range(2):
            for g in range(4):
                hd, hdk = hid[nh % 2]
                nh += 1
                for cc in range(2):
                    ph = self.bank()
                    for l in range(32):
                        P.op("pe", lambda e: e.matmul(ph[:, 0:127], lhsT=cw1[0:64, sel, l, cc * 128:(cc + 1) * 128], rhs=kcT[0:64, sel * 4 + g, l:l + 16 * 126 + 1:16],
                                                      start=(l == 0), stop=(l == 31)), r=[cw1k, kcTk[sel * 4 + g]], w=[ph.k])
                    P.op("act", lambda e: e.activation(out=hd[:, cc, 0:127], in_=ph[:, 0:127], func=AF.Gelu_apprx_tanh, bias=self.cbias[:, sel, cc:cc + 1]),
                         r=[ph.k, self.cbias.k], w=[hdk])
                if sel == 0:
                    pk = self.bank()
                    for cc in range(2):
                        P.op("pe", lambda e: e.matmul(pk[0:64, 0:127], lhsT=w2[:, (0 * 2 + cc) * 64:(0 * 2 + cc) * 64 + 64], rhs=hd[:, cc, 0:127], start=(cc == 0), stop=(cc == 1)),
                             r=[w2.k, hdk], w=[pk.k])
                    self.headnorm_rope(pk, 64, 127, self.vecs[0:64, VEC[("k_norm", 0)]:VEC[("k_norm", 0)] + 1],
                                       self.cos_b[0:64, 31:31 + 16 * 126 + 1:16], self.sin_b[0:64, 31:31 + 16 * 126 + 1:16], T,
                                       bias=self.vecs[0:64, VEC[("c_b2", 0)]:VEC[("c_b2", 0)] + 1])
                    P.op("dve", lambda e: e.tensor_tensor(out=self.KC[0:64, g, 0:127], in0=T["t1"][0][0:64, 0:127], in1=T["t2"][0][0:64, 0:127], op=ALU.add),
                         r=[T["t1"][1], T["t2"][1]], w=[self.KC.k])
                else:
                    pv = self.bank()
                    for cc in range(2):
                        P.op("pe", lambda e: e.matmul(pv[0:127, 0:64], lhsT=hd[:, cc, 0:127], rhs=w2[:, (1 * 2 + cc) * 64:(1 * 2 + cc) * 64 + 64], start=(cc == 0), stop=(cc == 1)),
                             r=[w2.k, hdk], w=[pv.k])
                    P.op("dve", lambda e: e.tensor_tensor(out=self.VC[0:127, g, 0:64], in0=pv[0:127, 0:64], in1=self.bvecs[0:127, BV_CB2V:BV_CB2V + 64], op=ALU.add),
                         r=[pv.k, self.bvecs.k], w=[self.VC.k])

    def b_layer(self, j):
        P = self.P
        v = self.phase_view
        KS, KSk = v(0, [128, 4, S], BF16), [Trk() for _ in range(4)]
        KW, KWk = v(16384, [128, 4, S], BF16), [Trk() for _ in range(4)]
        VS, VSk = v(32768, [128, 16, 4, 66], BF16), Trk()
        VW, VWk = v(41216, [128, 16, 4, 66], BF16), Trk()
        u, uk = v(49664, [128, 8, 512], BF16), Trk()
        Q, Qk = v(57856, [128, 4, 16, 128], BF16), [Trk() for _ in range(4)]
        Nk = [[Trk() for _ in range(4)] for _ in range(4)]
        oT, oTk = v(74240, [128, 8, 512], BF16), Trk()
        ob = [(v(82432 + i * 2048, [128, 1024], BF16), Trk()) for i in range(2)]
        PT = [(v(86528 + i * 1024, [128, 512], BF16), Trk()) for i in range(4)]
        sqn = [(v(90624 + i * 1024, [128, 512], BF16), Trk()) for i in range(2)]
        rstd = (v(92672, [128, 512], F32), Trk())
        gat, gatk = v(94720, [128, 4, 48], F32), Trk()
        T = {"sq": sqn[0], "rs": rstd, "qn": (v(95488, [128, 512], BF16), Trk()),
             "t1": (v(96512, [128, 512], BF16), Trk()), "t2": (v(97536, [128, 512], BF16), Trk())}
        rd, rdk = v(98560, [128, 16], F32), Trk()
        s3, s3k = v(98624, [128, 12], F32), Trk()
        sc, sck = v(98688, [128, 32], F32), Trk()
        sc2, sc2k = v(98816, [128, 32], F32), Trk()
        nsl, nslk = v(98944, [128, 32], F32), Trk()
        m8, m8k = v(99072, [128, 16], F32), Trk()
        acc, acck = v(99136, [128, 256], F32), Trk()
        trks = KSk + KWk + [VSk, VWk, uk, oTk, gatk, rdk, s3k, sck, sc2k, nslk, m8k, acck] + Qk + [k for _, k in ob] + [k for _, k in PT] \
            + [k for _, k in sqn] + [rstd[1]] + [T["qn"][1], T["t1"][1], T["t2"][1]] + [k for row in Nk for k in row]
        self.new_phase(trks)
        self.begin_phase([])
        self.pool_misc = [6]
        pool_st = [0, 1, 2]
        for g in range(4):
            P.dma("sp", KS[0:64, g, :], self.KSd[0, :, g, :], r=self.ksd_k[0][g], w=[KSk[g]])
            P.dma("sp", KW[0:64, g, :], self.KSd[1, :, g, :], r=self.ksd_k[1][g], w=[KWk[g]])
        P.dma("sp", VS.rearrange("p a b c -> p (a b c)"), self.Vd[0].rearrange("p a b c -> p (a b c)"), r=self.vd_k[0], w=[VSk])
        P.dma("sp", VW.rearrange("p a b c -> p (a b c)"), self.Vd[1].rearrange("p a b c -> p (a b c)"), r=self.vd_k[1], w=[VWk])
        est = self.ph[64:96, 57856 // 4:57856 // 4 + 2048]
        allq = Qk + [k for row in Nk for k in row]
        P.dma("sp", est, self.cst_d[64:96, CST["eind"]:CST["eind"] + 2048], w=allq)
        for g in range(4):
            P.op("dve", lambda e: e.tensor_copy(out=KS[64:96, g, :], in_=est), r=allq, w=[KSk[g]])
            P.op("pool", lambda e: e.memset(KW[64:96, g, :], 0.0), w=[KWk[g]])
        P.op("pool", lambda e: e.memset(Q[64:96].rearrange("p a b c -> p (a b c)"), 0.0), w=allq)
        gbo = BV_GB + 48 * j
        gats = [(gat, gatk), (v(91648, [128, 4, 48], F32), Trk())]
        self.phase_cur.append(gats[1][1])
        Qn, Qnk = self.Qnext, self.Qnext.k
        sq1 = [sqn[0]]
        pqb = self.ps[7]

        def prep_pieces(n):
            tsl = slice(n * TT, (n + 1) * TT)
            gt, gtk = gats[n % 2]
            pieces = []

            def p_norm():
                self.norm_tile(n, ("b_norm", j), u, uk, sq1, rstd, pool=[6])

            def p_gates():
                wg = self.next_w(("bg", j))
                for qb in range(4):
                    pg = self.bank([6])
                    for k in range(NCH):
                        P.op("pe", lambda e: e.matmul(pg[:, 0:48], lhsT=u[:, k, qb * 128:(qb + 1) * 128], rhs=wg[:, k * 48:(k + 1) * 48], start=(k == 0), stop=(k == NCH - 1)),
                             r=[wg.k, uk], w=[pg.k])
                    P.op("dve", lambda e: e.tensor_tensor(out=gt[:, qb, :], in0=pg[:, 0:48], in1=self.bvecs[:, gbo:gbo + 48], op=ALU.add), r=[pg.k, self.bvecs.k], w=[gtk])
                P.op("act", lambda e: e.activation(out=gt, in_=gt, func=AF.Tanh, scale=0.5), r=[gtk], w=[gtk])
                P.op("dve", lambda e: e.tensor_scalar(out=gt, in0=gt, scalar1=0.5, scalar2=0.5, op0=ALU.mult, op1=ALU.add), r=[gtk], w=[gtk])

            pieces += [p_norm, p_gates]
            sq, sqk = T["sq"]
            rs, rsk = T["rs"]
            qn, qnk = T["qn"]
            t1, t1k = T["t1"]
            t2, t2k = T["t2"]
            gq = self.vec(("q_norm", j))
            for m in range(NCH):
                def p_d(m=m):
                    w = self.next_w(("bq", j, m))
                    for k in range(NCH):
                        P.op("pe", lambda e: e.matmul(pqb[:], lhsT=w[:, k * 128:(k + 1) * 128], rhs=u[:, k, :], start=(k == 0), stop=(k == NCH - 1)), r=[w.k, uk], w=[pqb.k])
                    P.op("act", lambda e: e.activation(out=sq, in_=pqb[:], func=AF.Square), r=[pqb.k], w=[sqk])

                def p_e(m=m):
                    pss = self.bank([6])
                    P.op("pe", lambda e: e.matmul(pss[:], lhsT=self.bones_b[:], rhs=sq, start=True, stop=True), r=[sqk, self.bones_b.k], w=[pss.k])
                    P.op("act", lambda e: e.activation(out=rs, in_=pss[:], func=AF.Ln, scale=1.0 / 64.0, bias=EPS), r=[pss.k], w=[rsk])
                    P.op("act", lambda e: e.activation(out=rs, in_=rs, func=AF.Exp, scale=-0.5), r=[rsk], w=[rsk])
                    P.op("dve", lambda e: e.scalar_tensor_tensor(out=qn, in0=pqb[:], scalar=gq, in1=rs, op0=ALU.mult, op1=ALU.mult), r=[pqb.k, rsk, self.vecs.k], w=[qnk])

                def p_f(m=m):
                    prt = self.bank([6])
                    P.op("pe", lambda e: e.matmul(prt[:], lhsT=self.prot_b[:], rhs=qn, start=True, stop=True), r=[qnk, self.prot_b.k], w=[prt.k])
                    P.op("pool", lambda e: e.tensor_tensor(out=t1, in0=qn, in1=self.cos_b[:, tsl], op=ALU.mult), r=[qnk, self.cos_b.k], w=[t1k])
                    P.op("dve", lambda e: e.tensor_tensor(out=t2, in0=prt[:], in1=self.sin_b[:, tsl], op=ALU.mult), r=[prt.k, self.sin_b.k], w=[t2k])
                    P.op("dve", lambda e: e.tensor_tensor(out=Qn[:, m, :], in0=t1, in1=t2, op=ALU.add), r=[t1k, t2k], w=[Qnk])

                pieces += [p_d, p_e, p_f]
            return pieces

        def install_q():
            for m in range(NCH):
                for hh in range(2):
                    h = 2 * m + hh
                    src = Qn[hh * 64:(hh + 1) * 64, m, :].rearrange("p (a b) -> p a b", a=4)
                    if (h % 2) == 0:
                        P.op("dve", lambda e: e.tensor_copy(out=Q[0:64, :, h, :], in_=src), r=[Qnk], w=Qk)
                    else:
                        P.op("act", lambda e: e.activation(out=Q[0:64, :, h, :], in_=src, func=AF.Copy), r=[Qnk], w=Qk)

        for f in prep_pieces(0):
            f()
        for n in range(NT):
            tsl = slice(n * TT, (n + 1) * TT)
            install_q()
            bg = prep_pieces(n + 1) if n + 1 < NT else []
            gt, gtk = gats[n % 2]
            self.attn_tile(n, dict(KS=KS, KSk=KSk, KW=KW, KWk=KWk, VS=VS, VSk=VSk, VW=VW, VWk=VWk, Q=Q, Qk=Qk, Nk=Nk, oT=oT, oTk=oTk, ob=ob, PT=PT,
                                   gat=gt, gatk=gtk, rd=rd, rdk=rdk, s3=s3, s3k=s3k, sc=sc, sck=sck, sc2=sc2, sc2k=sc2k, nsl=nsl, nslk=nslk,
                                   m8=m8, m8k=m8k, acc=acc, acck=acck, pool_st=pool_st), bg)
            for mm in range(NCH):
                w = self.next_w(("bo", j, mm))
                po = self.bank(pool_st)
                for k in range(NCH):
                    P.op("pe", lambda e: e.matmul(po[:], lhsT=w[:, k * 128:(k + 1) * 128], rhs=oT[:, k, :], start=(k == 0), stop=(k == NCH - 1)), r=[w.k, oTk], w=[po.k])
                hsl = self.hT[:, mm, tsl]
                P.op("dve", lambda e: e.tensor_tensor(out=hsl, in0=po[:], in1=hsl, op=ALU.add), r=[po.k, self.hk[mm][n]], w=[self.hk[mm][n]])
        self.pool_misc = [0, 1, 2, 3, 4, 5, 6, 7]

    def attn_tile(self, n, C, bg=()):
        P = self.P
        KS, KSk, KW, KWk, VS, VSk, VW, VWk = C["KS"], C["KSk"], C["KW"], C["KWk"], C["VS"], C["VSk"], C["VW"], C["VWk"]
        Q, Qk, Nk, oT, oTk, ob, PT = C["Q"], C["Qk"], C["Nk"], C["oT"], C["oTk"], C["ob"], C["PT"]
        gat, gatk, rd, rdk, s3, s3k = C["gat"], C["gatk"], C["rd"], C["rdk"], C["s3"], C["s3k"]
        sc, sck, sc2, sc2k, nsl, nslk, m8, m8k, acc, acck = C["sc"], C["sck"], C["sc2"], C["sc2k"], C["nsl"], C["nslk"], C["m8"], C["m8k"], C["acc"], C["acck"]
        pool_st = C["pool_st"]
        poc, pos, pow_ = self.ps[3], self.ps[4], self.ps[5]
        st = {"npt": 0}
        pairs = [(qb, g) for qb in range(4) for g in range(4)]
        jobs = []

        def r4(ap):
            return ap.rearrange("p (a b) -> p a b", a=4)

        def mk_job(kind, qb, g, kt, first, last, pidx):
            qbg = 4 * n + qb
            job = {"pre": [], "post": []}
            box = {}

            def st1():
                pst = self.bank(pool_st)
                pt, ptk = PT[st["npt"] % 4]
                st["npt"] += 1
                box["pt"], box["ptk"] = pt, ptk
                Q96 = Q[0:96, qb, 4 * g:4 * g + 4, :]
                if kind == "c":
                    P.op("pe", lambda e: e.matmul(pst[0:127, :], lhsT=self.KC[0:96, g, 0:127], rhs=Q96, start=True, stop=True), r=[self.KC.k, Qk[qb]], w=[pst.k])
                    P.op("act", lambda e: e.activation(out=pt[0:127, :], in_=pst[0:127, :], func=AF.Exp, scale=SCALE), r=[pst.k], w=[ptk])
                    cm = self.cmask_b[0:127, qbg * 128:(qbg + 1) * 128].unsqueeze(1).broadcast_to([127, 4, 128])
                    P.op("pool", lambda e: e.tensor_tensor(out=r4(pt[0:127, :]), in0=r4(pt[0:127, :]), in1=cm, op=ALU.mult), r=[ptk, self.cmask_b.k], w=[ptk])
                    return
                if kind == "s":
                    P.op("pe", lambda e: e.matmul(pst[:], lhsT=KS[0:96, g, kt * 128:(kt + 1) * 128], rhs=Q96, start=True, stop=True),
                         r=[KSk[g], Qk[qb], Nk[qb][g]], w=[pst.k])
                else:
                    P.op("pe", lambda e: e.matmul(pst[:], lhsT=KW[0:96, g, kt * 128:(kt + 1) * 128], rhs=Q96, start=True, stop=True), r=[KWk[g], Qk[qb]], w=[pst.k])
                P.op("act", lambda e: e.activation(out=pt, in_=pst[:], func=AF.Exp, scale=SCALE), r=[pst.k], w=[ptk])
                msk = None
                if kt == qbg:
                    msk = self.triL_b
                elif kind == "w" and kt == qbg - 4:
                    msk = self.triU_b
                if msk is not None:
                    tm = msk[:].unsqueeze(1).broadcast_to([128, 4, 128])
                    P.op("pool", lambda e: e.tensor_tensor(out=r4(pt), in0=r4(pt), in1=tm, op=ALU.mult), r=[ptk, msk.k], w=[ptk])

            def st2():
                pt, ptk = box["pt"], box["ptk"]
                for h in range(4):
                    if kind == "c":
                        P.op("pe", lambda e: e.matmul(poc[:, h * 128:h * 128 + 97], lhsT=pt[0:127, h * 128:(h + 1) * 128], rhs=self.VC[0:127, g, 0:97],
                                                      start=(h == 0), stop=(h == 3), skip_group_check=True), r=[ptk, self.VC.k], w=[poc.k])
                    elif kind == "s":
                        P.op("pe", lambda e: e.matmul(pos[:, h * 128:h * 128 + 65], lhsT=pt[:, h * 128:(h + 1) * 128], rhs=VS[:, kt, g, 0:65],
                                                      start=(first and h == 0), stop=(last and h == 3), skip_group_check=True), r=[ptk, VSk], w=[pos.k])
                    else:
                        P.op("pe", lambda e: e.matmul(pow_[:, h * 128:h * 128 + 65], lhsT=pt[:, h * 128:(h + 1) * 128], rhs=VW[:, kt, g, 0:65],
                                                      start=(first and h == 0), stop=(last and h == 3), skip_group_check=True), r=[ptk, VWk], w=[pow_.k])

            job["st1"], job["st2"] = st1, st2
            return job

        def select_chain(qb, g, pidx):
            qbg = 4 * n + qb
            pcs = self.pocs[pidx % 2]
            P.op("act", lambda e: e.activation(out=pcs[:], in_=r4(poc[:])[:, :, 0:97], func=AF.Copy), r=[poc.k], w=[pcs.k])
            P.op("dve", lambda e: e.tensor_scalar(out=rd[:, 0:4], in0=pcs[:, :, 64], scalar1=1e-30, scalar2=None, op0=ALU.max), r=[pcs.k], w=[rdk])
            P.op("dve", lambda e: e.reciprocal(out=rd[:, 0:4], in_=rd[:, 0:4]), r=[rdk], w=[rdk])
            for h in range(4):
                src1 = self.selb[:, qbg, :] if h == 0 else sc
                P.op("dve", lambda e: e.scalar_tensor_tensor(out=sc, in0=pcs[:, h, 65:97], scalar=rd[:, h:h + 1], in1=src1, op0=ALU.mult, op1=ALU.add),
                     r=[pcs.k, rdk, sck, self.selb.k], w=[sck])
            P.op("dve", lambda e: e.max(out=m8[:, 0:8], in_=sc), r=[sck], w=[m8k])
            P.op("dve", lambda e: e.match_replace(out=sc2, in_to_replace=m8[:, 0:8], in_values=sc, imm_value=-3.0e38), r=[sck, m8k], w=[sc2k])
            P.op("dve", lambda e: e.max(out=m8[:, 8:16], in_=sc2), r=[sc2k], w=[m8k])
            P.op("dve", lambda e: e.tensor_scalar(out=nsl, in0=sc, scalar1=m8[:, 15:16], scalar2=NEGM, op0=ALU.is_lt, op1=ALU.mult), r=[sck, m8k], w=[nslk])

        def nsel_install(qb, g):
            pm = self.bank(self.pool_misc)
            P.op("pe", lambda e: e.transpose(out=pm[0:32, 0:128], in_=nsl, identity=self.ident_f[:]), r=[nslk, self.ident_f.k], w=[pm.k])
            P.op("act", lambda e: e.activation(out=Q[64:96, qb, 4 * g:4 * g + 4, :], in_=pm[0:32, 0:128].unsqueeze(1).broadcast_to([32, 4, 128]), func=AF.Copy),
                 r=[pm.k], w=[Nk[qb][g]])

        def evac_win():
            P.op("act", lambda e: e.activation(out=self.pows[:], in_=r4(pow_[:])[:, :, 0:65], func=AF.Copy), r=[pow_.k], w=[self.pows.k])

        def combine(qb, g, pidx):
            pcs = self.pocs[pidx % 2]
            o, ok_ = ob[qb % 2]
            P.op("dve", lambda e: e.tensor_scalar(out=rd[:, 4:8], in0=pcs[:, :, 64], scalar1=1e-30, scalar2=None, op0=ALU.max), r=[pcs.k], w=[rdk])
            P.op("dve", lambda e: e.tensor_scalar(out=rd[:, 8:12], in0=r4(pos[:])[:, :, 64], scalar1=1e-30, scalar2=None, op0=ALU.max), r=[pos.k], w=[rdk])
            P.op("dve", lambda e: e.tensor_scalar(out=rd[:, 12:16], in0=self.pows[:, :, 64], scalar1=1e-30, scalar2=None, op0=ALU.max), r=[self.pows.k], w=[rdk])
            P.op("dve", lambda e: e.reciprocal(out=rd[:, 4:16], in_=rd[:, 4:16]), r=[rdk], w=[rdk])
            P.op("dve", lambda e: e.tensor_tensor(out=s3.rearrange("p (a b) -> p a b", a=3), in0=rd[:, 4:16].rearrange("p (a b) -> p a b", a=3),
                                                  in1=gat[:, qb, :].rearrange("p (a b) -> p a b", a=3)[:, :, 4 * g:4 * g + 4], op=ALU.mult), r=[rdk, gatk], w=[s3k])
            for h in range(4):
                ah = acc[:, h * 64:(h + 1) * 64]
                P.op("dve", lambda e: e.tensor_scalar(out=ah, in0=pcs[:, h, 0:64], scalar1=s3[:, h:h + 1], scalar2=None, op0=ALU.mult), r=[pcs.k, s3k], w=[acck])
                P.op("dve", lambda e: e.scalar_tensor_tensor(out=ah, in0=self.pows[:, h, 0:64], scalar=s3[:, 8 + h:9 + h], in1=ah, op0=ALU.mult, op1=ALU.add),
                     r=[self.pows.k, s3k, acck], w=[acck])
                P.op("dve", lambda e: e.scalar_tensor_tensor(out=o[:, g * 256 + h * 64:g * 256 + (h + 1) * 64], in0=pos[:, h * 128:h * 128 + 64], scalar=s3[:, 4 + h:5 + h], in1=ah,
                                                             op0=ALU.mult, op1=ALU.add), r=[pos.k, s3k, acck], w=[ok_])

        def o_transpose(qb):
            o, ok_ = ob[qb % 2]
            pT = self.bank(self.pool_misc)
            pTb = pT[:].bitcast(BF16)
            for c in range(NCH):
                P.op("pe", lambda e: e.transpose(out=pTb[:, c * 128:(c + 1) * 128], in_=o[:, c * 128:(c + 1) * 128], identity=self.ident_b[:]), r=[ok_, self.ident_b.k], w=[pT.k])
            P.op("act", lambda e: e.activation(out=oT[:, :, qb * 128:(qb + 1) * 128], in_=pTb.rearrange("p (a b) -> p a b", a=8), func=AF.Copy), r=[pT.k], w=[oTk])

        def cjob(pidx):
            qb, g = pairs[pidx]
            j = mk_job("c", qb, g, 0, True, True, pidx)
            j["post"].append(lambda: select_chain(qb, g, pidx))
            return j

        jobs.append(cjob(0))
        pending_T = []
        for pidx, (qb, g) in enumerate(pairs):
            qbg = 4 * n + qb
            k0 = max(0, qbg - 4)
            wj = [mk_job("w", qb, g, kt, kt == k0, kt == qbg, pidx) for kt in range(k0, qbg + 1)]
            for f in pending_T:
                wj[min(2, len(wj) - 1)]["post"].append(f)
            pending_T = []
            wj[-1]["post"].append(evac_win)
            jobs += wj
            if pidx + 1 < len(pairs):
                jobs.append(cjob(pidx + 1))
            sj = [mk_job("s", qb, g, kt, kt == 0, kt == qbg, pidx) for kt in range(qbg + 1)]
            sj[0]["pre"].append(lambda qb=qb, g=g: nsel_install(qb, g))
            sj[-1]["post"].append(lambda qb=qb, g=g, pidx=pidx: combine(qb, g, pidx))
            jobs += sj
            if g == 3:
                pending_T.append(lambda qb=qb: o_transpose(qb))
        LA = 2
        for j in range(min(LA, len(jobs))):
            for f in jobs[j]["pre"]:
                f()
            jobs[j]["st1"]()
        bg = list(bg)
        every = max(2, (len(jobs) - 8) // (len(bg) + 1)) if bg else 0
        for i in range(len(jobs)):
            if i + LA < len(jobs):
                for f in jobs[i + LA]["pre"]:
                    f()
                jobs[i + LA]["st1"]()
            jobs[i]["st2"]()
            for f in jobs[i]["post"]:
                f()
            if bg and i >= 4 and (i - 4) % every == 0:
                bg.pop(0)()
        for f in pending_T:
            f()
        for f in bg:
            f()

    def make_plan(self):
        plan = []
        ser = -1
        for b in range(self.nb):
            for layer in range(self.n_layers):
                if layer < 2:
                    ser += 1
                    plan += [(("a_in", layer, c), ser) for c in range(8)]
                    plan += [(("a_out", layer, m), ser) for m in range(8)]
                else:
                    if layer == 2:
                        ser += 1
                        plan += [(("kvk", i), ser) for i in range(4)] + [(("kvc", i), ser) for i in range(4)] + [(("kvv", i), ser) for i in range(2)]
                        plan += [(("cw2",), ser)]
                    ser += 1
                    jj = layer - 2
                    plan += [(("bg", jj), ser)] + [(("bq", jj, m), ser) for m in range(8)]
                    for n in range(NT):
                        if n + 1 < NT:
                            plan += [(("bg", jj), ser)] + [(("bq", jj, m), ser) for m in range(8)]
                        plan += [(("bo", jj, m), ser) for m in range(8)]
                ser += 1
                for t2 in range(2):
                    plan += [(("f_in", layer, c), ser) for c in range(FC)]
                    plan += [(("f_out", layer, m, hf), ser) for m in range(8) for hf in range(2)]
        return plan

    def build(self):
        P = self.P
        self.convc = self.sb("convc", [128, 8, 3], F32)
        self.hst = self.sb("hst", [128, 8], F32)
        self.prologue()
        self.plan_weights(self.make_plan())
        phase_no = {"a0": 0, "f0": 1, "a1": 2, "f1": 3, "kv": 4, "b0": 5, "f2": 6, "b1": 7, "f3": 8}
        for b in range(self.nb):
            def pre(tag):
                if b == 0:
                    self.convert_phase(phase_no[tag] + 1)
            if b == 0:
                self.convert_phase(0)
            self.mark("load%d" % b)
            self.load_x(b)
            for layer in range(self.n_layers):
                if layer < 2:
                    pre("a%d" % layer)
                    self.mark("a%d.%d" % (b, layer))
                    self.a_layer(layer)
                else:
                    if layer == 2:
                        pre("kv")
                        self.mark("kv%d" % b)
                        self.kv_phase()
                    pre("b%d" % (layer - 2))
                    self.mark("b%d.%d" % (b, layer))
                    self.b_layer(layer - 2)
                pre("f%d" % layer)
                self.mark("f%d.%d" % (b, layer))
                self.ffn_layer(layer)
            self.mark("store%d" % b)
            self.store_y(b)
        self.mark("end")
        P.wait_all("sp", self.out_trks)
        return self.nc


_CACHE = {}


def _prep_inputs(inputs):
    L = weight_layout()
    wall = pack_weights(inputs, L)
    vecs, bvecs = pack_vecs(inputs)
    cst = make_consts()
    return wall, vecs, bvecs, cst


def kernel(**inputs):
    inputs = {k: np.asarray(v) for k, v in inputs.items()}
    x = np.ascontiguousarray(inputs["x"], dtype=np.float32)
    B = x.shape[0]
    nb = B // NCORES
    wall, vecs, bvecs, cst = _prep_inputs(inputs)
    nc = Builder(nb).build()
    in_maps = []
    for c in range(NCORES):
        in_maps.append({"x": np.ascontiguousarray(x[c * nb:(c + 1) * nb]), "wall": wall, "vecs": vecs, "bvecs": bvecs, "cst": cst})
    res = run_bass_kernel_spmd(nc, in_maps, core_ids=list(range(NCORES)))
    out = np.concatenate([np.asarray(r["y"]).reshape(nb, S, D) for r in res.results], axis=0)
    return out.astype(np.float32)
```

```python
import numpy as np
import concourse.bass as bass
import concourse.mybir as mybir
from concourse.bass_utils import run_bass_kernel_spmd
from concourse.alu_op_type import AluOpType as ALU

F32 = mybir.dt.float32
BF16 = mybir.dt.bfloat16
AF = mybir.ActivationFunctionType

S = 2048
D = 1024
NCH = 8
TT = 512
NT = S // TT
FH = 2816
FC = 22
NCORES = 8
EPS = 1e-6
CONV_CH = 128 * 2048
SLOT = 2304
SCALE = 0.125
NEGM = -30000.0


def _kt(W, c0, width=128):
    K = W.shape[0]
    return np.ascontiguousarray(W[:, c0:c0 + width].reshape(K // 128, 128, width).transpose(1, 0, 2)).reshape(128, -1)


class WLayout:
    def __init__(self):
        self.off = {}
        self.free = {}
        self.n = 0

    def add(self, name, free):
        self.off[name] = self.n
        self.free[name] = free
        self.n += 128 * free

    def total(self):
        return ((self.n + CONV_CH - 1) // CONV_CH) * CONV_CH


def weight_layout():
    L = WLayout()
    L.phase_end = []

    def a(l):
        for c in range(8):
            L.add(("a_in", l, c), 2304)
        for m in range(8):
            L.add(("a_out", l, m), 1024)
        L.phase_end.append(L.n)

    def f(l):
        for c in range(FC):
            L.add(("f_in", l, c), 2048)
        for m in range(8):
            for hf in range(2):
                L.add(("f_out", l, m, hf), FH // 2)
        L.phase_end.append(L.n)

    def b(j):
        L.add(("bg", j), 384)
        for m in range(8):
            L.add(("bq", j, m), 1024)
        for m in range(8):
            L.add(("bo", j, m), 1024)
        L.phase_end.append(L.n)

    a(0)
    f(0)
    a(1)
    f(1)
    for i in range(4):
        L.add(("kvk", i), 1024)
    for i in range(4):
        L.add(("kvc", i), 1024)
    for i in range(2):
        L.add(("kvv", i), 2048)
    for i in range(2):
        L.add(("cw1", i), 8192)
    L.add(("cw2",), 256)
    L.phase_end.append(L.n)
    b(0)
    f(2)
    b(1)
    f(3)
    return L


def pack_weights(inp, L):
    out = np.zeros(L.total(), np.float32)

    def put(name, arr):
        arr = np.asarray(arr, np.float32).reshape(128, -1)
        assert arr.shape[1] == L.free[name], (name, arr.shape)
        out[L.off[name]:L.off[name] + arr.size] = arr.reshape(-1)

    for l in range(2):
        Win = inp["a_w_in"][l]
        for c in range(8):
            put(("a_in", l, c), np.concatenate(
                [_kt(Win, c * 128), _kt(Win, 1024 + c * 128), inp["a_gate_w"][l][0, c], inp["a_gate_w"][l][1, c]], axis=1))
        for m in range(8):
            put(("a_out", l, m), _kt(inp["a_w_out"][l], m * 128))
    for l in range(4):
        W = inp["f_w_in"][l]
        for c in range(FC):
            put(("f_in", l, c), np.concatenate([_kt(W, c * 128), _kt(W, FH + c * 128)], axis=1))
        for m in range(8):
            t = _kt(inp["f_w_out"][l], m * 128)
            for hf in range(2):
                put(("f_out", l, m, hf), t[:, hf * (FH // 2):(hf + 1) * (FH // 2)])
    kvw = inp["kv_w"]
    i = 0
    for jj in (2, 4):
        for cc in range(2):
            put(("kvk", i), _kt(kvw, jj * 256 + cc * 128))
            i += 1
    i = 0
    for jj in (0, 1):
        for cc in range(2):
            put(("kvc", i), _kt(kvw, jj * 256 + cc * 128))
            i += 1
    for i, jj in enumerate((3, 5)):
        put(("kvv", i), _kt(kvw, jj * 256, 256))
    for kv in range(2):
        t = np.zeros((128, 32, 256), np.float32)
        t[:64] = inp["cmp_w1"][kv].reshape(32, 64, 256).transpose(1, 0, 2)
        put(("cw1", kv), t)
    t = np.zeros((128, 2, 2, 64), np.float32)
    for kv in range(2):
        t[:, kv] = inp["cmp_w2"][kv].reshape(2, 128, 64).transpose(1, 0, 2)
    put(("cw2",), t)
    for j in range(2):
        put(("bg", j), _kt(inp["b_w_in"][j], 1024, 48))
        for m in range(8):
            put(("bq", j, m), _kt(inp["b_w_in"][j], m * 128))
        for m in range(8):
            put(("bo", j, m), _kt(inp["b_w_out"][j], m * 128))
    return out


VEC = {}
_nv = 0


def _vadd(name, n):
    global _nv
    VEC[name] = _nv
    _nv += n


for _l in range(2):
    _vadd(("a_norm", _l), 8)
    for _k in range(4):
        _vadd(("a_cw", _l, _k), 8)
    _vadd(("a_cb", _l), 8)
    _vadd(("a_gb", _l, 0), 8)
    _vadd(("a_gb", _l, 1), 8)
    _vadd(("a_lam", _l), 8)
for _l in range(4):
    _vadd(("f_norm", _l), 8)
_vadd(("kv_norm",), 8)
for _j in range(2):
    _vadd(("b_norm", _j), 8)
    _vadd(("q_norm", _j), 1)
for _i in range(3):
    _vadd(("k_norm", _i), 1)
for _i in range(2):
    _vadd(("c_b1", _i), 2)
    _vadd(("c_b2", _i), 1)
    _vadd(("c_pos", _i), 32)
NV = _nv
BV_GB = 0
BV_CB2V = 96
NBV = 160


def pack_vecs(inp):
    v = np.zeros((128, NV), np.float32)

    def fm(x):
        return np.asarray(x, np.float32).reshape(8, 128).T

    for l in range(2):
        v[:, VEC[("a_norm", l)]:][:, :8] = fm(inp["a_norm"][l])
        for k in range(4):
            v[:, VEC[("a_cw", l, k)]:][:, :8] = fm(inp["a_conv_w"][l][k])
        v[:, VEC[("a_cb", l)]:][:, :8] = fm(inp["a_conv_b"][l])
        v[:, VEC[("a_gb", l, 0)]:][:, :8] = fm(inp["a_gate_b"][l][0])
        v[:, VEC[("a_gb", l, 1)]:][:, :8] = fm(inp["a_gate_b"][l][1])
        v[:, VEC[("a_lam", l)]:][:, :8] = fm(inp["a_lambda"][l])
    for l in range(4):
        v[:, VEC[("f_norm", l)]:][:, :8] = fm(inp["f_norm"][l])
    v[:, VEC[("kv_norm",)]:][:, :8] = fm(inp["kv_norm"])
    for j in range(2):
        v[:, VEC[("b_norm", j)]:][:, :8] = fm(inp["b_norm"][j])
        v[:, VEC[("q_norm", j)]] = np.tile(np.asarray(inp["q_norm"][j], np.float32), 2)
    for i in range(3):
        v[:, VEC[("k_norm", i)]] = np.tile(np.asarray(inp["k_norm"][i], np.float32), 2)
    for i in range(2):
        v[:, VEC[("c_b1", i)]:][:, :2] = np.asarray(inp["cmp_b1"][i], np.float32).reshape(2, 128).T
        v[:, VEC[("c_b2", i)]] = np.tile(np.asarray(inp["cmp_b2"][i], np.float32), 2)
        v[:64, VEC[("c_pos", i)]:][:, :32] = np.asarray(inp["cmp_pos"][i], np.float32).T
    bv = np.zeros((128, NBV), np.float32)
    for j in range(2):
        bv[:, BV_GB + 48 * j: BV_GB + 48 * j + 48] = np.asarray(inp["b_gate_b"][j], np.float32)[None, :]
    bv[:, BV_CB2V:BV_CB2V + 64] = np.asarray(inp["cmp_b2"][1], np.float32)[None, :]
    return v, bv


CST = {}
_nc_ = 0


def _cadd(name, n):
    global _nc_
    CST[name] = _nc_
    _nc_ += n


_cadd("ident", 128)
_cadd("prot", 128)
_cadd("bones", 128)
_cadd("triL", 128)
_cadd("triU", 128)
_cadd("ovl", 32)
_cadd("cos", 2048)
_cadd("sin", 2048)
_cadd("cmask", 2048)
_cadd("eind", 2048)
_cadd("selb", 512)
NCST = _nc_


def make_consts():
    c = np.zeros((128, NCST), np.float32)
    p = np.arange(128)
    c[:, CST["ident"]:][:, :128] = np.eye(128)
    perm = (p // 64) * 64 + ((p % 64) + 32) % 64
    pr = np.zeros((128, 128), np.float32)
    pr[perm, p] = 1.0
    c[:, CST["prot"]:][:, :128] = pr
    c[:, CST["bones"]:][:, :128] = (p[:, None] // 64 == p[None, :] // 64)
    c[:, CST["triL"]:][:, :128] = (p[:, None] <= p[None, :])
    c[:, CST["triU"]:][:, :128] = (p[:, None] > p[None, :])
    cs = np.arange(127) * 16
    sl = np.arange(32) * 64
    ov = np.clip(np.minimum(cs[:, None] + 32, sl[None, :] + 64) - np.maximum(cs[:, None], sl[None, :]), 0, None) / 32.0
    c[:127, CST["ovl"]:][:, :32] = ov
    t = np.arange(2048, dtype=np.float64)
    fi = (p % 64) % 32
    freqs = (10000.0 ** (-(np.arange(32, dtype=np.float32) / np.float32(32)))).astype(np.float32)
    ang = (t[None, :].astype(np.float32) * freqs[fi][:, None]).astype(np.float32)
    c[:, CST["cos"]:][:, :2048] = np.cos(ang)
    sg = np.where((p % 64) < 32, -1.0, 1.0)[:, None]
    c[:, CST["sin"]:][:, :2048] = np.sin(ang) * sg
    cl = np.arange(127) * 16 + 31
    c[:127, CST["cmask"]:][:, :2048] = (cl[:, None] <= t[None, :])
    b = np.arange(32)
    c[64:96, CST["eind"]:][:, :2048] = (t[None, :].astype(np.int64) // 64 == b[:, None])
    sb = np.zeros((128, 16, 32), np.float32)
    for qb in range(16):
        tq = qb * 128 + p
        cur = (tq // 64)[:, None]
        causal = b[None, :] <= cur
        forced = (b[None, :] == 0) | (causal & (cur - b[None, :] < 2))
        sb[:, qb, :] = np.where(forced, 1e30, np.where(causal, 0.0, -1e30))
    c[:, CST["selb"]:][:, :512] = sb.reshape(128, 512)
    return c


class Trk:
    __slots__ = ("w", "r")

    def __init__(self):
        self.w = None
        self.r = {}


NDS = 24


class Prog:
    def __init__(self, nc):
        self.nc = nc
        self.eng = {"pe": nc.tensor, "act": nc.scalar, "dve": nc.vector, "pool": nc.gpsimd, "sp": nc.sync}
        self.sem = {e: nc.alloc_semaphore(name="s_" + e) for e in ("pe", "act", "dve", "pool")}
        self.cnt = {e: 0 for e in ("pe", "act", "dve", "pool")}
        self.seen = {e: {} for e in self.eng}
        self.dsem = [nc.alloc_semaphore(name="s_dma%d" % i) for i in range(NDS)]
        self.dn = 0
        self.ninst = 0

    def _semof(self, key):
        return self.sem[key] if isinstance(key, str) else self.dsem[key[1]]

    def _wait(self, e, key, val):
        if key == "pe" and e == "pe":
            return
        if self.seen[e].get(key, 0) >= val:
            return
        self.eng[e].wait_ge(self._semof(key), val)
        self.seen[e][key] = val

    def _deps(self, e, r, w):
        for t in r:
            if t.w is not None:
                self._wait(e, t.w[0], t.w[1])
        for t in w:
            if t.w is not None:
                self._wait(e, t.w[0], t.w[1])
            for k, v in t.r.items():
                self._wait(e, k, v)

    def op(self, e, fn, r=(), w=()):
        self._deps(e, r, w)
        inst = fn(self.eng[e])
        self.cnt[e] += 1
        self.ninst += 1
        inst.then_inc(self.sem[e], 1)
        v = self.cnt[e]
        for t in r:
            t.r[e] = v
        for t in w:
            t.w = (e, v)
            t.r = {}
        return inst

    def dma(self, q, out, in_, r=(), w=()):
        self._deps(q, r, w)
        i = self.dn % NDS
        gen = self.dn // NDS
        key = ("d", i)
        if gen > 0:
            self._wait(q, key, 16 * gen)
        inst = self.eng[q].dma_start(out=out, in_=in_)
        inst.then_inc(self.dsem[i], 16)
        self.dn += 1
        self.ninst += 1
        v = 16 * (gen + 1)
        for t in r:
            t.r[key] = v
        for t in w:
            t.w = (key, v)
            t.r = {}
        return (key, v)

    def wait_all(self, e, trks):
        self._deps(e, trks, trks)


class Buf:
    def __init__(self, t):
        self.t = t
        self.k = Trk()

    def __getitem__(self, idx):
        return self.t[idx]


class Builder:
    def __init__(self, nb, n_layers=4, debug_out=None):
        self.nb = nb
        self.n_layers = n_layers
        self.L = weight_layout()
        nc = bass.Bass("TRN2", target_bir_lowering=False)
        self.nc = nc
        self.P = Prog(nc)
        NW = self.L.total()
        self.x = nc.dram_tensor("x", [nb, S, D], F32, kind="ExternalInput").ap()
        self.wall = nc.dram_tensor("wall", [NW], F32, kind="ExternalInput").ap()
        self.vecs_d = nc.dram_tensor("vecs", [128, NV], F32, kind="ExternalInput").ap()
        self.bvecs_d = nc.dram_tensor("bvecs", [128, NBV], F32, kind="ExternalInput").ap()
        self.cst_d = nc.dram_tensor("cst", [128, NCST], F32, kind="ExternalInput").ap()
        self.y = nc.dram_tensor("y", [nb, S, D], F32, kind="ExternalOutput").ap()
        self.wbf = nc.dram_tensor("wbf", [NW], BF16, kind="Internal").ap()
        self.wchunk = [Trk() for _ in range(NW // CONV_CH)]
        self.out_trks = []
        self.marks = []

        def sb(name, shape, dt):
            return Buf(nc.alloc_sbuf_tensor(name, shape, dt))

        self.sb = sb
        self.hT = nc.alloc_sbuf_tensor("hT", [128, NCH, S], F32)
        self.hk = [[Trk() for _ in range(NT)] for _ in range(NCH)]
        self.vecs = sb("vecs_sb", [128, NV], F32)
        self.bvecs = sb("bvecs_sb", [128, NBV], F32)
        self.ident_f = sb("ident_f", [128, 128], F32)
        self.ident_b = sb("ident_b", [128, 128], BF16)
        self.ones_b = sb("ones_b", [128, 128], BF16)
        self.coef = sb("coef", [128, 2, 2, 8], F32)
        self.hgb = sb("hgb", [128, 2, 2, 8], F32)
        self.lnhalf = sb("lnhalf", [128, 2], F32)
        self.cos_b = sb("cos_b", [128, S], BF16)
        self.sin_b = sb("sin_b", [128, S], BF16)
        self.cmask_b = sb("cmask_b", [128, S], BF16)
        self.triL_b = sb("triL_b", [128, 128], BF16)
        self.triU_b = sb("triU_b", [128, 128], BF16)
        self.prot_b = sb("prot_b", [128, 128], BF16)
        self.bones_b = sb("bones_b", [128, 128], BF16)
        self.selb = sb("selb", [128, 16, 32], BF16)
        self.Qnext = sb("Qnext", [128, 8, 512], BF16)
        self.KC = sb("KC", [128, 4, 128], BF16)
        self.VC = sb("VC", [128, 4, 98], BF16)
        self.posb = sb("posb", [128, 2, 32], BF16)
        self.cbias = sb("cbias", [128, 2, 2], F32)
        self.pocs = [sb("pocs%d" % i, [128, 4, 97], F32) for i in range(2)]
        self.pows = sb("pows", [128, 4, 65], F32)
        self.KSd = nc.dram_tensor("KSd", [2, 64, 4, S], BF16, kind="Internal").ap()
        self.Vd = nc.dram_tensor("Vd", [2, 128, 16, 4, 66], BF16, kind="Internal").ap()
        self.ksd_k = [[[Trk() for _ in range(NT)] for _ in range(4)] for _ in range(2)]
        self.vd_k = [[Trk() for _ in range(NT)] for _ in range(2)]
        self.ps = [Buf(nc.alloc_psum_tensor("ps%d" % i, [128, 512], F32)) for i in range(8)]
        self.ps_rr = 0
        self.pool_misc = [0, 1, 2, 3, 4, 5, 6, 7]
        self.NSLOT = 3
        self.ring = [sb("wring%d" % i, [128, SLOT], BF16) for i in range(self.NSLOT)]
        self.perm_ids = set(id(b) for b in self.ring)
        self.PHB = 100352
        self.ph = nc.alloc_sbuf_tensor("phase", [128, self.PHB // 4], F32)
        self.phk = {}

    def bank(self, pool=None):
        if pool is None:
            b = self.ps[self.ps_rr % 8]
        else:
            b = self.ps[pool[self.ps_rr % len(pool)]]
        self.ps_rr += 1
        return b

    def vec(self, name, c=0):
        i = VEC[name] + c
        return self.vecs[:, i:i + 1]

    def phase_view(self, byte_off, shape, dt):
        esz = 4 if dt == F32 else 2
        n = int(np.prod(shape[1:]))
        assert byte_off % 4 == 0 and byte_off + n * esz <= self.PHB, (byte_off, shape)
        if dt == F32:
            ap = self.ph[:, byte_off // 4: byte_off // 4 + n]
        else:
            ap = self.ph[:].bitcast(BF16)[:, byte_off // 2: byte_off // 2 + n]
        if len(shape) == 2:
            return ap
        names = " ".join("a%d" % i for i in range(len(shape) - 1))
        kw = {"a%d" % i: shape[i + 1] for i in range(len(shape) - 1)}
        return ap.rearrange("p (%s) -> p %s" % (names, names), **kw)

    def mark(self, name):
        self.marks.append((name, dict(self.P.cnt)))

    def new_phase(self, trks):
        merged = {}
        for t in self.phase_cur:
            if t.w is not None:
                merged[t.w[0]] = max(merged.get(t.w[0], 0), t.w[1])
            for k, v in t.r.items():
                merged[k] = max(merged.get(k, 0), v)
        for t in trks:
            t.w = None
            t.r = dict(merged)
        self.phase_cur = list(trks)

    def plan_weights(self, entries):
        self.wplan = list(entries)
        self.wplan_i = 0
        self.wq = []
        self.free_perm = list(self.ring)
        self.free_ext = []
        self.inuse = None
        self.serial = -1

    def begin_phase(self, ext_slots):
        self._release()
        self.serial += 1
        self.free_ext = list(ext_slots)
        self.ext_ids = set(id(b) for b in ext_slots)

    def _release(self):
        if self.inuse is not None:
            slot, ser = self.inuse
            if id(slot) in self.perm_ids:
                self.free_perm.append(slot)
            elif ser == self.serial:
                self.free_ext.append(slot)
            self.inuse = None

    def _topup(self):
        while self.wplan_i < len(self.wplan):
            name, ser = self.wplan[self.wplan_i]
            if ser == self.serial and self.free_ext:
                slot = self.free_ext.pop(0)
            elif self.free_perm:
                slot = self.free_perm.pop(0)
            else:
                break
            self.wplan_i += 1
            off = self.L.off[name]
            free = self.L.free[name]
            src = self.wbf[off:off + 128 * free].rearrange("(p f) -> p f", p=128)
            c0 = off // CONV_CH
            c1 = (off + 128 * free - 1) // CONV_CH
            self.P.dma("sp", slot[:, 0:free], src, r=[self.wchunk[c] for c in range(c0, c1 + 1)], w=[slot.k])
            self.wq.append((name, slot, ser))

    def next_w(self, name, hold=False):
        self._release()
        self._topup()
        n, slot, ser = self.wq.pop(0)
        assert n == name and ser == self.serial, (n, name, ser, self.serial)
        if not hold:
            self.inuse = (slot, ser)
        return slot

    def release_slot(self, slot):
        if id(slot) in self.perm_ids:
            self.free_perm.append(slot)
        elif id(slot) in self.ext_ids:
            self.free_ext.append(slot)

    def ext_slots(self, byte_off):
        out = []
        o = byte_off
        while o + 2 * SLOT <= self.PHB:
            b = Buf(self.phase_view(o, [128, SLOT], BF16))
            out.append(b)
            o += 2 * SLOT
        return out

    def prologue(self):
        P = self.P
        nc = self.nc
        P.dma("sp", self.vecs[:], self.vecs_d, w=[self.vecs.k])
        P.dma("sp", self.bvecs[:], self.bvecs_d, w=[self.bvecs.k])
        P.dma("sp", self.ident_f[:], self.cst_d[:, CST["ident"]:CST["ident"] + 128], w=[self.ident_f.k])
        P.op("dve", lambda e: e.tensor_copy(out=self.ident_b[:], in_=self.ident_f[:]), r=[self.ident_f.k], w=[self.ident_b.k])
        P.op("pool", lambda e: e.memset(self.ones_b[:], 1.0), w=[self.ones_b.k])
        tmp = self.sb("coef_tmp", [128, 16], F32)
        for l in range(2):
            lam = self.vecs[:, VEC[("a_lam", l)]:VEC[("a_lam", l)] + 8]
            P.op("act", lambda e: e.activation(out=tmp[:, l * 8:l * 8 + 8], in_=lam, func=AF.Exp, scale=-1.0), r=[self.vecs.k], w=[tmp.k])
            P.op("act", lambda e: e.activation(out=tmp[:, l * 8:l * 8 + 8], in_=tmp[:, l * 8:l * 8 + 8], func=AF.Ln, bias=1.0), r=[tmp.k], w=[tmp.k])
            P.op("dve", lambda e: e.tensor_scalar(out=self.coef[:, l, 0, :], in0=tmp[:, l * 8:l * 8 + 8], scalar1=-4.0, scalar2=None, op0=ALU.mult), r=[tmp.k], w=[self.coef.k])
            P.op("dve", lambda e: e.tensor_scalar(out=self.coef[:, l, 1, :], in0=tmp[:, l * 8:l * 8 + 8], scalar1=-8.0, scalar2=None, op0=ALU.mult), r=[tmp.k], w=[self.coef.k])
            for kk in range(2):
                gbo = VEC[("a_gb", l, kk)]
                P.op("dve", lambda e: e.tensor_scalar(out=self.hgb[:, l, kk, :], in0=self.vecs[:, gbo:gbo + 8], scalar1=0.5, scalar2=None, op0=ALU.mult), r=[self.vecs.k], w=[self.hgb.k])
        P.op("pool", lambda e: e.memset(self.lnhalf[:], -0.6931471805599453), w=[self.lnhalf.k])
        stg = self.phase_view(0, [128, 2048], F32)
        stk = Trk()
        self.phase_cur = [stk]
        self.conv_done = 0

        def ctab(name, n, dst, dstk, eng):
            P.dma("sp", stg[:, 0:n], self.cst_d[:, CST[name]:CST[name] + n], w=[stk])
            if eng == "act":
                P.op("act", lambda e: e.activation(out=dst, in_=stg[:, 0:n], func=AF.Copy), r=[stk], w=[dstk])
            else:
                P.op(eng, lambda e: e.tensor_copy(out=dst, in_=stg[:, 0:n]), r=[stk], w=[dstk])

        ctab("cos", 2048, self.cos_b[:], self.cos_b.k, "dve")
        ctab("sin", 2048, self.sin_b[:], self.sin_b.k, "act")
        ctab("cmask", 2048, self.cmask_b[:], self.cmask_b.k, "dve")
        ctab("triL", 128, self.triL_b[:], self.triL_b.k, "dve")
        ctab("triU", 128, self.triU_b[:], self.triU_b.k, "dve")
        ctab("prot", 128, self.prot_b[:], self.prot_b.k, "dve")
        ctab("bones", 128, self.bones_b[:], self.bones_b.k, "dve")
        ctab("selb", 512, self.selb[:].rearrange("p a b -> p (a b)"), self.selb.k, "dve")
        P.op("pool", lambda e: e.memset(self.VC[:], 0.0), w=[self.VC.k])
        P.op("pool", lambda e: e.memset(self.KC[:], 0.0), w=[self.KC.k])
        P.op("pool", lambda e: e.memset(self.VC[:, :, 64:65], 1.0), w=[self.VC.k])
        P.dma("sp", stg[:, 0:32], self.cst_d[:, CST["ovl"]:CST["ovl"] + 32], w=[stk])
        for g in range(4):
            P.op("dve", lambda e: e.tensor_copy(out=self.VC[:, g, 65:97], in_=stg[:, 0:32]), r=[stk], w=[self.VC.k])
        for kv in range(2):
            o0 = VEC[("c_pos", kv)]
            P.op("dve", lambda e: e.tensor_copy(out=self.posb[:, kv, :], in_=self.vecs[:, o0:o0 + 32]), r=[self.vecs.k], w=[self.posb.k])
        self.conv_done = 0

    def convert_upto(self, nchunks):
        nchunks = min(nchunks, len(self.wchunk))
        for i in range(self.conv_done, nchunks):
            src = self.wall[i * CONV_CH:(i + 1) * CONV_CH].rearrange("(p f) -> p f", p=128)
            dst = self.wbf[i * CONV_CH:(i + 1) * CONV_CH].rearrange("(p f) -> p f", p=128)
            self.P.dma("pool", dst, src, w=[self.wchunk[i]])
        self.conv_done = max(self.conv_done, nchunks)

    def convert_phase(self, ph):
        ends = self.L.phase_end
        ph = min(ph, len(ends) - 1)
        self.convert_upto((ends[ph] + CONV_CH - 1) // CONV_CH)

    def load_x(self, b):
        P = self.P
        xin = [(self.phase_view(j * 4096, [128, 1024], F32), Trk()) for j in range(4)]
        self.new_phase([k for _, k in xin])
        for n in range(NT):
            for j in range(4):
                t0 = n * TT + j * 128
                P.dma("sp", xin[j][0], self.x[b, t0:t0 + 128, :], w=[xin[j][1]])
            for c in range(NCH):
                pb = self.bank()
                for j in range(4):
                    P.op("pe", lambda e: e.transpose(out=pb[:, j * 128:(j + 1) * 128], in_=xin[j][0][:, c * 128:(c + 1) * 128], identity=self.ident_f[:]),
                         r=[xin[j][1], self.ident_f.k], w=[pb.k])
                eng = "act" if c % 2 == 0 else "dve"
                dst = self.hT[:, c, n * TT:(n + 1) * TT]
                if eng == "act":
                    P.op("act", lambda e: e.activation(out=dst, in_=pb[:], func=AF.Copy), r=[pb.k], w=[self.hk[c][n]])
                else:
                    P.op("dve", lambda e: e.tensor_copy(out=dst, in_=pb[:]), r=[pb.k], w=[self.hk[c][n]])

    def store_y(self, b):
        P = self.P
        yo = [(self.phase_view(j * 4096, [128, 1024], F32), Trk()) for j in range(4)]
        self.new_phase([k for _, k in yo])
        for n in range(NT):
            for j in range(4):
                t0 = n * TT + j * 128
                for half in range(2):
                    pb = self.bank()
                    for cc in range(4):
                        c = half * 4 + cc
                        P.op("pe", lambda e: e.transpose(out=pb[:, cc * 128:(cc + 1) * 128], in_=self.hT[:, c, t0:t0 + 128], identity=self.ident_f[:]),
                             r=[self.hk[c][n], self.ident_f.k], w=[pb.k])
                    dst = yo[j][0][:, half * 512:(half + 1) * 512]
                    wl = [yo[j][1]]
                    if half == 0:
                        P.op("act", lambda e: e.activation(out=dst, in_=pb[:], func=AF.Copy), r=[pb.k], w=wl)
                    else:
                        P.op("dve", lambda e: e.tensor_copy(out=dst, in_=pb[:]), r=[pb.k], w=wl)
                ot = Trk()
                P.dma("sp", self.y[b, t0:t0 + 128, :], yo[j][0], r=[yo[j][1]], w=[ot])
                self.out_trks.append(ot)

    def norm_tile(self, n, gname, u_ap, u_k, sq_bufs, rstd_buf, pool=None):
        P = self.P
        pb = self.bank(pool)
        for c in range(NCH):
            sq, sk = sq_bufs[c % len(sq_bufs)]
            hsl = self.hT[:, c, n * TT:(n + 1) * TT]
            P.op("act", lambda e: e.activation(out=sq, in_=hsl, func=AF.Square), r=[self.hk[c][n]], w=[sk])
            P.op("pe", lambda e: e.matmul(pb[:], lhsT=self.ones_b[:], rhs=sq, start=(c == 0), stop=(c == NCH - 1)),
                 r=[sk, self.ones_b.k], w=[pb.k])
        rs, rk = rstd_buf
        P.op("act", lambda e: e.activation(out=rs, in_=pb[:], func=AF.Ln, scale=1.0 / D, bias=EPS), r=[pb.k], w=[rk])
        P.op("act", lambda e: e.activation(out=rs, in_=rs, func=AF.Exp, scale=-0.5), r=[rk], w=[rk])
        for c in range(NCH):
            hsl = self.hT[:, c, n * TT:(n + 1) * TT]
            g = self.vec(gname, c)
            P.op("dve", lambda e: e.scalar_tensor_tensor(out=u_ap[:, c, :], in0=hsl, scalar=g, in1=rs, op0=ALU.mult, op1=ALU.mult),
                 r=[self.hk[c][n], rk, self.vecs.k], w=[u_k])

    def a_phase_setup(self):
        v = self.phase_view
        A = {}
        A["u"] = [(v(n * 8192, [128, 8, 512], BF16), Trk()) for n in range(NT)]
        A["m"] = [(v(32768 + n * 8192, [128, 8, 512], BF16), Trk()) for n in range(NT)]
        A["sq"] = [(v(65536 + i * 1024, [128, 512], BF16), Trk()) for i in range(2)]
        A["rstd"] = (v(67584, [128, 512], F32), Trk())
        A["xp"] = [(v(69632 + i * 2080, [128, 516], F32), Trk()) for i in range(2)]
        base = 73792
        s1 = []
        for i in range(3):
            o = base + i * 4096
            xr = (v(o + 2048, [128, 512], F32), Trk())
            s1.append({"y": (v(o, [128, 512], BF16), Trk()), "xrb": (v(o + 1024, [128, 512], BF16), Trk()), "xr": xr, "hr": xr})
        s2 = []
        for i in range(2):
            o = base + 12288 + i * 6144
            rr = (v(o, [128, 512], F32), Trk())
            s2.append({"r": rr, "a2": rr, "i": (v(o + 2048, [128, 512], F32), Trk()), "a": (v(o + 4096, [128, 512], F32), Trk())})
        sets = s1 + s2
        A["s1"] = s1
        A["s2"] = s2
        A["s1_extra"] = {"y": (v(65536, [128, 512], BF16), Trk()), "xrb": (v(66560, [128, 512], BF16), Trk())}
        xr4 = (v(67584, [128, 512], F32), Trk())
        A["s1_extra"]["xr"] = xr4
        A["s1_extra"]["hr"] = xr4
        A["sets"] = sets
        trks = [k for _, k in A["u"]] + [k for _, k in A["m"]] + [A["rstd"][1]] + [k for _, k in A["sq"]] + [k for _, k in A["xp"]]
        for st in sets:
            trks += [k for _, k in st.values()]
        self.new_phase(list(dict((id(t), t) for t in trks).values()))
        self.begin_phase([])
        self.A = A

    def a_layer(self, l):
        P = self.P
        self.a_phase_setup()
        A = self.A
        P.op("pool", lambda e: e.memset(self.convc[:], 0.0), w=[self.convc.k])
        P.op("pool", lambda e: e.memset(self.hst[:], 0.0), w=[self.hst.k])
        for n in range(NT):
            self.norm_tile(n, ("a_norm", l), A["u"][n][0], A["u"][n][1], A["sq"], A["rstd"])
        merged = {}
        for t in [k for _, k in A["sq"]] + [A["rstd"][1]]:
            if t.w is not None:
                merged[t.w[0]] = max(merged.get(t.w[0], 0), t.w[1])
            for kk_, vv_ in t.r.items():
                merged[kk_] = max(merged.get(kk_, 0), vv_)
        ex = A["s1_extra"]
        for _, k in ex.values():
            k.w = None
            k.r = dict(merged)
            if k not in self.phase_cur:
                self.phase_cur.append(k)
        S1 = A["s1"] + [ex]
        items = [(c, n) for c in range(NCH) for n in range(NT)]
        wslot = {}

        def stage1(it):
            c, n = items[it]
            if n == 0:
                wslot[c] = self.next_w(("a_in", l, c), hold=True)
            w = wslot[c]
            u, uk = A["u"][n]
            pg = self.bank()
            pr = self.bank()
            for k in range(NCH):
                P.op("pe", lambda e: e.matmul(pg[:], lhsT=w[:, k * 128:(k + 1) * 128], rhs=u[:, k, :], start=(k == 0), stop=(k == NCH - 1)),
                     r=[w.k, uk], w=[pg.k])
            for k in range(NCH):
                P.op("pe", lambda e: e.matmul(pr[:], lhsT=w[:, 1024 + k * 128:1024 + (k + 1) * 128], rhs=u[:, k, :], start=(k == 0), stop=(k == NCH - 1)),
                     r=[w.k, uk], w=[pr.k])
            st = S1[it % 4]
            xp, xpk = A["xp"][it % 2]
            xpn, xpnk = A["xp"][(it + 1) % 2]
            y, yk = st["y"]
            xr, xrk = st["xr"]
            xrb, xrbk = st["xrb"]
            P.op("act", lambda e: e.activation(out=y, in_=pg[:], func=AF.Gelu_apprx_tanh), r=[pg.k], w=[yk])
            if n == 0:
                P.op("pool", lambda e: e.memset(xp[:, 0:3], 0.0), w=[xpk])
            P.op("act", lambda e: e.activation(out=xp[:, 3:515], in_=pr[:], func=AF.Copy), r=[pr.k], w=[xpk])
            if n < NT - 1:
                P.op("pool", lambda e: e.tensor_copy(out=xpn[:, 0:3], in_=xp[:, 512:515]), r=[xpk], w=[xpnk])
            P.op("dve", lambda e: e.tensor_scalar(out=xr, in0=xp[:, 0:512], scalar1=self.vec(("a_cw", l, 0), c), scalar2=self.vec(("a_cb", l), c),
                                                  op0=ALU.mult, op1=ALU.add), r=[xpk, self.vecs.k], w=[xrk])
            for kk in range(1, 4):
                P.op("dve", lambda e: e.scalar_tensor_tensor(out=xr, in0=xp[:, kk:kk + 512], scalar=self.vec(("a_cw", l, kk), c), in1=xr,
                                                             op0=ALU.mult, op1=ALU.add), r=[xpk, xrk, self.vecs.k], w=[xrk])
            P.op("dve", lambda e: e.tensor_copy(out=xrb, in_=xr), r=[xrk], w=[xrbk])

        def stage2(it):
            c, n = items[it]
            w = wslot[c]
            m, mk = A["m"][n]
            st = S1[it % 4]
            y, yk = st["y"]
            xr, xrk = st["xr"]
            xrb, xrbk = st["xrb"]
            hr, hrk = st["hr"]
            hprev, hprevk = S1[(it - 1) % 4]["hr"]
            t2 = A["s2"][it % 2]
            rr, rrk = t2["r"]
            ii, iik = t2["i"]
            aa, aak = t2["a"]
            p1 = self.bank()
            p2 = self.bank()
            P.op("pe", lambda e: e.matmul(p1[:], lhsT=w[:, 2048:2176], rhs=xrb, start=True, stop=True), r=[w.k, xrbk], w=[p1.k])
            P.op("pe", lambda e: e.matmul(p2[:], lhsT=w[:, 2176:2304], rhs=xrb, start=True, stop=True), r=[w.k, xrbk], w=[p2.k])
            if n == NT - 1:
                self.release_slot(w)
            P.op("act", lambda e: e.activation(out=rr, in_=p1[:], func=AF.Tanh, scale=0.5, bias=self.hgb[:, l, 0, c:c + 1]), r=[p1.k, self.hgb.k], w=[rrk])
            P.op("act", lambda e: e.activation(out=ii, in_=p2[:], func=AF.Tanh, scale=0.5, bias=self.hgb[:, l, 1, c:c + 1]), r=[p2.k, self.hgb.k], w=[iik])
            P.op("act", lambda e: e.activation(out=aa, in_=rr, func=AF.Exp, scale=self.coef[:, l, 0, c:c + 1], bias=self.coef[:, l, 0, c:c + 1]), r=[rrk, self.coef.k], w=[aak])
            P.op("act", lambda e: e.activation(out=rr, in_=rr, func=AF.Exp, scale=self.coef[:, l, 1, c:c + 1], bias=self.coef[:, l, 1, c:c + 1]), r=[rrk, self.coef.k], w=[rrk])
            P.op("act", lambda e: e.activation(out=rr, in_=rr, func=AF.Ln, scale=-0.999999, bias=1.0), r=[rrk], w=[rrk])
            P.op("act", lambda e: e.activation(out=rr, in_=rr, func=AF.Exp, scale=0.5, bias=self.lnhalf[:, 0:1]), r=[rrk, self.lnhalf.k], w=[rrk])
            P.op("dve", lambda e: e.scalar_tensor_tensor(out=ii, in0=ii, scalar=1.0, in1=xr, op0=ALU.add, op1=ALU.mult), r=[iik, xrk], w=[iik])
            P.op("dve", lambda e: e.tensor_tensor(out=ii, in0=ii, in1=rr, op=ALU.mult), r=[iik, rrk], w=[iik])
            if n == 0:
                P.op("dve", lambda e: e.tensor_tensor_scan(out=hr, data0=aa, data1=ii, initial=0.0, op0=ALU.mult, op1=ALU.add), r=[aak, iik], w=[hrk])
            else:
                P.op("dve", lambda e: e.tensor_tensor_scan(out=hr, data0=aa, data1=ii, initial=hprev[:, 511:512], op0=ALU.mult, op1=ALU.add),
                     r=[aak, iik, hprevk], w=[hrk])
            P.op("pool", lambda e: e.tensor_tensor(out=m[:, c, :], in0=hr, in1=y, op=ALU.mult), r=[hrk, yk], w=[mk])

        LA = 2
        for it in range(min(LA, len(items))):
            stage1(it)
        for it in range(len(items)):
            if it + LA < len(items):
                stage1(it + LA)
            stage2(it)
        for mm in range(NCH):
            w = self.next_w(("a_out", l, mm))
            for n in range(NT):
                m, mk = A["m"][n]
                po = self.bank()
                for k in range(NCH):
                    P.op("pe", lambda e: e.matmul(po[:], lhsT=w[:, k * 128:(k + 1) * 128], rhs=m[:, k, :], start=(k == 0), stop=(k == NCH - 1)),
                         r=[w.k, mk], w=[po.k])
                hsl = self.hT[:, mm, n * TT:(n + 1) * TT]
                P.op("dve", lambda e: e.tensor_tensor(out=hsl, in0=po[:], in1=hsl, op=ALU.add), r=[po.k, self.hk[mm][n]], w=[self.hk[mm][n]])

    def ffn_phase_setup(self):
        v = self.phase_view
        Fz = {}
        Fz["u"] = [(v(s * 8192, [128, 8, 512], BF16), Trk()) for s in range(2)]
        Fz["act"] = [[(v(16384 + (c * 2 + s) * 1024, [128, 512], BF16), Trk()) for s in range(2)] for c in range(FC)]
        Fz["sq"] = [(v(61440 + i * 1024, [128, 512], BF16), Trk()) for i in range(2)]
        Fz["rstd"] = (v(63488, [128, 512], F32), Trk())
        Fz["sg"] = [(v(65536 + i * 1024, [128, 512], BF16), Trk()) for i in range(4)]
        ext = self.ext_slots(69632)
        trks = [k for _, k in Fz["u"]] + [k for row in Fz["act"] for _, k in row] + [k for _, k in Fz["sq"]] + [Fz["rstd"][1]] + [k for _, k in Fz["sg"]] + [b.k for b in ext]
        self.new_phase(trks)
        self.begin_phase(ext)
        self.F = Fz

    def ffn_tile(self, L, t2):
        P = self.P
        Fz = self.F
        if t2 == 0:
            for s in range(2):
                self.norm_tile(2 * t2 + s, ("f_norm", L), Fz["u"][s][0], Fz["u"][s][1], Fz["sq"], Fz["rstd"])
        nsg = 0
        for c in range(FC):
            w = self.next_w(("f_in", L, c))
            for s in range(2):
                u, uk = Fz["u"][s]
                pg = self.bank()
                pu = self.bank()
                for k in range(NCH):
                    P.op("pe", lambda e: e.matmul(pg[:], lhsT=w[:, k * 128:(k + 1) * 128], rhs=u[:, k, :], start=(k == 0), stop=(k == NCH - 1)),
                         r=[w.k, uk], w=[pg.k])
                for k in range(NCH):
                    P.op("pe", lambda e: e.matmul(pu[:], lhsT=w[:, 1024 + k * 128:1024 + (k + 1) * 128], rhs=u[:, k, :], start=(k == 0), stop=(k == NCH - 1)),
                         r=[w.k, uk], w=[pu.k])
                sg, sgk = Fz["sg"][nsg % 4]
                nsg += 1
                a, ak = Fz["act"][c][s]
                P.op("act", lambda e: e.activation(out=sg, in_=pg[:], func=AF.Silu), r=[pg.k], w=[sgk])
                P.op("dve", lambda e: e.tensor_tensor(out=a, in0=sg, in1=pu[:], op=ALU.mult), r=[sgk, pu.k], w=[ak])
        HC = FC // 2
        for mm in range(NCH):
            pos_ = [self.bank(), self.bank()]
            for hf in range(2):
                w = self.next_w(("f_out", L, mm, hf))
                for s in range(2):
                    po = pos_[s]
                    for cc in range(HC):
                        c = hf * HC + cc
                        a, ak = Fz["act"][c][s]
                        P.op("pe", lambda e: e.matmul(po[:], lhsT=w[:, cc * 128:(cc + 1) * 128], rhs=a, start=(c == 0), stop=(c == FC - 1)),
                             r=[w.k, ak], w=[po.k])
            for s in range(2):
                n = 2 * t2 + s
                po = pos_[s]
                hsl = self.hT[:, mm, n * TT:(n + 1) * TT]
                P.op("dve", lambda e: e.tensor_tensor(out=hsl, in0=po[:], in1=hsl, op=ALU.add), r=[po.k, self.hk[mm][n]], w=[self.hk[mm][n]])
            if t2 == 0 and mm in (2, 5):
                s_ = 0 if mm == 2 else 1
                self.norm_tile(2 + s_, ("f_norm", L), Fz["u"][s_][0], Fz["u"][s_][1], Fz["sq"], Fz["rstd"])

    def ffn_layer(self, L):
        self.ffn_phase_setup()
        for t2 in range(2):
            self.ffn_tile(L, t2)

    def headnorm_rope(self, pk, R, C, gvec, cos_ap, sin_ap, T, bias=None):
        P = self.P
        sq, sqk = T["sq"]
        rs, rsk = T["rs"]
        qn, qnk = T["qn"]
        t1, t1k = T["t1"]
        t2, t2k = T["t2"]
        if bias is not None:
            xf, xfk = T["xf"]
            P.op("act", lambda e: e.activation(out=xf[0:R, 0:C], in_=pk[0:R, 0:C], func=AF.Identity, bias=bias), r=[pk.k, self.vecs.k], w=[xfk])
            src, srck = xf[0:R, 0:C], xfk
        else:
            src, srck = pk[0:R, 0:C], pk.k
        P.op("act", lambda e: e.activation(out=sq[0:R, 0:C], in_=src, func=AF.Square), r=[srck], w=[sqk])
        pss = self.bank(self.pool_misc)
        P.op("pe", lambda e: e.matmul(pss[0:R, 0:C], lhsT=self.bones_b[0:R, 0:R], rhs=sq[0:R, 0:C], start=True, stop=True), r=[sqk, self.bones_b.k], w=[pss.k])
        P.op("act", lambda e: e.activation(out=rs[0:R, 0:C], in_=pss[0:R, 0:C], func=AF.Ln, scale=1.0 / 64.0, bias=EPS), r=[pss.k], w=[rsk])
        P.op("act", lambda e: e.activation(out=rs[0:R, 0:C], in_=rs[0:R, 0:C], func=AF.Exp, scale=-0.5), r=[rsk], w=[rsk])
        P.op("dve", lambda e: e.scalar_tensor_tensor(out=qn[0:R, 0:C], in0=src, scalar=gvec, in1=rs[0:R, 0:C], op0=ALU.mult, op1=ALU.mult),
             r=[srck, rsk, self.vecs.k], w=[qnk])
        prt = self.bank(self.pool_misc)
        P.op("pe", lambda e: e.matmul(prt[0:R, 0:C], lhsT=self.prot_b[0:R, 0:R], rhs=qn[0:R, 0:C], start=True, stop=True), r=[qnk, self.prot_b.k], w=[prt.k])
        P.op("pool", lambda e: e.tensor_tensor(out=t1[0:R, 0:C], in0=qn[0:R, 0:C], in1=cos_ap, op=ALU.mult), r=[qnk, self.cos_b.k], w=[t1k])
        P.op("dve", lambda e: e.tensor_tensor(out=t2[0:R, 0:C], in0=prt[0:R, 0:C], in1=sin_ap, op=ALU.mult), r=[prt.k, self.sin_b.k], w=[t2k])

    def kv_phase(self):
        P = self.P
        v = self.phase_view
        U = [(v(n * 8192, [128, 8, 512], BF16), Trk()) for n in range(NT)]
        sqn = [(v(32768 + i * 1024, [128, 512], BF16), Trk()) for i in range(2)]
        rstd = (v(34816, [128, 512], F32), Trk())
        kcT, kcTk = v(36864, [128, 8, S], BF16), [Trk() for _ in range(8)]
        cw1, cw1k = v(0, [128, 2, 32, 256], BF16), Trk()
        T = {"sq": (v(69632, [128, 512], BF16), Trk()), "rs": (v(70656, [128, 512], F32), Trk()), "qn": (v(72704, [128, 512], BF16), Trk()),
             "t1": (v(73728, [128, 512], F32), Trk()), "t2": (v(75776, [128, 512], F32), Trk()), "xf": (v(77824, [128, 512], F32), Trk())}
        kout = [(v(79872 + i * 1024, [128, 512], BF16), Trk()) for i in range(2)]
        vst = [(v(81920 + i * 528, [128, 4, 66], BF16), Trk()) for i in range(2)]
        hid = [(v(83008 + i * 512, [128, 2, 128], BF16), Trk()) for i in range(2)]
        ext = self.ext_slots(84032)
        trks = [k for _, k in U] + [rstd[1], cw1k] + [k for _, k in sqn] + kcTk + [k for _, k in T.values()] + [k for _, k in kout] + [k for _, k in vst] \
            + [k for _, k in hid] + [b.k for b in ext]
        self.new_phase(trks)
        self.begin_phase(ext)
        self.pool_misc = [0, 1, 2, 3, 4, 5, 6, 7]
        for i in range(2):
            P.op("pool", lambda e: e.memset(vst[i][0][:, :, 64:66], 1.0), w=[vst[i][1]])
        for n in range(NT):
            self.norm_tile(n, ("kv_norm",), U[n][0], U[n][1], sqn, rstd)
        nko = 0
        nvs = 0
        for i in range(4):
            which, cc = i // 2, i % 2
            w = self.next_w(("kvk", i))
            for n in range(NT):
                tsl = slice(n * TT, (n + 1) * TT)
                u, uk = U[n]
                pk = self.bank()
                for k in range(NCH):
                    P.op("pe", lambda e: e.matmul(pk[:], lhsT=w[:, k * 128:(k + 1) * 128], rhs=u[:, k, :], start=(k == 0), stop=(k == NCH - 1)), r=[w.k, uk], w=[pk.k])
                self.headnorm_rope(pk, 128, 512, self.vec(("k_norm", 1 + which)), self.cos_b[:, tsl], self.sin_b[:, tsl], T)
                ko, kok = kout[nko % 2]
                nko += 1
                P.op("dve", lambda e: e.tensor_tensor(out=ko, in0=T["t1"][0], in1=T["t2"][0], op=ALU.add), r=[T["t1"][1], T["t2"][1]], w=[kok])
                for hh in range(2):
                    P.dma("sp", self.KSd[which, :, 2 * cc + hh, tsl], ko[hh * 64:(hh + 1) * 64, :], r=[kok], w=[self.ksd_k[which][2 * cc + hh][n]])
        for i in range(4):
            sel, cc = i // 2, i % 2
            w = self.next_w(("kvc", i))
            for n in range(NT):
                tsl = slice(n * TT, (n + 1) * TT)
                u, uk = U[n]
                for gg in range(2):
                    pc = self.bank()
                    for k in range(NCH):
                        P.op("pe", lambda e: e.matmul(pc[0:64, :], lhsT=w[:, k * 128 + gg * 64:k * 128 + gg * 64 + 64], rhs=u[:, k, :], start=(k == 0), stop=(k == NCH - 1)),
                             r=[w.k, uk], w=[pc.k])
                    idx = sel * 4 + 2 * cc + gg
                    P.op("act", lambda e: e.activation(out=kcT[0:64, idx, tsl], in_=pc[0:64, :], func=AF.Copy), r=[pc.k], w=[kcTk[idx]])
        for i in range(2):
            w = self.next_w(("kvv", i))
            for n in range(NT):
                u, uk = U[n]
                for jb in range(4):
                    pv = self.bank()
                    for k in range(NCH):
                        P.op("pe", lambda e: e.matmul(pv[:, 0:256], lhsT=u[:, k, jb * 128:(jb + 1) * 128], rhs=w[:, k * 256:(k + 1) * 256], start=(k == 0), stop=(k == NCH - 1)),
                             r=[w.k, uk], w=[pv.k])
                    vs, vsk = vst[nvs % 2]
                    nvs += 1
                    P.op("act", lambda e: e.activation(out=vs[:, :, 0:64], in_=pv[:, 0:256].rearrange("p (g d) -> p g d", g=4), func=AF.Copy), r=[pv.k], w=[vsk])
                    P.dma("sp", self.Vd[i, :, 4 * n + jb, :, :], vs, r=[vsk], w=[self.vd_k[i][n]])
        for kv in range(2):
            off = self.L.off[("cw1", kv)]
            src = self.wbf[off:off + 128 * 8192].rearrange("(p f) -> p f", p=128)
            c0, c1 = off // CONV_CH, (off + 128 * 8192 - 1) // CONV_CH
            P.dma("sp", cw1[0:64, kv].rearrange("p a b -> p (a b)"), src[0:64, :], r=[self.wchunk[c] for c in range(c0, c1 + 1)], w=[cw1k] + [k for _, k in U])
        w2 = self.next_w(("cw2",))
        for sel in range(2):
            pcv = self.bank()
            for cc in range(2):
                for l in range(32):
                    P.op("pe", lambda e: e.matmul(pcv[:, cc:cc + 1], lhsT=cw1[0:64, sel, l, cc * 128:(cc + 1) * 128], rhs=self.posb[0:64, sel, l:l + 1],
                                                  start=(cc == 0 and l == 0), stop=(cc == 1 and l == 31)), r=[cw1k, self.posb.k], w=[pcv.k])
            b1o = VEC[("c_b1", sel)]
            P.op("dve", lambda e: e.tensor_tensor(out=self.cbias[:, sel, :], in0=pcv[:, 0:2], in1=self.vecs[:, b1o:b1o + 2], op=ALU.add), r=[pcv.k, self.vecs.k], w=[self.cbias.k])
        nh = 0
        for sel in range(2):
            for g in range(4):
                hd, hdk = hid[nh % 2]
                nh += 1
                for cc in range(2):
                    ph = self.bank()
                    for l in range(32):
                        P.op("pe", lambda e: e.matmul(ph[:, 0:127], lhsT=cw1[0:64, sel, l, cc * 128:(cc + 1) * 128], rhs=kcT[0:64, sel * 4 + g, l:l + 16 * 126 + 1:16],
                                                      start=(l == 0), stop=(l == 31)), r=[cw1k, kcTk[sel * 4 + g]], w=[ph.k])
                    P.op("act", lambda e: e.activation(out=hd[:, cc, 0:127], in_=ph[:, 0:127], func=AF.Gelu_apprx_tanh, bias=self.cbias[:, sel, cc:cc + 1]),
                         r=[ph.k, self.cbias.k], w=[hdk])
                if sel == 0:
                    pk = self.bank()
                    for cc in range(2):
                        P.op("pe", lambda e: e.matmul(pk[0:64, 0:127], lhsT=w2[:, (0 * 2 + cc) * 64:(0 * 2 + cc) * 64 + 64], rhs=hd[:, cc, 0:127], start=(cc == 0), stop=(cc == 1)),
                             r=[w2.k, hdk], w=[pk.k])
                    self.headnorm_rope(pk, 64, 127, self.vecs[0:64, VEC[("k_norm", 0)]:VEC[("k_norm", 0)] + 1],
                                       self.cos_b[0:64, 31:31 + 16 * 126 + 1:16], self.sin_b[0:64, 31:31 + 16 * 126 + 1:16], T,
                                       bias=self.vecs[0:64, VEC[("c_b2", 0)]:VEC[("c_b2", 0)] + 1])
                    P.op("dve", lambda e: e.tensor_tensor(out=self.KC[0:64, g, 0:127], in0=T["t1"][0][0:64, 0:127], in1=T["t2"][0][0:64, 0:127], op=ALU.add),
                         r=[T["t1"][1], T["t2"][1]], w=[self.KC.k])
                else:
                    pv = self.bank()
                    for cc in range(2):
                        P.op("pe", lambda e: e.matmul(pv[0:127, 0:64], lhsT=hd[:, cc, 0:127], rhs=w2[:, (1 * 2 + cc) * 64:(1 * 2 + cc) * 64 + 64], start=(cc == 0), stop=(cc == 1)),
                             r=[w2.k, hdk], w=[pv.k])
                    P.op("dve", lambda e: e.tensor_tensor(out=self.VC[0:127, g, 0:64], in0=pv[0:127, 0:64], in1=self.bvecs[0:127, BV_CB2V:BV_CB2V + 64], op=ALU.add),
                         r=[pv.k, self.bvecs.k], w=[self.VC.k])

    def b_layer(self, j):
        P = self.P
        v = self.phase_view
        KS, KSk = v(0, [128, 4, S], BF16), [Trk() for _ in range(4)]
        KW, KWk = v(16384, [128, 4, S], BF16), [Trk() for _ in range(4)]
        VS, VSk = v(32768, [128, 16, 4, 66], BF16), Trk()
        VW, VWk = v(41216, [128, 16, 4, 66], BF16), Trk()
        u, uk = v(49664, [128, 8, 512], BF16), Trk()
        Q, Qk = v(57856, [128, 4, 16, 128], BF16), [Trk() for _ in range(4)]
        Nk = [[Trk() for _ in range(4)] for _ in range(4)]
        oT, oTk = v(74240, [128, 8, 512], BF16), Trk()
        ob = [(v(82432 + i * 2048, [128, 1024], BF16), Trk()) for i in range(2)]
        PT = [(v(86528 + i * 1024, [128, 512], BF16), Trk()) for i in range(4)]
        sqn = [(v(90624 + i * 1024, [128, 512], BF16), Trk()) for i in range(2)]
        rstd = (v(92672, [128, 512], F32), Trk())
        gat, gatk = v(94720, [128, 4, 48], F32), Trk()
        T = {"sq": sqn[0], "rs": rstd, "qn": (v(95488, [128, 512], BF16), Trk()),
             "t1": (v(96512, [128, 512], BF16), Trk()), "t2": (v(97536, [128, 512], BF16), Trk())}
        rd, rdk = v(98560, [128, 16], F32), Trk()
        s3, s3k = v(98624, [128, 12], F32), Trk()
        sc, sck = v(98688, [128, 32], F32), Trk()
        sc2, sc2k = v(98816, [128, 32], F32), Trk()
        nsl, nslk = v(98944, [128, 32], F32), Trk()
        m8, m8k = v(99072, [128, 16], F32), Trk()
        acc, acck = v(99136, [128, 256], F32), Trk()
        trks = KSk + KWk + [VSk, VWk, uk, oTk, gatk, rdk, s3k, sck, sc2k, nslk, m8k, acck] + Qk + [k for _, k in ob] + [k for _, k in PT] \
            + [k for _, k in sqn] + [rstd[1]] + [T["qn"][1], T["t1"][1], T["t2"][1]] + [k for row in Nk for k in row]
        self.new_phase(trks)
        self.begin_phase([])
        self.pool_misc = [6]
        pool_st = [0, 1, 2]
        for g in range(4):
            P.dma("sp", KS[0:64, g, :], self.KSd[0, :, g, :], r=self.ksd_k[0][g], w=[KSk[g]])
            P.dma("sp", KW[0:64, g, :], self.KSd[1, :, g, :], r=self.ksd_k[1][g], w=[KWk[g]])
        P.dma("sp", VS.rearrange("p a b c -> p (a b c)"), self.Vd[0].rearrange("p a b c -> p (a b c)"), r=self.vd_k[0], w=[VSk])
        P.dma("sp", VW.rearrange("p a b c -> p (a b c)"), self.Vd[1].rearrange("p a b c -> p (a b c)"), r=self.vd_k[1], w=[VWk])
        est = self.ph[64:96, 57856 // 4:57856 // 4 + 2048]
        allq = Qk + [k for row in Nk for k in row]
        P.dma("sp", est, self.cst_d[64:96, CST["eind"]:CST["eind"] + 2048], w=allq)
        for g in range(4):
            P.op("dve", lambda e: e.tensor_copy(out=KS[64:96, g, :], in_=est), r=allq, w=[KSk[g]])
            P.op("pool", lambda e: e.memset(KW[64:96, g, :], 0.0), w=[KWk[g]])
        P.op("pool", lambda e: e.memset(Q[64:96].rearrange("p a b c -> p (a b c)"), 0.0), w=allq)
        gbo = BV_GB + 48 * j
        gats = [(gat, gatk), (v(91648, [128, 4, 48], F32), Trk())]
        self.phase_cur.append(gats[1][1])
        Qn, Qnk = self.Qnext, self.Qnext.k
        sq1 = [sqn[0]]
        pqb = self.ps[7]

        def prep_pieces(n):
            tsl = slice(n * TT, (n + 1) * TT)
            gt, gtk = gats[n % 2]
            pieces = []

            def p_norm():
                self.norm_tile(n, ("b_norm", j), u, uk, sq1, rstd, pool=[6])

            def p_gates():
                wg = self.next_w(("bg", j))
                for qb in range(4):
                    pg = self.bank([6])
                    for k in range(NCH):
                        P.op("pe", lambda e: e.matmul(pg[:, 0:48], lhsT=u[:, k, qb * 128:(qb + 1) * 128], rhs=wg[:, k * 48:(k + 1) * 48], start=(k == 0), stop=(k == NCH - 1)),
                             r=[wg.k, uk], w=[pg.k])
                    P.op("dve", lambda e: e.tensor_tensor(out=gt[:, qb, :], in0=pg[:, 0:48], in1=self.bvecs[:, gbo:gbo + 48], op=ALU.add), r=[pg.k, self.bvecs.k], w=[gtk])
                P.op("act", lambda e: e.activation(out=gt, in_=gt, func=AF.Tanh, scale=0.5), r=[gtk], w=[gtk])
                P.op("dve", lambda e: e.tensor_scalar(out=gt, in0=gt, scalar1=0.5, scalar2=0.5, op0=ALU.mult, op1=ALU.add), r=[gtk], w=[gtk])

            pieces += [p_norm, p_gates]
            sq, sqk = T["sq"]
            rs, rsk = T["rs"]
            qn, qnk = T["qn"]
            t1, t1k = T["t1"]
            t2, t2k = T["t2"]
            gq = self.vec(("q_norm", j))
            for m in range(NCH):
                def p_d(m=m):
                    w = self.next_w(("bq", j, m))
                    for k in range(NCH):
                        P.op("pe", lambda e: e.matmul(pqb[:], lhsT=w[:, k * 128:(k + 1) * 128], rhs=u[:, k, :], start=(k == 0), stop=(k == NCH - 1)), r=[w.k, uk], w=[pqb.k])
                    P.op("act", lambda e: e.activation(out=sq, in_=pqb[:], func=AF.Square), r=[pqb.k], w=[sqk])

                def p_e(m=m):
                    pss = self.bank([6])
                    P.op("pe", lambda e: e.matmul(pss[:], lhsT=self.bones_b[:], rhs=sq, start=True, stop=True), r=[sqk, self.bones_b.k], w=[pss.k])
                    P.op("act", lambda e: e.activation(out=rs, in_=pss[:], func=AF.Ln, scale=1.0 / 64.0, bias=EPS), r=[pss.k], w=[rsk])
                    P.op("act", lambda e: e.activation(out=rs, in_=rs, func=AF.Exp, scale=-0.5), r=[rsk], w=[rsk])
                    P.op("dve", lambda e: e.scalar_tensor_tensor(out=qn, in0=pqb[:], scalar=gq, in1=rs, op0=ALU.mult, op1=ALU.mult), r=[pqb.k, rsk, self.vecs.k], w=[qnk])

                def p_f(m=m):
                    prt = self.bank([6])
                    P.op("pe", lambda e: e.matmul(prt[:], lhsT=self.prot_b[:], rhs=qn, start=True, stop=True), r=[qnk, self.prot_b.k], w=[prt.k])
                    P.op("pool", lambda e: e.tensor_tensor(out=t1, in0=qn, in1=self.cos_b[:, tsl], op=ALU.mult), r=[qnk, self.cos_b.k], w=[t1k])
                    P.op("dve", lambda e: e.tensor_tensor(out=t2, in0=prt[:], in1=self.sin_b[:, tsl], op=ALU.mult), r=[prt.k, self.sin_b.k], w=[t2k])
                    P.op("dve", lambda e: e.tensor_tensor(out=Qn[:, m, :], in0=t1, in1=t2, op=ALU.add), r=[t1k, t2k], w=[Qnk])

                pieces += [p_d, p_e, p_f]
            return pieces

        def install_q():
            for m in range(NCH):
                for hh in range(2):
                    h = 2 * m + hh
                    src = Qn[hh * 64:(hh + 1) * 64, m, :].rearrange("p (a b) -> p a b", a=4)
                    if (h % 2) == 0:
                        P.op("dve", lambda e: e.tensor_copy(out=Q[0:64, :, h, :], in_=src), r=[Qnk], w=Qk)
                    else:
                        P.op("act", lambda e: e.activation(out=Q[0:64, :, h, :], in_=src, func=AF.Copy), r=[Qnk], w=Qk)

        for f in prep_pieces(0):
            f()
        for n in range(NT):
            tsl = slice(n * TT, (n + 1) * TT)
            install_q()
            bg = prep_pieces(n + 1) if n + 1 < NT else []
            gt, gtk = gats[n % 2]
            self.attn_tile(n, dict(KS=KS, KSk=KSk, KW=KW, KWk=KWk, VS=VS, VSk=VSk, VW=VW, VWk=VWk, Q=Q, Qk=Qk, Nk=Nk, oT=oT, oTk=oTk, ob=ob, PT=PT,
                                   gat=gt, gatk=gtk, rd=rd, rdk=rdk, s3=s3, s3k=s3k, sc=sc, sck=sck, sc2=sc2, sc2k=sc2k, nsl=nsl, nslk=nslk,
                                   m8=m8, m8k=m8k, acc=acc, acck=acck, pool_st=pool_st), bg)
            for mm in range(NCH):
                w = self.next_w(("bo", j, mm))
                po = self.bank(pool_st)
                for k in range(NCH):
                    P.op("pe", lambda e: e.matmul(po[:], lhsT=w[:, k * 128:(k + 1) * 128], rhs=oT[:, k, :], start=(k == 0), stop=(k == NCH - 1)), r=[w.k, oTk], w=[po.k])
                hsl = self.hT[:, mm, tsl]
                P.op("dve", lambda e: e.tensor_tensor(out=hsl, in0=po[:], in1=hsl, op=ALU.add), r=[po.k, self.hk[mm][n]], w=[self.hk[mm][n]])
        self.pool_misc = [0, 1, 2, 3, 4, 5, 6, 7]

    def attn_tile(self, n, C, bg=()):
        P = self.P
        KS, KSk, KW, KWk, VS, VSk, VW, VWk = C["KS"], C["KSk"], C["KW"], C["KWk"], C["VS"], C["VSk"], C["VW"], C["VWk"]
        Q, Qk, Nk, oT, oTk, ob, PT = C["Q"], C["Qk"], C["Nk"], C["oT"], C["oTk"], C["ob"], C["PT"]
        gat, gatk, rd, rdk, s3, s3k = C["gat"], C["gatk"], C["rd"], C["rdk"], C["s3"], C["s3k"]
        sc, sck, sc2, sc2k, nsl, nslk, m8, m8k, acc, acck = C["sc"], C["sck"], C["sc2"], C["sc2k"], C["nsl"], C["nslk"], C["m8"], C["m8k"], C["acc"], C["acck"]
        pool_st = C["pool_st"]
        poc, pos, pow_ = self.ps[3], self.ps[4], self.ps[5]
        st = {"npt": 0}
        pairs = [(qb, g) for qb in range(4) for g in range(4)]
        jobs = []

        def r4(ap):
            return ap.rearrange("p (a b) -> p a b", a=4)

        def mk_job(kind, qb, g, kt, first, last, pidx):
            qbg = 4 * n + qb
            job = {"pre": [], "post": []}
            box = {}

            def st1():
                pst = self.bank(pool_st)
                pt, ptk = PT[st["npt"] % 4]
                st["npt"] += 1
                box["pt"], box["ptk"] = pt, ptk
                Q96 = Q[0:96, qb, 4 * g:4 * g + 4, :]
                if kind == "c":
                    P.op("pe", lambda e: e.matmul(pst[0:127, :], lhsT=self.KC[0:96, g, 0:127], rhs=Q96, start=True, stop=True), r=[self.KC.k, Qk[qb]], w=[pst.k])
                    P.op("act", lambda e: e.activation(out=pt[0:127, :], in_=pst[0:127, :], func=AF.Exp, scale=SCALE), r=[pst.k], w=[ptk])
                    cm = self.cmask_b[0:127, qbg * 128:(qbg + 1) * 128].unsqueeze(1).broadcast_to([127, 4, 128])
                    P.op("pool", lambda e: e.tensor_tensor(out=r4(pt[0:127, :]), in0=r4(pt[0:127, :]), in1=cm, op=ALU.mult), r=[ptk, self.cmask_b.k], w=[ptk])
                    return
                if kind == "s":
                    P.op("pe", lambda e: e.matmul(pst[:], lhsT=KS[0:96, g, kt * 128:(kt + 1) * 128], rhs=Q96, start=True, stop=True),
                         r=[KSk[g], Qk[qb], Nk[qb][g]], w=[pst.k])
                else:
                    P.op("pe", lambda e: e.matmul(pst[:], lhsT=KW[0:96, g, kt * 128:(kt + 1) * 128], rhs=Q96, start=True, stop=True), r=[KWk[g], Qk[qb]], w=[pst.k])
                P.op("act", lambda e: e.activation(out=pt, in_=pst[:], func=AF.Exp, scale=SCALE), r=[pst.k], w=[ptk])
                msk = None
                if kt == qbg:
                    msk = self.triL_b
                elif kind == "w" and kt == qbg - 4:
                    msk = self.triU_b
                if msk is not None:
                    tm = msk[:].unsqueeze(1).broadcast_to([128, 4, 128])
                    P.op("pool", lambda e: e.tensor_tensor(out=r4(pt), in0=r4(pt), in1=tm, op=ALU.mult), r=[ptk, msk.k], w=[ptk])

            def st2():
                pt, ptk = box["pt"], box["ptk"]
                for h in range(4):
                    if kind == "c":
                        P.op("pe", lambda e: e.matmul(poc[:, h * 128:h * 128 + 97], lhsT=pt[0:127, h * 128:(h + 1) * 128], rhs=self.VC[0:127, g, 0:97],
                                                      start=(h == 0), stop=(h == 3), skip_group_check=True), r=[ptk, self.VC.k], w=[poc.k])
                    elif kind == "s":
                        P.op("pe", lambda e: e.matmul(pos[:, h * 128:h * 128 + 65], lhsT=pt[:, h * 128:(h + 1) * 128], rhs=VS[:, kt, g, 0:65],
                                                      start=(first and h == 0), stop=(last and h == 3), skip_group_check=True), r=[ptk, VSk], w=[pos.k])
                    else:
                        P.op("pe", lambda e: e.matmul(pow_[:, h * 128:h * 128 + 65], lhsT=pt[:, h * 128:(h + 1) * 128], rhs=VW[:, kt, g, 0:65],
                                                      start=(first and h == 0), stop=(last and h == 3), skip_group_check=True), r=[ptk, VWk], w=[pow_.k])

            job["st1"], job["st2"] = st1, st2
            return job

        def select_chain(qb, g, pidx):
            qbg = 4 * n + qb
            pcs = self.pocs[pidx % 2]
            P.op("act", lambda e: e.activation(out=pcs[:], in_=r4(poc[:])[:, :, 0:97], func=AF.Copy), r=[poc.k], w=[pcs.k])
            P.op("dve", lambda e: e.tensor_scalar(out=rd[:, 0:4], in0=pcs[:, :, 64], scalar1=1e-30, scalar2=None, op0=ALU.max), r=[pcs.k], w=[rdk])
            P.op("dve", lambda e: e.reciprocal(out=rd[:, 0:4], in_=rd[:, 0:4]), r=[rdk], w=[rdk])
            for h in range(4):
                src1 = self.selb[:, qbg, :] if h == 0 else sc
                P.op("dve", lambda e: e.scalar_tensor_tensor(out=sc, in0=pcs[:, h, 65:97], scalar=rd[:, h:h + 1], in1=src1, op0=ALU.mult, op1=ALU.add),
                     r=[pcs.k, rdk, sck, self.selb.k], w=[sck])
            P.op("dve", lambda e: e.max(out=m8[:, 0:8], in_=sc), r=[sck], w=[m8k])
            P.op("dve", lambda e: e.match_replace(out=sc2, in_to_replace=m8[:, 0:8], in_values=sc, imm_value=-3.0e38), r=[sck, m8k], w=[sc2k])
            P.op("dve", lambda e: e.max(out=m8[:, 8:16], in_=sc2), r=[sc2k], w=[m8k])
            P.op("dve", lambda e: e.tensor_scalar(out=nsl, in0=sc, scalar1=m8[:, 15:16], scalar2=NEGM, op0=ALU.is_lt, op1=ALU.mult), r=[sck, m8k], w=[nslk])

        def nsel_install(qb, g):
            pm = self.bank(self.pool_misc)
            P.op("pe", lambda e: e.transpose(out=pm[0:32, 0:128], in_=nsl, identity=self.ident_f[:]), r=[nslk, self.ident_f.k], w=[pm.k])
            P.op("act", lambda e: e.activation(out=Q[64:96, qb, 4 * g:4 * g + 4, :], in_=pm[0:32, 0:128].unsqueeze(1).broadcast_to([32, 4, 128]), func=AF.Copy),
                 r=[pm.k], w=[Nk[qb][g]])

        def evac_win():
            P.op("act", lambda e: e.activation(out=self.pows[:], in_=r4(pow_[:])[:, :, 0:65], func=AF.Copy), r=[pow_.k], w=[self.pows.k])

        def combine(qb, g, pidx):
            pcs = self.pocs[pidx % 2]
            o, ok_ = ob[qb % 2]
            P.op("dve", lambda e: e.tensor_scalar(out=rd[:, 4:8], in0=pcs[:, :, 64], scalar1=1e-30, scalar2=None, op0=ALU.max), r=[pcs.k], w=[rdk])
            P.op("dve", lambda e: e.tensor_scalar(out=rd[:, 8:12], in0=r4(pos[:])[:, :, 64], scalar1=1e-30, scalar2=None, op0=ALU.max), r=[pos.k], w=[rdk])
            P.op("dve", lambda e: e.tensor_scalar(out=rd[:, 12:16], in0=self.pows[:, :, 64], scalar1=1e-30, scalar2=None, op0=ALU.max), r=[self.pows.k], w=[rdk])
            P.op("dve", lambda e: e.reciprocal(out=rd[:, 4:16], in_=rd[:, 4:16]), r=[rdk], w=[rdk])
            P.op("dve", lambda e: e.tensor_tensor(out=s3.rearrange("p (a b) -> p a b", a=3), in0=rd[:, 4:16].rearrange("p (a b) -> p a b", a=3),
                                                  in1=gat[:, qb, :].rearrange("p (a b) -> p a b", a=3)[:, :, 4 * g:4 * g + 4], op=ALU.mult), r=[rdk, gatk], w=[s3k])
            for h in range(4):
                ah = acc[:, h * 64:(h + 1) * 64]
                P.op("dve", lambda e: e.tensor_scalar(out=ah, in0=pcs[:, h, 0:64], scalar1=s3[:, h:h + 1], scalar2=None, op0=ALU.mult), r=[pcs.k, s3k], w=[acck])
                P.op("dve", lambda e: e.scalar_tensor_tensor(out=ah, in0=self.pows[:, h, 0:64], scalar=s3[:, 8 + h:9 + h], in1=ah, op0=ALU.mult, op1=ALU.add),
                     r=[self.pows.k, s3k, acck], w=[acck])
                P.op("dve", lambda e: e.scalar_tensor_tensor(out=o[:, g * 256 + h * 64:g * 256 + (h + 1) * 64], in0=pos[:, h * 128:h * 128 + 64], scalar=s3[:, 4 + h:5 + h], in1=ah,
                                                             op0=ALU.mult, op1=ALU.add), r=[pos.k, s3k, acck], w=[ok_])

        def o_transpose(qb):
            o, ok_ = ob[qb % 2]
            pT = self.bank(self.pool_misc)
            pTb = pT[:].bitcast(BF16)
            for c in range(NCH):
                P.op("pe", lambda e: e.transpose(out=pTb[:, c * 128:(c + 1) * 128], in_=o[:, c * 128:(c + 1) * 128], identity=self.ident_b[:]), r=[ok_, self.ident_b.k], w=[pT.k])
            P.op("act", lambda e: e.activation(out=oT[:, :, qb * 128:(qb + 1) * 128], in_=pTb.rearrange("p (a b) -> p a b", a=8), func=AF.Copy), r=[pT.k], w=[oTk])

        def cjob(pidx):
            qb, g = pairs[pidx]
            j = mk_job("c", qb, g, 0, True, True, pidx)
            j["post"].append(lambda: select_chain(qb, g, pidx))
            return j

        jobs.append(cjob(0))
        pending_T = []
        for pidx, (qb, g) in enumerate(pairs):
            qbg = 4 * n + qb
            k0 = max(0, qbg - 4)
            wj = [mk_job("w", qb, g, kt, kt == k0, kt == qbg, pidx) for kt in range(k0, qbg + 1)]
            for f in pending_T:
                wj[min(2, len(wj) - 1)]["post"].append(f)
            pending_T = []
            wj[-1]["post"].append(evac_win)
            jobs += wj
            if pidx + 1 < len(pairs):
                jobs.append(cjob(pidx + 1))
            sj = [mk_job("s", qb, g, kt, kt == 0, kt == qbg, pidx) for kt in range(qbg + 1)]
            sj[0]["pre"].append(lambda qb=qb, g=g: nsel_install(qb, g))
            sj[-1]["post"].append(lambda qb=qb, g=g, pidx=pidx: combine(qb, g, pidx))
            jobs += sj
            if g == 3:
                pending_T.append(lambda qb=qb: o_transpose(qb))
        LA = 2
        for j in range(min(LA, len(jobs))):
            for f in jobs[j]["pre"]:
                f()
            jobs[j]["st1"]()
        bg = list(bg)
        every = max(2, (len(jobs) - 8) // (len(bg) + 1)) if bg else 0
        for i in range(len(jobs)):
            if i + LA < len(jobs):
                for f in jobs[i + LA]["pre"]:
                    f()
                jobs[i + LA]["st1"]()
            jobs[i]["st2"]()
            for f in jobs[i]["post"]:
                f()
            if bg and i >= 4 and (i - 4) % every == 0:
                bg.pop(0)()
        for f in pending_T:
            f()
        for f in bg:
            f()

    def make_plan(self):
        plan = []
        ser = -1
        for b in range(self.nb):
            for layer in range(self.n_layers):
                if layer < 2:
                    ser += 1
                    plan += [(("a_in", layer, c), ser) for c in range(8)]
                    plan += [(("a_out", layer, m), ser) for m in range(8)]
                else:
                    if layer == 2:
                        ser += 1
                        plan += [(("kvk", i), ser) for i in range(4)] + [(("kvc", i), ser) for i in range(4)] + [(("kvv", i), ser) for i in range(2)]
                        plan += [(("cw2",), ser)]
                    ser += 1
                    jj = layer - 2
                    plan += [(("bg", jj), ser)] + [(("bq", jj, m), ser) for m in range(8)]
                    for n in range(NT):
                        if n + 1 < NT:
                            plan += [(("bg", jj), ser)] + [(("bq", jj, m), ser) for m in range(8)]
                        plan += [(("bo", jj, m), ser) for m in range(8)]
                ser += 1
                for t2 in range(2):
                    plan += [(("f_in", layer, c), ser) for c in range(FC)]
                    plan += [(("f_out", layer, m, hf), ser) for m in range(8) for hf in range(2)]
        return plan

    def build(self):
        P = self.P
        self.convc = self.sb("convc", [128, 8, 3], F32)
        self.hst = self.sb("hst", [128, 8], F32)
        self.prologue()
        self.plan_weights(self.make_plan())
        phase_no = {"a0": 0, "f0": 1, "a1": 2, "f1": 3, "kv": 4, "b0": 5, "f2": 6, "b1": 7, "f3": 8}
        for b in range(self.nb):
            def pre(tag):
                if b == 0:
                    self.convert_phase(phase_no[tag] + 1)
            if b == 0:
                self.convert_phase(0)
            self.mark("load%d" % b)
            self.load_x(b)
            for layer in range(self.n_layers):
                if layer < 2:
                    pre("a%d" % layer)
                    self.mark("a%d.%d" % (b, layer))
                    self.a_layer(layer)
                else:
                    if layer == 2:
                        pre("kv")
                        self.mark("kv%d" % b)
                        self.kv_phase()
                    pre("b%d" % (layer - 2))
                    self.mark("b%d.%d" % (b, layer))
                    self.b_layer(layer - 2)
                pre("f%d" % layer)
                self.mark("f%d.%d" % (b, layer))
                self.ffn_layer(layer)
            self.mark("store%d" % b)
            self.store_y(b)
        self.mark("end")
        P.wait_all("sp", self.out_trks)
        return self.nc


_CACHE = {}


def _prep_inputs(inputs):
    L = weight_layout()
    wall = pack_weights(inputs, L)
    vecs, bvecs = pack_vecs(inputs)
    cst = make_consts()
    return wall, vecs, bvecs, cst


def kernel(**inputs):
    inputs = {k: np.asarray(v) for k, v in inputs.items()}
    x = np.ascontiguousarray(inputs["x"], dtype=np.float32)
    B = x.shape[0]
    nb = B // NCORES
    wall, vecs, bvecs, cst = _prep_inputs(inputs)
    nc = Builder(nb).build()
    in_maps = []
    for c in range(NCORES):
        in_maps.append({"x": np.ascontiguousarray(x[c * nb:(c + 1) * nb]), "wall": wall, "vecs": vecs, "bvecs": bvecs, "cst": cst})
    res = run_bass_kernel_spmd(nc, in_maps, core_ids=list(range(NCORES)))
    out = np.concatenate([np.asarray(r["y"]).reshape(nb, S, D) for r in res.results], axis=0)
    return out.astype(np.float32)
```

```python
import numpy as np
import concourse.bass as bass
import concourse.mybir as mybir
from concourse.bass_utils import run_bass_kernel_spmd
from concourse.alu_op_type import AluOpType as ALU

F32 = mybir.dt.float32
BF16 = mybir.dt.bfloat16
AF = mybir.ActivationFunctionType

S = 2048
D = 1024
NCH = 8
TT = 512
NT = S // TT
FH = 2816
FC = 22
NCORES = 8
EPS = 1e-6
CONV_CH = 128 * 2048
SLOT = 2304
SCALE = 0.125
NEGM = -30000.0


def _kt(W, c0, width=128):
    K = W.shape[0]
    return np.ascontiguousarray(W[:, c0:c0 + width].reshape(K // 128, 128, width).transpose(1, 0, 2)).reshape(128, -1)


class WLayout:
    def __init__(self):
        self.off = {}
        self.free = {}
        self.n = 0

    def add(self, name, free):
        self.off[name] = self.n
        self.free[name] = free
        self.n += 128 * free

    def total(self):
        return ((self.n + CONV_CH - 1) // CONV_CH) * CONV_CH


def weight_layout():
    L = WLayout()
    L.phase_end = []

    def a(l):
        for c in range(8):
            L.add(("a_in", l, c), 2304)
        for m in range(8):
            L.add(("a_out", l, m), 1024)
        L.phase_end.append(L.n)

    def f(l):
        for c in range(FC):
            L.add(("f_in", l, c), 2048)
        for m in range(8):
            for hf in range(2):
                L.add(("f_out", l, m, hf), FH // 2)
        L.phase_end.append(L.n)

    def b(j):
        L.add(("bg", j), 384)
        for m in range(8):
            L.add(("bq", j, m), 1024)
        for m in range(8):
            L.add(("bo", j, m), 1024)
        L.phase_end.append(L.n)

    a(0)
    f(0)
    a(1)
    f(1)
    for i in range(4):
        L.add(("kvk", i), 1024)
    for i in range(4):
        L.add(("kvc", i), 1024)
    for i in range(2):
        L.add(("kvv", i), 2048)
    for i in range(2):
        L.add(("cw1", i), 8192)
    L.add(("cw2",), 256)
    L.phase_end.append(L.n)
    b(0)
    f(2)
    b(1)
    f(3)
    return L


def pack_weights(inp, L):
    out = np.zeros(L.total(), np.float32)

    def put(name, arr):
        arr = np.asarray(arr, np.float32).reshape(128, -1)
        assert arr.shape[1] == L.free[name], (name, arr.shape)
        out[L.off[name]:L.off[name] + arr.size] = arr.reshape(-1)

    for l in range(2):
        Win = inp["a_w_in"][l]
        for c in range(8):
            put(("a_in", l, c), np.concatenate(
                [_kt(Win, c * 128), _kt(Win, 1024 + c * 128), inp["a_gate_w"][l][0, c], inp["a_gate_w"][l][1, c]], axis=1))
        for m in range(8):
            put(("a_out", l, m), _kt(inp["a_w_out"][l], m * 128))
    for l in range(4):
        W = inp["f_w_in"][l]
        for c in range(FC):
            put(("f_in", l, c), np.concatenate([_kt(W, c * 128), _kt(W, FH + c * 128)], axis=1))
        for m in range(8):
            t = _kt(inp["f_w_out"][l], m * 128)
            for hf in range(2):
                put(("f_out", l, m, hf), t[:, hf * (FH // 2):(hf + 1) * (FH // 2)])
    kvw = inp["kv_w"]
    i = 0
    for jj in (2, 4):
        for cc in range(2):
            put(("kvk", i), _kt(kvw, jj * 256 + cc * 128))
            i += 1
    i = 0
    for jj in (0, 1):
        for cc in range(2):
            put(("kvc", i), _kt(kvw, jj * 256 + cc * 128))
            i += 1
    for i, jj in enumerate((3, 5)):
        put(("kvv", i), _kt(kvw, jj * 256, 256))
    for kv in range(2):
        t = np.zeros((128, 32, 256), np.float32)
        t[:64] = inp["cmp_w1"][kv].reshape(32, 64, 256).transpose(1, 0, 2)
        put(("cw1", kv), t)
    t = np.zeros((128, 2, 2, 64), np.float32)
    for kv in range(2):
        t[:, kv] = inp["cmp_w2"][kv].reshape(2, 128, 64).transpose(1, 0, 2)
    put(("cw2",), t)
    for j in range(2):
        put(("bg", j), _kt(inp["b_w_in"][j], 1024, 48))
        for m in range(8):
            put(("bq", j, m), _kt(inp["b_w_in"][j], m * 128))
        for m in range(8):
            put(("bo", j, m), _kt(inp["b_w_out"][j], m * 128))
    return out


VEC = {}
_nv = 0


def _vadd(name, n):
    global _nv
    VEC[name] = _nv
    _nv += n


for _l in range(2):
    _vadd(("a_norm", _l), 8)
    for _k in range(4):
        _vadd(("a_cw", _l, _k), 8)
    _vadd(("a_cb", _l), 8)
    _vadd(("a_gb", _l, 0), 8)
    _vadd(("a_gb", _l, 1), 8)
    _vadd(("a_lam", _l), 8)
for _l in range(4):
    _vadd(("f_norm", _l), 8)
_vadd(("kv_norm",), 8)
for _j in range(2):
    _vadd(("b_norm", _j), 8)
    _vadd(("q_norm", _j), 1)
for _i in range(3):
    _vadd(("k_norm", _i), 1)
for _i in range(2):
    _vadd(("c_b1", _i), 2)
    _vadd(("c_b2", _i), 1)
    _vadd(("c_pos", _i), 32)
NV = _nv
BV_GB = 0
BV_CB2V = 96
NBV = 160


def pack_vecs(inp):
    v = np.zeros((128, NV), np.float32)

    def fm(x):
        return np.asarray(x, np.float32).reshape(8, 128).T

    for l in range(2):
        v[:, VEC[("a_norm", l)]:][:, :8] = fm(inp["a_norm"][l])
        for k in range(4):
            v[:, VEC[("a_cw", l, k)]:][:, :8] = fm(inp["a_conv_w"][l][k])
        v[:, VEC[("a_cb", l)]:][:, :8] = fm(inp["a_conv_b"][l])
        v[:, VEC[("a_gb", l, 0)]:][:, :8] = fm(inp["a_gate_b"][l][0])
        v[:, VEC[("a_gb", l, 1)]:][:, :8] = fm(inp["a_gate_b"][l][1])
        v[:, VEC[("a_lam", l)]:][:, :8] = fm(inp["a_lambda"][l])
    for l in range(4):
        v[:, VEC[("f_norm", l)]:][:, :8] = fm(inp["f_norm"][l])
    v[:, VEC[("kv_norm",)]:][:, :8] = fm(inp["kv_norm"])
    for j in range(2):
        v[:, VEC[("b_norm", j)]:][:, :8] = fm(inp["b_norm"][j])
        v[:, VEC[("q_norm", j)]] = np.tile(np.asarray(inp["q_norm"][j], np.float32), 2)
    for i in range(3):
        v[:, VEC[("k_norm", i)]] = np.tile(np.asarray(inp["k_norm"][i], np.float32), 2)
    for i in range(2):
        v[:, VEC[("c_b1", i)]:][:, :2] = np.asarray(inp["cmp_b1"][i], np.float32).reshape(2, 128).T
        v[:, VEC[("c_b2", i)]] = np.tile(np.asarray(inp["cmp_b2"][i], np.float32), 2)
        v[:64, VEC[("c_pos", i)]:][:, :32] = np.asarray(inp["cmp_pos"][i], np.float32).T
    bv = np.zeros((128, NBV), np.float32)
    for j in range(2):
        bv[:, BV_GB + 48 * j: BV_GB + 48 * j + 48] = np.asarray(inp["b_gate_b"][j], np.float32)[None, :]
    bv[:, BV_CB2V:BV_CB2V + 64] = np.asarray(inp["cmp_b2"][1], np.float32)[None, :]
    return v, bv


CST = {}
_nc_ = 0


def _cadd(name, n):
    global _nc_
    CST[name] = _nc_
    _nc_ += n


_cadd("ident", 128)
_cadd("prot", 128)
_cadd("bones", 128)
_cadd("triL", 128)
_cadd("triU", 128)
_cadd("ovl", 32)
_cadd("cos", 2048)
_cadd("sin", 2048)
_cadd("cmask", 2048)
_cadd("eind", 2048)
_cadd("selb", 512)
NCST = _nc_


def make_consts():
    c = np.zeros((128, NCST), np.float32)
    p = np.arange(128)
    c[:, CST["ident"]:][:, :128] = np.eye(128)
    perm = (p // 64) * 64 + ((p % 64) + 32) % 64
    pr = np.zeros((128, 128), np.float32)
    pr[perm, p] = 1.0
    c[:, CST["prot"]:][:, :128] = pr
    c[:, CST["bones"]:][:, :128] = (p[:, None] // 64 == p[None, :] // 64)
    c[:, CST["triL"]:][:, :128] = (p[:, None] <= p[None, :])
    c[:, CST["triU"]:][:, :128] = (p[:, None] > p[None, :])
    cs = np.arange(127) * 16
    sl = np.arange(32) * 64
    ov = np.clip(np.minimum(cs[:, None] + 32, sl[None, :] + 64) - np.maximum(cs[:, None], sl[None, :]), 0, None) / 32.0
    c[:127, CST["ovl"]:][:, :32] = ov
    t = np.arange(2048, dtype=np.float64)
    fi = (p % 64) % 32
    freqs = (10000.0 ** (-(np.arange(32, dtype=np.float32) / np.float32(32)))).astype(np.float32)
    ang = (t[None, :].astype(np.float32) * freqs[fi][:, None]).astype(np.float32)
    c[:, CST["cos"]:][:, :2048] = np.cos(ang)
    sg = np.where((p % 64) < 32, -1.0, 1.0)[:, None]
    c[:, CST["sin"]:][:, :2048] = np.sin(ang) * sg
    cl = np.arange(127) * 16 + 31
    c[:127, CST["cmask"]:][:, :2048] = (cl[:, None] <= t[None, :])
    b = np.arange(32)
    c[64:96, CST["eind"]:][:, :2048] = (t[None, :].astype(np.int64) // 64 == b[:, None])
    sb = np.zeros((128, 16, 32), np.float32)
    for qb in range(16):
        tq = qb * 128 + p
        cur = (tq // 64)[:, None]
        causal = b[None, :] <= cur
        forced = (b[None, :] == 0) | (causal & (cur - b[None, :] < 2))
        sb[:, qb, :] = np.where(forced, 1e30, np.where(causal, 0.0, -1e30))
    c[:, CST["selb"]:][:, :512] = sb.reshape(128, 512)
    return c


class Trk:
    __slots__ = ("w", "r")

    def __init__(self):
        self.w = None
        self.r = {}


NDS = 24


class Prog:
    def __init__(self, nc):
        self.nc = nc
        self.eng = {"pe": nc.tensor, "act": nc.scalar, "dve": nc.vector, "pool": nc.gpsimd, "sp": nc.sync}
        self.sem = {e: nc.alloc_semaphore(name="s_" + e) for e in ("pe", "act", "dve", "pool")}
        self.cnt = {e: 0 for e in ("pe", "act", "dve", "pool")}
        self.seen = {e: {} for e in self.eng}
        self.dsem = [nc.alloc_semaphore(name="s_dma%d" % i) for i in range(NDS)]
        self.dn = 0
        self.ninst = 0

    def _semof(self, key):
        return self.sem[key] if isinstance(key, str) else self.dsem[key[1]]

    def _wait(self, e, key, val):
        if key == "pe" and e == "pe":
            return
        if self.seen[e].get(key, 0) >= val:
            return
        self.eng[e].wait_ge(self._semof(key), val)
        self.seen[e][key] = val

    def _deps(self, e, r, w):
        for t in r:
            if t.w is not None:
                self._wait(e, t.w[0], t.w[1])
        for t in w:
            if t.w is not None:
                self._wait(e, t.w[0], t.w[1])
            for k, v in t.r.items():
                self._wait(e, k, v)

    def op(self, e, fn, r=(), w=()):
        self._deps(e, r, w)
        inst = fn(self.eng[e])
        self.cnt[e] += 1
        self.ninst += 1
        inst.then_inc(self.sem[e], 1)
        v = self.cnt[e]
        for t in r:
            t.r[e] = v
        for t in w:
            t.w = (e, v)
            t.r = {}
        return inst

    def dma(self, q, out, in_, r=(), w=()):
        self._deps(q, r, w)
        i = self.dn % NDS
        gen = self.dn // NDS
        key = ("d", i)
        if gen > 0:
            self._wait(q, key, 16 * gen)
        inst = self.eng[q].dma_start(out=out, in_=in_)
        inst.then_inc(self.dsem[i], 16)
        self.dn += 1
        self.ninst += 1
        v = 16 * (gen + 1)
        for t in r:
            t.r[key] = v
        for t in w:
            t.w = (key, v)
            t.r = {}
        return (key, v)

    def wait_all(self, e, trks):
        self._deps(e, trks, trks)


class Buf:
    def __init__(self, t):
        self.t = t
        self.k = Trk()

    def __getitem__(self, idx):
        return self.t[idx]


class Builder:
    def __init__(self, nb, n_layers=4, debug_out=None):
        self.nb = nb
        self.n_layers = n_layers
        self.L = weight_layout()
        nc = bass.Bass("TRN2", target_bir_lowering=False)
        self.nc = nc
        self.P = Prog(nc)
        NW = self.L.total()
        self.x = nc.dram_tensor("x", [nb, S, D], F32, kind="ExternalInput").ap()
        self.wall = nc.dram_tensor("wall", [NW], F32, kind="ExternalInput").ap()
        self.vecs_d = nc.dram_tensor("vecs", [128, NV], F32, kind="ExternalInput").ap()
        self.bvecs_d = nc.dram_tensor("bvecs", [128, NBV], F32, kind="ExternalInput").ap()
        self.cst_d = nc.dram_tensor("cst", [128, NCST], F32, kind="ExternalInput").ap()
        self.y = nc.dram_tensor("y", [nb, S, D], F32, kind="ExternalOutput").ap()
        self.wbf = nc.dram_tensor("wbf", [NW], BF16, kind="Internal").ap()
        self.wchunk = [Trk() for _ in range(NW // CONV_CH)]
        self.out_trks = []
        self.marks = []

        def sb(name, shape, dt):
            return Buf(nc.alloc_sbuf_tensor(name, shape, dt))

        self.sb = sb
        self.hT = nc.alloc_sbuf_tensor("hT", [128, NCH, S], F32)
        self.hk = [[Trk() for _ in range(NT)] for _ in range(NCH)]
        self.vecs = sb("vecs_sb", [128, NV], F32)
        self.bvecs = sb("bvecs_sb", [128, NBV], F32)
        self.ident_f = sb("ident_f", [128, 128], F32)
        self.ident_b = sb("ident_b", [128, 128], BF16)
        self.ones_b = sb("ones_b", [128, 128], BF16)
        self.coef = sb("coef", [128, 2, 2, 8], F32)
        self.hgb = sb("hgb", [128, 2, 2, 8], F32)
        self.lnhalf = sb("lnhalf", [128, 2], F32)
        self.cos_b = sb("cos_b", [128, S], BF16)
        self.sin_b = sb("sin_b", [128, S], BF16)
        self.cmask_b = sb("cmask_b", [128, S], BF16)
        self.triL_b = sb("triL_b", [128, 128], BF16)
        self.triU_b = sb("triU_b", [128, 128], BF16)
        self.prot_b = sb("prot_b", [128, 128], BF16)
        self.bones_b = sb("bones_b", [128, 128], BF16)
        self.selb = sb("selb", [128, 16, 32], BF16)
        self.Qnext = sb("Qnext", [128, 8, 512], BF16)
        self.KC = sb("KC", [128, 4, 128], BF16)
        self.VC = sb("VC", [128, 4, 98], BF16)
        self.posb = sb("posb", [128, 2, 32], BF16)
        self.cbias = sb("cbias", [128, 2, 2], F32)
        self.pocs = [sb("pocs%d" % i, [128, 4, 97], F32) for i in range(2)]
        self.pows = sb("pows", [128, 4, 65], F32)
        self.KSd = nc.dram_tensor("KSd", [2, 64, 4, S], BF16, kind="Internal").ap()
        self.Vd = nc.dram_tensor("Vd", [2, 128, 16, 4, 66], BF16, kind="Internal").ap()
        self.ksd_k = [[[Trk() for _ in range(NT)] for _ in range(4)] for _ in range(2)]
        self.vd_k = [[Trk() for _ in range(NT)] for _ in range(2)]
        self.ps = [Buf(nc.alloc_psum_tensor("ps%d" % i, [128, 512], F32)) for i in range(8)]
        self.ps_rr = 0
        self.pool_misc = [0, 1, 2, 3, 4, 5, 6, 7]
        self.NSLOT = 3
        self.ring = [sb("wring%d" % i, [128, SLOT], BF16) for i in range(self.NSLOT)]
        self.perm_ids = set(id(b) for b in self.ring)
        self.PHB = 100352
        self.ph = nc.alloc_sbuf_tensor("phase", [128, self.PHB // 4], F32)
        self.phk = {}

    def bank(self, pool=None):
        if pool is None:
            b = self.ps[self.ps_rr % 8]
        else:
            b = self.ps[pool[self.ps_rr % len(pool)]]
        self.ps_rr += 1
        return b

    def vec(self, name, c=0):
        i = VEC[name] + c
        return self.vecs[:, i:i + 1]

    def phase_view(self, byte_off, shape, dt):
        esz = 4 if dt == F32 else 2
        n = int(np.prod(shape[1:]))
        assert byte_off % 4 == 0 and byte_off + n * esz <= self.PHB, (byte_off, shape)
        if dt == F32:
            ap = self.ph[:, byte_off // 4: byte_off // 4 + n]
        else:
            ap = self.ph[:].bitcast(BF16)[:, byte_off // 2: byte_off // 2 + n]
        if len(shape) == 2:
            return ap
        names = " ".join("a%d" % i for i in range(len(shape) - 1))
        kw = {"a%d" % i: shape[i + 1] for i in range(len(shape) - 1)}
        return ap.rearrange("p (%s) -> p %s" % (names, names), **kw)

    def mark(self, name):
        self.marks.append((name, dict(self.P.cnt)))

    def new_phase(self, trks):
        merged = {}
        for t in self.phase_cur:
            if t.w is not None:
                merged[t.w[0]] = max(merged.get(t.w[0], 0), t.w[1])
            for k, v in t.r.items():
                merged[k] = max(merged.get(k, 0), v)
        for t in trks:
            t.w = None
            t.r = dict(merged)
        self.phase_cur = list(trks)

    def plan_weights(self, entries):
        self.wplan = list(entries)
        self.wplan_i = 0
        self.wq = []
        self.free_perm = list(self.ring)
        self.free_ext = []
        self.inuse = None
        self.serial = -1

    def begin_phase(self, ext_slots):
        self._release()
        self.serial += 1
        self.free_ext = list(ext_slots)
        self.ext_ids = set(id(b) for b in ext_slots)

    def _release(self):
        if self.inuse is not None:
            slot, ser = self.inuse
            if id(slot) in self.perm_ids:
                self.free_perm.append(slot)
            elif ser == self.serial:
                self.free_ext.append(slot)
            self.inuse = None

    def _topup(self):
        while self.wplan_i < len(self.wplan):
            name, ser = self.wplan[self.wplan_i]
            if ser == self.serial and self.free_ext:
                slot = self.free_ext.pop(0)
            elif self.free_perm:
                slot = self.free_perm.pop(0)
            else:
                break
            self.wplan_i += 1
            off = self.L.off[name]
            free = self.L.free[name]
            src = self.wbf[off:off + 128 * free].rearrange("(p f) -> p f", p=128)
            c0 = off // CONV_CH
            c1 = (off + 128 * free - 1) // CONV_CH
            self.P.dma("sp", slot[:, 0:free], src, r=[self.wchunk[c] for c in range(c0, c1 + 1)], w=[slot.k])
            self.wq.append((name, slot, ser))

    def next_w(self, name, hold=False):
        self._release()
        self._topup()
        n, slot, ser = self.wq.pop(0)
        assert n == name and ser == self.serial, (n, name, ser, self.serial)
        if not hold:
            self.inuse = (slot, ser)
        return slot

    def release_slot(self, slot):
        if id(slot) in self.perm_ids:
            self.free_perm.append(slot)
        elif id(slot) in self.ext_ids:
            self.free_ext.append(slot)

    def ext_slots(self, byte_off):
        out = []
        o = byte_off
        while o + 2 * SLOT <= self.PHB:
            b = Buf(self.phase_view(o, [128, SLOT], BF16))
            out.append(b)
            o += 2 * SLOT
        return out

    def prologue(self):
        P = self.P
        nc = self.nc
        P.dma("sp", self.vecs[:], self.vecs_d, w=[self.vecs.k])
        P.dma("sp", self.bvecs[:], self.bvecs_d, w=[self.bvecs.k])
        P.dma("sp", self.ident_f[:], self.cst_d[:, CST["ident"]:CST["ident"] + 128], w=[self.ident_f.k])
        P.op("dve", lambda e: e.tensor_copy(out=self.ident_b[:], in_=self.ident_f[:]), r=[self.ident_f.k], w=[self.ident_b.k])
        P.op("pool", lambda e: e.memset(self.ones_b[:], 1.0), w=[self.ones_b.k])
        tmp = self.sb("coef_tmp", [128, 16], F32)
        for l in range(2):
            lam = self.vecs[:, VEC[("a_lam", l)]:VEC[("a_lam", l)] + 8]
            P.op("act", lambda e: e.activation(out=tmp[:, l * 8:l * 8 + 8], in_=lam, func=AF.Exp, scale=-1.0), r=[self.vecs.k], w=[tmp.k])
            P.op("act", lambda e: e.activation(out=tmp[:, l * 8:l * 8 + 8], in_=tmp[:, l * 8:l * 8 + 8], func=AF.Ln, bias=1.0), r=[tmp.k], w=[tmp.k])
            P.op("dve", lambda e: e.tensor_scalar(out=self.coef[:, l, 0, :], in0=tmp[:, l * 8:l * 8 + 8], scalar1=-4.0, scalar2=None, op0=ALU.mult), r=[tmp.k], w=[self.coef.k])
            P.op("dve", lambda e: e.tensor_scalar(out=self.coef[:, l, 1, :], in0=tmp[:, l * 8:l * 8 + 8], scalar1=-8.0, scalar2=None, op0=ALU.mult), r=[tmp.k], w=[self.coef.k])
            for kk in range(2):
                gbo = VEC[("a_gb", l, kk)]
                P.op("dve", lambda e: e.tensor_scalar(out=self.hgb[:, l, kk, :], in0=self.vecs[:, gbo:gbo + 8], scalar1=0.5, scalar2=None, op0=ALU.mult), r=[self.vecs.k], w=[self.hgb.k])
        P.op("pool", lambda e: e.memset(self.lnhalf[:], -0.6931471805599453), w=[self.lnhalf.k])
        stg = self.phase_view(0, [128, 2048], F32)
        stk = Trk()
        self.phase_cur = [stk]
        self.conv_done = 0

        def ctab(name, n, dst, dstk, eng):
            P.dma("sp", stg[:, 0:n], self.cst_d[:, CST[name]:CST[name] + n], w=[stk])
            if eng == "act":
                P.op("act", lambda e: e.activation(out=dst, in_=stg[:, 0:n], func=AF.Copy), r=[stk], w=[dstk])
            else:
                P.op(eng, lambda e: e.tensor_copy(out=dst, in_=stg[:, 0:n]), r=[stk], w=[dstk])

        ctab("cos", 2048, self.cos_b[:], self.cos_b.k, "dve")
        ctab("sin", 2048, self.sin_b[:], self.sin_b.k, "act")
        ctab("cmask", 2048, self.cmask_b[:], self.cmask_b.k, "dve")
        ctab("triL", 128, self.triL_b[:], self.triL_b.k, "dve")
        ctab("triU", 128, self.triU_b[:], self.triU_b.k, "dve")
        ctab("prot", 128, self.prot_b[:], self.prot_b.k, "dve")
        ctab("bones", 128, self.bones_b[:], self.bones_b.k, "dve")
        ctab("selb", 512, self.selb[:].rearrange("p a b -> p (a b)"), self.selb.k, "dve")
        P.op("pool", lambda e: e.memset(self.VC[:], 0.0), w=[self.VC.k])
        P.op("pool", lambda e: e.memset(self.KC[:], 0.0), w=[self.KC.k])
        P.op("pool", lambda e: e.memset(self.VC[:, :, 64:65], 1.0), w=[self.VC.k])
        P.dma("sp", stg[:, 0:32], self.cst_d[:, CST["ovl"]:CST["ovl"] + 32], w=[stk])
        for g in range(4):
            P.op("dve", lambda e: e.tensor_copy(out=self.VC[:, g, 65:97], in_=stg[:, 0:32]), r=[stk], w=[self.VC.k])
        for kv in range(2):
            o0 = VEC[("c_pos", kv)]
            P.op("dve", lambda e: e.tensor_copy(out=self.posb[:, kv, :], in_=self.vecs[:, o0:o0 + 32]), r=[self.vecs.k], w=[self.posb.k])
        self.conv_done = 0

    def convert_upto(self, nchunks):
        nchunks = min(nchunks, len(self.wchunk))
        for i in range(self.conv_done, nchunks):
            src = self.wall[i * CONV_CH:(i + 1) * CONV_CH].rearrange("(p f) -> p f", p=128)
            dst = self.wbf[i * CONV_CH:(i + 1) * CONV_CH].rearrange("(p f) -> p f", p=128)
            self.P.dma("pool", dst, src, w=[self.wchunk[i]])
        self.conv_done = max(self.conv_done, nchunks)

    def convert_phase(self, ph):
        ends = self.L.phase_end
        ph = min(ph, len(ends) - 1)
        self.convert_upto((ends[ph] + CONV_CH - 1) // CONV_CH)

    def load_x(self, b):
        P = self.P
        xin = [(self.phase_view(j * 4096, [128, 1024], F32), Trk()) for j in range(4)]
        self.new_phase([k for _, k in xin])
        for n in range(NT):
            for j in range(4):
                t0 = n * TT + j * 128
                P.dma("sp", xin[j][0], self.x[b, t0:t0 + 128, :], w=[xin[j][1]])
            for c in range(NCH):
                pb = self.bank()
                for j in range(4):
                    P.op("pe", lambda e: e.transpose(out=pb[:, j * 128:(j + 1) * 128], in_=xin[j][0][:, c * 128:(c + 1) * 128], identity=self.ident_f[:]),
                         r=[xin[j][1], self.ident_f.k], w=[pb.k])
                eng = "act" if c % 2 == 0 else "dve"
                dst = self.hT[:, c, n * TT:(n + 1) * TT]
                if eng == "act":
                    P.op("act", lambda e: e.activation(out=dst, in_=pb[:], func=AF.Copy), r=[pb.k], w=[self.hk[c][n]])
                else:
                    P.op("dve", lambda e: e.tensor_copy(out=dst, in_=pb[:]), r=[pb.k], w=[self.hk[c][n]])

    def store_y(self, b):
        P = self.P
        yo = [(self.phase_view(j * 4096, [128, 1024], F32), Trk()) for j in range(4)]
        self.new_phase([k for _, k in yo])
        for n in range(NT):
            for j in range(4):
                t0 = n * TT + j * 128
                for half in range(2):
                    pb = self.bank()
                    for cc in range(4):
                        c = half * 4 + cc
                        P.op("pe", lambda e: e.transpose(out=pb[:, cc * 128:(cc + 1) * 128], in_=self.hT[:, c, t0:t0 + 128], identity=self.ident_f[:]),
                             r=[self.hk[c][n], self.ident_f.k], w=[pb.k])
                    dst = yo[j][0][:, half * 512:(half + 1) * 512]
                    wl = [yo[j][1]]
                    if half == 0:
                        P.op("act", lambda e: e.activation(out=dst, in_=pb[:], func=AF.Copy), r=[pb.k], w=wl)
                    else:
                        P.op("dve", lambda e: e.tensor_copy(out=dst, in_=pb[:]), r=[pb.k], w=wl)
                ot = Trk()
                P.dma("sp", self.y[b, t0:t0 + 128, :], yo[j][0], r=[yo[j][1]], w=[ot])
                self.out_trks.append(ot)

    def norm_tile(self, n, gname, u_ap, u_k, sq_bufs, rstd_buf, pool=None):
        P = self.P
        pb = self.bank(pool)
        for c in range(NCH):
            sq, sk = sq_bufs[c % len(sq_bufs)]
            hsl = self.hT[:, c, n * TT:(n + 1) * TT]
            P.op("act", lambda e: e.activation(out=sq, in_=hsl, func=AF.Square), r=[self.hk[c][n]], w=[sk])
            P.op("pe", lambda e: e.matmul(pb[:], lhsT=self.ones_b[:], rhs=sq, start=(c == 0), stop=(c == NCH - 1)),
                 r=[sk, self.ones_b.k], w=[pb.k])
        rs, rk = rstd_buf
        P.op("act", lambda e: e.activation(out=rs, in_=pb[:], func=AF.Ln, scale=1.0 / D, bias=EPS), r=[pb.k], w=[rk])
        P.op("act", lambda e: e.activation(out=rs, in_=rs, func=AF.Exp, scale=-0.5), r=[rk], w=[rk])
        for c in range(NCH):
            hsl = self.hT[:, c, n * TT:(n + 1) * TT]
            g = self.vec(gname, c)
            P.op("dve", lambda e: e.scalar_tensor_tensor(out=u_ap[:, c, :], in0=hsl, scalar=g, in1=rs, op0=ALU.mult, op1=ALU.mult),
                 r=[self.hk[c][n], rk, self.vecs.k], w=[u_k])

    def a_phase_setup(self):
        v = self.phase_view
        A = {}
        A["u"] = [(v(n * 8192, [128, 8, 512], BF16), Trk()) for n in range(NT)]
        A["m"] = [(v(32768 + n * 8192, [128, 8, 512], BF16), Trk()) for n in range(NT)]
        A["sq"] = [(v(65536 + i * 1024, [128, 512], BF16), Trk()) for i in range(2)]
        A["rstd"] = (v(67584, [128, 512], F32), Trk())
        A["xp"] = [(v(69632 + i * 2080, [128, 516], F32), Trk()) for i in range(2)]
        base = 73792
        s1 = []
        for i in range(3):
            o = base + i * 4096
            xr = (v(o + 2048, [128, 512], F32), Trk())
            s1.append({"y": (v(o, [128, 512], BF16), Trk()), "xrb": (v(o + 1024, [128, 512], BF16), Trk()), "xr": xr, "hr": xr})
        s2 = []
        for i in range(2):
            o = base + 12288 + i * 6144
            rr = (v(o, [128, 512], F32), Trk())
            s2.append({"r": rr, "a2": rr, "i": (v(o + 2048, [128, 512], F32), Trk()), "a": (v(o + 4096, [128, 512], F32), Trk())})
        sets = s1 + s2
        A["s1"] = s1
        A["s2"] = s2
        A["s1_extra"] = {"y": (v(65536, [128, 512], BF16), Trk()), "xrb": (v(66560, [128, 512], BF16), Trk())}
        xr4 = (v(67584, [128, 512], F32), Trk())
        A["s1_extra"]["xr"] = xr4
        A["s1_extra"]["hr"] = xr4
        A["sets"] = sets
        trks = [k for _, k in A["u"]] + [k for _, k in A["m"]] + [A["rstd"][1]] + [k for _, k in A["sq"]] + [k for _, k in A["xp"]]
        for st in sets:
            trks += [k for _, k in st.values()]
        self.new_phase(list(dict((id(t), t) for t in trks).values()))
        self.begin_phase([])
        self.A = A

    def a_layer(self, l):
        P = self.P
        self.a_phase_setup()
        A = self.A
        P.op("pool", lambda e: e.memset(self.convc[:], 0.0), w=[self.convc.k])
        P.op("pool", lambda e: e.memset(self.hst[:], 0.0), w=[self.hst.k])
        for n in range(NT):
            self.norm_tile(n, ("a_norm", l), A["u"][n][0], A["u"][n][1], A["sq"], A["rstd"])
        merged = {}
        for t in [k for _, k in A["sq"]] + [A["rstd"][1]]:
            if t.w is not None:
                merged[t.w[0]] = max(merged.get(t.w[0], 0), t.w[1])
            for kk_, vv_ in t.r.items():
                merged[kk_] = max(merged.get(kk_, 0), vv_)
        ex = A["s1_extra"]
        for _, k in ex.values():
            k.w = None
            k.r = dict(merged)
            if k not in self.phase_cur:
                self.phase_cur.append(k)
        S1 = A["s1"] + [ex]
        items = [(c, n) for c in range(NCH) for n in range(NT)]
        wslot = {}

        def stage1(it):
            c, n = items[it]
            if n == 0:
                wslot[c] = self.next_w(("a_in", l, c), hold=True)
            w = wslot[c]
            u, uk = A["u"][n]
            pg = self.bank()
            pr = self.bank()
            for k in range(NCH):
                P.op("pe", lambda e: e.matmul(pg[:], lhsT=w[:, k * 128:(k + 1) * 128], rhs=u[:, k, :], start=(k == 0), stop=(k == NCH - 1)),
                     r=[w.k, uk], w=[pg.k])
            for k in range(NCH):
                P.op("pe", lambda e: e.matmul(pr[:], lhsT=w[:, 1024 + k * 128:1024 + (k + 1) * 128], rhs=u[:, k, :], start=(k == 0), stop=(k == NCH - 1)),
                     r=[w.k, uk], w=[pr.k])
            st = S1[it % 4]
            xp, xpk = A["xp"][it % 2]
            xpn, xpnk = A["xp"][(it + 1) % 2]
            y, yk = st["y"]
            xr, xrk = st["xr"]
            xrb, xrbk = st["xrb"]
            P.op("act", lambda e: e.activation(out=y, in_=pg[:], func=AF.Gelu_apprx_tanh), r=[pg.k], w=[yk])
            if n == 0:
                P.op("pool", lambda e: e.memset(xp[:, 0:3], 0.0), w=[xpk])
            P.op("act", lambda e: e.activation(out=xp[:, 3:515], in_=pr[:], func=AF.Copy), r=[pr.k], w=[xpk])
            if n < NT - 1:
                P.op("pool", lambda e: e.tensor_copy(out=xpn[:, 0:3], in_=xp[:, 512:515]), r=[xpk], w=[xpnk])
            P.op("dve", lambda e: e.tensor_scalar(out=xr, in0=xp[:, 0:512], scalar1=self.vec(("a_cw", l, 0), c), scalar2=self.vec(("a_cb", l), c),
                                                  op0=ALU.mult, op1=ALU.add), r=[xpk, self.vecs.k], w=[xrk])
            for kk in range(1, 4):
                P.op("dve", lambda e: e.scalar_tensor_tensor(out=xr, in0=xp[:, kk:kk + 512], scalar=self.vec(("a_cw", l, kk), c), in1=xr,
                                                             op0=ALU.mult, op1=ALU.add), r=[xpk, xrk, self.vecs.k], w=[xrk])
            P.op("dve", lambda e: e.tensor_copy(out=xrb, in_=xr), r=[xrk], w=[xrbk])

        def stage2(it):
            c, n = items[it]
            w = wslot[c]
            m, mk = A["m"][n]
            st = S1[it % 4]
            y, yk = st["y"]
            xr, xrk = st["xr"]
            xrb, xrbk = st["xrb"]
            hr, hrk = st["hr"]
            hprev, hprevk = S1[(it - 1) % 4]["hr"]
            t2 = A["s2"][it % 2]
            rr, rrk = t2["r"]
            ii, iik = t2["i"]
            aa, aak = t2["a"]
            p1 = self.bank()
            p2 = self.bank()
            P.op("pe", lambda e: e.matmul(p1[:], lhsT=w[:, 2048:2176], rhs=xrb, start=True, stop=True), r=[w.k, xrbk], w=[p1.k])
            P.op("pe", lambda e: e.matmul(p2[:], lhsT=w[:, 2176:2304], rhs=xrb, start=True, stop=True), r=[w.k, xrbk], w=[p2.k])
            if n == NT - 1:
                self.release_slot(w)
            P.op("act", lambda e: e.activation(out=rr, in_=p1[:], func=AF.Tanh, scale=0.5, bias=self.hgb[:, l, 0, c:c + 1]), r=[p1.k, self.hgb.k], w=[rrk])
            P.op("act", lambda e: e.activation(out=ii, in_=p2[:], func=AF.Tanh, scale=0.5, bias=self.hgb[:, l, 1, c:c + 1]), r=[p2.k, self.hgb.k], w=[iik])
            P.op("act", lambda e: e.activation(out=aa, in_=rr, func=AF.Exp, scale=self.coef[:, l, 0, c:c + 1], bias=self.coef[:, l, 0, c:c + 1]), r=[rrk, self.coef.k], w=[aak])
            P.op("act", lambda e: e.activation(out=rr, in_=rr, func=AF.Exp, scale=self.coef[:, l, 1, c:c + 1], bias=self.coef[:, l, 1, c:c + 1]), r=[rrk, self.coef.k], w=[rrk])
            P.op("act", lambda e: e.activation(out=rr, in_=rr, func=AF.Ln, scale=-0.999999, bias=1.0), r=[rrk], w=[rrk])
            P.op("act", lambda e: e.activation(out=rr, in_=rr, func=AF.Exp, scale=0.5, bias=self.lnhalf[:, 0:1]), r=[rrk, self.lnhalf.k], w=[rrk])
            P.op("dve", lambda e: e.scalar_tensor_tensor(out=ii, in0=ii, scalar=1.0, in1=xr, op0=ALU.add, op1=ALU.mult), r=[iik, xrk], w=[iik])
            P.op("dve", lambda e: e.tensor_tensor(out=ii, in0=ii, in1=rr, op=ALU.mult), r=[iik, rrk], w=[iik])
            if n == 0:
                P.op("dve", lambda e: e.tensor_tensor_scan(out=hr, data0=aa, data1=ii, initial=0.0, op0=ALU.mult, op1=ALU.add), r=[aak, iik], w=[hrk])
            else:
                P.op("dve", lambda e: e.tensor_tensor_scan(out=hr, data0=aa, data1=ii, initial=hprev[:, 511:512], op0=ALU.mult, op1=ALU.add),
                     r=[aak, iik, hprevk], w=[hrk])
            P.op("pool", lambda e: e.tensor_tensor(out=m[:, c, :], in0=hr, in1=y, op=ALU.mult), r=[hrk, yk], w=[mk])

        LA = 2
        for it in range(min(LA, len(items))):
            stage1(it)
        for it in range(len(items)):
            if it + LA < len(items):
                stage1(it + LA)
            stage2(it)
        for mm in range(NCH):
            w = self.next_w(("a_out", l, mm))
            for n in range(NT):
                m, mk = A["m"][n]
                po = self.bank()
                for k in range(NCH):
                    P.op("pe", lambda e: e.matmul(po[:], lhsT=w[:, k * 128:(k + 1) * 128], rhs=m[:, k, :], start=(k == 0), stop=(k == NCH - 1)),
                         r=[w.k, mk], w=[po.k])
                hsl = self.hT[:, mm, n * TT:(n + 1) * TT]
                P.op("dve", lambda e: e.tensor_tensor(out=hsl, in0=po[:], in1=hsl, op=ALU.add), r=[po.k, self.hk[mm][n]], w=[self.hk[mm][n]])

    def ffn_phase_setup(self):
        v = self.phase_view
        Fz = {}
        Fz["u"] = [(v(s * 8192, [128, 8, 512], BF16), Trk()) for s in range(2)]
        Fz["act"] = [[(v(16384 + (c * 2 + s) * 1024, [128, 512], BF16), Trk()) for s in range(2)] for c in range(FC)]
        Fz["sq"] = [(v(61440 + i * 1024, [128, 512], BF16), Trk()) for i in range(2)]
        Fz["rstd"] = (v(63488, [128, 512], F32), Trk())
        Fz["sg"] = [(v(65536 + i * 1024, [128, 512], BF16), Trk()) for i in range(4)]
        ext = self.ext_slots(69632)
        trks = [k for _, k in Fz["u"]] + [k for row in Fz["act"] for _, k in row] + [k for _, k in Fz["sq"]] + [Fz["rstd"][1]] + [k for _, k in Fz["sg"]] + [b.k for b in ext]
        self.new_phase(trks)
        self.begin_phase(ext)
        self.F = Fz

    def ffn_tile(self, L, t2):
        P = self.P
        Fz = self.F
        if t2 == 0:
            for s in range(2):
                self.norm_tile(2 * t2 + s, ("f_norm", L), Fz["u"][s][0], Fz["u"][s][1], Fz["sq"], Fz["rstd"])
        nsg = 0
        for c in range(FC):
            w = self.next_w(("f_in", L, c))
            for s in range(2):
                u, uk = Fz["u"][s]
                pg = self.bank()
                pu = self.bank()
                for k in range(NCH):
                    P.op("pe", lambda e: e.matmul(pg[:], lhsT=w[:, k * 128:(k + 1) * 128], rhs=u[:, k, :], start=(k == 0), stop=(k == NCH - 1)),
                         r=[w.k, uk], w=[pg.k])
                for k in range(NCH):
                    P.op("pe", lambda e: e.matmul(pu[:], lhsT=w[:, 1024 + k * 128:1024 + (k + 1) * 128], rhs=u[:, k, :], start=(k == 0), stop=(k == NCH - 1)),
                         r=[w.k, uk], w=[pu.k])
                sg, sgk = Fz["sg"][nsg % 4]
                nsg += 1
                a, ak = Fz["act"][c][s]
                P.op("act", lambda e: e.activation(out=sg, in_=pg[:], func=AF.Silu), r=[pg.k], w=[sgk])
                P.op("dve", lambda e: e.tensor_tensor(out=a, in0=sg, in1=pu[:], op=ALU.mult), r=[sgk, pu.k], w=[ak])
        HC = FC // 2
        for mm in range(NCH):
            pos_ = [self.bank(), self.bank()]
            for hf in range(2):
                w = self.next_w(("f_out", L, mm, hf))
                for s in range(2):
                    po = pos_[s]
                    for cc in range(HC):
                        c = hf * HC + cc
                        a, ak = Fz["act"][c][s]
                        P.op("pe", lambda e: e.matmul(po[:], lhsT=w[:, cc * 128:(cc + 1) * 128], rhs=a, start=(c == 0), stop=(c == FC - 1)),
                             r=[w.k, ak], w=[po.k])
            for s in range(2):
                n = 2 * t2 + s
                po = pos_[s]
                hsl = self.hT[:, mm, n * TT:(n + 1) * TT]
                P.op("dve", lambda e: e.tensor_tensor(out=hsl, in0=po[:], in1=hsl, op=ALU.add), r=[po.k, self.hk[mm][n]], w=[self.hk[mm][n]])
            if t2 == 0 and mm in (2, 5):
                s_ = 0 if mm == 2 else 1
                self.norm_tile(2 + s_, ("f_norm", L), Fz["u"][s_][0], Fz["u"][s_][1], Fz["sq"], Fz["rstd"])

    def ffn_layer(self, L):
        self.ffn_phase_setup()
        for t2 in range(2):
            self.ffn_tile(L, t2)

    def headnorm_rope(self, pk, R, C, gvec, cos_ap, sin_ap, T, bias=None):
        P = self.P
        sq, sqk = T["sq"]
        rs, rsk = T["rs"]
        qn, qnk = T["qn"]
        t1, t1k = T["t1"]
        t2, t2k = T["t2"]
        if bias is not None:
            xf, xfk = T["xf"]
            P.op("act", lambda e: e.activation(out=xf[0:R, 0:C], in_=pk[0:R, 0:C], func=AF.Identity, bias=bias), r=[pk.k, self.vecs.k], w=[xfk])
            src, srck = xf[0:R, 0:C], xfk
        else:
            src, srck = pk[0:R, 0:C], pk.k
        P.op("act", lambda e: e.activation(out=sq[0:R, 0:C], in_=src, func=AF.Square), r=[srck], w=[sqk])
        pss = self.bank(self.pool_misc)
        P.op("pe", lambda e: e.matmul(pss[0:R, 0:C], lhsT=self.bones_b[0:R, 0:R], rhs=sq[0:R, 0:C], start=True, stop=True), r=[sqk, self.bones_b.k], w=[pss.k])
        P.op("act", lambda e: e.activation(out=rs[0:R, 0:C], in_=pss[0:R, 0:C], func=AF.Ln, scale=1.0 / 64.0, bias=EPS), r=[pss.k], w=[rsk])
        P.op("act", lambda e: e.activation(out=rs[0:R, 0:C], in_=rs[0:R, 0:C], func=AF.Exp, scale=-0.5), r=[rsk], w=[rsk])
        P.op("dve", lambda e: e.scalar_tensor_tensor(out=qn[0:R, 0:C], in0=src, scalar=gvec, in1=rs[0:R, 0:C], op0=ALU.mult, op1=ALU.mult),
             r=[srck, rsk, self.vecs.k], w=[qnk])
        prt = self.bank(self.pool_misc)
        P.op("pe", lambda e: e.matmul(prt[0:R, 0:C], lhsT=self.prot_b[0:R, 0:R], rhs=qn[0:R, 0:C], start=True, stop=True), r=[qnk, self.prot_b.k], w=[prt.k])
        P.op("pool", lambda e: e.tensor_tensor(out=t1[0:R, 0:C], in0=qn[0:R, 0:C], in1=cos_ap, op=ALU.mult), r=[qnk, self.cos_b.k], w=[t1k])
        P.op("dve", lambda e: e.tensor_tensor(out=t2[0:R, 0:C], in0=prt[0:R, 0:C], in1=sin_ap, op=ALU.mult), r=[prt.k, self.sin_b.k], w=[t2k])

    def kv_phase(self):
        P = self.P
        v = self.phase_view
        U = [(v(n * 8192, [128, 8, 512], BF16), Trk()) for n in range(NT)]
        sqn = [(v(32768 + i * 1024, [128, 512], BF16), Trk()) for i in range(2)]
        rstd = (v(34816, [128, 512], F32), Trk())
        kcT, kcTk = v(36864, [128, 8, S], BF16), [Trk() for _ in range(8)]
        cw1, cw1k = v(0, [128, 2, 32, 256], BF16), Trk()
        T = {"sq": (v(69632, [128, 512], BF16), Trk()), "rs": (v(70656, [128, 512], F32), Trk()), "qn": (v(72704, [128, 512], BF16), Trk()),
             "t1": (v(73728, [128, 512], F32), Trk()), "t2": (v(75776, [128, 512], F32), Trk()), "xf": (v(77824, [128, 512], F32), Trk())}
        kout = [(v(79872 + i * 1024, [128, 512], BF16), Trk()) for i in range(2)]
        vst = [(v(81920 + i * 528, [128, 4, 66], BF16), Trk()) for i in range(2)]
        hid = [(v(83008 + i * 512, [128, 2, 128], BF16), Trk()) for i in range(2)]
        ext = self.ext_slots(84032)
        trks = [k for _, k in U] + [rstd[1], cw1k] + [k for _, k in sqn] + kcTk + [k for _, k in T.values()] + [k for _, k in kout] + [k for _, k in vst] \
            + [k for _, k in hid] + [b.k for b in ext]
        self.new_phase(trks)
        self.begin_phase(ext)
        self.pool_misc = [0, 1, 2, 3, 4, 5, 6, 7]
        for i in range(2):
            P.op("pool", lambda e: e.memset(vst[i][0][:, :, 64:66], 1.0), w=[vst[i][1]])
        for n in range(NT):
            self.norm_tile(n, ("kv_norm",), U[n][0], U[n][1], sqn, rstd)
        nko = 0
        nvs = 0
        for i in range(4):
            which, cc = i // 2, i % 2
            w = self.next_w(("kvk", i))
            for n in range(NT):
                tsl = slice(n * TT, (n + 1) * TT)
                u, uk = U[n]
                pk = self.bank()
                for k in range(NCH):
                    P.op("pe", lambda e: e.matmul(pk[:], lhsT=w[:, k * 128:(k + 1) * 128], rhs=u[:, k, :], start=(k == 0), stop=(k == NCH - 1)), r=[w.k, uk], w=[pk.k])
                self.headnorm_rope(pk, 128, 512, self.vec(("k_norm", 1 + which)), self.cos_b[:, tsl], self.sin_b[:, tsl], T)
                ko, kok = kout[nko % 2]
                nko += 1
                P.op("dve", lambda e: e.tensor_tensor(out=ko, in0=T["t1"][0], in1=T["t2"][0], op=ALU.add), r=[T["t1"][1], T["t2"][1]], w=[kok])
                for hh in range(2):
                    P.dma("sp", self.KSd[which, :, 2 * cc + hh, tsl], ko[hh * 64:(hh + 1) * 64, :], r=[kok], w=[self.ksd_k[which][2 * cc + hh][n]])
        for i in range(4):
            sel, cc = i // 2, i % 2
            w = self.next_w(("kvc", i))
            for n in range(NT):
                tsl = slice(n * TT, (n + 1) * TT)
                u, uk = U[n]
                for gg in range(2):
                    pc = self.bank()
                    for k in range(NCH):
                        P.op("pe", lambda e: e.matmul(pc[0:64, :], lhsT=w[:, k * 128 + gg * 64:k * 128 + gg * 64 + 64], rhs=u[:, k, :], start=(k == 0), stop=(k == NCH - 1)),
                             r=[w.k, uk], w=[pc.k])
                    idx = sel * 4 + 2 * cc + gg
                    P.op("act", lambda e: e.activation(out=kcT[0:64, idx, tsl], in_=pc[0:64, :], func=AF.Copy), r=[pc.k], w=[kcTk[idx]])
        for i in range(2):
            w = self.next_w(("kvv", i))
            for n in range(NT):
                u, uk = U[n]
                for jb in range(4):
                    pv = self.bank()
                    for k in range(NCH):
                        P.op("pe", lambda e: e.matmul(pv[:, 0:256], lhsT=u[:, k, jb * 128:(jb + 1) * 128], rhs=w[:, k * 256:(k + 1) * 256], start=(k == 0), stop=(k == NCH - 1)),
                             r=[w.k, uk], w=[pv.k])
                    vs, vsk = vst[nvs % 2]
                    nvs += 1
                    P.op("act", lambda e: e.activation(out=vs[:, :, 0:64], in_=pv[:, 0:256].rearrange("p (g d) -> p g d", g=4), func=AF.Copy), r=[pv.k], w=[vsk])
                    P.dma("sp", self.Vd[i, :, 4 * n + jb, :, :], vs, r=[vsk], w=[self.vd_k[i][n]])
        for kv in range(2):
            off = self.L.off[("cw1", kv)]
            src = self.wbf[off:off + 128 * 8192].rearrange("(p f) -> p f", p=128)
            c0, c1 = off // CONV_CH, (off + 128 * 8192 - 1) // CONV_CH
            P.dma("sp", cw1[0:64, kv].rearrange("p a b -> p (a b)"), src[0:64, :], r=[self.wchunk[c] for c in range(c0, c1 + 1)], w=[cw1k] + [k for _, k in U])
        w2 = self.next_w(("cw2",))
        for sel in range(2):
            pcv = self.bank()
            for cc in range(2):
                for l in range(32):
                    P.op("pe", lambda e: e.matmul(pcv[:, cc:cc + 1], lhsT=cw1[0:64, sel, l, cc * 128:(cc + 1) * 128], rhs=self.posb[0:64, sel, l:l + 1],
                                                  start=(cc == 0 and l == 0), stop=(cc == 1 and l == 31)), r=[cw1k, self.posb.k], w=[pcv.k])
            b1o = VEC[("c_b1", sel)]
            P.op("dve", lambda e: e.tensor_tensor(out=self.cbias[:, sel, :], in0=pcv[:, 0:2], in1=self.vecs[:, b1o:b1o + 2], op=ALU.add), r=[pcv.k, self.vecs.k], w=[self.cbias.k])
        nh = 0
        for sel in range(2):
            for g in range(4):
                hd, hdk = hid[nh % 2]
                nh += 1
                for cc in range(2):
                    ph = self.bank()
                    for l in range(32):
                        P.op("pe", lambda e: e.matmul(ph[:, 0:127], lhsT=cw1[0:64, sel, l, cc * 128:(cc + 1) * 128], rhs=kcT[0:64, sel * 4 + g, l:l + 16 * 126 + 1:16],
                                                      start=(l == 0), stop=(l == 31)), r=[cw1k, kcTk[sel * 4 + g]], w=[ph.k])
                    P.op("act", lambda e: e.activation(out=hd[:, cc, 0:127], in_=ph[:, 0:127], func=AF.Gelu_apprx_tanh, bias=self.cbias[:, sel, cc:cc + 1]),
                         r=[ph.k, self.cbias.k], w=[hdk])
                if sel == 0:
                    pk = self.bank()
                    for cc in range(2):
                        P.op("pe", lambda e: e.matmul(pk[0:64, 0:127], lhsT=w2[:, (0 * 2 + cc) * 64:(0 * 2 + cc) * 64 + 64], rhs=hd[:, cc, 0:127], start=(cc == 0), stop=(cc == 1)),
                             r=[w2.k, hdk], w=[pk.k])
                    self.headnorm_rope(pk, 64, 127, self.vecs[0:64, VEC[("k_norm", 0)]:VEC[("k_norm", 0)] + 1],
                                       self.cos_b[0:64, 31:31 + 16 * 126 + 1:16], self.sin_b[0:64, 31:31 + 16 * 126 + 1:16], T,
                                       bias=self.vecs[0:64, VEC[("c_b2", 0)]:VEC[("c_b2", 0)] + 1])
                    P.op("dve", lambda e: e.tensor_tensor(out=self.KC[0:64, g, 0:127], in0=T["t1"][0][0:64, 0:127], in1=T["t2"][0][0:64, 0:127], op=ALU.add),
                         r=[T["t1"][1], T["t2"][1]], w=[self.KC.k])
                else:
                    pv = self.bank()
                    for cc in range(2):
                        P.op("pe", lambda e: e.matmul(pv[0:127, 0:64], lhsT=hd[:, cc, 0:127], rhs=w2[:, (1 * 2 + cc) * 64:(1 * 2 + cc) * 64 + 64], start=(cc == 0), stop=(cc == 1)),
                             r=[w2.k, hdk], w=[pv.k])
                    P.op("dve", lambda e: e.tensor_tensor(out=self.VC[0:127, g, 0:64], in0=pv[0:127, 0:64], in1=self.bvecs[0:127, BV_CB2V:BV_CB2V + 64], op=ALU.add),
                         r=[pv.k, self.bvecs.k], w=[self.VC.k])

    def b_layer(self, j):
        P = self.P
        v = self.phase_view
        KS, KSk = v(0, [128, 4, S], BF16), [Trk() for _ in range(4)]
        KW, KWk = v(16384, [128, 4, S], BF16), [Trk() for _ in range(4)]
        VS, VSk = v(32768, [128, 16, 4, 66], BF16), Trk()
        VW, VWk = v(41216, [128, 16, 4, 66], BF16), Trk()
        u, uk = v(49664, [128, 8, 512], BF16), Trk()
        Q, Qk = v(57856, [128, 4, 16, 128], BF16), [Trk() for _ in range(4)]
        Nk = [[Trk() for _ in range(4)] for _ in range(4)]
        oT, oTk = v(74240, [128, 8, 512], BF16), Trk()
        ob = [(v(82432 + i * 2048, [128, 1024], BF16), Trk()) for i in range(2)]
        PT = [(v(86528 + i * 1024, [128, 512], BF16), Trk()) for i in range(4)]
        sqn = [(v(90624 + i * 1024, [128, 512], BF16), Trk()) for i in range(2)]
        rstd = (v(92672, [128, 512], F32), Trk())
        gat, gatk = v(94720, [128, 4, 48], F32), Trk()
        T = {"sq": sqn[0], "rs": rstd, "qn": (v(95488, [128, 512], BF16), Trk()),
             "t1": (v(96512, [128, 512], BF16), Trk()), "t2": (v(97536, [128, 512], BF16), Trk())}
        rd, rdk = v(98560, [128, 16], F32), Trk()
        s3, s3k = v(98624, [128, 12], F32), Trk()
        sc, sck = v(98688, [128, 32], F32), Trk()
        sc2, sc2k = v(98816, [128, 32], F32), Trk()
        nsl, nslk = v(98944, [128, 32], F32), Trk()
        m8, m8k = v(99072, [128, 16], F32), Trk()
        acc, acck = v(99136, [128, 256], F32), Trk()
        trks = KSk + KWk + [VSk, VWk, uk, oTk, gatk, rdk, s3k, sck, sc2k, nslk, m8k, acck] + Qk + [k for _, k in ob] + [k for _, k in PT] \
            + [k for _, k in sqn] + [rstd[1]] + [T["qn"][1], T["t1"][1], T["t2"][1]] + [k for row in Nk for k in row]
        self.new_phase(trks)
        self.begin_phase([])
        self.pool_misc = [6]
        pool_st = [0, 1, 2]
        for g in range(4):
            P.dma("sp", KS[0:64, g, :], self.KSd[0, :, g, :], r=self.ksd_k[0][g], w=[KSk[g]])
            P.dma("sp", KW[0:64, g, :], self.KSd[1, :, g, :], r=self.ksd_k[1][g], w=[KWk[g]])
        P.dma("sp", VS.rearrange("p a b c -> p (a b c)"), self.Vd[0].rearrange("p a b c -> p (a b c)"), r=self.vd_k[0], w=[VSk])
        P.dma("sp", VW.rearrange("p a b c -> p (a b c)"), self.Vd[1].rearrange("p a b c -> p (a b c)"), r=self.vd_k[1], w=[VWk])
        est = self.ph[64:96, 57856 // 4:57856 // 4 + 2048]
        allq = Qk + [k for row in Nk for k in row]
        P.dma("sp", est, self.cst_d[64:96, CST["eind"]:CST["eind"] + 2048], w=allq)
        for g in range(4):
            P.op("dve", lambda e: e.tensor_copy(out=KS[64:96, g, :], in_=est), r=allq, w=[KSk[g]])
            P.op("pool", lambda e: e.memset(KW[64:96, g, :], 0.0), w=[KWk[g]])
        P.op("pool", lambda e: e.memset(Q[64:96].rearrange("p a b c -> p (a b c)"), 0.0), w=allq)
        gbo = BV_GB + 48 * j
        gats = [(gat, gatk), (v(91648, [128, 4, 48], F32), Trk())]
        self.phase_cur.append(gats[1][1])
        Qn, Qnk = self.Qnext, self.Qnext.k
        sq1 = [sqn[0]]
        pqb = self.ps[7]

        def prep_pieces(n):
            tsl = slice(n * TT, (n + 1) * TT)
            gt, gtk = gats[n % 2]
            pieces = []

            def p_norm():
                self.norm_tile(n, ("b_norm", j), u, uk, sq1, rstd, pool=[6])

            def p_gates():
                wg = self.next_w(("bg", j))
                for qb in range(4):
                    pg = self.bank([6])
                    for k in range(NCH):
                        P.op("pe", lambda e: e.matmul(pg[:, 0:48], lhsT=u[:, k, qb * 128:(qb + 1) * 128], rhs=wg[:, k * 48:(k + 1) * 48], start=(k == 0), stop=(k == NCH - 1)),
                             r=[wg.k, uk], w=[pg.k])
                    P.op("dve", lambda e: e.tensor_tensor(out=gt[:, qb, :], in0=pg[:, 0:48], in1=self.bvecs[:, gbo:gbo + 48], op=ALU.add), r=[pg.k, self.bvecs.k], w=[gtk])
                P.op("act", lambda e: e.activation(out=gt, in_=gt, func=AF.Tanh, scale=0.5), r=[gtk], w=[gtk])
                P.op("dve", lambda e: e.tensor_scalar(out=gt, in0=gt, scalar1=0.5, scalar2=0.5, op0=ALU.mult, op1=ALU.add), r=[gtk], w=[gtk])

            pieces += [p_norm, p_gates]
            sq, sqk = T["sq"]
            rs, rsk = T["rs"]
            qn, qnk = T["qn"]
            t1, t1k = T["t1"]
            t2, t2k = T["t2"]
            gq = self.vec(("q_norm", j))
            for m in range(NCH):
                def p_d(m=m):
                    w = self.next_w(("bq", j, m))
                    for k in range(NCH):
                        P.op("pe", lambda e: e.matmul(pqb[:], lhsT=w[:, k * 128:(k + 1) * 128], rhs=u[:, k, :], start=(k == 0), stop=(k == NCH - 1)), r=[w.k, uk], w=[pqb.k])
                    P.op("act", lambda e: e.activation(out=sq, in_=pqb[:], func=AF.Square), r=[pqb.k], w=[sqk])

                def p_e(m=m):
                    pss = self.bank([6])
                    P.op("pe", lambda e: e.matmul(pss[:], lhsT=self.bones_b[:], rhs=sq, start=True, stop=True), r=[sqk, self.bones_b.k], w=[pss.k])
                    P.op("act", lambda e: e.activation(out=rs, in_=pss[:], func=AF.Ln, scale=1.0 / 64.0, bias=EPS), r=[pss.k], w=[rsk])
                    P.op("act", lambda e: e.activation(out=rs, in_=rs, func=AF.Exp, scale=-0.5), r=[rsk], w=[rsk])
                    P.op("dve", lambda e: e.scalar_tensor_tensor(out=qn, in0=pqb[:], scalar=gq, in1=rs, op0=ALU.mult, op1=ALU.mult), r=[pqb.k, rsk, self.vecs.k], w=[qnk])

                def p_f(m=m):
                    prt = self.bank([6])
                    P.op("pe", lambda e: e.matmul(prt[:], lhsT=self.prot_b[:], rhs=qn, start=True, stop=True), r=[qnk, self.prot_b.k], w=[prt.k])
                    P.op("pool", lambda e: e.tensor_tensor(out=t1, in0=qn, in1=self.cos_b[:, tsl], op=ALU.mult), r=[qnk, self.cos_b.k], w=[t1k])
                    P.op("dve", lambda e: e.tensor_tensor(out=t2, in0=prt[:], in1=self.sin_b[:, tsl], op=ALU.mult), r=[prt.k, self.sin_b.k], w=[t2k])
                    P.op("dve", lambda e: e.tensor_tensor(out=Qn[:, m, :], in0=t1, in1=t2, op=ALU.add), r=[t1k, t2k], w=[Qnk])

                pieces += [p_d, p_e, p_f]
            return pieces

        def install_q():
            for m in range(NCH):
                for hh in range(2):
                    h = 2 * m + hh
                    src = Qn[hh * 64:(hh + 1) * 64, m, :].rearrange("p (a b) -> p a b", a=4)
                    if (h % 2) == 0:
                        P.op("dve", lambda e: e.tensor_copy(out=Q[0:64, :, h, :], in_=src), r=[Qnk], w=Qk)
                    else:
                        P.op("act", lambda e: e.activation(out=Q[0:64, :, h, :], in_=src, func=AF.Copy), r=[Qnk], w=Qk)

        for f in prep_pieces(0):
            f()
        for n in range(NT):
            tsl = slice(n * TT, (n + 1) * TT)
            install_q()
            bg = prep_pieces(n + 1) if n + 1 < NT else []
            gt, gtk = gats[n % 2]
            self.attn_tile(n, dict(KS=KS, KSk=KSk, KW=KW, KWk=KWk, VS=VS, VSk=VSk, VW=VW, VWk=VWk, Q=Q, Qk=Qk, Nk=Nk, oT=oT, oTk=oTk, ob=ob, PT=PT,
                                   gat=gt, gatk=gtk, rd=rd, rdk=rdk, s3=s3, s3k=s3k, sc=sc, sck=sck, sc2=sc2, sc2k=sc2k, nsl=nsl, nslk=nslk,
                                   m8=m8, m8k=m8k, acc=acc, acck=acck, pool_st=pool_st), bg)
            for mm in range(NCH):
                w = self.next_w(("bo", j, mm))
                po = self.bank(pool_st)
                for k in range(NCH):
                    P.op("pe", lambda e: e.matmul(po[:], lhsT=w[:, k * 128:(k + 1) * 128], rhs=oT[:, k, :], start=(k == 0), stop=(k == NCH - 1)), r=[w.k, oTk], w=[po.k])
                hsl = self.hT[:, mm, tsl]
                P.op("dve", lambda e: e.tensor_tensor(out=hsl, in0=po[:], in1=hsl, op=ALU.add), r=[po.k, self.hk[mm][n]], w=[self.hk[mm][n]])
        self.pool_misc = [0, 1, 2, 3, 4, 5, 6, 7]

    def attn_tile(self, n, C, bg=()):
        P = self.P
        KS, KSk, KW, KWk, VS, VSk, VW, VWk = C["KS"], C["KSk"], C["KW"], C["KWk"], C["VS"], C["VSk"], C["VW"], C["VWk"]
        Q, Qk, Nk, oT, oTk, ob, PT = C["Q"], C["Qk"], C["Nk"], C["oT"], C["oTk"], C["ob"], C["PT"]
        gat, gatk, rd, rdk, s3, s3k = C["gat"], C["gatk"], C["rd"], C["rdk"], C["s3"], C["s3k"]
        sc, sck, sc2, sc2k, nsl, nslk, m8, m8k, acc, acck = C["sc"], C["sck"], C["sc2"], C["sc2k"], C["nsl"], C["nslk"], C["m8"], C["m8k"], C["acc"], C["acck"]
        pool_st = C["pool_st"]
        poc, pos, pow_ = self.ps[3], self.ps[4], self.ps[5]
        st = {"npt": 0}
        pairs = [(qb, g) for qb in range(4) for g in range(4)]
        jobs = []

        def r4(ap):
            return ap.rearrange("p (a b) -> p a b", a=4)

        def mk_job(kind, qb, g, kt, first, last, pidx):
            qbg = 4 * n + qb
            job = {"pre": [], "post": []}
            box = {}

            def st1():
                pst = self.bank(pool_st)
                pt, ptk = PT[st["npt"] % 4]
                st["npt"] += 1
                box["pt"], box["ptk"] = pt, ptk
                Q96 = Q[0:96, qb, 4 * g:4 * g + 4, :]
                if kind == "c":
                    P.op("pe", lambda e: e.matmul(pst[0:127, :], lhsT=self.KC[0:96, g, 0:127], rhs=Q96, start=True, stop=True), r=[self.KC.k, Qk[qb]], w=[pst.k])
                    P.op("act", lambda e: e.activation(out=pt[0:127, :], in_=pst[0:127, :], func=AF.Exp, scale=SCALE), r=[pst.k], w=[ptk])
                    cm = self.cmask_b[0:127, qbg * 128:(qbg + 1) * 128].unsqueeze(1).broadcast_to([127, 4, 128])
                    P.op("pool", lambda e: e.tensor_tensor(out=r4(pt[0:127, :]), in0=r4(pt[0:127, :]), in1=cm, op=ALU.mult), r=[ptk, self.cmask_b.k], w=[ptk])
                    return
                if kind == "s":
                    P.op("pe", lambda e: e.matmul(pst[:], lhsT=KS[0:96, g, kt * 128:(kt + 1) * 128], rhs=Q96, start=True, stop=True),
                         r=[KSk[g], Qk[qb], Nk[qb][g]], w=[pst.k])
                else:
                    P.op("pe", lambda e: e.matmul(pst[:], lhsT=KW[0:96, g, kt * 128:(kt + 1) * 128], rhs=Q96, start=True, stop=True), r=[KWk[g], Qk[qb]], w=[pst.k])
                P.op("act", lambda e: e.activation(out=pt, in_=pst[:], func=AF.Exp, scale=SCALE), r=[pst.k], w=[ptk])
                msk = None
                if kt == qbg:
                    msk = self.triL_b
                elif kind == "w" and kt == qbg - 4:
                    msk = self.triU_b
                if msk is not None:
                    tm = msk[:].unsqueeze(1).broadcast_to([128, 4, 128])
                    P.op("pool", lambda e: e.tensor_tensor(out=r4(pt), in0=r4(pt), in1=tm, op=ALU.mult), r=[ptk, msk.k], w=[ptk])

            def st2():
                pt, ptk = box["pt"], box["ptk"]
                for h in range(4):
                    if kind == "c":
                        P.op("pe", lambda e: e.matmul(poc[:, h * 128:h * 128 + 97], lhsT=pt[0:127, h * 128:(h + 1) * 128], rhs=self.VC[0:127, g, 0:97],
                                                      start=(h == 0), stop=(h == 3), skip_group_check=True), r=[ptk, self.VC.k], w=[poc.k])
                    elif kind == "s":
                        P.op("pe", lambda e: e.matmul(pos[:, h * 128:h * 128 + 65], lhsT=pt[:, h * 128:(h + 1) * 128], rhs=VS[:, kt, g, 0:65],
                                                      start=(first and h == 0), stop=(last and h == 3), skip_group_check=True), r=[ptk, VSk], w=[pos.k])
                    else:
                        P.op("pe", lambda e: e.matmul(pow_[:, h * 128:h * 128 + 65], lhsT=pt[:, h * 128:(h + 1) * 128], rhs=VW[:, kt, g, 0:65],
                                                      start=(first and h == 0), stop=(last and h == 3), skip_group_check=True), r=[ptk, VWk], w=[pow_.k])

            job["st1"], job["st2"] = st1, st2
            return job

        need_sel = (4 * n + 3) >= 8

        def select_chain(qb, g, pidx):
            qbg = 4 * n + qb
            pcs = self.pocs[pidx % 2]
            P.op("dve", lambda e: e.tensor_copy(out=pcs[:], in_=r4(poc[:])[:, :, 0:97]), r=[poc.k], w=[pcs.k])
            if not need_sel:
                return
            P.op("dve", lambda e: e.tensor_scalar(out=rd[:, 0:4], in0=pcs[:, :, 64], scalar1=1e-30, scalar2=None, op0=ALU.max), r=[pcs.k], w=[rdk])
            P.op("dve", lambda e: e.reciprocal(out=rd[:, 0:4], in_=rd[:, 0:4]), r=[rdk], w=[rdk])
            for h in range(4):
                src1 = self.selb[:, qbg, :] if h == 0 else sc
                P.op("dve", lambda e: e.scalar_tensor_tensor(out=sc, in0=pcs[:, h, 65:97], scalar=rd[:, h:h + 1], in1=src1, op0=ALU.mult, op1=ALU.add),
                     r=[pcs.k, rdk, sck, self.selb.k], w=[sck])
            P.op("dve", lambda e: e.max(out=m8[:, 0:8], in_=sc), r=[sck], w=[m8k])
            P.op("dve", lambda e: e.match_replace(out=sc2, in_to_replace=m8[:, 0:8], in_values=sc, imm_value=-3.0e38), r=[sck, m8k], w=[sc2k])
            P.op("dve", lambda e: e.max(out=m8[:, 8:16], in_=sc2), r=[sc2k], w=[m8k])
            P.op("dve", lambda e: e.tensor_scalar(out=nsl, in0=sc, scalar1=m8[:, 15:16], scalar2=NEGM, op0=ALU.is_lt, op1=ALU.mult), r=[sck, m8k], w=[nslk])

        def nsel_install(qb, g):
            if not need_sel:
                return
            pm = self.bank(self.pool_misc)
            P.op("pe", lambda e: e.transpose(out=pm[0:32, 0:128], in_=nsl, identity=self.ident_f[:]), r=[nslk, self.ident_f.k], w=[pm.k])
            P.op("act", lambda e: e.activation(out=Q[64:96, qb, 4 * g:4 * g + 4, :], in_=pm[0:32, 0:128].unsqueeze(1).broadcast_to([32, 4, 128]), func=AF.Copy),
                 r=[pm.k], w=[Nk[qb][g]])

        def evac_win():
            P.op("dve", lambda e: e.tensor_copy(out=self.pows[:], in_=r4(pow_[:])[:, :, 0:65]), r=[pow_.k], w=[self.pows.k])

        def combine(qb, g, pidx):
            pcs = self.pocs[pidx % 2]
            o, ok_ = ob[qb % 2]
            P.op("dve", lambda e: e.tensor_scalar(out=rd[:, 4:8], in0=pcs[:, :, 64], scalar1=1e-30, scalar2=None, op0=ALU.max), r=[pcs.k], w=[rdk])
            P.op("dve", lambda e: e.tensor_scalar(out=rd[:, 8:12], in0=r4(pos[:])[:, :, 64], scalar1=1e-30, scalar2=None, op0=ALU.max), r=[pos.k], w=[rdk])
            P.op("dve", lambda e: e.tensor_scalar(out=rd[:, 12:16], in0=self.pows[:, :, 64], scalar1=1e-30, scalar2=None, op0=ALU.max), r=[self.pows.k], w=[rdk])
            P.op("dve", lambda e: e.reciprocal(out=rd[:, 4:16], in_=rd[:, 4:16]), r=[rdk], w=[rdk])
            P.op("dve", lambda e: e.tensor_tensor(out=s3.rearrange("p (a b) -> p a b", a=3), in0=rd[:, 4:16].rearrange("p (a b) -> p a b", a=3),
                                                  in1=gat[:, qb, :].rearrange("p (a b) -> p a b", a=3)[:, :, 4 * g:4 * g + 4], op=ALU.mult), r=[rdk, gatk], w=[s3k])
            for h in range(4):
                ah = acc[:, h * 64:(h + 1) * 64]
                P.op("dve", lambda e: e.tensor_scalar(out=ah, in0=pcs[:, h, 0:64], scalar1=s3[:, h:h + 1], scalar2=None, op0=ALU.mult), r=[pcs.k, s3k], w=[acck])
                P.op("dve", lambda e: e.scalar_tensor_tensor(out=ah, in0=self.pows[:, h, 0:64], scalar=s3[:, 8 + h:9 + h], in1=ah, op0=ALU.mult, op1=ALU.add),
                     r=[self.pows.k, s3k, acck], w=[acck])
                P.op("dve", lambda e: e.scalar_tensor_tensor(out=o[:, g * 256 + h * 64:g * 256 + (h + 1) * 64], in0=pos[:, h * 128:h * 128 + 64], scalar=s3[:, 4 + h:5 + h], in1=ah,
                                                             op0=ALU.mult, op1=ALU.add), r=[pos.k, s3k, acck], w=[ok_])

        def o_transpose(qb):
            o, ok_ = ob[qb % 2]
            pT = self.bank(self.pool_misc)
            pTb = pT[:].bitcast(BF16)
            for c in range(NCH):
                P.op("pe", lambda e: e.transpose(out=pTb[:, c * 128:(c + 1) * 128], in_=o[:, c * 128:(c + 1) * 128], identity=self.ident_b[:]), r=[ok_, self.ident_b.k], w=[pT.k])
            P.op("act", lambda e: e.activation(out=oT[:, :, qb * 128:(qb + 1) * 128], in_=pTb.rearrange("p (a b) -> p a b", a=8), func=AF.Copy), r=[pT.k], w=[oTk])

        def cjob(pidx):
            qb, g = pairs[pidx]
            j = mk_job("c", qb, g, 0, True, True, pidx)
            j["post"].append(lambda: select_chain(qb, g, pidx))
            return j

        jobs.append(cjob(0))
        pending_T = []
        for pidx, (qb, g) in enumerate(pairs):
            qbg = 4 * n + qb
            k0 = max(0, qbg - 4)
            wj = [mk_job("w", qb, g, kt, kt == k0, kt == qbg, pidx) for kt in range(k0, qbg + 1)]
            for f in pending_T:
                wj[min(2, len(wj) - 1)]["post"].append(f)
            pending_T = []
            wj[-1]["post"].append(evac_win)
            jobs += wj
            if pidx + 1 < len(pairs):
                jobs.append(cjob(pidx + 1))
            sj = [mk_job("s", qb, g, kt, kt == 0, kt == qbg, pidx) for kt in range(qbg + 1)]
            sj[0]["pre"].append(lambda qb=qb, g=g: nsel_install(qb, g))
            sj[-1]["post"].append(lambda qb=qb, g=g, pidx=pidx: combine(qb, g, pidx))
            jobs += sj
            if g == 3:
                pending_T.append(lambda qb=qb: o_transpose(qb))
        LA = 2
        for j in range(min(LA, len(jobs))):
            for f in jobs[j]["pre"]:
                f()
            jobs[j]["st1"]()
        bg = list(bg)
        every = max(2, (len(jobs) - 8) // (len(bg) + 1)) if bg else 0
        for i in range(len(jobs)):
            if i + LA < len(jobs):
                for f in jobs[i + LA]["pre"]:
                    f()
                jobs[i + LA]["st1"]()
            jobs[i]["st2"]()
            for f in jobs[i]["post"]:
                f()
            if bg and i >= 4 and (i - 4) % every == 0:
                bg.pop(0)()
        for f in pending_T:
            f()
        for f in bg:
            f()

    def make_plan(self):
        plan = []
        ser = -1
        for b in range(self.nb):
            for layer in range(self.n_layers):
                if layer < 2:
                    ser += 1
                    plan += [(("a_in", layer, c), ser) for c in range(8)]
                    plan += [(("a_out", layer, m), ser) for m in range(8)]
                else:
                    if layer == 2:
                        ser += 1
                        plan += [(("kvk", i), ser) for i in range(4)] + [(("kvc", i), ser) for i in range(4)] + [(("kvv", i), ser) for i in range(2)]
                        plan += [(("cw2",), ser)]
                    ser += 1
                    jj = layer - 2
                    plan += [(("bg", jj), ser)] + [(("bq", jj, m), ser) for m in range(8)]
                    for n in range(NT):
                        if n + 1 < NT:
                            plan += [(("bg", jj), ser)] + [(("bq", jj, m), ser) for m in range(8)]
                        plan += [(("bo", jj, m), ser) for m in range(8)]
                ser += 1
                for t2 in range(2):
                    plan += [(("f_in", layer, c), ser) for c in range(FC)]
                    plan += [(("f_out", layer, m, hf), ser) for m in range(8) for hf in range(2)]
        return plan

    def build(self):
        P = self.P
        self.convc = self.sb("convc", [128, 8, 3], F32)
        self.hst = self.sb("hst", [128, 8], F32)
        self.prologue()
        self.plan_weights(self.make_plan())
        phase_no = {"a0": 0, "f0": 1, "a1": 2, "f1": 3, "kv": 4, "b0": 5, "f2": 6, "b1": 7, "f3": 8}
        for b in range(self.nb):
            def pre(tag):
                if b == 0:
                    self.convert_phase(phase_no[tag] + 1)
            if b == 0:
                self.convert_phase(0)
            self.mark("load%d" % b)
            self.load_x(b)
            for layer in range(self.n_layers):
                if layer < 2:
                    pre("a%d" % layer)
                    self.mark("a%d.%d" % (b, layer))
                    self.a_layer(layer)
                else:
                    if layer == 2:
                        pre("kv")
                        self.mark("kv%d" % b)
                        self.kv_phase()
                    pre("b%d" % (layer - 2))
                    self.mark("b%d.%d" % (b, layer))
                    self.b_layer(layer - 2)
                pre("f%d" % layer)
                self.mark("f%d.%d" % (b, layer))
                self.ffn_layer(layer)
            self.mark("store%d" % b)
            self.store_y(b)
        self.mark("end")
        P.wait_all("sp", self.out_trks)
        return self.nc


_CACHE = {}


def _prep_inputs(inputs):
    L = weight_layout()
    wall = pack_weights(inputs, L)
    vecs, bvecs = pack_vecs(inputs)
    cst = make_consts()
    return wall, vecs, bvecs, cst


def kernel(**inputs):
    inputs = {k: np.asarray(v) for k, v in inputs.items()}
    x = np.ascontiguousarray(inputs["x"], dtype=np.float32)
    B = x.shape[0]
    nb = B // NCORES
    wall, vecs, bvecs, cst = _prep_inputs(inputs)
    nc = Builder(nb).build()
    in_maps = []
    for c in range(NCORES):
        in_maps.append({"x": np.ascontiguousarray(x[c * nb:(c + 1) * nb]), "wall": wall, "vecs": vecs, "bvecs": bvecs, "cst": cst})
    res = run_bass_kernel_spmd(nc, in_maps, core_ids=list(range(NCORES)))
    out = np.concatenate([np.asarray(r["y"]).reshape(nb, S, D) for r in res.results], axis=0)
    return out.astype(np.float32)
```

```python
import numpy as np
import concourse.bass as bass
import concourse.mybir as mybir
from concourse.bass_utils import run_bass_kernel_spmd
from concourse.alu_op_type import AluOpType as ALU

F32 = mybir.dt.float32
BF16 = mybir.dt.bfloat16
AF = mybir.ActivationFunctionType

S = 2048
D = 1024
NCH = 8
TT = 512
NT = S // TT
FH = 2816
FC = 22
NCORES = 8
EPS = 1e-6
CONV_CH = 128 * 2048
SLOT = 2304
SCALE = 0.125
NEGM = -30000.0


def _kt(W, c0, width=128):
    K = W.shape[0]
    return np.ascontiguousarray(W[:, c0:c0 + width].reshape(K // 128, 128, width).transpose(1, 0, 2)).reshape(128, -1)


class WLayout:
    def __init__(self):
        self.off = {}
        self.free = {}
        self.n = 0

    def add(self, name, free):
        self.off[name] = self.n
        self.free[name] = free
        self.n += 128 * free

    def total(self):
        return ((self.n + CONV_CH - 1) // CONV_CH) * CONV_CH


def weight_layout():
    L = WLayout()
    L.phase_end = []

    def a(l):
        for c in range(8):
            L.add(("a_in", l, c), 2304)
        for m in range(8):
            L.add(("a_out", l, m), 1024)
        L.phase_end.append(L.n)

    def f(l):
        for c in range(FC):
            L.add(("f_in", l, c), 2048)
        for m in range(8):
            for hf in range(2):
                L.add(("f_out", l, m, hf), FH // 2)
        L.phase_end.append(L.n)

    def b(j):
        L.add(("bg", j), 384)
        for m in range(8):
            L.add(("bq", j, m), 1024)
        for m in range(8):
            L.add(("bo", j, m), 1024)
        L.phase_end.append(L.n)

    a(0)
    f(0)
    a(1)
    f(1)
    for i in range(4):
        L.add(("kvk", i), 1024)
    for i in range(4):
        L.add(("kvc", i), 1024)
    for i in range(2):
        L.add(("kvv", i), 2048)
    for i in range(2):
        L.add(("cw1", i), 8192)
    L.add(("cw2",), 256)
    L.phase_end.append(L.n)
    b(0)
    f(2)
    b(1)
    f(3)
    return L


def pack_weights(inp, L):
    out = np.zeros(L.total(), np.float32)

    def put(name, arr):
        arr = np.asarray(arr, np.float32).reshape(128, -1)
        assert arr.shape[1] == L.free[name], (name, arr.shape)
        out[L.off[name]:L.off[name] + arr.size] = arr.reshape(-1)

    for l in range(2):
        Win = inp["a_w_in"][l]
        for c in range(8):
            put(("a_in", l, c), np.concatenate(
                [_kt(Win, c * 128), _kt(Win, 1024 + c * 128), inp["a_gate_w"][l][0, c], inp["a_gate_w"][l][1, c]], axis=1))
        for m in range(8):
            put(("a_out", l, m), _kt(inp["a_w_out"][l], m * 128))
    for l in range(4):
        W = inp["f_w_in"][l]
        for c in range(FC):
            put(("f_in", l, c), np.concatenate([_kt(W, c * 128), _kt(W, FH + c * 128)], axis=1))
        for m in range(8):
            t = _kt(inp["f_w_out"][l], m * 128)
            for hf in range(2):
                put(("f_out", l, m, hf), t[:, hf * (FH // 2):(hf + 1) * (FH // 2)])
    kvw = inp["kv_w"]
    i = 0
    for jj in (2, 4):
        for cc in range(2):
            put(("kvk", i), _kt(kvw, jj * 256 + cc * 128))
            i += 1
    i = 0
    for jj in (0, 1):
        for cc in range(2):
            put(("kvc", i), _kt(kvw, jj * 256 + cc * 128))
            i += 1
    for i, jj in enumerate((3, 5)):
        put(("kvv", i), _kt(kvw, jj * 256, 256))
    for kv in range(2):
        t = np.zeros((128, 32, 256), np.float32)
        t[:64] = inp["cmp_w1"][kv].reshape(32, 64, 256).transpose(1, 0, 2)
        put(("cw1", kv), t)
    t = np.zeros((128, 2, 2, 64), np.float32)
    for kv in range(2):
        t[:, kv] = inp["cmp_w2"][kv].reshape(2, 128, 64).transpose(1, 0, 2)
    put(("cw2",), t)
    for j in range(2):
        put(("bg", j), _kt(inp["b_w_in"][j], 1024, 48))
        for m in range(8):
            put(("bq", j, m), _kt(inp["b_w_in"][j], m * 128))
        for m in range(8):
            put(("bo", j, m), _kt(inp["b_w_out"][j], m * 128))
    return out


VEC = {}
_nv = 0


def _vadd(name, n):
    global _nv
    VEC[name] = _nv
    _nv += n


for _l in range(2):
    _vadd(("a_norm", _l), 8)
    for _k in range(4):
        _vadd(("a_cw", _l, _k), 8)
    _vadd(("a_cb", _l), 8)
    _vadd(("a_gb", _l, 0), 8)
    _vadd(("a_gb", _l, 1), 8)
    _vadd(("a_lam", _l), 8)
for _l in range(4):
    _vadd(("f_norm", _l), 8)
_vadd(("kv_norm",), 8)
for _j in range(2):
    _vadd(("b_norm", _j), 8)
    _vadd(("q_norm", _j), 1)
for _i in range(3):
    _vadd(("k_norm", _i), 1)
for _i in range(2):
    _vadd(("c_b1", _i), 2)
    _vadd(("c_b2", _i), 1)
    _vadd(("c_pos", _i), 32)
NV = _nv
BV_GB = 0
BV_CB2V = 96
NBV = 160


def pack_vecs(inp):
    v = np.zeros((128, NV), np.float32)

    def fm(x):
        return np.asarray(x, np.float32).reshape(8, 128).T

    for l in range(2):
        v[:, VEC[("a_norm", l)]:][:, :8] = fm(inp["a_norm"][l])
        for k in range(4):
            v[:, VEC[("a_cw", l, k)]:][:, :8] = fm(inp["a_conv_w"][l][k])
        v[:, VEC[("a_cb", l)]:][:, :8] = fm(inp["a_conv_b"][l])
        v[:, VEC[("a_gb", l, 0)]:][:, :8] = fm(inp["a_gate_b"][l][0])
        v[:, VEC[("a_gb", l, 1)]:][:, :8] = fm(inp["a_gate_b"][l][1])
        v[:, VEC[("a_lam", l)]:][:, :8] = fm(inp["a_lambda"][l])
    for l in range(4):
        v[:, VEC[("f_norm", l)]:][:, :8] = fm(inp["f_norm"][l])
    v[:, VEC[("kv_norm",)]:][:, :8] = fm(inp["kv_norm"])
    for j in range(2):
        v[:, VEC[("b_norm", j)]:][:, :8] = fm(inp["b_norm"][j])
        v[:, VEC[("q_norm", j)]] = np.tile(np.asarray(inp["q_norm"][j], np.float32), 2)
    for i in range(3):
        v[:, VEC[("k_norm", i)]] = np.tile(np.asarray(inp["k_norm"][i], np.float32), 2)
    for i in range(2):
        v[:, VEC[("c_b1", i)]:][:, :2] = np.asarray(inp["cmp_b1"][i], np.float32).reshape(2, 128).T
        v[:, VEC[("c_b2", i)]] = np.tile(np.asarray(inp["cmp_b2"][i], np.float32), 2)
        v[:64, VEC[("c_pos", i)]:][:, :32] = np.asarray(inp["cmp_pos"][i], np.float32).T
    bv = np.zeros((128, NBV), np.float32)
    for j in range(2):
        bv[:, BV_GB + 48 * j: BV_GB + 48 * j + 48] = np.asarray(inp["b_gate_b"][j], np.float32)[None, :]
    bv[:, BV_CB2V:BV_CB2V + 64] = np.asarray(inp["cmp_b2"][1], np.float32)[None, :]
    return v, bv


CST = {}
_nc_ = 0


def _cadd(name, n):
    global _nc_
    CST[name] = _nc_
    _nc_ += n


_cadd("ident", 128)
_cadd("prot", 128)
_cadd("bones", 128)
_cadd("triL", 128)
_cadd("triU", 128)
_cadd("ovl", 32)
_cadd("cos", 2048)
_cadd("sin", 2048)
_cadd("cmask", 2048)
_cadd("eind", 2048)
_cadd("selb", 512)
NCST = _nc_


def make_consts():
    c = np.zeros((128, NCST), np.float32)
    p = np.arange(128)
    c[:, CST["ident"]:][:, :128] = np.eye(128)
    perm = (p // 64) * 64 + ((p % 64) + 32) % 64
    pr = np.zeros((128, 128), np.float32)
    pr[perm, p] = 1.0
    c[:, CST["prot"]:][:, :128] = pr
    c[:, CST["bones"]:][:, :128] = (p[:, None] // 64 == p[None, :] // 64)
    c[:, CST["triL"]:][:, :128] = (p[:, None] <= p[None, :])
    c[:, CST["triU"]:][:, :128] = (p[:, None] > p[None, :])
    cs = np.arange(127) * 16
    sl = np.arange(32) * 64
    ov = np.clip(np.minimum(cs[:, None] + 32, sl[None, :] + 64) - np.maximum(cs[:, None], sl[None, :]), 0, None) / 32.0
    c[:127, CST["ovl"]:][:, :32] = ov
    t = np.arange(2048, dtype=np.float64)
    fi = (p % 64) % 32
    freqs = (10000.0 ** (-(np.arange(32, dtype=np.float32) / np.float32(32)))).astype(np.float32)
    ang = (t[None, :].astype(np.float32) * freqs[fi][:, None]).astype(np.float32)
    c[:, CST["cos"]:][:, :2048] = np.cos(ang)
    sg = np.where((p % 64) < 32, -1.0, 1.0)[:, None]
    c[:, CST["sin"]:][:, :2048] = np.sin(ang) * sg
    cl = np.arange(127) * 16 + 31
    c[:127, CST["cmask"]:][:, :2048] = (cl[:, None] <= t[None, :])
    b = np.arange(32)
    c[64:96, CST["eind"]:][:, :2048] = (t[None, :].astype(np.int64) // 64 == b[:, None])
    sb = np.zeros((128, 16, 32), np.float32)
    for qb in range(16):
        tq = qb * 128 + p
        cur = (tq // 64)[:, None]
        causal = b[None, :] <= cur
        forced = (b[None, :] == 0) | (causal & (cur - b[None, :] < 2))
        sb[:, qb, :] = np.where(forced, 1e30, np.where(causal, 0.0, -1e30))
    c[:, CST["selb"]:][:, :512] = sb.reshape(128, 512)
    return c


class Trk:
    __slots__ = ("w", "r")

    def __init__(self):
        self.w = None
        self.r = {}


NDS = 24
SKIP_SAME_ENGINE_WAX = True


class Prog:
    def __init__(self, nc):
        self.nc = nc
        self.eng = {"pe": nc.tensor, "act": nc.scalar, "dve": nc.vector, "pool": nc.gpsimd, "sp": nc.sync}
        self.sem = {e: nc.alloc_semaphore(name="s_" + e) for e in ("pe", "act", "dve", "pool")}
        self.cnt = {e: 0 for e in ("pe", "act", "dve", "pool")}
        self.seen = {e: {} for e in self.eng}
        self.dsem = [nc.alloc_semaphore(name="s_dma%d" % i) for i in range(NDS)]
        self.dn = 0
        self.ninst = 0

    def _semof(self, key):
        return self.sem[key] if isinstance(key, str) else self.dsem[key[1]]

    def _wait(self, e, key, val):
        if key == "pe" and e == "pe":
            return
        if self.seen[e].get(key, 0) >= val:
            return
        self.eng[e].wait_ge(self._semof(key), val)
        self.seen[e][key] = val

    def _deps(self, e, r, w):
        for t in r:
            if t.w is not None:
                self._wait(e, t.w[0], t.w[1])
        for t in w:
            if t.w is not None and not (SKIP_SAME_ENGINE_WAX and t.w[0] == e):
                self._wait(e, t.w[0], t.w[1])
            for k, v in t.r.items():
                if SKIP_SAME_ENGINE_WAX and k == e:
                    continue
                self._wait(e, k, v)

    def op(self, e, fn, r=(), w=()):
        self._deps(e, r, w)
        inst = fn(self.eng[e])
        self.cnt[e] += 1
        self.ninst += 1
        inst.then_inc(self.sem[e], 1)
        v = self.cnt[e]
        for t in r:
            t.r[e] = v
        for t in w:
            t.w = (e, v)
            t.r = {}
        return inst

    def dma(self, q, out, in_, r=(), w=()):
        self._deps(q, r, w)
        i = self.dn % NDS
        gen = self.dn // NDS
        key = ("d", i)
        if gen > 0:
            self._wait(q, key, 16 * gen)
        inst = self.eng[q].dma_start(out=out, in_=in_)
        inst.then_inc(self.dsem[i], 16)
        self.dn += 1
        self.ninst += 1
        v = 16 * (gen + 1)
        for t in r:
            t.r[key] = v
        for t in w:
            t.w = (key, v)
            t.r = {}
        return (key, v)

    def wait_all(self, e, trks):
        self._deps(e, trks, trks)


class Buf:
    def __init__(self, t):
        self.t = t
        self.k = Trk()

    def __getitem__(self, idx):
        return self.t[idx]


class Builder:
    def __init__(self, nb, n_layers=4, debug_out=None):
        self.nb = nb
        self.n_layers = n_layers
        self.L = weight_layout()
        nc = bass.Bass("TRN2", target_bir_lowering=False)
        self.nc = nc
        self.P = Prog(nc)
        NW = self.L.total()
        self.x = nc.dram_tensor("x", [nb, S, D], F32, kind="ExternalInput").ap()
        self.wall = nc.dram_tensor("wall", [NW], F32, kind="ExternalInput").ap()
        self.vecs_d = nc.dram_tensor("vecs", [128, NV], F32, kind="ExternalInput").ap()
        self.bvecs_d = nc.dram_tensor("bvecs", [128, NBV], F32, kind="ExternalInput").ap()
        self.cst_d = nc.dram_tensor("cst", [128, NCST], F32, kind="ExternalInput").ap()
        self.y = nc.dram_tensor("y", [nb, S, D], F32, kind="ExternalOutput").ap()
        self.wbf = nc.dram_tensor("wbf", [NW], BF16, kind="Internal").ap()
        self.wchunk = [Trk() for _ in range(NW // CONV_CH)]
        self.out_trks = []
        self.marks = []

        def sb(name, shape, dt):
            return Buf(nc.alloc_sbuf_tensor(name, shape, dt))

        self.sb = sb
        self.hT = nc.alloc_sbuf_tensor("hT", [128, NCH, S], F32)
        self.hk = [[Trk() for _ in range(NT)] for _ in range(NCH)]
        self.vecs = sb("vecs_sb", [128, NV], F32)
        self.bvecs = sb("bvecs_sb", [128, NBV], F32)
        self.ident_f = sb("ident_f", [128, 128], F32)
        self.ident_b = sb("ident_b", [128, 128], BF16)
        self.ones_b = sb("ones_b", [128, 128], BF16)
        self.coef = sb("coef", [128, 2, 2, 8], F32)
        self.hgb = sb("hgb", [128, 2, 2, 8], F32)
        self.lnhalf = sb("lnhalf", [128, 2], F32)
        self.cos_b = sb("cos_b", [128, S], BF16)
        self.sin_b = sb("sin_b", [128, S], BF16)
        self.cmask_b = sb("cmask_b", [128, S], BF16)
        self.triL_b = sb("triL_b", [128, 128], BF16)
        self.triU_b = sb("triU_b", [128, 128], BF16)
        self.prot_b = sb("prot_b", [128, 128], BF16)
        self.bones_b = sb("bones_b", [128, 128], BF16)
        self.selb = sb("selb", [128, 16, 32], BF16)
        self.Qnext = sb("Qnext", [128, 8, 512], BF16)
        self.KC = sb("KC", [128, 4, 128], BF16)
        self.VC = sb("VC", [128, 4, 98], BF16)
        self.posb = sb("posb", [128, 2, 32], BF16)
        self.cbias = sb("cbias", [128, 2, 2], F32)
        self.pocs = [sb("pocs%d" % i, [128, 4, 97], F32) for i in range(2)]
        self.pows = sb("pows", [128, 4, 65], F32)
        self.KSd = nc.dram_tensor("KSd", [2, 64, 4, S], BF16, kind="Internal").ap()
        self.Vd = nc.dram_tensor("Vd", [2, 128, 16, 4, 66], BF16, kind="Internal").ap()
        self.ksd_k = [[[Trk() for _ in range(NT)] for _ in range(4)] for _ in range(2)]
        self.vd_k = [[Trk() for _ in range(NT)] for _ in range(2)]
        self.ps = [Buf(nc.alloc_psum_tensor("ps%d" % i, [128, 512], F32)) for i in range(8)]
        self.ps_rr = 0
        self.pool_misc = [0, 1, 2, 3, 4, 5, 6, 7]
        self.NSLOT = 3
        self.ring = [sb("wring%d" % i, [128, SLOT], BF16) for i in range(self.NSLOT)]
        self.perm_ids = set(id(b) for b in self.ring)
        self.PHB = 100352
        self.ph = nc.alloc_sbuf_tensor("phase", [128, self.PHB // 4], F32)
        self.phk = {}

    def bank(self, pool=None):
        if pool is None:
            b = self.ps[self.ps_rr % 8]
        else:
            b = self.ps[pool[self.ps_rr % len(pool)]]
        self.ps_rr += 1
        return b

    def vec(self, name, c=0):
        i = VEC[name] + c
        return self.vecs[:, i:i + 1]

    def phase_view(self, byte_off, shape, dt):
        esz = 4 if dt == F32 else 2
        n = int(np.prod(shape[1:]))
        assert byte_off % 4 == 0 and byte_off + n * esz <= self.PHB, (byte_off, shape)
        if dt == F32:
            ap = self.ph[:, byte_off // 4: byte_off // 4 + n]
        else:
            ap = self.ph[:].bitcast(BF16)[:, byte_off // 2: byte_off // 2 + n]
        if len(shape) == 2:
            return ap
        names = " ".join("a%d" % i for i in range(len(shape) - 1))
        kw = {"a%d" % i: shape[i + 1] for i in range(len(shape) - 1)}
        return ap.rearrange("p (%s) -> p %s" % (names, names), **kw)

    def mark(self, name):
        self.marks.append((name, dict(self.P.cnt)))

    def new_phase(self, trks):
        merged = {}
        for t in self.phase_cur:
            if t.w is not None:
                merged[t.w[0]] = max(merged.get(t.w[0], 0), t.w[1])
            for k, v in t.r.items():
                merged[k] = max(merged.get(k, 0), v)
        for t in trks:
            t.w = None
            t.r = dict(merged)
        self.phase_cur = list(trks)

    def plan_weights(self, entries):
        self.wplan = list(entries)
        self.wplan_i = 0
        self.wq = []
        self.free_perm = list(self.ring)
        self.free_ext = []
        self.inuse = None
        self.serial = -1

    def begin_phase(self, ext_slots):
        self._release()
        self.serial += 1
        self.free_ext = list(ext_slots)
        self.ext_ids = set(id(b) for b in ext_slots)

    def _release(self):
        if self.inuse is not None:
            slot, ser = self.inuse
            if id(slot) in self.perm_ids:
                self.free_perm.append(slot)
            elif ser == self.serial:
                self.free_ext.append(slot)
            self.inuse = None

    def _topup(self):
        while self.wplan_i < len(self.wplan):
            name, ser = self.wplan[self.wplan_i]
            if ser == self.serial and self.free_ext:
                slot = self.free_ext.pop(0)
            elif self.free_perm:
                slot = self.free_perm.pop(0)
            else:
                break
            self.wplan_i += 1
            off = self.L.off[name]
            free = self.L.free[name]
            src = self.wbf[off:off + 128 * free].rearrange("(p f) -> p f", p=128)
            c0 = off // CONV_CH
            c1 = (off + 128 * free - 1) // CONV_CH
            self.P.dma("sp", slot[:, 0:free], src, r=[self.wchunk[c] for c in range(c0, c1 + 1)], w=[slot.k])
            self.wq.append((name, slot, ser))

    def next_w(self, name, hold=False):
        self._release()
        self._topup()
        n, slot, ser = self.wq.pop(0)
        assert n == name and ser == self.serial, (n, name, ser, self.serial)
        if not hold:
            self.inuse = (slot, ser)
        return slot

    def release_slot(self, slot):
        if id(slot) in self.perm_ids:
            self.free_perm.append(slot)
        elif id(slot) in self.ext_ids:
            self.free_ext.append(slot)

    def ext_slots(self, byte_off):
        out = []
        o = byte_off
        while o + 2 * SLOT <= self.PHB:
            b = Buf(self.phase_view(o, [128, SLOT], BF16))
            out.append(b)
            o += 2 * SLOT
        return out

    def prologue(self):
        P = self.P
        nc = self.nc
        P.dma("sp", self.vecs[:], self.vecs_d, w=[self.vecs.k])
        P.dma("sp", self.bvecs[:], self.bvecs_d, w=[self.bvecs.k])
        P.dma("sp", self.ident_f[:], self.cst_d[:, CST["ident"]:CST["ident"] + 128], w=[self.ident_f.k])
        P.op("dve", lambda e: e.tensor_copy(out=self.ident_b[:], in_=self.ident_f[:]), r=[self.ident_f.k], w=[self.ident_b.k])
        P.op("pool", lambda e: e.memset(self.ones_b[:], 1.0), w=[self.ones_b.k])
        tmp = self.sb("coef_tmp", [128, 16], F32)
        for l in range(2):
            lam = self.vecs[:, VEC[("a_lam", l)]:VEC[("a_lam", l)] + 8]
            P.op("act", lambda e: e.activation(out=tmp[:, l * 8:l * 8 + 8], in_=lam, func=AF.Exp, scale=-1.0), r=[self.vecs.k], w=[tmp.k])
            P.op("act", lambda e: e.activation(out=tmp[:, l * 8:l * 8 + 8], in_=tmp[:, l * 8:l * 8 + 8], func=AF.Ln, bias=1.0), r=[tmp.k], w=[tmp.k])
            P.op("dve", lambda e: e.tensor_scalar(out=self.coef[:, l, 0, :], in0=tmp[:, l * 8:l * 8 + 8], scalar1=-4.0, scalar2=None, op0=ALU.mult), r=[tmp.k], w=[self.coef.k])
            P.op("dve", lambda e: e.tensor_scalar(out=self.coef[:, l, 1, :], in0=tmp[:, l * 8:l * 8 + 8], scalar1=-8.0, scalar2=None, op0=ALU.mult), r=[tmp.k], w=[self.coef.k])
            for kk in range(2):
                gbo = VEC[("a_gb", l, kk)]
                P.op("dve", lambda e: e.tensor_scalar(out=self.hgb[:, l, kk, :], in0=self.vecs[:, gbo:gbo + 8], scalar1=0.5, scalar2=None, op0=ALU.mult), r=[self.vecs.k], w=[self.hgb.k])
        P.op("pool", lambda e: e.memset(self.lnhalf[:], -0.6931471805599453), w=[self.lnhalf.k])
        stg = self.phase_view(0, [128, 2048], F32)
        stk = Trk()
        self.phase_cur = [stk]
        self.conv_done = 0

        def ctab(name, n, dst, dstk, eng):
            P.dma("sp", stg[:, 0:n], self.cst_d[:, CST[name]:CST[name] + n], w=[stk])
            if eng == "act":
                P.op("act", lambda e: e.activation(out=dst, in_=stg[:, 0:n], func=AF.Copy), r=[stk], w=[dstk])
            else:
                P.op(eng, lambda e: e.tensor_copy(out=dst, in_=stg[:, 0:n]), r=[stk], w=[dstk])

        ctab("cos", 2048, self.cos_b[:], self.cos_b.k, "dve")
        ctab("sin", 2048, self.sin_b[:], self.sin_b.k, "act")
        ctab("cmask", 2048, self.cmask_b[:], self.cmask_b.k, "dve")
        ctab("triL", 128, self.triL_b[:], self.triL_b.k, "dve")
        ctab("triU", 128, self.triU_b[:], self.triU_b.k, "dve")
        ctab("prot", 128, self.prot_b[:], self.prot_b.k, "dve")
        ctab("bones", 128, self.bones_b[:], self.bones_b.k, "dve")
        ctab("selb", 512, self.selb[:].rearrange("p a b -> p (a b)"), self.selb.k, "dve")
        P.op("pool", lambda e: e.memset(self.VC[:], 0.0), w=[self.VC.k])
        P.op("pool", lambda e: e.memset(self.KC[:], 0.0), w=[self.KC.k])
        P.op("pool", lambda e: e.memset(self.VC[:, :, 64:65], 1.0), w=[self.VC.k])
        P.dma("sp", stg[:, 0:32], self.cst_d[:, CST["ovl"]:CST["ovl"] + 32], w=[stk])
        for g in range(4):
            P.op("dve", lambda e: e.tensor_copy(out=self.VC[:, g, 65:97], in_=stg[:, 0:32]), r=[stk], w=[self.VC.k])
        for kv in range(2):
            o0 = VEC[("c_pos", kv)]
            P.op("dve", lambda e: e.tensor_copy(out=self.posb[:, kv, :], in_=self.vecs[:, o0:o0 + 32]), r=[self.vecs.k], w=[self.posb.k])
        self.conv_done = 0

    def convert_upto(self, nchunks):
        nchunks = min(nchunks, len(self.wchunk))
        for i in range(self.conv_done, nchunks):
            src = self.wall[i * CONV_CH:(i + 1) * CONV_CH].rearrange("(p f) -> p f", p=128)
            dst = self.wbf[i * CONV_CH:(i + 1) * CONV_CH].rearrange("(p f) -> p f", p=128)
            self.P.dma("pool", dst, src, w=[self.wchunk[i]])
        self.conv_done = max(self.conv_done, nchunks)

    def convert_phase(self, ph):
        ends = self.L.phase_end
        ph = min(ph, len(ends) - 1)
        self.convert_upto((ends[ph] + CONV_CH - 1) // CONV_CH)

    def load_x(self, b):
        P = self.P
        xin = [(self.phase_view(j * 4096, [128, 1024], F32), Trk()) for j in range(4)]
        self.new_phase([k for _, k in xin])
        for n in range(NT):
            for j in range(4):
                t0 = n * TT + j * 128
                P.dma("sp", xin[j][0], self.x[b, t0:t0 + 128, :], w=[xin[j][1]])
            for c in range(NCH):
                pb = self.bank()
                for j in range(4):
                    P.op("pe", lambda e: e.transpose(out=pb[:, j * 128:(j + 1) * 128], in_=xin[j][0][:, c * 128:(c + 1) * 128], identity=self.ident_f[:]),
                         r=[xin[j][1], self.ident_f.k], w=[pb.k])
                eng = "act" if c % 2 == 0 else "dve"
                dst = self.hT[:, c, n * TT:(n + 1) * TT]
                if eng == "act":
                    P.op("act", lambda e: e.activation(out=dst, in_=pb[:], func=AF.Copy), r=[pb.k], w=[self.hk[c][n]])
                else:
                    P.op("dve", lambda e: e.tensor_copy(out=dst, in_=pb[:]), r=[pb.k], w=[self.hk[c][n]])

    def store_y(self, b):
        P = self.P
        yo = [(self.phase_view(j * 4096, [128, 1024], F32), Trk()) for j in range(4)]
        self.new_phase([k for _, k in yo])
        for n in range(NT):
            for j in range(4):
                t0 = n * TT + j * 128
                for half in range(2):
                    pb = self.bank()
                    for cc in range(4):
                        c = half * 4 + cc
                        P.op("pe", lambda e: e.transpose(out=pb[:, cc * 128:(cc + 1) * 128], in_=self.hT[:, c, t0:t0 + 128], identity=self.ident_f[:]),
                             r=[self.hk[c][n], self.ident_f.k], w=[pb.k])
                    dst = yo[j][0][:, half * 512:(half + 1) * 512]
                    wl = [yo[j][1]]
                    if half == 0:
                        P.op("act", lambda e: e.activation(out=dst, in_=pb[:], func=AF.Copy), r=[pb.k], w=wl)
                    else:
                        P.op("dve", lambda e: e.tensor_copy(out=dst, in_=pb[:]), r=[pb.k], w=wl)
                ot = Trk()
                P.dma("sp", self.y[b, t0:t0 + 128, :], yo[j][0], r=[yo[j][1]], w=[ot])
                self.out_trks.append(ot)

    def norm_tile(self, n, gname, u_ap, u_k, sq_bufs, rstd_buf, pool=None):
        P = self.P
        pb = self.bank(pool)
        for c in range(NCH):
            sq, sk = sq_bufs[c % len(sq_bufs)]
            hsl = self.hT[:, c, n * TT:(n + 1) * TT]
            P.op("act", lambda e: e.activation(out=sq, in_=hsl, func=AF.Square), r=[self.hk[c][n]], w=[sk])
            P.op("pe", lambda e: e.matmul(pb[:], lhsT=self.ones_b[:], rhs=sq, start=(c == 0), stop=(c == NCH - 1)),
                 r=[sk, self.ones_b.k], w=[pb.k])
        rs, rk = rstd_buf
        P.op("act", lambda e: e.activation(out=rs, in_=pb[:], func=AF.Ln, scale=1.0 / D, bias=EPS), r=[pb.k], w=[rk])
        P.op("act", lambda e: e.activation(out=rs, in_=rs, func=AF.Exp, scale=-0.5), r=[rk], w=[rk])
        for c in range(NCH):
            hsl = self.hT[:, c, n * TT:(n + 1) * TT]
            g = self.vec(gname, c)
            P.op("dve", lambda e: e.scalar_tensor_tensor(out=u_ap[:, c, :], in0=hsl, scalar=g, in1=rs, op0=ALU.mult, op1=ALU.mult),
                 r=[self.hk[c][n], rk, self.vecs.k], w=[u_k])

    def a_phase_setup(self):
        v = self.phase_view
        A = {}
        A["u"] = [(v(n * 8192, [128, 8, 512], BF16), Trk()) for n in range(NT)]
        A["m"] = [(v(32768 + n * 8192, [128, 8, 512], BF16), Trk()) for n in range(NT)]
        A["sq"] = [(v(65536 + i * 1024, [128, 512], BF16), Trk()) for i in range(2)]
        A["rstd"] = (v(67584, [128, 512], F32), Trk())
        A["xp"] = [(v(69632 + i * 2080, [128, 516], F32), Trk()) for i in range(2)]
        base = 73792
        s1 = []
        for i in range(3):
            o = base + i * 4096
            xr = (v(o + 2048, [128, 512], F32), Trk())
            s1.append({"y": (v(o, [128, 512], BF16), Trk()), "xrb": (v(o + 1024, [128, 512], BF16), Trk()), "xr": xr, "hr": xr})
        s2 = []
        for i in range(2):
            o = base + 12288 + i * 6144
            rr = (v(o, [128, 512], F32), Trk())
            s2.append({"r": rr, "a2": rr, "i": (v(o + 2048, [128, 512], F32), Trk()), "a": (v(o + 4096, [128, 512], F32), Trk())})
        sets = s1 + s2
        A["s1"] = s1
        A["s2"] = s2
        A["s1_extra"] = {"y": (v(65536, [128, 512], BF16), Trk()), "xrb": (v(66560, [128, 512], BF16), Trk())}
        xr4 = (v(67584, [128, 512], F32), Trk())
        A["s1_extra"]["xr"] = xr4
        A["s1_extra"]["hr"] = xr4
        A["sets"] = sets
        trks = [k for _, k in A["u"]] + [k for _, k in A["m"]] + [A["rstd"][1]] + [k for _, k in A["sq"]] + [k for _, k in A["xp"]]
        for st in sets:
            trks += [k for _, k in st.values()]
        self.new_phase(list(dict((id(t), t) for t in trks).values()))
        self.begin_phase([])
        self.A = A

    def a_layer(self, l):
        P = self.P
        self.a_phase_setup()
        A = self.A
        P.op("pool", lambda e: e.memset(self.convc[:], 0.0), w=[self.convc.k])
        P.op("pool", lambda e: e.memset(self.hst[:], 0.0), w=[self.hst.k])
        for n in range(NT):
            self.norm_tile(n, ("a_norm", l), A["u"][n][0], A["u"][n][1], A["sq"], A["rstd"])
        merged = {}
        for t in [k for _, k in A["sq"]] + [A["rstd"][1]]:
            if t.w is not None:
                merged[t.w[0]] = max(merged.get(t.w[0], 0), t.w[1])
            for kk_, vv_ in t.r.items():
                merged[kk_] = max(merged.get(kk_, 0), vv_)
        ex = A["s1_extra"]
        for _, k in ex.values():
            k.w = None
            k.r = dict(merged)
            if k not in self.phase_cur:
                self.phase_cur.append(k)
        S1 = A["s1"] + [ex]
        items = [(c, n) for c in range(NCH) for n in range(NT)]
        wslot = {}

        def stage1(it):
            c, n = items[it]
            if n == 0:
                wslot[c] = self.next_w(("a_in", l, c), hold=True)
            w = wslot[c]
            u, uk = A["u"][n]
            pg = self.bank()
            pr = self.bank()
            for k in range(NCH):
                P.op("pe", lambda e: e.matmul(pg[:], lhsT=w[:, k * 128:(k + 1) * 128], rhs=u[:, k, :], start=(k == 0), stop=(k == NCH - 1)),
                     r=[w.k, uk], w=[pg.k])
            for k in range(NCH):
                P.op("pe", lambda e: e.matmul(pr[:], lhsT=w[:, 1024 + k * 128:1024 + (k + 1) * 128], rhs=u[:, k, :], start=(k == 0), stop=(k == NCH - 1)),
                     r=[w.k, uk], w=[pr.k])
            st = S1[it % 4]
            xp, xpk = A["xp"][it % 2]
            xpn, xpnk = A["xp"][(it + 1) % 2]
            y, yk = st["y"]
            xr, xrk = st["xr"]
            xrb, xrbk = st["xrb"]
            P.op("act", lambda e: e.activation(out=y, in_=pg[:], func=AF.Gelu_apprx_tanh), r=[pg.k], w=[yk])
            if n == 0:
                P.op("pool", lambda e: e.memset(xp[:, 0:3], 0.0), w=[xpk])
            P.op("act", lambda e: e.activation(out=xp[:, 3:515], in_=pr[:], func=AF.Copy), r=[pr.k], w=[xpk])
            if n < NT - 1:
                P.op("pool", lambda e: e.tensor_copy(out=xpn[:, 0:3], in_=xp[:, 512:515]), r=[xpk], w=[xpnk])
            P.op("dve", lambda e: e.tensor_scalar(out=xr, in0=xp[:, 0:512], scalar1=self.vec(("a_cw", l, 0), c), scalar2=self.vec(("a_cb", l), c),
                                                  op0=ALU.mult, op1=ALU.add), r=[xpk, self.vecs.k], w=[xrk])
            for kk in range(1, 4):
                P.op("dve", lambda e: e.scalar_tensor_tensor(out=xr, in0=xp[:, kk:kk + 512], scalar=self.vec(("a_cw", l, kk), c), in1=xr,
                                                             op0=ALU.mult, op1=ALU.add), r=[xpk, xrk, self.vecs.k], w=[xrk])
            P.op("dve", lambda e: e.tensor_copy(out=xrb, in_=xr), r=[xrk], w=[xrbk])

        def stage2(it):
            c, n = items[it]
            w = wslot[c]
            m, mk = A["m"][n]
            st = S1[it % 4]
            y, yk = st["y"]
            xr, xrk = st["xr"]
            xrb, xrbk = st["xrb"]
            hr, hrk = st["hr"]
            hprev, hprevk = S1[(it - 1) % 4]["hr"]
            t2 = A["s2"][it % 2]
            rr, rrk = t2["r"]
            ii, iik = t2["i"]
            aa, aak = t2["a"]
            p1 = self.bank()
            p2 = self.bank()
            P.op("pe", lambda e: e.matmul(p1[:], lhsT=w[:, 2048:2176], rhs=xrb, start=True, stop=True), r=[w.k, xrbk], w=[p1.k])
            P.op("pe", lambda e: e.matmul(p2[:], lhsT=w[:, 2176:2304], rhs=xrb, start=True, stop=True), r=[w.k, xrbk], w=[p2.k])
            if n == NT - 1:
                self.release_slot(w)
            P.op("act", lambda e: e.activation(out=rr, in_=p1[:], func=AF.Tanh, scale=0.5, bias=self.hgb[:, l, 0, c:c + 1]), r=[p1.k, self.hgb.k], w=[rrk])
            P.op("act", lambda e: e.activation(out=ii, in_=p2[:], func=AF.Tanh, scale=0.5, bias=self.hgb[:, l, 1, c:c + 1]), r=[p2.k, self.hgb.k], w=[iik])
            P.op("act", lambda e: e.activation(out=aa, in_=rr, func=AF.Exp, scale=self.coef[:, l, 0, c:c + 1], bias=self.coef[:, l, 0, c:c + 1]), r=[rrk, self.coef.k], w=[aak])
            P.op("act", lambda e: e.activation(out=rr, in_=rr, func=AF.Exp, scale=self.coef[:, l, 1, c:c + 1], bias=self.coef[:, l, 1, c:c + 1]), r=[rrk, self.coef.k], w=[rrk])
            P.op("act", lambda e: e.activation(out=rr, in_=rr, func=AF.Ln, scale=-0.999999, bias=1.0), r=[rrk], w=[rrk])
            P.op("act", lambda e: e.activation(out=rr, in_=rr, func=AF.Exp, scale=0.5, bias=self.lnhalf[:, 0:1]), r=[rrk, self.lnhalf.k], w=[rrk])
            P.op("dve", lambda e: e.scalar_tensor_tensor(out=ii, in0=ii, scalar=1.0, in1=xr, op0=ALU.add, op1=ALU.mult), r=[iik, xrk], w=[iik])
            P.op("dve", lambda e: e.tensor_tensor(out=ii, in0=ii, in1=rr, op=ALU.mult), r=[iik, rrk], w=[iik])
            if n == 0:
                P.op("dve", lambda e: e.tensor_tensor_scan(out=hr, data0=aa, data1=ii, initial=0.0, op0=ALU.mult, op1=ALU.add), r=[aak, iik], w=[hrk])
            else:
                P.op("dve", lambda e: e.tensor_tensor_scan(out=hr, data0=aa, data1=ii, initial=hprev[:, 511:512], op0=ALU.mult, op1=ALU.add),
                     r=[aak, iik, hprevk], w=[hrk])
            P.op("pool", lambda e: e.tensor_tensor(out=m[:, c, :], in0=hr, in1=y, op=ALU.mult), r=[hrk, yk], w=[mk])

        LA = 2
        for it in range(min(LA, len(items))):
            stage1(it)
        for it in range(len(items)):
            if it + LA < len(items):
                stage1(it + LA)
            stage2(it)
        for mm in range(NCH):
            w = self.next_w(("a_out", l, mm))
            for n in range(NT):
                m, mk = A["m"][n]
                po = self.bank()
                for k in range(NCH):
                    P.op("pe", lambda e: e.matmul(po[:], lhsT=w[:, k * 128:(k + 1) * 128], rhs=m[:, k, :], start=(k == 0), stop=(k == NCH - 1)),
                         r=[w.k, mk], w=[po.k])
                hsl = self.hT[:, mm, n * TT:(n + 1) * TT]
                P.op("dve", lambda e: e.tensor_tensor(out=hsl, in0=po[:], in1=hsl, op=ALU.add), r=[po.k, self.hk[mm][n]], w=[self.hk[mm][n]])

    def ffn_phase_setup(self):
        v = self.phase_view
        Fz = {}
        Fz["u"] = [(v(s * 8192, [128, 8, 512], BF16), Trk()) for s in range(2)]
        Fz["act"] = [[(v(16384 + (c * 2 + s) * 1024, [128, 512], BF16), Trk()) for s in range(2)] for c in range(FC)]
        Fz["sq"] = [(v(61440 + i * 1024, [128, 512], BF16), Trk()) for i in range(2)]
        Fz["rstd"] = (v(63488, [128, 512], F32), Trk())
        Fz["sg"] = [(v(65536 + i * 1024, [128, 512], BF16), Trk()) for i in range(4)]
        ext = self.ext_slots(69632)
        trks = [k for _, k in Fz["u"]] + [k for row in Fz["act"] for _, k in row] + [k for _, k in Fz["sq"]] + [Fz["rstd"][1]] + [k for _, k in Fz["sg"]] + [b.k for b in ext]
        self.new_phase(trks)
        self.begin_phase(ext)
        self.F = Fz

    def ffn_tile(self, L, t2):
        P = self.P
        Fz = self.F
        if t2 == 0:
            for s in range(2):
                self.norm_tile(2 * t2 + s, ("f_norm", L), Fz["u"][s][0], Fz["u"][s][1], Fz["sq"], Fz["rstd"])
        nsg = 0
        for c in range(FC):
            w = self.next_w(("f_in", L, c))
            for s in range(2):
                u, uk = Fz["u"][s]
                pg = self.bank()
                pu = self.bank()
                for k in range(NCH):
                    P.op("pe", lambda e: e.matmul(pg[:], lhsT=w[:, k * 128:(k + 1) * 128], rhs=u[:, k, :], start=(k == 0), stop=(k == NCH - 1)),
                         r=[w.k, uk], w=[pg.k])
                for k in range(NCH):
                    P.op("pe", lambda e: e.matmul(pu[:], lhsT=w[:, 1024 + k * 128:1024 + (k + 1) * 128], rhs=u[:, k, :], start=(k == 0), stop=(k == NCH - 1)),
                         r=[w.k, uk], w=[pu.k])
                sg, sgk = Fz["sg"][nsg % 4]
                nsg += 1
                a, ak = Fz["act"][c][s]
                P.op("act", lambda e: e.activation(out=sg, in_=pg[:], func=AF.Silu), r=[pg.k], w=[sgk])
                P.op("dve", lambda e: e.tensor_tensor(out=a, in0=sg, in1=pu[:], op=ALU.mult), r=[sgk, pu.k], w=[ak])
        HC = FC // 2
        for mm in range(NCH):
            pos_ = [self.bank(), self.bank()]
            for hf in range(2):
                w = self.next_w(("f_out", L, mm, hf))
                for s in range(2):
                    po = pos_[s]
                    for cc in range(HC):
                        c = hf * HC + cc
                        a, ak = Fz["act"][c][s]
                        P.op("pe", lambda e: e.matmul(po[:], lhsT=w[:, cc * 128:(cc + 1) * 128], rhs=a, start=(c == 0), stop=(c == FC - 1)),
                             r=[w.k, ak], w=[po.k])
            for s in range(2):
                n = 2 * t2 + s
                po = pos_[s]
                hsl = self.hT[:, mm, n * TT:(n + 1) * TT]
                P.op("dve", lambda e: e.tensor_tensor(out=hsl, in0=po[:], in1=hsl, op=ALU.add), r=[po.k, self.hk[mm][n]], w=[self.hk[mm][n]])
            if t2 == 0 and mm in (2, 5):
                s_ = 0 if mm == 2 else 1
                self.norm_tile(2 + s_, ("f_norm", L), Fz["u"][s_][0], Fz["u"][s_][1], Fz["sq"], Fz["rstd"])

    def ffn_layer(self, L):
        self.ffn_phase_setup()
        for t2 in range(2):
            self.ffn_tile(L, t2)

    def headnorm_rope(self, pk, R, C, gvec, cos_ap, sin_ap, T, bias=None):
        P = self.P
        sq, sqk = T["sq"]
        rs, rsk = T["rs"]
        qn, qnk = T["qn"]
        t1, t1k = T["t1"]
        t2, t2k = T["t2"]
        if bias is not None:
            xf, xfk = T["xf"]
            P.op("act", lambda e: e.activation(out=xf[0:R, 0:C], in_=pk[0:R, 0:C], func=AF.Identity, bias=bias), r=[pk.k, self.vecs.k], w=[xfk])
            src, srck = xf[0:R, 0:C], xfk
        else:
            src, srck = pk[0:R, 0:C], pk.k
        P.op("act", lambda e: e.activation(out=sq[0:R, 0:C], in_=src, func=AF.Square), r=[srck], w=[sqk])
        pss = self.bank(self.pool_misc)
        P.op("pe", lambda e: e.matmul(pss[0:R, 0:C], lhsT=self.bones_b[0:R, 0:R], rhs=sq[0:R, 0:C], start=True, stop=True), r=[sqk, self.bones_b.k], w=[pss.k])
        P.op("act", lambda e: e.activation(out=rs[0:R, 0:C], in_=pss[0:R, 0:C], func=AF.Ln, scale=1.0 / 64.0, bias=EPS), r=[pss.k], w=[rsk])
        P.op("act", lambda e: e.activation(out=rs[0:R, 0:C], in_=rs[0:R, 0:C], func=AF.Exp, scale=-0.5), r=[rsk], w=[rsk])
        P.op("dve", lambda e: e.scalar_tensor_tensor(out=qn[0:R, 0:C], in0=src, scalar=gvec, in1=rs[0:R, 0:C], op0=ALU.mult, op1=ALU.mult),
             r=[srck, rsk, self.vecs.k], w=[qnk])
        prt = self.bank(self.pool_misc)
        P.op("pe", lambda e: e.matmul(prt[0:R, 0:C], lhsT=self.prot_b[0:R, 0:R], rhs=qn[0:R, 0:C], start=True, stop=True), r=[qnk, self.prot_b.k], w=[prt.k])
        P.op("pool", lambda e: e.tensor_tensor(out=t1[0:R, 0:C], in0=qn[0:R, 0:C], in1=cos_ap, op=ALU.mult), r=[qnk, self.cos_b.k], w=[t1k])
        P.op("dve", lambda e: e.tensor_tensor(out=t2[0:R, 0:C], in0=prt[0:R, 0:C], in1=sin_ap, op=ALU.mult), r=[prt.k, self.sin_b.k], w=[t2k])

    def kv_phase(self):
        P = self.P
        v = self.phase_view
        U = [(v(n * 8192, [128, 8, 512], BF16), Trk()) for n in range(NT)]
        sqn = [(v(32768 + i * 1024, [128, 512], BF16), Trk()) for i in range(2)]
        rstd = (v(34816, [128, 512], F32), Trk())
        kcT, kcTk = v(36864, [128, 8, S], BF16), [Trk() for _ in range(8)]
        cw1, cw1k = v(0, [128, 2, 32, 256], BF16), Trk()
        T = {"sq": (v(69632, [128, 512], BF16), Trk()), "rs": (v(70656, [128, 512], F32), Trk()), "qn": (v(72704, [128, 512], BF16), Trk()),
             "t1": (v(73728, [128, 512], F32), Trk()), "t2": (v(75776, [128, 512], F32), Trk()), "xf": (v(77824, [128, 512], F32), Trk())}
        kout = [(v(79872 + i * 1024, [128, 512], BF16), Trk()) for i in range(2)]
        vst = [(v(81920 + i * 528, [128, 4, 66], BF16), Trk()) for i in range(2)]
        hid = [(v(83008 + i * 512, [128, 2, 128], BF16), Trk()) for i in range(2)]
        ext = self.ext_slots(84032)
        trks = [k for _, k in U] + [rstd[1], cw1k] + [k for _, k in sqn] + kcTk + [k for _, k in T.values()] + [k for _, k in kout] + [k for _, k in vst] \
            + [k for _, k in hid] + [b.k for b in ext]
        self.new_phase(trks)
        self.begin_phase(ext)
        self.pool_misc = [0, 1, 2, 3, 4, 5, 6, 7]
        for i in range(2):
            P.op("pool", lambda e: e.memset(vst[i][0][:, :, 64:66], 1.0), w=[vst[i][1]])
        for n in range(NT):
            self.norm_tile(n, ("kv_norm",), U[n][0], U[n][1], sqn, rstd)
        nko = 0
        nvs = 0
        for i in range(4):
            which, cc = i // 2, i % 2
            w = self.next_w(("kvk", i))
            for n in range(NT):
                tsl = slice(n * TT, (n + 1) * TT)
                u, uk = U[n]
                pk = self.bank()
                for k in range(NCH):
                    P.op("pe", lambda e: e.matmul(pk[:], lhsT=w[:, k * 128:(k + 1) * 128], rhs=u[:, k, :], start=(k == 0), stop=(k == NCH - 1)), r=[w.k, uk], w=[pk.k])
                self.headnorm_rope(pk, 128, 512, self.vec(("k_norm", 1 + which)), self.cos_b[:, tsl], self.sin_b[:, tsl], T)
                ko, kok = kout[nko % 2]
                nko += 1
                P.op("dve", lambda e: e.tensor_tensor(out=ko, in0=T["t1"][0], in1=T["t2"][0], op=ALU.add), r=[T["t1"][1], T["t2"][1]], w=[kok])
                for hh in range(2):
                    P.dma("sp", self.KSd[which, :, 2 * cc + hh, tsl], ko[hh * 64:(hh + 1) * 64, :], r=[kok], w=[self.ksd_k[which][2 * cc + hh][n]])
        for i in range(4):
            sel, cc = i // 2, i % 2
            w = self.next_w(("kvc", i))
            for n in range(NT):
                tsl = slice(n * TT, (n + 1) * TT)
                u, uk = U[n]
                for gg in range(2):
                    pc = self.bank()
                    for k in range(NCH):
                        P.op("pe", lambda e: e.matmul(pc[0:64, :], lhsT=w[:, k * 128 + gg * 64:k * 128 + gg * 64 + 64], rhs=u[:, k, :], start=(k == 0), stop=(k == NCH - 1)),
                             r=[w.k, uk], w=[pc.k])
                    idx = sel * 4 + 2 * cc + gg
                    P.op("act", lambda e: e.activation(out=kcT[0:64, idx, tsl], in_=pc[0:64, :], func=AF.Copy), r=[pc.k], w=[kcTk[idx]])
        for i in range(2):
            w = self.next_w(("kvv", i))
            for n in range(NT):
                u, uk = U[n]
                for jb in range(4):
                    pv = self.bank()
                    for k in range(NCH):
                        P.op("pe", lambda e: e.matmul(pv[:, 0:256], lhsT=u[:, k, jb * 128:(jb + 1) * 128], rhs=w[:, k * 256:(k + 1) * 256], start=(k == 0), stop=(k == NCH - 1)),
                             r=[w.k, uk], w=[pv.k])
                    vs, vsk = vst[nvs % 2]
                    nvs += 1
                    P.op("act", lambda e: e.activation(out=vs[:, :, 0:64], in_=pv[:, 0:256].rearrange("p (g d) -> p g d", g=4), func=AF.Copy), r=[pv.k], w=[vsk])
                    P.dma("sp", self.Vd[i, :, 4 * n + jb, :, :], vs, r=[vsk], w=[self.vd_k[i][n]])
        for kv in range(2):
            off = self.L.off[("cw1", kv)]
            src = self.wbf[off:off + 128 * 8192].rearrange("(p f) -> p f", p=128)
            c0, c1 = off // CONV_CH, (off + 128 * 8192 - 1) // CONV_CH
            P.dma("sp", cw1[0:64, kv].rearrange("p a b -> p (a b)"), src[0:64, :], r=[self.wchunk[c] for c in range(c0, c1 + 1)], w=[cw1k] + [k for _, k in U])
        w2 = self.next_w(("cw2",))
        for sel in range(2):
            pcv = self.bank()
            for cc in range(2):
                for l in range(32):
                    P.op("pe", lambda e: e.matmul(pcv[:, cc:cc + 1], lhsT=cw1[0:64, sel, l, cc * 128:(cc + 1) * 128], rhs=self.posb[0:64, sel, l:l + 1],
                                                  start=(cc == 0 and l == 0), stop=(cc == 1 and l == 31)), r=[cw1k, self.posb.k], w=[pcv.k])
            b1o = VEC[("c_b1", sel)]
            P.op("dve", lambda e: e.tensor_tensor(out=self.cbias[:, sel, :], in0=pcv[:, 0:2], in1=self.vecs[:, b1o:b1o + 2], op=ALU.add), r=[pcv.k, self.vecs.k], w=[self.cbias.k])
        nh = 0
        for sel in range(2):
            for g in range(4):
                hd, hdk = hid[nh % 2]
                nh += 1
                for cc in range(2):
                    ph = self.bank()
                    for l in range(32):
                        P.op("pe", lambda e: e.matmul(ph[:, 0:127], lhsT=cw1[0:64, sel, l, cc * 128:(cc + 1) * 128], rhs=kcT[0:64, sel * 4 + g, l:l + 16 * 126 + 1:16],
                                                      start=(l == 0), stop=(l == 31)), r=[cw1k, kcTk[sel * 4 + g]], w=[ph.k])
                    P.op("act", lambda e: e.activation(out=hd[:, cc, 0:127], in_=ph[:, 0:127], func=AF.Gelu_apprx_tanh, bias=self.cbias[:, sel, cc:cc + 1]),
                         r=[ph.k, self.cbias.k], w=[hdk])
                if sel == 0:
                    pk = self.bank()
                    for cc in range(2):
                        P.op("pe", lambda e: e.matmul(pk[0:64, 0:127], lhsT=w2[:, (0 * 2 + cc) * 64:(0 * 2 + cc) * 64 + 64], rhs=hd[:, cc, 0:127], start=(cc == 0), stop=(cc == 1)),
                             r=[w2.k, hdk], w=[pk.k])
                    self.headnorm_rope(pk, 64, 127, self.vecs[0:64, VEC[("k_norm", 0)]:VEC[("k_norm", 0)] + 1],
                                       self.cos_b[0:64, 31:31 + 16 * 126 + 1:16], self.sin_b[0:64, 31:31 + 16 * 126 + 1:16], T,
                                       bias=self.vecs[0:64, VEC[("c_b2", 0)]:VEC[("c_b2", 0)] + 1])
                    P.op("dve", lambda e: e.tensor_tensor(out=self.KC[0:64, g, 0:127], in0=T["t1"][0][0:64, 0:127], in1=T["t2"][0][0:64, 0:127], op=ALU.add),
                         r=[T["t1"][1], T["t2"][1]], w=[self.KC.k])
                else:
                    pv = self.bank()
                    for cc in range(2):
                        P.op("pe", lambda e: e.matmul(pv[0:127, 0:64], lhsT=hd[:, cc, 0:127], rhs=w2[:, (1 * 2 + cc) * 64:(1 * 2 + cc) * 64 + 64], start=(cc == 0), stop=(cc == 1)),
                             r=[w2.k, hdk], w=[pv.k])
                    P.op("dve", lambda e: e.tensor_tensor(out=self.VC[0:127, g, 0:64], in0=pv[0:127, 0:64], in1=self.bvecs[0:127, BV_CB2V:BV_CB2V + 64], op=ALU.add),
                         r=[pv.k, self.bvecs.k], w=[self.VC.k])

    def b_layer(self, j):
        P = self.P
        v = self.phase_view
        KS, KSk = v(0, [128, 4, S], BF16), [Trk() for _ in range(4)]
        KW, KWk = v(16384, [128, 4, S], BF16), [Trk() for _ in range(4)]
        VS, VSk = v(32768, [128, 16, 4, 66], BF16), Trk()
        VW, VWk = v(41216, [128, 16, 4, 66], BF16), Trk()
        u, uk = v(49664, [128, 8, 512], BF16), Trk()
        Q, Qk = v(57856, [128, 4, 16, 128], BF16), [Trk() for _ in range(4)]
        Nk = [[Trk() for _ in range(4)] for _ in range(4)]
        oT, oTk = v(74240, [128, 8, 512], BF16), Trk()
        ob = [(v(82432 + i * 2048, [128, 1024], BF16), Trk()) for i in range(2)]
        PT = [(v(86528 + i * 1024, [128, 512], BF16), Trk()) for i in range(4)]
        sqn = [(v(90624 + i * 1024, [128, 512], BF16), Trk()) for i in range(2)]
        rstd = (v(92672, [128, 512], F32), Trk())
        gat, gatk = v(94720, [128, 4, 48], F32), Trk()
        T = {"sq": sqn[0], "rs": rstd, "qn": (v(95488, [128, 512], BF16), Trk()),
             "t1": (v(96512, [128, 512], BF16), Trk()), "t2": (v(97536, [128, 512], BF16), Trk())}
        rd, rdk = v(98560, [128, 16], F32), Trk()
        s3, s3k = v(98624, [128, 12], F32), Trk()
        sc, sck = v(98688, [128, 32], F32), Trk()
        sc2, sc2k = v(98816, [128, 32], F32), Trk()
        nsl, nslk = v(98944, [128, 32], F32), Trk()
        m8, m8k = v(99072, [128, 16], F32), Trk()
        acc, acck = v(99136, [128, 256], F32), Trk()
        trks = KSk + KWk + [VSk, VWk, uk, oTk, gatk, rdk, s3k, sck, sc2k, nslk, m8k, acck] + Qk + [k for _, k in ob] + [k for _, k in PT] \
            + [k for _, k in sqn] + [rstd[1]] + [T["qn"][1], T["t1"][1], T["t2"][1]] + [k for row in Nk for k in row]
        self.new_phase(trks)
        self.begin_phase([])
        self.pool_misc = [6]
        pool_st = [0, 1, 2]
        for g in range(4):
            P.dma("sp", KS[0:64, g, :], self.KSd[0, :, g, :], r=self.ksd_k[0][g], w=[KSk[g]])
            P.dma("sp", KW[0:64, g, :], self.KSd[1, :, g, :], r=self.ksd_k[1][g], w=[KWk[g]])
        P.dma("sp", VS.rearrange("p a b c -> p (a b c)"), self.Vd[0].rearrange("p a b c -> p (a b c)"), r=self.vd_k[0], w=[VSk])
        P.dma("sp", VW.rearrange("p a b c -> p (a b c)"), self.Vd[1].rearrange("p a b c -> p (a b c)"), r=self.vd_k[1], w=[VWk])
        est = self.ph[64:96, 57856 // 4:57856 // 4 + 2048]
        allq = Qk + [k for row in Nk for k in row]
        P.dma("sp", est, self.cst_d[64:96, CST["eind"]:CST["eind"] + 2048], w=allq)
        for g in range(4):
            P.op("dve", lambda e: e.tensor_copy(out=KS[64:96, g, :], in_=est), r=allq, w=[KSk[g]])
            P.op("pool", lambda e: e.memset(KW[64:96, g, :], 0.0), w=[KWk[g]])
        P.op("pool", lambda e: e.memset(Q[64:96].rearrange("p a b c -> p (a b c)"), 0.0), w=allq)
        gbo = BV_GB + 48 * j
        gats = [(gat, gatk), (v(91648, [128, 4, 48], F32), Trk())]
        self.phase_cur.append(gats[1][1])
        Qn, Qnk = self.Qnext, self.Qnext.k
        sq1 = [sqn[0]]
        pqb = self.ps[7]

        def prep_pieces(n):
            tsl = slice(n * TT, (n + 1) * TT)
            gt, gtk = gats[n % 2]
            pieces = []

            def p_norm():
                self.norm_tile(n, ("b_norm", j), u, uk, sq1, rstd, pool=[6])

            def p_gates():
                wg = self.next_w(("bg", j))
                for qb in range(4):
                    pg = self.bank([6])
                    for k in range(NCH):
                        P.op("pe", lambda e: e.matmul(pg[:, 0:48], lhsT=u[:, k, qb * 128:(qb + 1) * 128], rhs=wg[:, k * 48:(k + 1) * 48], start=(k == 0), stop=(k == NCH - 1)),
                             r=[wg.k, uk], w=[pg.k])
                    P.op("dve", lambda e: e.tensor_tensor(out=gt[:, qb, :], in0=pg[:, 0:48], in1=self.bvecs[:, gbo:gbo + 48], op=ALU.add), r=[pg.k, self.bvecs.k], w=[gtk])
                P.op("act", lambda e: e.activation(out=gt, in_=gt, func=AF.Tanh, scale=0.5), r=[gtk], w=[gtk])
                P.op("dve", lambda e: e.tensor_scalar(out=gt, in0=gt, scalar1=0.5, scalar2=0.5, op0=ALU.mult, op1=ALU.add), r=[gtk], w=[gtk])

            pieces += [p_norm, p_gates]
            sq, sqk = T["sq"]
            rs, rsk = T["rs"]
            qn, qnk = T["qn"]
            t1, t1k = T["t1"]
            t2, t2k = T["t2"]
            gq = self.vec(("q_norm", j))
            for m in range(NCH):
                def p_d(m=m):
                    w = self.next_w(("bq", j, m))
                    for k in range(NCH):
                        P.op("pe", lambda e: e.matmul(pqb[:], lhsT=w[:, k * 128:(k + 1) * 128], rhs=u[:, k, :], start=(k == 0), stop=(k == NCH - 1)), r=[w.k, uk], w=[pqb.k])
                    P.op("act", lambda e: e.activation(out=sq, in_=pqb[:], func=AF.Square), r=[pqb.k], w=[sqk])

                def p_e(m=m):
                    pss = self.bank([6])
                    P.op("pe", lambda e: e.matmul(pss[:], lhsT=self.bones_b[:], rhs=sq, start=True, stop=True), r=[sqk, self.bones_b.k], w=[pss.k])
                    P.op("act", lambda e: e.activation(out=rs, in_=pss[:], func=AF.Ln, scale=1.0 / 64.0, bias=EPS), r=[pss.k], w=[rsk])
                    P.op("act", lambda e: e.activation(out=rs, in_=rs, func=AF.Exp, scale=-0.5), r=[rsk], w=[rsk])
                    P.op("dve", lambda e: e.scalar_tensor_tensor(out=qn, in0=pqb[:], scalar=gq, in1=rs, op0=ALU.mult, op1=ALU.mult), r=[pqb.k, rsk, self.vecs.k], w=[qnk])

                def p_f(m=m):
                    prt = self.bank([6])
                    P.op("pe", lambda e: e.matmul(prt[:], lhsT=self.prot_b[:], rhs=qn, start=True, stop=True), r=[qnk, self.prot_b.k], w=[prt.k])
                    P.op("pool", lambda e: e.tensor_tensor(out=t1, in0=qn, in1=self.cos_b[:, tsl], op=ALU.mult), r=[qnk, self.cos_b.k], w=[t1k])
                    P.op("dve", lambda e: e.tensor_tensor(out=t2, in0=prt[:], in1=self.sin_b[:, tsl], op=ALU.mult), r=[prt.k, self.sin_b.k], w=[t2k])
                    P.op("dve", lambda e: e.tensor_tensor(out=Qn[:, m, :], in0=t1, in1=t2, op=ALU.add), r=[t1k, t2k], w=[Qnk])

                pieces += [p_d, p_e, p_f]
            return pieces

        def install_q():
            for m in range(NCH):
                for hh in range(2):
                    h = 2 * m + hh
                    src = Qn[hh * 64:(hh + 1) * 64, m, :].rearrange("p (a b) -> p a b", a=4)
                    if (h % 2) == 0:
                        P.op("dve", lambda e: e.tensor_copy(out=Q[0:64, :, h, :], in_=src), r=[Qnk], w=Qk)
                    else:
                        P.op("act", lambda e: e.activation(out=Q[0:64, :, h, :], in_=src, func=AF.Copy), r=[Qnk], w=Qk)

        for f in prep_pieces(0):
            f()
        for n in range(NT):
            tsl = slice(n * TT, (n + 1) * TT)
            install_q()
            bg = prep_pieces(n + 1) if n + 1 < NT else []
            gt, gtk = gats[n % 2]
            self.attn_tile(n, dict(KS=KS, KSk=KSk, KW=KW, KWk=KWk, VS=VS, VSk=VSk, VW=VW, VWk=VWk, Q=Q, Qk=Qk, Nk=Nk, oT=oT, oTk=oTk, ob=ob, PT=PT,
                                   gat=gt, gatk=gtk, rd=rd, rdk=rdk, s3=s3, s3k=s3k, sc=sc, sck=sck, sc2=sc2, sc2k=sc2k, nsl=nsl, nslk=nslk,
                                   m8=m8, m8k=m8k, acc=acc, acck=acck, pool_st=pool_st), bg)
            for mm in range(NCH):
                w = self.next_w(("bo", j, mm))
                po = self.bank(pool_st)
                for k in range(NCH):
                    P.op("pe", lambda e: e.matmul(po[:], lhsT=w[:, k * 128:(k + 1) * 128], rhs=oT[:, k, :], start=(k == 0), stop=(k == NCH - 1)), r=[w.k, oTk], w=[po.k])
                hsl = self.hT[:, mm, tsl]
                P.op("dve", lambda e: e.tensor_tensor(out=hsl, in0=po[:], in1=hsl, op=ALU.add), r=[po.k, self.hk[mm][n]], w=[self.hk[mm][n]])
        self.pool_misc = [0, 1, 2, 3, 4, 5, 6, 7]

    def attn_tile(self, n, C, bg=()):
        P = self.P
        KS, KSk, KW, KWk, VS, VSk, VW, VWk = C["KS"], C["KSk"], C["KW"], C["KWk"], C["VS"], C["VSk"], C["VW"], C["VWk"]
        Q, Qk, Nk, oT, oTk, ob, PT = C["Q"], C["Qk"], C["Nk"], C["oT"], C["oTk"], C["ob"], C["PT"]
        gat, gatk, rd, rdk, s3, s3k = C["gat"], C["gatk"], C["rd"], C["rdk"], C["s3"], C["s3k"]
        sc, sck, sc2, sc2k, nsl, nslk, m8, m8k, acc, acck = C["sc"], C["sck"], C["sc2"], C["sc2k"], C["nsl"], C["nslk"], C["m8"], C["m8k"], C["acc"], C["acck"]
        pool_st = C["pool_st"]
        poc, pos, pow_ = self.ps[3], self.ps[4], self.ps[5]
        st = {"npt": 0}
        pairs = [(qb, g) for qb in range(4) for g in range(4)]
        jobs = []

        def r4(ap):
            return ap.rearrange("p (a b) -> p a b", a=4)

        def mk_job(kind, qb, g, kt, first, last, pidx):
            qbg = 4 * n + qb
            job = {"pre": [], "post": []}
            box = {}

            def st1():
                pst = self.bank(pool_st)
                pt, ptk = PT[st["npt"] % 4]
                st["npt"] += 1
                box["pt"], box["ptk"] = pt, ptk
                Q96 = Q[0:96, qb, 4 * g:4 * g + 4, :]
                if kind == "c":
                    P.op("pe", lambda e: e.matmul(pst[0:127, :], lhsT=self.KC[0:96, g, 0:127], rhs=Q96, start=True, stop=True), r=[self.KC.k, Qk[qb]], w=[pst.k])
                    P.op("act", lambda e: e.activation(out=pt[0:127, :], in_=pst[0:127, :], func=AF.Exp, scale=SCALE), r=[pst.k], w=[ptk])
                    cm = self.cmask_b[0:127, qbg * 128:(qbg + 1) * 128].unsqueeze(1).broadcast_to([127, 4, 128])
                    P.op("pool", lambda e: e.tensor_tensor(out=r4(pt[0:127, :]), in0=r4(pt[0:127, :]), in1=cm, op=ALU.mult), r=[ptk, self.cmask_b.k], w=[ptk])
                    return
                if kind == "s":
                    P.op("pe", lambda e: e.matmul(pst[:], lhsT=KS[0:96, g, kt * 128:(kt + 1) * 128], rhs=Q96, start=True, stop=True),
                         r=[KSk[g], Qk[qb], Nk[qb][g]], w=[pst.k])
                else:
                    P.op("pe", lambda e: e.matmul(pst[:], lhsT=KW[0:96, g, kt * 128:(kt + 1) * 128], rhs=Q96, start=True, stop=True), r=[KWk[g], Qk[qb]], w=[pst.k])
                P.op("act", lambda e: e.activation(out=pt, in_=pst[:], func=AF.Exp, scale=SCALE), r=[pst.k], w=[ptk])
                msk = None
                if kt == qbg:
                    msk = self.triL_b
                elif kind == "w" and kt == qbg - 4:
                    msk = self.triU_b
                if msk is not None:
                    tm = msk[:].unsqueeze(1).broadcast_to([128, 4, 128])
                    P.op("pool", lambda e: e.tensor_tensor(out=r4(pt), in0=r4(pt), in1=tm, op=ALU.mult), r=[ptk, msk.k], w=[ptk])

            def st2():
                pt, ptk = box["pt"], box["ptk"]
                for h in range(4):
                    if kind == "c":
                        P.op("pe", lambda e: e.matmul(poc[:, h * 128:h * 128 + 97], lhsT=pt[0:127, h * 128:(h + 1) * 128], rhs=self.VC[0:127, g, 0:97],
                                                      start=(h == 0), stop=(h == 3), skip_group_check=True), r=[ptk, self.VC.k], w=[poc.k])
                    elif kind == "s":
                        P.op("pe", lambda e: e.matmul(pos[:, h * 128:h * 128 + 65], lhsT=pt[:, h * 128:(h + 1) * 128], rhs=VS[:, kt, g, 0:65],
                                                      start=(first and h == 0), stop=(last and h == 3), skip_group_check=True), r=[ptk, VSk], w=[pos.k])
                    else:
                        P.op("pe", lambda e: e.matmul(pow_[:, h * 128:h * 128 + 65], lhsT=pt[:, h * 128:(h + 1) * 128], rhs=VW[:, kt, g, 0:65],
                                                      start=(first and h == 0), stop=(last and h == 3), skip_group_check=True), r=[ptk, VWk], w=[pow_.k])

            job["st1"], job["st2"] = st1, st2
            return job

        need_sel = (4 * n + 3) >= 8

        def select_chain(qb, g, pidx):
            qbg = 4 * n + qb
            pcs = self.pocs[pidx % 2]
            P.op("dve", lambda e: e.tensor_copy(out=pcs[:], in_=r4(poc[:])[:, :, 0:97]), r=[poc.k], w=[pcs.k])
            if not need_sel:
                return
            P.op("dve", lambda e: e.tensor_scalar(out=rd[:, 0:4], in0=pcs[:, :, 64], scalar1=1e-30, scalar2=None, op0=ALU.max), r=[pcs.k], w=[rdk])
            P.op("dve", lambda e: e.reciprocal(out=rd[:, 0:4], in_=rd[:, 0:4]), r=[rdk], w=[rdk])
            for h in range(4):
                src1 = self.selb[:, qbg, :] if h == 0 else sc
                P.op("dve", lambda e: e.scalar_tensor_tensor(out=sc, in0=pcs[:, h, 65:97], scalar=rd[:, h:h + 1], in1=src1, op0=ALU.mult, op1=ALU.add),
                     r=[pcs.k, rdk, sck, self.selb.k], w=[sck])
            P.op("dve", lambda e: e.max(out=m8[:, 0:8], in_=sc), r=[sck], w=[m8k])
            P.op("dve", lambda e: e.match_replace(out=sc2, in_to_replace=m8[:, 0:8], in_values=sc, imm_value=-3.0e38), r=[sck, m8k], w=[sc2k])
            P.op("dve", lambda e: e.max(out=m8[:, 8:16], in_=sc2), r=[sc2k], w=[m8k])
            P.op("dve", lambda e: e.tensor_scalar(out=nsl, in0=sc, scalar1=m8[:, 15:16], scalar2=NEGM, op0=ALU.is_lt, op1=ALU.mult), r=[sck, m8k], w=[nslk])

        def nsel_install(qb, g):
            if not need_sel:
                return
            pm = self.bank(self.pool_misc)
            P.op("pe", lambda e: e.transpose(out=pm[0:32, 0:128], in_=nsl, identity=self.ident_f[:]), r=[nslk, self.ident_f.k], w=[pm.k])
            P.op("act", lambda e: e.activation(out=Q[64:96, qb, 4 * g:4 * g + 4, :], in_=pm[0:32, 0:128].unsqueeze(1).broadcast_to([32, 4, 128]), func=AF.Copy),
                 r=[pm.k], w=[Nk[qb][g]])

        def evac_win():
            P.op("dve", lambda e: e.tensor_copy(out=self.pows[:], in_=r4(pow_[:])[:, :, 0:65]), r=[pow_.k], w=[self.pows.k])

        def combine(qb, g, pidx):
            pcs = self.pocs[pidx % 2]
            o, ok_ = ob[qb % 2]
            P.op("dve", lambda e: e.tensor_scalar(out=rd[:, 4:8], in0=pcs[:, :, 64], scalar1=1e-30, scalar2=None, op0=ALU.max), r=[pcs.k], w=[rdk])
            P.op("dve", lambda e: e.tensor_scalar(out=rd[:, 8:12], in0=r4(pos[:])[:, :, 64], scalar1=1e-30, scalar2=None, op0=ALU.max), r=[pos.k], w=[rdk])
            P.op("dve", lambda e: e.tensor_scalar(out=rd[:, 12:16], in0=self.pows[:, :, 64], scalar1=1e-30, scalar2=None, op0=ALU.max), r=[self.pows.k], w=[rdk])
            P.op("dve", lambda e: e.reciprocal(out=rd[:, 4:16], in_=rd[:, 4:16]), r=[rdk], w=[rdk])
            P.op("dve", lambda e: e.tensor_tensor(out=s3.rearrange("p (a b) -> p a b", a=3), in0=rd[:, 4:16].rearrange("p (a b) -> p a b", a=3),
                                                  in1=gat[:, qb, :].rearrange("p (a b) -> p a b", a=3)[:, :, 4 * g:4 * g + 4], op=ALU.mult), r=[rdk, gatk], w=[s3k])
            for h in range(4):
                ah = acc[:, h * 64:(h + 1) * 64]
                P.op("dve", lambda e: e.tensor_scalar(out=ah, in0=pcs[:, h, 0:64], scalar1=s3[:, h:h + 1], scalar2=None, op0=ALU.mult), r=[pcs.k, s3k], w=[acck])
                P.op("dve", lambda e: e.scalar_tensor_tensor(out=ah, in0=self.pows[:, h, 0:64], scalar=s3[:, 8 + h:9 + h], in1=ah, op0=ALU.mult, op1=ALU.add),
                     r=[self.pows.k, s3k, acck], w=[acck])
                P.op("dve", lambda e: e.scalar_tensor_tensor(out=o[:, g * 256 + h * 64:g * 256 + (h + 1) * 64], in0=pos[:, h * 128:h * 128 + 64], scalar=s3[:, 4 + h:5 + h], in1=ah,
                                                             op0=ALU.mult, op1=ALU.add), r=[pos.k, s3k, acck], w=[ok_])

        def o_transpose(qb):
            o, ok_ = ob[qb % 2]
            pT = self.bank(self.pool_misc)
            pTb = pT[:].bitcast(BF16)
            for c in range(NCH):
                P.op("pe", lambda e: e.transpose(out=pTb[:, c * 128:(c + 1) * 128], in_=o[:, c * 128:(c + 1) * 128], identity=self.ident_b[:]), r=[ok_, self.ident_b.k], w=[pT.k])
            P.op("act", lambda e: e.activation(out=oT[:, :, qb * 128:(qb + 1) * 128], in_=pTb.rearrange("p (a b) -> p a b", a=8), func=AF.Copy), r=[pT.k], w=[oTk])

        def cjob(pidx):
            qb, g = pairs[pidx]
            j = mk_job("c", qb, g, 0, True, True, pidx)
            j["post"].append(lambda: select_chain(qb, g, pidx))
            return j

        jobs.append(cjob(0))
        pending_T = []
        for pidx, (qb, g) in enumerate(pairs):
            qbg = 4 * n + qb
            k0 = max(0, qbg - 4)
            wj = [mk_job("w", qb, g, kt, kt == k0, kt == qbg, pidx) for kt in range(k0, qbg + 1)]
            for f in pending_T:
                wj[min(2, len(wj) - 1)]["post"].append(f)
            pending_T = []
            wj[-1]["post"].append(evac_win)
            jobs += wj
            if pidx + 1 < len(pairs):
                jobs.append(cjob(pidx + 1))
            sj = [mk_job("s", qb, g, kt, kt == 0, kt == qbg, pidx) for kt in range(qbg + 1)]
            sj[0]["pre"].append(lambda qb=qb, g=g: nsel_install(qb, g))
            sj[-1]["post"].append(lambda qb=qb, g=g, pidx=pidx: combine(qb, g, pidx))
            jobs += sj
            if g == 3:
                pending_T.append(lambda qb=qb: o_transpose(qb))
        LA = 2
        for j in range(min(LA, len(jobs))):
            for f in jobs[j]["pre"]:
                f()
            jobs[j]["st1"]()
        bg = list(bg)
        every = max(2, (len(jobs) - 8) // (len(bg) + 1)) if bg else 0
        for i in range(len(jobs)):
            if i + LA < len(jobs):
                for f in jobs[i + LA]["pre"]:
                    f()
                jobs[i + LA]["st1"]()
            jobs[i]["st2"]()
            for f in jobs[i]["post"]:
                f()
            if bg and i >= 4 and (i - 4) % every == 0:
                bg.pop(0)()
        for f in pending_T:
            f()
        for f in bg:
            f()

    def make_plan(self):
        plan = []
        ser = -1
        for b in range(self.nb):
            for layer in range(self.n_layers):
                if layer < 2:
                    ser += 1
                    plan += [(("a_in", layer, c), ser) for c in range(8)]
                    plan += [(("a_out", layer, m), ser) for m in range(8)]
                else:
                    if layer == 2:
                        ser += 1
                        plan += [(("kvk", i), ser) for i in range(4)] + [(("kvc", i), ser) for i in range(4)] + [(("kvv", i), ser) for i in range(2)]
                        plan += [(("cw2",), ser)]
                    ser += 1
                    jj = layer - 2
                    plan += [(("bg", jj), ser)] + [(("bq", jj, m), ser) for m in range(8)]
                    for n in range(NT):
                        if n + 1 < NT:
                            plan += [(("bg", jj), ser)] + [(("bq", jj, m), ser) for m in range(8)]
                        plan += [(("bo", jj, m), ser) for m in range(8)]
                ser += 1
                for t2 in range(2):
                    plan += [(("f_in", layer, c), ser) for c in range(FC)]
                    plan += [(("f_out", layer, m, hf), ser) for m in range(8) for hf in range(2)]
        return plan

    def build(self):
        P = self.P
        self.convc = self.sb("convc", [128, 8, 3], F32)
        self.hst = self.sb("hst", [128, 8], F32)
        self.prologue()
        self.plan_weights(self.make_plan())
        phase_no = {"a0": 0, "f0": 1, "a1": 2, "f1": 3, "kv": 4, "b0": 5, "f2": 6, "b1": 7, "f3": 8}
        for b in range(self.nb):
            def pre(tag):
                if b == 0:
                    self.convert_phase(phase_no[tag] + 1)
            if b == 0:
                self.convert_phase(0)
            self.mark("load%d" % b)
            self.load_x(b)
            for layer in range(self.n_layers):
                if layer < 2:
                    pre("a%d" % layer)
                    self.mark("a%d.%d" % (b, layer))
                    self.a_layer(layer)
                else:
                    if layer == 2:
                        pre("kv")
                        self.mark("kv%d" % b)
                        self.kv_phase()
                    pre("b%d" % (layer - 2))
                    self.mark("b%d.%d" % (b, layer))
                    self.b_layer(layer - 2)
                pre("f%d" % layer)
                self.mark("f%d.%d" % (b, layer))
                self.ffn_layer(layer)
            self.mark("store%d" % b)
            self.store_y(b)
        self.mark("end")
        P.wait_all("sp", self.out_trks)
        return self.nc


_CACHE = {}


def _prep_inputs(inputs):
    L = weight_layout()
    wall = pack_weights(inputs, L)
    vecs, bvecs = pack_vecs(inputs)
    cst = make_consts()
    return wall, vecs, bvecs, cst


def kernel(**inputs):
    inputs = {k: np.asarray(v) for k, v in inputs.items()}
    x = np.ascontiguousarray(inputs["x"], dtype=np.float32)
    B = x.shape[0]
    nb = B // NCORES
    wall, vecs, bvecs, cst = _prep_inputs(inputs)
    nc = Builder(nb).build()
    in_maps = []
    for c in range(NCORES):
        in_maps.append({"x": np.ascontiguousarray(x[c * nb:(c + 1) * nb]), "wall": wall, "vecs": vecs, "bvecs": bvecs, "cst": cst})
    res = run_bass_kernel_spmd(nc, in_maps, core_ids=list(range(NCORES)))
    out = np.concatenate([np.asarray(r["y"]).reshape(nb, S, D) for r in res.results], axis=0)
    return out.astype(np.float32)
```
